# Optimizing a Trainium2 kernel written in Bass

```python
import jax, jax.numpy as jnp
from jax import lax
import numpy as np

D_MODEL = 1024
BATCH = 2
SEQ = 16384
DEPTH = 1
DEC_BATCH = 8
DEC_SEQ = 4096
PAST_LEN = 128

GRID_W = 64
D_CONV = 512
CONV_K = 31
N_HEADS = 8
HEAD_DIM = 64
D_ATTN = N_HEADS * HEAD_DIM
WIN_ROWS = 8
WIN_COLS = 16
Q_COL_BLOCK = 16
K_COL_BLOCK = 32
D_IN = 2 * D_CONV + 3 * D_ATTN
PEER_HEADS = 8
PEER_KEYS = 128
PEER_EXPERTS = PEER_KEYS * PEER_KEYS
PEER_DK_HALF = 128
PEER_TOPK = 16
PEER_CHUNK = 128
EPS = 1e-6
NEG = -1e30

kernel_name = "hybrid_conv_natten_peer_encoder"


def _rmsnorm(x, g):
    x32 = x.astype(jnp.float32)
    y = x32 * lax.rsqrt(jnp.mean(x32 * x32, axis=-1, keepdims=True) + EPS)
    return (y * g.astype(jnp.float32)).astype(x.dtype)


def _layernorm(x, g, b):
    x32 = x.astype(jnp.float32)
    mu = jnp.mean(x32, axis=-1, keepdims=True)
    xc = x32 - mu
    y = xc * lax.rsqrt(jnp.mean(xc * xc, axis=-1, keepdims=True) + EPS)
    return (y * g.astype(jnp.float32) + b.astype(jnp.float32)).astype(x.dtype)


def _col_tables():
    n_cb = GRID_W // Q_COL_BLOCK
    qcol = np.arange(n_cb)[:, None] * Q_COL_BLOCK + np.arange(Q_COL_BLOCK)[None, :]
    kstart = np.clip(np.arange(n_cb) * Q_COL_BLOCK - WIN_COLS // 2, 0, GRID_W - K_COL_BLOCK)
    kcol = kstart[:, None] + np.arange(K_COL_BLOCK)[None, :]
    cs = np.clip(qcol - WIN_COLS // 2, 0, GRID_W - WIN_COLS)[..., None]
    kc = kcol[:, None, :]
    mask = (kc >= cs) & (kc < cs + WIN_COLS)
    off_idx = np.clip(kc - qcol[..., None] + WIN_COLS - 1, 0, 2 * WIN_COLS - 2)
    return kcol, mask, off_idx


def _conformer_conv(a, gate, conv_w, conv_b, ln_g, ln_b):
    u = a * jax.nn.sigmoid(gate)
    y = lax.conv_general_dilated(
        u, conv_w[:, None, :].astype(u.dtype), window_strides=(1,),
        padding=[(CONV_K // 2, CONV_K // 2)],
        dimension_numbers=("NWC", "WIO", "NWC"),
        feature_group_count=D_CONV) + conv_b.astype(u.dtype)
    y = _layernorm(y, ln_g, ln_b)
    return jax.nn.silu(y)


def _neighbourhood_attention(q, k, v, rpb):
    b, t = q.shape[0], q.shape[1]
    rows = t // GRID_W
    wr = min(WIN_ROWS, rows)
    n_cb = GRID_W // Q_COL_BLOCK
    kcol, mask, off_idx = _col_tables()
    mask = jnp.asarray(mask)[:, :, None, :]
    q = q.reshape(b, rows, GRID_W, N_HEADS, HEAD_DIM)
    k = k.reshape(b, rows, GRID_W, N_HEADS, HEAD_DIM)
    v = v.reshape(b, rows, GRID_W, N_HEADS, HEAD_DIM)
    scale = HEAD_DIM ** -0.5

    def row_step(r):
        rs = jnp.clip(r - wr // 2, 0, rows - wr)
        qr = lax.dynamic_index_in_dim(q, r, axis=1, keepdims=False)
        qr = qr.reshape(b, n_cb, Q_COL_BLOCK, N_HEADS, HEAD_DIM)
        kb = lax.dynamic_slice_in_dim(k, rs, wr, axis=1)[:, :, kcol]
        vb = lax.dynamic_slice_in_dim(v, rs, wr, axis=1)[:, :, kcol]
        roff_idx = rs + jnp.arange(wr) - r + WIN_ROWS - 1
        bias = rpb[:, roff_idx][:, :, off_idx]
        bias = jnp.transpose(bias, (0, 2, 3, 1, 4)).astype(jnp.float32)
        s = jnp.einsum("bjqhd,brjkhd->bhjqrk", qr, kb).astype(jnp.float32) * scale + bias
        s = jnp.where(mask, s, NEG)
        p = jax.nn.softmax(s.reshape(b, N_HEADS, n_cb, Q_COL_BLOCK, wr * K_COL_BLOCK), axis=-1)
        p = p.reshape(s.shape).astype(v.dtype)
        o = jnp.einsum("bhjqrk,brjkhd->bjqhd", p, vb)
        return o.reshape(b, GRID_W, D_ATTN)

    out = lax.map(row_step, jnp.arange(rows))
    return jnp.transpose(out, (1, 0, 2, 3)).reshape(b, t, D_ATTN)


def _peer(h, w_query, sub_keys, expert_u, expert_v):
    b, t, d = h.shape
    xs = h.reshape(-1, PEER_CHUNK, d)

    def chunk(xc):
        qry = (xc @ w_query).reshape(PEER_CHUNK, PEER_HEADS, 2, PEER_DK_HALF)
        s = jnp.einsum("chpd,hpkd->chpk", qry, sub_keys).astype(jnp.float32)
        sv, si = lax.top_k(s, PEER_TOPK)
        cand = (sv[:, :, 0, :, None] + sv[:, :, 1, None, :]).reshape(
            PEER_CHUNK, PEER_HEADS, PEER_TOPK * PEER_TOPK)
        cv, ci = lax.top_k(cand, PEER_TOPK)
        i1 = jnp.take_along_axis(si[:, :, 0], ci // PEER_TOPK, axis=-1)
        i2 = jnp.take_along_axis(si[:, :, 1], ci % PEER_TOPK, axis=-1)
        e = i1 * PEER_KEYS + i2
        g = jax.nn.softmax(cv, axis=-1)
        u = expert_u[e]
        a = jax.nn.gelu(jnp.einsum("cd,chkd->chk", xc, u).astype(jnp.float32), approximate=False)
        return jnp.einsum("chk,chkd->cd", (g * a).astype(xc.dtype), expert_v[e])

    return lax.map(chunk, xs).reshape(b, t, d)


def _layer(x, g_mix, w_in, conv_w, conv_b, conv_ln_g, conv_ln_b, q_norm_g, k_norm_g, rpb,
           g_out_conv, g_out_attn, w_out, g_ffn, w_query, sub_keys, expert_u, expert_v):
    b, t, _ = x.shape
    h = _rmsnorm(x, g_mix)
    proj = h @ w_in
    a, gate, q, k, v = jnp.split(
        proj, [D_CONV, 2 * D_CONV, 2 * D_CONV + D_ATTN, 2 * D_CONV + 2 * D_ATTN], axis=-1)
    y_conv = _conformer_conv(a, gate, conv_w, conv_b, conv_ln_g, conv_ln_b)
    q = _rmsnorm(q.reshape(b, t, N_HEADS, HEAD_DIM), q_norm_g)
    k = _rmsnorm(k.reshape(b, t, N_HEADS, HEAD_DIM), k_norm_g)
    v = v.reshape(b, t, N_HEADS, HEAD_DIM)
    y_attn = _neighbourhood_attention(q, k, v, rpb)
    mixed = jnp.concatenate([_rmsnorm(y_conv, g_out_conv), _rmsnorm(y_attn, g_out_attn)], axis=-1)
    x = x + mixed @ w_out
    x = x + _peer(_rmsnorm(x, g_ffn), w_query, sub_keys, expert_u, expert_v)
    return x


def setup_inputs(seed: int = 0) -> dict:
    key = jax.random.key(seed)
    ks = jax.random.split(key, 20)
    f32 = jnp.float32
    L = DEPTH

    def nrm(k, shape, std):
        return jax.random.normal(k, shape, f32) * std

    def gain(k, shape):
        return 1.0 + 0.02 * jax.random.normal(k, shape, f32)

    return {
        "x_prompt": jax.random.normal(ks[0], (BATCH, SEQ, D_MODEL), f32),
        "x_sample": jax.random.normal(ks[1], (DEC_BATCH, DEC_SEQ, D_MODEL), f32),
        "g_mix": gain(ks[2], (L, D_MODEL)),
        "w_in": nrm(ks[3], (L, D_MODEL, D_IN), D_MODEL ** -0.5),
        "conv_w": nrm(ks[4], (L, CONV_K, D_CONV), CONV_K ** -0.5),
        "conv_b": nrm(ks[5], (L, D_CONV), 0.02),
        "conv_ln_g": gain(ks[6], (L, D_CONV)),
        "conv_ln_b": nrm(ks[7], (L, D_CONV), 0.02),
        "q_norm_g": gain(ks[8], (L, HEAD_DIM)),
        "k_norm_g": gain(ks[9], (L, HEAD_DIM)),
        "rpb": nrm(ks[10], (L, N_HEADS, 2 * WIN_ROWS - 1, 2 * WIN_COLS - 1), 0.1),
        "g_out_conv": gain(ks[11], (L, D_CONV)),
        "g_out_attn": gain(ks[12], (L, D_ATTN)),
        "w_out": nrm(ks[13], (L, D_CONV + D_ATTN, D_MODEL), (D_CONV + D_ATTN) ** -0.5),
        "g_ffn": gain(ks[14], (L, D_MODEL)),
        "w_query": nrm(ks[15], (L, D_MODEL, PEER_HEADS * 2 * PEER_DK_HALF), D_MODEL ** -0.5),
        "sub_keys": nrm(ks[16], (L, PEER_HEADS, 2, PEER_KEYS, PEER_DK_HALF), PEER_DK_HALF ** -0.5),
        "expert_u": nrm(ks[17], (L, PEER_EXPERTS, D_MODEL), D_MODEL ** -0.5),
        "expert_v": nrm(ks[18], (L, PEER_EXPERTS, D_MODEL), 0.25),
    }


def reference(x_prompt, x_sample, g_mix, w_in, conv_w, conv_b, conv_ln_g, conv_ln_b,
              q_norm_g, k_norm_g, rpb, g_out_conv, g_out_attn, w_out, g_ffn,
              w_query, sub_keys, expert_u, expert_v):
    y_prompt = x_prompt
    y_sample = x_sample
    for l in range(DEPTH):
        p = (g_mix[l], w_in[l], conv_w[l], conv_b[l], conv_ln_g[l], conv_ln_b[l],
             q_norm_g[l], k_norm_g[l], rpb[l], g_out_conv[l], g_out_attn[l], w_out[l],
             g_ffn[l], w_query[l], sub_keys[l], expert_u[l], expert_v[l])
        y_prompt = _layer(y_prompt, *p)
        y_sample = _layer(y_sample, *p)
    return (y_prompt, y_sample)
```

```python
import numpy as np
import concourse.bass as bass
import concourse.mybir as mybir
from concourse.bass_utils import run_bass_kernel_spmd

F32 = mybir.dt.float32
BF16 = mybir.dt.bfloat16
I32 = mybir.dt.int32
U32 = mybir.dt.uint32
ALU = mybir.AluOpType
AF = mybir.ActivationFunctionType
AX = mybir.AxisListType

D = 1024
NEXP = 16384
EPS = 1e-6
NEGM = -30000.0


class Buf:
    __slots__ = ("name", "w", "r")

    def __init__(self, name=""):
        self.name = name
        self.w = None
        self.r = []


class Op:
    __slots__ = ("eng", "fn", "deps", "signal", "semkey", "count", "is_dma")

    def __init__(self, eng, fn):
        self.eng = eng
        self.fn = fn
        self.deps = []
        self.signal = False
        self.semkey = None
        self.count = 0
        self.is_dma = False


class Sched:
    ENGS = ("pe", "dve", "act", "pool", "sp")
    SELF_SYNC = {"pe": False, "dve": True, "act": True, "pool": True, "sp": False}

    def __init__(self, nc, n_dma_sems=None):
        self.nc = nc
        self.ops = {e: [] for e in self.ENGS}
        self.n_dma_sems = n_dma_sems or {"sp": 16, "act": 8, "pool": 24}
        self.dma_rr = {}
        self.dma_last = {}
        self.dma_cnt = {}
        self.last_real = {}

    def _add_deps(self, op, reads, writes):
        deps = []
        for b in reads:
            if b.w is not None:
                deps.append(b.w)
        for b in writes:
            if b.w is not None:
                deps.append(b.w)
            deps.extend(b.r)
        for d in deps:
            if d is op:
                continue
            if (not d.is_dma) and d.eng == op.eng and not self.SELF_SYNC[op.eng]:
                continue
            op.deps.append(d)
            d.signal = True
        for b in reads:
            b.r.append(op)
        for b in writes:
            b.w = op
            b.r = []

    def op(self, eng, fn, reads=(), writes=()):
        o = Op(eng, fn)
        self._add_deps(o, reads, writes)
        self.ops[eng].append(o)
        self.last_real[eng] = o
        return o

    def dma(self, eng, fn, reads=(), writes=()):
        o = Op(eng, fn)
        o.is_dma = True
        o.signal = True
        rr = self.dma_rr.get(eng, 0)
        self.dma_rr[eng] = rr + 1
        key = ("dma", eng, rr % self.n_dma_sems[eng])
        o.semkey = key
        prev = self.dma_last.get(key)
        if prev is not None:
            o.deps.append(prev)
        self.dma_last[key] = o
        c = self.dma_cnt.get(key, 0) + 16
        self.dma_cnt[key] = c
        o.count = c
        self._add_deps(o, reads, writes)
        self.ops[eng].append(o)
        return o

    def barrier(self):
        lasts = [o for o in self.last_real.values()] + list(self.dma_last.values())
        for e in self.ENGS:
            o = Op(e, None)
            for d in lasts:
                if (not d.is_dma) and d.eng == e:
                    continue
                o.deps.append(d)
                d.signal = True
            self.ops[e].append(o)

    def emit(self, final_wait_eng="sp"):
        nc = self.nc
        self.barrier()
        for e in self.ENGS:
            c = 0
            for o in self.ops[e]:
                if o.is_dma or o.fn is None:
                    continue
                o.semkey = ("eng", e)
                if o.signal:
                    c += 1
                    o.count = c
        keys = set()
        for e in self.ENGS:
            for o in self.ops[e]:
                if o.signal and o.fn is not None:
                    keys.add(o.semkey)
        sems = {}
        for k in sorted(keys, key=str):
            sems[k] = nc.alloc_semaphore(name="s_" + "_".join(str(x) for x in k))
        stats = {"ins": 0, "wait": 0}
        with nc.Block() as block:
            deco = {"pe": block.tensor, "dve": block.vector, "act": block.scalar,
                    "pool": block.gpsimd, "sp": block.sync}
            for e in self.ENGS:
                ops = self.ops[e]

                def body(eng, ops=ops):
                    seen = {}
                    for o in ops:
                        for d in o.deps:
                            if seen.get(d.semkey, 0) < d.count:
                                eng.wait_ge(sems[d.semkey], d.count)
                                seen[d.semkey] = d.count
                                stats["wait"] += 1
                        if o.fn is None:
                            continue
                        ins = o.fn(eng)
                        stats["ins"] += 1
                        if o.signal:
                            ins.then_inc(sems[o.semkey], 16 if o.is_dma else 1)

                deco[e](body)
        self.stats = stats


class Arena:
    def __init__(self, handle, nwords):
        self.h = handle
        self.n = nwords
        self.off = 0

    def alloc(self, free_shape, dtype=F32, align=16):
        n = 1
        for s in free_shape:
            n *= s
        size = 4 if dtype in (F32, I32, U32) else 2
        words = (n * size + 3) // 4
        self.off = (self.off + align - 1) // align * align
        assert self.off + words <= self.n, ("arena overflow", self.off, words, self.n)
        ap = self.h[:, self.off:self.off + words]
        self.off += words
        if dtype != F32:
            ap = ap.bitcast(dtype)
            if ap.shape[1] != n:
                ap = ap[:, 0:n]
        if len(free_shape) == 2:
            ap = ap.rearrange("p (a b) -> p a b", a=free_shape[0])
        elif len(free_shape) == 3:
            ap = ap.rearrange("p (a b c) -> p a b c", a=free_shape[0], b=free_shape[1])
        return ap


def phase_b(S, nc, sb, ps, C, x1_tiles, y_tiles, NB=6):
    ident = sb.alloc([128], BF16); b_ident = Buf()
    iota16 = sb.alloc([16], F32); b_iota = Buf()
    cst = sb.alloc([144], F32); b_cst = Buf()
    wq = sb.alloc([8, 2048], BF16); b_wq = Buf()
    skT = sb.alloc([16, 128], BF16); b_skT = Buf()
    gffn = sb.alloc([1024], F32); b_gffn = Buf()
    S.dma("sp", lambda e: e.dma_start(out=cst, in_=C["consts"]), writes=[b_cst])
    S.op("dve", lambda e: e.tensor_copy(out=ident, in_=cst[:, 0:128]), reads=[b_cst], writes=[b_ident])
    S.op("dve", lambda e: e.tensor_copy(out=iota16, in_=cst[:, 128:144]), reads=[b_cst], writes=[b_iota])
    S.dma("sp", lambda e: e.dma_start(out=gffn, in_=C["g_ffn"].partition_broadcast(128)), writes=[b_gffn])
    stg = [sb.alloc([2048], F32) for _ in range(2)]
    b_stg = [Buf(), Buf()]
    wqv = C["w_query"].rearrange("(c p) n -> c p n", p=128)
    for c in range(8):
        S.dma("sp", lambda e, c=c: e.dma_start(out=stg[c % 2], in_=wqv[c]), writes=[b_stg[c % 2]])
        S.op("act" if c % 2 else "dve",
             (lambda e, c=c: e.copy(out=wq[:, c, :], in_=stg[c % 2])) if c % 2 else
             (lambda e, c=c: e.tensor_copy(out=wq[:, c, :], in_=stg[c % 2])),
             reads=[b_stg[c % 2]], writes=[b_wq])
    skv = C["sub_keys"].rearrange("g k d -> k g d")
    skn = stg[0].rearrange("p (g d) -> p g d", g=16)
    skb = sb.alloc([16, 128], BF16); b_skb = Buf()
    pbig = ps.alloc([16, 128], F32, align=512); b_pbig = Buf()
    pT = ps.alloc([1024], BF16, align=512); b_pT = Buf()
    pTv = pT.rearrange("p (c n) -> p c n", c=8)
    S.dma("sp", lambda e: e.dma_start(out=skn, in_=skv), writes=[b_stg[0]])
    S.op("dve", lambda e: e.tensor_copy(out=skb, in_=skn), reads=[b_stg[0]], writes=[b_skb])
    for half in range(2):
        for j in range(8):
            g = half * 8 + j
            S.op("pe", lambda e, g=g, j=j: e.transpose(out=pTv[:, j, :], in_=skb[:, g, :], identity=ident),
                 reads=[b_skb, b_ident], writes=[b_pT])
        S.op("dve", lambda e, half=half: e.tensor_copy(out=skT[:, half * 8:(half + 1) * 8, :], in_=pTv),
             reads=[b_pT], writes=[b_skT])

    x1t = [sb.alloc([1024], F32) for _ in range(2)]; b_x1t = [Buf(), Buf()]
    junk = sb.alloc([1024], F32); b_junk = Buf()
    hn = sb.alloc([1024], F32); b_hn = Buf()
    hb = sb.alloc([1024], BF16); b_hb = Buf()
    hT = sb.alloc([8, 128], BF16); b_hT = Buf()
    qT = sb.alloc([16, 128], BF16); b_qT = Buf()
    s = sb.alloc([16, 128], F32); b_s = Buf()
    s2 = sb.alloc([16, 128], F32); b_s2 = Buf()
    sv = sb.alloc([16, 16], F32); b_sv = Buf()
    siu = sb.alloc([16, 16], U32); b_siu = Buf()
    sif = sb.alloc([16, 16], F32); b_sif = Buf()
    cand = sb.alloc([8, 256], F32); b_cand = Buf()
    cand2 = sb.alloc([8, 256], F32); b_cand2 = Buf()
    cv = sb.alloc([8, 16], F32); b_cv = Buf()
    ciu = sb.alloc([8, 16], U32); b_ciu = Buf()
    rcu = sb.alloc([2, 128], U32); b_rcu = Buf()
    rcf = sb.alloc([2, 128], F32); b_rcf = Buf()
    oh = sb.alloc([128, 16], F32); b_oh = Buf()
    i12 = sb.alloc([2, 128], F32); b_i12 = Buf()
    ef = sb.alloc([128], F32); b_ef = Buf()
    ei = [sb.alloc([128], I32) for _ in range(2)]; b_ei = [Buf(), Buf()]
    araw = sb.alloc([128], F32); b_araw = Buf()
    aw = sb.alloc([128], F32); b_aw = Buf()
    gsm = sb.alloc([8, 16], F32); b_gsm = Buf()
    small = sb.alloc([32], F32); b_small = Buf()
    acc = [sb.alloc([1024], F32) for _ in range(2)]; b_acc = [Buf(), Buf()]
    ub = [sb.alloc([1024], F32) for _ in range(NB)]; b_ub = [Buf() for _ in range(NB)]
    vb = [sb.alloc([1024], F32) for _ in range(NB)]; b_vb = [Buf() for _ in range(NB)]
    gcount = [0, 0]

    sv4 = sv.rearrange("p (h t) k -> p h t k", t=2)
    sif4 = sif.rearrange("p (h t) k -> p h t k", t=2)
    cand4 = cand.rearrange("p h (i j) -> p h i j", i=16)
    oh4 = oh.rearrange("p (h k) i -> p h k i", h=8)

    for ti in range(len(x1_tiles)):
        xb = x1t[ti % 2]; bxb = b_x1t[ti % 2]
        eib = ei[ti % 2]; beib = b_ei[ti % 2]
        ac = acc[ti % 2]; bac = b_acc[ti % 2]
        S.dma("sp", lambda e, xb=xb, ti=ti: e.dma_start(out=xb, in_=x1_tiles[ti]), writes=[bxb])
        S.op("act", lambda e, xb=xb: e.activation(out=junk, in_=xb, func=AF.Square, accum_out=small[:, 0:1]),
             reads=[bxb], writes=[b_junk, b_small])
        S.op("act", lambda e: e.activation(out=small[:, 1:2], in_=small[:, 0:1], func=AF.Sqrt, bias=EPS, scale=1.0 / D),
             reads=[b_small], writes=[b_small])
        S.op("dve", lambda e: e.reciprocal(out=small[:, 2:3], in_=small[:, 1:2]), reads=[b_small], writes=[b_small])
        S.op("dve", lambda e, xb=xb: e.scalar_tensor_tensor(out=hn, in0=xb, scalar=small[:, 2:3], in1=gffn,
                                                            op0=ALU.mult, op1=ALU.mult),
             reads=[bxb, b_small, b_gffn], writes=[b_hn])
        S.op("act", lambda e: e.copy(out=hb, in_=hn), reads=[b_hn], writes=[b_hb])
        for c in range(8):
            S.op("pe", lambda e, c=c: e.transpose(out=pTv[:, c, :], in_=hb[:, c * 128:(c + 1) * 128], identity=ident),
                 reads=[b_hb, b_ident], writes=[b_pT])
        S.op("act", lambda e: e.copy(out=hT, in_=pTv), reads=[b_pT], writes=[b_hT])
        for g in range(16):
            for c in range(8):
                S.op("pe", lambda e, g=g, c=c: e.matmul(pbig[:, g, :], lhsT=wq[:, c, g * 128:(g + 1) * 128],
                                                        rhs=hT[:, c, :], start=(c == 0), stop=(c == 7)),
                     reads=[b_wq, b_hT], writes=[b_pbig])
        S.op("act", lambda e: e.copy(out=qT, in_=pbig), reads=[b_pbig], writes=[b_qT])
        for g in range(16):
            S.op("pe", lambda e, g=g: e.matmul(pbig[:, g, :], lhsT=qT[:, g, :], rhs=skT[:, g, :], start=True, stop=True),
                 reads=[b_qT, b_skT], writes=[b_pbig])
        S.op("dve", lambda e: e.tensor_copy(out=s, in_=pbig), reads=[b_pbig], writes=[b_s])
        for g in range(16):
            S.op("dve", lambda e, g=g: e.max(out=sv[:, g, 0:8], in_=s[:, g, :]), reads=[b_s], writes=[b_sv])
            S.op("dve", lambda e, g=g: e.match_replace(out=s2[:, g, :], in_to_replace=sv[:, g, 0:8],
                                                       in_values=s[:, g, :], imm_value=-1e30),
                 reads=[b_s, b_sv], writes=[b_s2])
            S.op("dve", lambda e, g=g: e.max(out=sv[:, g, 8:16], in_=s2[:, g, :]), reads=[b_s2], writes=[b_sv])
            S.op("dve", lambda e, g=g: e.max_index(out=siu[:, g, 0:8], in_max=sv[:, g, 0:8], in_values=s[:, g, :]),
                 reads=[b_s, b_sv], writes=[b_siu])
            S.op("dve", lambda e, g=g: e.max_index(out=siu[:, g, 8:16], in_max=sv[:, g, 8:16], in_values=s[:, g, :]),
                 reads=[b_s, b_sv], writes=[b_siu])
        S.op("dve", lambda e: e.tensor_copy(out=sif, in_=siu), reads=[b_siu], writes=[b_sif])
        for h in range(8):
            S.op("dve", lambda e, h=h: e.tensor_tensor(
                out=cand4[:, h], in0=sv4[:, h, 0, :].unsqueeze(2).broadcast_to([128, 16, 16]),
                in1=sv4[:, h, 1, :].unsqueeze(1).broadcast_to([128, 16, 16]), op=ALU.add),
                 reads=[b_sv], writes=[b_cand])
        for h in range(8):
            S.op("dve", lambda e, h=h: e.max(out=cv[:, h, 0:8], in_=cand[:, h, :]), reads=[b_cand], writes=[b_cv])
            S.op("dve", lambda e, h=h: e.match_replace(out=cand2[:, h, :], in_to_replace=cv[:, h, 0:8],
                                                       in_values=cand[:, h, :], imm_value=-1e30),
                 reads=[b_cand, b_cv], writes=[b_cand2])
            S.op("dve", lambda e, h=h: e.max(out=cv[:, h, 8:16], in_=cand2[:, h, :]), reads=[b_cand2], writes=[b_cv])
            S.op("dve", lambda e, h=h: e.max_index(out=ciu[:, h, 0:8], in_max=cv[:, h, 0:8], in_values=cand[:, h, :]),
                 reads=[b_cand, b_cv], writes=[b_ciu])
            S.op("dve", lambda e, h=h: e.max_index(out=ciu[:, h, 8:16], in_max=cv[:, h, 8:16], in_values=cand[:, h, :]),
                 reads=[b_cand, b_cv], writes=[b_ciu])
        ciu_f = ciu.rearrange("p h k -> p (h k)")
        S.op("dve", lambda e: e.tensor_single_scalar(out=rcu[:, 0, :], in_=ciu_f, scalar=4, op=ALU.logical_shift_right),
             reads=[b_ciu], writes=[b_rcu])
        S.op("dve", lambda e: e.tensor_single_scalar(out=rcu[:, 1, :], in_=ciu_f, scalar=15, op=ALU.bitwise_and),
             reads=[b_ciu], writes=[b_rcu])
        S.op("dve", lambda e: e.tensor_copy(out=rcf, in_=rcu), reads=[b_rcu], writes=[b_rcf])
        for t in range(2):
            S.op("dve", lambda e, t=t: e.tensor_tensor(
                out=oh, in0=rcf[:, t, :].unsqueeze(2).broadcast_to([128, 128, 16]),
                in1=iota16.unsqueeze(1).broadcast_to([128, 128, 16]), op=ALU.is_equal),
                 reads=[b_rcf, b_iota], writes=[b_oh])
            for h in range(8):
                S.op("dve", lambda e, h=h, t=t: e.tensor_tensor(
                    out=oh4[:, h], in0=oh4[:, h], in1=sif4[:, h, t, :].unsqueeze(1).broadcast_to([128, 16, 16]),
                    op=ALU.mult), reads=[b_oh, b_sif], writes=[b_oh])
            S.op("dve", lambda e, t=t: e.tensor_reduce(out=i12[:, t, :], in_=oh, axis=AX.X, op=ALU.add),
                 reads=[b_oh], writes=[b_i12])
        S.op("dve", lambda e: e.scalar_tensor_tensor(out=ef, in0=i12[:, 0, :], scalar=128.0, in1=i12[:, 1, :],
                                                     op0=ALU.mult, op1=ALU.add), reads=[b_i12], writes=[b_ef])
        S.op("dve", lambda e, eib=eib: e.tensor_copy(out=eib, in_=ef), reads=[b_ef], writes=[beib])
        S.op("dve", lambda e: e.tensor_tensor(out=gsm, in0=cv, in1=cv[:, :, 0:1].broadcast_to([128, 8, 16]),
                                              op=ALU.subtract), reads=[b_cv], writes=[b_gsm])
        S.op("act", lambda e: e.activation(out=gsm, in_=gsm, func=AF.Exp), reads=[b_gsm], writes=[b_gsm])
        S.op("dve", lambda e: e.tensor_reduce(out=small[:, 8:16], in_=gsm, axis=AX.X, op=ALU.add),
             reads=[b_gsm], writes=[b_small])
        S.op("dve", lambda e: e.reciprocal(out=small[:, 16:24], in_=small[:, 8:16]), reads=[b_small], writes=[b_small])
        S.op("dve", lambda e: e.tensor_tensor(out=gsm, in0=gsm,
                                              in1=small[:, 16:24].unsqueeze(2).broadcast_to([128, 8, 16]),
                                              op=ALU.mult), reads=[b_gsm, b_small], writes=[b_gsm])
        for k in range(128):
            j = gcount[0] % NB; gcount[0] += 1
            S.dma("pool", lambda e, k=k, j=j, eib=eib: e.indirect_dma_start(
                out=ub[j], out_offset=None, in_=C["expert_u"],
                in_offset=bass.IndirectOffsetOnAxis(ap=eib[:, k:k + 1], axis=0)),
                  reads=[beib], writes=[b_ub[j]])
            S.op("dve", lambda e, k=k, j=j: e.scalar_tensor_tensor(
                out=ub[j], in0=ub[j], scalar=1.0, in1=hn, op0=ALU.mult, op1=ALU.mult, accum_out=araw[:, k:k + 1]),
                 reads=[b_hn], writes=[b_ub[j]] + ([b_araw] if k in (0, 127) else []))
        S.op("act", lambda e: e.activation(out=aw, in_=araw, func=AF.Gelu), reads=[b_araw], writes=[b_aw])
        S.op("dve", lambda e: e.tensor_tensor(out=aw, in0=aw, in1=gsm.rearrange("p h k -> p (h k)"), op=ALU.mult),
             reads=[b_aw, b_gsm], writes=[b_aw])
        for k in range(128):
            j = gcount[1] % NB; gcount[1] += 1
            S.dma("pool", lambda e, k=k, j=j, eib=eib: e.indirect_dma_start(
                out=vb[j], out_offset=None, in_=C["expert_v"],
                in_offset=bass.IndirectOffsetOnAxis(ap=eib[:, k:k + 1], axis=0)),
                  reads=[beib], writes=[b_vb[j]])
            src = xb if k == 0 else ac
            S.op("dve", lambda e, k=k, j=j, src=src, ac=ac: e.scalar_tensor_tensor(
                out=ac, in0=vb[j], scalar=aw[:, k:k + 1], in1=src, op0=ALU.mult, op1=ALU.add),
                 reads=[b_vb[j], b_aw] + ([bxb] if k == 0 else []), writes=[bac])
        S.dma("sp", lambda e, ac=ac, ti=ti: e.dma_start(out=y_tiles[ti], in_=ac), reads=[bac])


def make_consts():
    c = np.zeros((128, 144), np.float32)
    c[:, :128] = np.eye(128, dtype=np.float32)
    c[:, 128:144] = np.arange(16, dtype=np.float32)[None, :]
    return c


PP_GMIX, PP_GOUT, PP_CB, PP_LNG, PP_LNB, PP_GQ, PP_GK, PP_CW, PP_N = 0, 8, 16, 20, 24, 28, 29, 32, 160
RING = 8
URING = 6
QRING = 5


def key_tiles(s, NQ):
    kts = list(range(s - 2, s + 3))
    if s == 2:
        kts.append(5)
    if s == NQ + 1:
        kts.insert(0, NQ - 2)
    return kts


def phase_a(S, nc, sb, ps, C, NQ, xs_tiles, x1_tiles):
    NS = NQ + 4
    nslab = len(xs_tiles)
    cst = sb.alloc([144], F32); b_cst = Buf()
    ident = sb.alloc([128], BF16); b_ident = Buf()
    onesf = sb.alloc([128], F32); b_onesf = Buf()
    blk = sb.alloc([128], BF16); b_blk = Buf()
    pp = sb.alloc([PP_N], F32); b_pp = Buf()
    gq8 = sb.alloc([1], F32); b_gq8 = Buf()
    win = sb.alloc([8, 2560], BF16); b_win = Buf()
    wout = sb.alloc([8, 1024], BF16); b_wout = Buf()
    T = sb.alloc([8, 16, 64], F32); b_T = Buf()
    ntile_idx = {}
    for s in range(2, NQ + 2):
        for kt in key_tiles(s, NQ):
            ntile_idx[(s, kt)] = len(ntile_idx)
    NTI = len(ntile_idx)
    rmask = sb.alloc([nslab, NTI, 2], F32); b_rmask = Buf()
    S.dma("sp", lambda e: e.dma_start(out=cst, in_=C["consts"]), writes=[b_cst])
    S.dma("sp", lambda e: e.dma_start(out=pp, in_=C["pp"]), writes=[b_pp])
    S.dma("sp", lambda e: e.dma_start(out=T.rearrange("p h j c -> p (h j c)"), in_=C["ttab"]), writes=[b_T])
    S.dma("sp", lambda e: e.dma_start(out=rmask.rearrange("p a b c -> p (a b c)"), in_=C["rmask"]), writes=[b_rmask])
    S.op("dve", lambda e: e.tensor_copy(out=ident, in_=cst[:, 0:128]), reads=[b_cst], writes=[b_ident])
    S.op("dve", lambda e: e.memset(onesf, 1.0), writes=[b_onesf])
    S.op("dve", lambda e: e.memset(blk, 0.0), writes=[b_blk])
    S.op("dve", lambda e: e.memset(blk[0:64, 0:64], 1.0), writes=[b_blk])
    S.op("dve", lambda e: e.memset(blk[64:128, 64:128], 1.0), writes=[b_blk])
    S.op("dve", lambda e: e.tensor_scalar(out=gq8, in0=pp[:, PP_GQ:PP_GQ + 1], scalar1=0.125, scalar2=None, op0=ALU.mult),
         reads=[b_pp], writes=[b_gq8])
    stg = [sb.alloc([1280], F32) for _ in range(2)]; b_stg = [Buf(), Buf()]
    winv = C["w_in"].rearrange("(c p) n -> c p n", p=128)
    woutv = C["w_out"].rearrange("(c p) n -> c p n", p=128)
    n = 0
    for c in range(8):
        for hf in range(2):
            j = n % 2; n += 1
            S.dma("sp", lambda e, c=c, hf=hf, j=j: e.dma_start(out=stg[j], in_=winv[c][:, hf * 1280:(hf + 1) * 1280]),
                  writes=[b_stg[j]])
            S.op("dve", lambda e, c=c, hf=hf, j=j: e.tensor_scalar(
                out=win[:, c, hf * 1280:(hf + 1) * 1280], in0=stg[j], scalar1=pp[:, PP_GMIX + c:PP_GMIX + c + 1],
                scalar2=None, op0=ALU.mult), reads=[b_stg[j], b_pp], writes=[b_win])
    for c in range(8):
        j = n % 2; n += 1
        S.dma("sp", lambda e, c=c, j=j: e.dma_start(out=stg[j][:, 0:1024], in_=woutv[c]), writes=[b_stg[j]])
        S.op("dve", lambda e, c=c, j=j: e.tensor_scalar(
            out=wout[:, c, :], in0=stg[j][:, 0:1024], scalar1=pp[:, PP_GOUT + c:PP_GOUT + c + 1],
            scalar2=None, op0=ALU.mult), reads=[b_stg[j], b_pp], writes=[b_wout])
    cw = pp[:, PP_CW:PP_CW + 124].rearrange("p (c k) -> p c k", c=4)

    kT = [sb.alloc([4, 128], BF16) for _ in range(RING)]; b_kT = [Buf() for _ in range(RING)]
    va = [sb.alloc([8, 65], BF16) for _ in range(RING)]; b_va = [Buf() for _ in range(RING)]
    uT = [sb.alloc([4, 160], F32) for _ in range(URING)]; b_uT = [Buf() for _ in range(URING)]
    qA = [sb.alloc([4, 128], BF16) for _ in range(QRING)]; b_qA = [Buf() for _ in range(QRING)]
    qB = [sb.alloc([4, 128], BF16) for _ in range(QRING)]; b_qB = [Buf() for _ in range(QRING)]
    for r in range(RING):
        S.op("pool", lambda e, r=r: e.memset(va[r], 1.0), writes=[b_va[r]])
    for r in range(QRING):
        S.op("pool", lambda e, r=r: e.memset(qA[r], 0.0), writes=[b_qA[r]])
        S.op("pool", lambda e, r=r: e.memset(qB[r], 0.0), writes=[b_qB[r]])
    for r in range(URING):
        S.op("pool", lambda e, r=r: e.memset(uT[r], 0.0), writes=[b_uT[r]])
    xt = [sb.alloc([1024], F32) for _ in range(2)]; b_xt = [Buf(), Buf()]
    xr = [sb.alloc([1024], F32) for _ in range(2)]; b_xr = [Buf(), Buf()]
    junk = sb.alloc([1024], BF16); b_junk = Buf()
    small = sb.alloc([16], F32); b_small = Buf()
    xn = sb.alloc([1024], BF16); b_xn = Buf()
    hT = sb.alloc([8, 128], BF16); b_hT = Buf()
    qkr = sb.alloc([8, 128], F32); b_qkr = Buf()
    sq = sb.alloc([8, 128], BF16); b_sq = Buf()
    qks = sb.alloc([8, 128], F32); b_qks = Buf()
    sg = sb.alloc([4, 128], F32); b_sg = Buf()
    NTMP = 3
    tmp = [sb.alloc([128], F32) for _ in range(NTMP)]; b_tmp = [Buf() for _ in range(NTMP)]
    PT = [sb.alloc([128], BF16) for _ in range(NTMP)]; b_PT = [Buf() for _ in range(NTMP)]
    ya = sb.alloc([8, 64], F32); b_ya = Buf()
    yan = sb.alloc([512], BF16); b_yan = Buf()
    yaT = sb.alloc([4, 128], BF16); b_yaT = Buf()
    yconv = sb.alloc([4, 128], F32); b_yc = [Buf() for _ in range(4)]
    csq = sb.alloc([4, 128], F32); b_csq = Buf()
    ctmp = sb.alloc([128], F32); b_ctmp = Buf()
    st = sb.alloc([4, 128], F32); b_st = Buf()
    z = sb.alloc([4, 128], F32); b_z = Buf()
    ycn = sb.alloc([4, 128], BF16); b_ycn = Buf()
    bank0 = ps.alloc([512], F32, align=512); b_b0 = Buf()
    pT = bank0.bitcast(BF16); b_pT = b_b0
    pTv = pT.rearrange("p (c n) -> p c n", c=8)
    pV = bank0; b_pV = b_b0
    pX = ps.alloc([8, 128], F32, align=512); b_pX = Buf()
    pS = ps.alloc([8, 128], F32, align=512); b_pS = Buf()
    pY = pS.rearrange("p a b -> p (a b)")
    pMa = ps.alloc([4, 128], F32, align=512)
    pMb = ps.alloc([4, 128], F32, align=512)
    pMs = [pMa[:, 0, :], pMb[:, 0, :]]; b_pMs = [Buf(), Buf()]
    pO = ps.alloc([4, 128], F32, align=512); b_pO = Buf()
    pM = pO; b_pM = [b_pO, b_pO, b_pO, b_pO]

    def stage_p(sl, s):
        x = xt[s % 2]; bx = b_xt[s % 2]
        r = s % RING
        S.dma("sp", lambda e: e.dma_start(out=x, in_=xs_tiles[sl][s]), writes=[bx])
        S.op("act", lambda e: e.activation(out=junk, in_=x, func=AF.Square, accum_out=small[:, 0:1]),
             reads=[bx], writes=[b_junk, b_small])
        S.op("act", lambda e: e.activation(out=small[:, 1:2], in_=small[:, 0:1], func=AF.Sqrt, bias=EPS, scale=1.0 / D),
             reads=[b_small], writes=[b_small])
        S.op("dve", lambda e: e.reciprocal(out=small[:, 2:3], in_=small[:, 1:2]), reads=[b_small], writes=[b_small])
        S.op("dve", lambda e: e.tensor_scalar(out=xn, in0=x, scalar1=small[:, 2:3], scalar2=None, op0=ALU.mult),
             reads=[bx, b_small], writes=[b_xn])
        for c in range(8):
            S.op("pe", lambda e, c=c: e.transpose(out=pTv[:, c, :], in_=xn[:, c * 128:(c + 1) * 128], identity=ident),
                 reads=[b_xn, b_ident], writes=[b_pT])
        S.op("act", lambda e: e.copy(out=hT, in_=pTv), reads=[b_pT], writes=[b_hT])
        for j in range(8):
            for c in range(8):
                S.op("pe", lambda e, j=j, c=c: e.matmul(pX[:, j, :], lhsT=win[:, c, j * 128:(j + 1) * 128], rhs=hT[:, c, :],
                                                        start=(c == 0), stop=(c == 7)),
                     reads=[b_win, b_hT], writes=[b_pX])
        ur = s % URING
        S.op("act", lambda e: e.activation(out=sg, in_=pX[:, 4:8, :], func=AF.Sigmoid), reads=[b_pX], writes=[b_sg])
        S.op("dve", lambda e: e.tensor_tensor(out=uT[ur][:, :, 16:144], in0=pX[:, 0:4, :], in1=sg, op=ALU.mult),
             reads=[b_pX, b_sg], writes=[b_uT[ur]])
        if s >= 1:
            up = (s - 1) % URING
            S.op("pool", lambda e: e.tensor_copy(out=uT[up][:, :, 144:159], in_=uT[ur][:, :, 16:31]),
                 reads=[b_uT[ur]], writes=[b_uT[up]])
        if s + 1 < NS:
            un = (s + 1) % URING
            S.op("pool", lambda e: e.tensor_copy(out=uT[un][:, :, 1:16], in_=uT[ur][:, :, 129:144]),
                 reads=[b_uT[ur]], writes=[b_uT[un]])
        for j in range(8):
            for c in range(8):
                S.op("pe", lambda e, j=j, c=c: e.matmul(pX[:, j, :], lhsT=win[:, c, 1024 + j * 128:1024 + (j + 1) * 128],
                                                        rhs=hT[:, c, :], start=(c == 0), stop=(c == 7)),
                     reads=[b_win, b_hT], writes=[b_pX])
        S.op("act", lambda e: e.copy(out=qkr, in_=pX), reads=[b_pX], writes=[b_qkr])
        S.op("act", lambda e: e.activation(out=sq, in_=qkr, func=AF.Square), reads=[b_qkr], writes=[b_sq])
        for j in range(8):
            S.op("pe", lambda e, j=j: e.matmul(pS[:, j, :], lhsT=blk, rhs=sq[:, j, :], start=True, stop=True),
                 reads=[b_blk, b_sq], writes=[b_pS])
        S.op("act", lambda e: e.activation(out=qks, in_=pS, func=AF.Sqrt, bias=EPS, scale=1.0 / 64), reads=[b_pS], writes=[b_qks])
        S.op("dve", lambda e: e.reciprocal(out=qks, in_=qks), reads=[b_qks], writes=[b_qks])
        S.op("dve", lambda e: e.tensor_tensor(out=qkr, in0=qkr, in1=qks, op=ALU.mult), reads=[b_qkr, b_qks], writes=[b_qkr])
        S.op("dve", lambda e: e.tensor_scalar(out=kT[r], in0=qkr[:, 4:8, :], scalar1=pp[:, PP_GK:PP_GK + 1], scalar2=None,
                                              op0=ALU.mult), reads=[b_qkr, b_pp], writes=[b_kT[r]])
        if 2 <= s < NQ + 2:
            qr = s % QRING
            S.op("dve", lambda e: e.tensor_scalar(out=qA[qr][0:64], in0=qkr[0:64, 0:4, :], scalar1=gq8[0:64], scalar2=None,
                                                  op0=ALU.mult), reads=[b_qkr, b_gq8], writes=[b_qA[qr]])
            S.op("dve", lambda e: e.tensor_scalar(out=qB[qr][64:128], in0=qkr[64:128, 0:4, :], scalar1=gq8[64:128],
                                                  scalar2=None, op0=ALU.mult), reads=[b_qkr, b_gq8], writes=[b_qB[qr]])
        for c in range(8):
            S.op("pe", lambda e, c=c: e.matmul(pV, lhsT=hT[:, c, :], rhs=win[:, c, 2048:2560], start=(c == 0), stop=(c == 7)),
                 reads=[b_win, b_hT], writes=[b_pV])
        S.op("act", lambda e: e.copy(out=va[r][:, :, 0:64], in_=pV.rearrange("p (h d) -> p h d", h=8)),
             reads=[b_pV], writes=[b_va[r]])

    cnt = [0]

    def stage_m(sl, s):
        i = s - 2
        qr = s % QRING
        x = xr[i % 2]; bx = b_xr[i % 2]
        S.dma("sp", lambda e: e.dma_start(out=x, in_=xs_tiles[sl][s]), writes=[bx])
        kts = key_tiles(s, NQ)
        for grp in range(2):
            for hh in range(4):
                h = grp * 4 + hh
                c = h // 2
                qm = (qA if h % 2 == 0 else qB)[qr]
                bqm = (b_qA if h % 2 == 0 else b_qB)[qr]
                for n_, kt in enumerate(kts):
                    kr = kt % RING
                    m = cnt[0] % 2; cnt[0] += 1
                    t_ = cnt[0] % NTMP
                    j0 = 8 - 2 * (kt - s)
                    ti = ntile_idx[(s, kt)]
                    S.op("pe", lambda e, kr=kr, c=c, qm=qm, m=m: e.matmul(pMs[m], lhsT=kT[kr][:, c, :], rhs=qm[:, c, :],
                                                                        start=True, stop=True),
                         reads=[b_kT[kr], bqm], writes=[b_pMs[m]])
                    S.op("dve", lambda e, m=m, t_=t_, h=h, j0=j0: e.tensor_tensor(
                        out=tmp[t_], in0=pMs[m], in1=T[:, h, j0:j0 + 2, :].rearrange("p a b -> p (a b)"), op=ALU.add),
                         reads=[b_pMs[m], b_T], writes=[b_tmp[t_]])
                    for qh in range(2):
                        S.op("act", lambda e, t_=t_, qh=qh, ti=ti: e.activation(
                            out=PT[t_][:, qh * 64:(qh + 1) * 64], in_=tmp[t_][:, qh * 64:(qh + 1) * 64], func=AF.Exp,
                            bias=rmask[:, sl, ti, qh:qh + 1]), reads=[b_tmp[t_], b_rmask], writes=[b_PT[t_]])
                    S.op("pe", lambda e, t_=t_, kr=kr, h=h, hh=hh, n_=n_: e.matmul(
                        pO[:, hh, 0:65], lhsT=PT[t_], rhs=va[kr][:, h, :], start=(n_ == 0), stop=(n_ == len(kts) - 1)),
                         reads=[b_PT[t_], b_va[kr]], writes=[b_pO])
            S.op("dve", lambda e: e.reciprocal(out=small[:, 4:8], in_=pO[:, :, 64]), reads=[b_pO], writes=[b_small])
            S.op("dve", lambda e, grp=grp: e.tensor_tensor(
                out=ya[:, grp * 4:(grp + 1) * 4, :], in0=pO[:, :, 0:64],
                in1=small[:, 4:8].unsqueeze(2).broadcast_to([128, 4, 64]), op=ALU.mult),
                 reads=[b_pO, b_small], writes=[b_ya])
        yaf = ya.rearrange("p h d -> p (h d)")
        S.op("act", lambda e: e.activation(out=junk[:, 0:512], in_=yaf, func=AF.Square, accum_out=small[:, 8:9]),
             reads=[b_ya], writes=[b_junk, b_small])
        S.op("act", lambda e: e.activation(out=small[:, 9:10], in_=small[:, 8:9], func=AF.Sqrt, bias=EPS, scale=1.0 / 512),
             reads=[b_small], writes=[b_small])
        S.op("dve", lambda e: e.reciprocal(out=small[:, 10:11], in_=small[:, 9:10]), reads=[b_small], writes=[b_small])
        S.op("dve", lambda e: e.tensor_scalar(out=yan, in0=yaf, scalar1=small[:, 10:11], scalar2=None, op0=ALU.mult),
             reads=[b_ya, b_small], writes=[b_yan])
        for c in range(4):
            S.op("pe", lambda e, c=c: e.transpose(out=pTv[:, c, :], in_=yan[:, c * 128:(c + 1) * 128], identity=ident),
                 reads=[b_yan, b_ident], writes=[b_pT])
        S.op("act", lambda e: e.copy(out=yaT, in_=pTv[:, 0:4, :]), reads=[b_pT], writes=[b_yaT])
        ur = s % URING
        for c in range(4):
            eng = "pool" if c < 1 else "dve"
            S.op(eng, lambda e, c=c: e.tensor_scalar(out=yconv[:, c, :], in0=uT[ur][:, c, 1:129], scalar1=cw[:, c, 0:1],
                                                     scalar2=pp[:, PP_CB + c:PP_CB + c + 1], op0=ALU.mult, op1=ALU.add),
                 reads=[b_uT[ur], b_pp], writes=[b_yc[c]])
            for k in range(1, 31):
                if eng == "dve":
                    S.op(eng, lambda e, c=c, k=k: e.scalar_tensor_tensor(
                        out=yconv[:, c, :], in0=uT[ur][:, c, k + 1:k + 129], scalar=cw[:, c, k:k + 1], in1=yconv[:, c, :],
                        op0=ALU.mult, op1=ALU.add), reads=[b_uT[ur], b_pp], writes=[b_yc[c]])
                else:
                    S.op(eng, lambda e, c=c, k=k: e.tensor_scalar(
                        out=ctmp, in0=uT[ur][:, c, k + 1:k + 129], scalar1=cw[:, c, k:k + 1], scalar2=None, op0=ALU.mult),
                         reads=[b_uT[ur], b_pp], writes=[b_ctmp])
                    S.op(eng, lambda e, c=c: e.tensor_tensor(out=yconv[:, c, :], in0=yconv[:, c, :], in1=ctmp, op=ALU.add),
                         reads=[b_ctmp], writes=[b_yc[c]])
        S.op("act", lambda e: e.activation(out=csq, in_=yconv, func=AF.Square), reads=b_yc, writes=[b_csq])
        for c in range(4):
            S.op("pe", lambda e, c=c: e.matmul(pM[:, 2, :], lhsT=onesf, rhs=yconv[:, c, :], start=(c == 0), stop=(c == 3)),
                 reads=[b_onesf] + b_yc, writes=[b_pM[2]])
        for c in range(4):
            S.op("pe", lambda e, c=c: e.matmul(pM[:, 3, :], lhsT=onesf, rhs=csq[:, c, :], start=(c == 0), stop=(c == 3)),
                 reads=[b_onesf, b_csq], writes=[b_pM[3]])
        S.op("dve", lambda e: e.tensor_scalar(out=st[:, 0, :], in0=pM[:, 2, :], scalar1=1.0 / 512, scalar2=None, op0=ALU.mult),
             reads=[b_pM[2]], writes=[b_st])
        S.op("dve", lambda e: e.tensor_tensor(out=st[:, 1, :], in0=st[:, 0, :], in1=st[:, 0, :], op=ALU.mult),
             reads=[b_st], writes=[b_st])
        S.op("dve", lambda e: e.scalar_tensor_tensor(out=st[:, 2, :], in0=pM[:, 3, :], scalar=1.0 / 512, in1=st[:, 1, :],
                                                     op0=ALU.mult, op1=ALU.subtract), reads=[b_pM[3], b_st], writes=[b_st])
        S.op("act", lambda e: e.activation(out=st[:, 3, :], in_=st[:, 2, :], func=AF.Sqrt, bias=EPS, scale=1.0),
             reads=[b_st], writes=[b_st])
        S.op("dve", lambda e: e.reciprocal(out=st[:, 3, :], in_=st[:, 3, :]), reads=[b_st], writes=[b_st])
        S.op("dve", lambda e: e.tensor_tensor(out=z, in0=yconv, in1=st[:, 0, :].unsqueeze(1).broadcast_to([128, 4, 128]),
                                              op=ALU.subtract), reads=b_yc + [b_st], writes=[b_z])
        S.op("dve", lambda e: e.tensor_tensor(out=z, in0=z, in1=st[:, 3, :].unsqueeze(1).broadcast_to([128, 4, 128]),
                                              op=ALU.mult), reads=[b_z, b_st], writes=[b_z])
        for c in range(4):
            S.op("dve", lambda e, c=c: e.tensor_scalar(out=z[:, c, :], in0=z[:, c, :], scalar1=pp[:, PP_LNG + c:PP_LNG + c + 1],
                                                       scalar2=pp[:, PP_LNB + c:PP_LNB + c + 1], op0=ALU.mult, op1=ALU.add),
                 reads=[b_z, b_pp], writes=[b_z])
        S.op("act", lambda e: e.activation(out=z, in_=z, func=AF.Silu), reads=[b_z], writes=[b_z])
        S.op("act", lambda e: e.activation(out=csq, in_=z, func=AF.Square), reads=[b_z], writes=[b_csq])
        for c in range(4):
            S.op("pe", lambda e, c=c: e.matmul(pM[:, 2, :], lhsT=onesf, rhs=csq[:, c, :], start=(c == 0), stop=(c == 3)),
                 reads=[b_onesf, b_csq], writes=[b_pM[2]])
        S.op("act", lambda e: e.activation(out=st[:, 0, :], in_=pM[:, 2, :], func=AF.Sqrt, bias=EPS, scale=1.0 / 512),
             reads=[b_pM[2]], writes=[b_st])
        S.op("dve", lambda e: e.reciprocal(out=st[:, 0, :], in_=st[:, 0, :]), reads=[b_st], writes=[b_st])
        S.op("dve", lambda e: e.tensor_tensor(out=ycn, in0=z, in1=st[:, 0, :].unsqueeze(1).broadcast_to([128, 4, 128]),
                                              op=ALU.mult), reads=[b_z, b_st], writes=[b_ycn])
        for hf in range(2):
            for c in range(8):
                lhs = ycn[:, c, :] if c < 4 else yaT[:, c - 4, :]
                S.op("pe", lambda e, c=c, hf=hf, lhs=lhs: e.matmul(pY[:, hf * 512:(hf + 1) * 512], lhsT=lhs,
                                                                   rhs=wout[:, c, hf * 512:(hf + 1) * 512],
                                                                   start=(c == 0), stop=(c == 7)),
                     reads=[b_ycn, b_yaT, b_wout], writes=[b_pS])
        S.op("dve", lambda e: e.tensor_tensor(out=x, in0=pY, in1=x, op=ALU.add), reads=[b_pS, bx], writes=[bx])
        S.dma("sp", lambda e: e.dma_start(out=x1_tiles[sl][i], in_=x), reads=[bx])

    for sl in range(nslab):
        for step in range(NS + 3):
            if step < NS:
                stage_p(sl, step)
            s = step - 3
            if 2 <= s < NQ + 2:
                stage_m(sl, s)


def row_mask_table(NQ, kind, q=0, R=None):
    idx = {}
    for s in range(2, NQ + 2):
        for kt in key_tiles(s, NQ):
            idx[(s, kt)] = len(idx)
    out = np.full((128, len(idx), 2), NEGM, np.float32)
    nrows = 2 * NQ
    if kind == "full":
        R = nrows; base = 0
    else:
        base = q
    for (s, kt), ti in idx.items():
        for qh in range(2):
            r = base + 2 * (s - 2) + qh
            rs = min(max(r - 4, 0), R - 8)
            for kh in range(2):
                rk = base + 2 * kt + kh - 4
                if rs <= rk < rs + 8:
                    out[kh * 64:(kh + 1) * 64, ti, qh] = 0.0
    return out


def bias_table(rpb):
    T = np.full((128, 8, 16, 64), NEGM, np.float32)
    cq = np.arange(64)
    cs = np.clip(cq - 8, 0, 48)
    for kh in range(2):
        for j in range(16):
            dr = kh - j + 8
            if abs(dr) > 7:
                continue
            for cp in range(64):
                ok = (cp >= cs) & (cp < cs + 16)
                off = np.clip(cp - cq + 15, 0, 30)
                vals = rpb[:, dr + 7, :][:, off]
                T[kh * 64 + cp, :, j, :] = np.where(ok[None, :], vals, NEGM)
    return T


def small_params(g_mix, g_out_conv, g_out_attn, conv_w, conv_b, ln_g, ln_b, q_g, k_g):
    pp = np.zeros((128, PP_N), np.float32)
    pp[:, PP_GMIX:PP_GMIX + 8] = g_mix.reshape(8, 128).T
    pp[:, PP_GOUT:PP_GOUT + 4] = g_out_conv.reshape(4, 128).T
    pp[:, PP_GOUT + 4:PP_GOUT + 8] = g_out_attn.reshape(4, 128).T
    pp[:, PP_CB:PP_CB + 4] = conv_b.reshape(4, 128).T
    pp[:, PP_LNG:PP_LNG + 4] = ln_g.reshape(4, 128).T
    pp[:, PP_LNB:PP_LNB + 4] = ln_b.reshape(4, 128).T
    pp[:, PP_GQ] = np.tile(q_g, 2)
    pp[:, PP_GK] = np.tile(k_g, 2)
    pp[:, PP_CW:PP_CW + 124] = conv_w.T.reshape(4, 128, 31).transpose(1, 0, 2).reshape(128, 124)
    return pp


NQ_FULL = 32
N_CORES = 8
ARENA_WORDS = 51200


def build_program(NQ=NQ_FULL):
    NS = NQ + 4
    nc = bass.Bass("TRN2", target_bir_lowering=False)
    xs = nc.dram_tensor("xs", [2, NS * 128, D], F32, kind="ExternalInput").ap()
    y = nc.dram_tensor("y", [2, NQ * 128, D], F32, kind="ExternalOutput").ap()
    nti = sum(len(key_tiles(s, NQ)) for s in range(2, NQ + 2))
    C = {}
    for name, shape in (("consts", [128, 144]), ("pp", [128, PP_N]), ("ttab", [128, 8 * 16 * 64]),
                        ("rmask", [128, 2 * nti * 2]), ("w_in", [D, 2560]), ("w_out", [D, D]),
                        ("g_ffn", [D]), ("w_query", [D, 2048]), ("sub_keys", [16, 128, 128]),
                        ("expert_u", [NEXP, D]), ("expert_v", [NEXP, D])):
        C[name] = nc.dram_tensor(name, shape, F32, kind="ExternalInput").ap()
    S = Sched(nc)
    sbh = nc.alloc_sbuf_tensor("arena", [128, ARENA_WORDS], F32)
    psh = nc.alloc_psum_tensor("parena", [128, 4096], F32)
    xst = xs.rearrange("a (n p) d -> a n p d", p=128)
    yt = y.rearrange("a (n p) d -> a n p d", p=128)
    sb = Arena(sbh, ARENA_WORDS); ps = Arena(psh, 4096)
    phase_a(S, nc, sb, ps, C, NQ, [[xst[a, i] for i in range(NS)] for a in range(2)],
            [[yt[a, i] for i in range(NQ)] for a in range(2)])
    S.barrier()
    sb = Arena(sbh, ARENA_WORDS); ps = Arena(psh, 4096)
    ytl = [yt[a, i] for a in range(2) for i in range(NQ)]
    phase_b(S, nc, sb, ps, C, ytl, ytl)
    S.emit()
    return nc, S


def kernel(x_prompt, x_sample, g_mix, w_in, conv_w, conv_b, conv_ln_g, conv_ln_b,
           q_norm_g, k_norm_g, rpb, g_out_conv, g_out_attn, w_out, g_ffn,
           w_query, sub_keys, expert_u, expert_v):
    f = lambda a: np.ascontiguousarray(np.asarray(a, dtype=np.float32))
    x_prompt, x_sample = f(x_prompt), f(x_sample)
    NQ = NQ_FULL
    NS = NQ + 4
    T = NQ * 128
    H = 256
    shared = {
        "consts": make_consts(),
        "pp": small_params(f(g_mix)[0], f(g_out_conv)[0], f(g_out_attn)[0], f(conv_w)[0], f(conv_b)[0],
                           f(conv_ln_g)[0], f(conv_ln_b)[0], f(q_norm_g)[0], f(k_norm_g)[0]),
        "ttab": bias_table(f(rpb)[0]).reshape(128, -1),
        "w_in": f(w_in)[0], "w_out": f(w_out)[0], "g_ffn": f(g_ffn)[0], "w_query": f(w_query)[0],
        "sub_keys": f(sub_keys)[0].reshape(16, 128, 128),
        "expert_u": f(expert_u)[0], "expert_v": f(expert_v)[0],
    }
    rm_full = row_mask_table(NQ, "full")
    in_maps = []
    for c in range(N_CORES):
        b, q = c // 4, c % 4
        xs = np.zeros((2, NS * 128, D), np.float32)
        xs[0, H:H + T] = x_sample[c]
        lo, hi = q * T - H, q * T + T + H
        clo, chi = max(lo, 0), min(hi, x_prompt.shape[1])
        xs[1, clo - lo:clo - lo + (chi - clo)] = x_prompt[b, clo:chi]
        rm = np.stack([rm_full, row_mask_table(NQ, "chunk", q=2 * NQ * q, R=x_prompt.shape[1] // 64)], axis=1)
        m = dict(shared)
        m["xs"] = xs
        m["rmask"] = np.ascontiguousarray(rm.reshape(128, -1))
        in_maps.append(m)
    nc, _ = build_program(NQ)
    res = run_bass_kernel_spmd(nc, in_maps, core_ids=list(range(N_CORES)))
    y_prompt = np.zeros_like(x_prompt)
    y_sample = np.zeros_like(x_sample)
    for c in range(N_CORES):
        yc = np.asarray(res.results[c]["y"], dtype=np.float32)
        y_sample[c] = yc[0]
        y_prompt[c // 4, (c % 4) * T:(c % 4 + 1) * T] = yc[1]
    return (y_prompt, y_sample)
```

```python
import numpy as np
import concourse.bass as bass
import concourse.mybir as mybir
from concourse.bass_utils import run_bass_kernel_spmd

F32 = mybir.dt.float32
BF16 = mybir.dt.bfloat16
I32 = mybir.dt.int32
U32 = mybir.dt.uint32
ALU = mybir.AluOpType
AF = mybir.ActivationFunctionType
AX = mybir.AxisListType

D = 1024
NEXP = 16384
EPS = 1e-6
NEGM = -30000.0


class Buf:
    __slots__ = ("name", "w", "r")

    def __init__(self, name=""):
        self.name = name
        self.w = None
        self.r = []


class Op:
    __slots__ = ("eng", "fn", "deps", "signal", "semkey", "count", "is_dma")

    def __init__(self, eng, fn):
        self.eng = eng
        self.fn = fn
        self.deps = []
        self.signal = False
        self.semkey = None
        self.count = 0
        self.is_dma = False


class Sched:
    ENGS = ("pe", "dve", "act", "pool", "sp")
    SELF_SYNC = {"pe": False, "dve": True, "act": True, "pool": True, "sp": False}

    def __init__(self, nc, n_dma_sems=None):
        self.nc = nc
        self.ops = {e: [] for e in self.ENGS}
        self.n_dma_sems = n_dma_sems or {"sp": 16, "act": 8, "pool": 24}
        self.dma_rr = {}
        self.dma_last = {}
        self.dma_cnt = {}
        self.last_real = {}

    def _add_deps(self, op, reads, writes):
        deps = []
        for b in reads:
            if b.w is not None:
                deps.append(b.w)
        for b in writes:
            if b.w is not None:
                deps.append(b.w)
            deps.extend(b.r)
        for d in deps:
            if d is op:
                continue
            if (not d.is_dma) and d.eng == op.eng and not self.SELF_SYNC[op.eng]:
                continue
            op.deps.append(d)
            d.signal = True
        for b in reads:
            b.r.append(op)
        for b in writes:
            b.w = op
            b.r = []

    def op(self, eng, fn, reads=(), writes=()):
        o = Op(eng, fn)
        self._add_deps(o, reads, writes)
        self.ops[eng].append(o)
        self.last_real[eng] = o
        return o

    def dma(self, eng, fn, reads=(), writes=()):
        o = Op(eng, fn)
        o.is_dma = True
        o.signal = True
        rr = self.dma_rr.get(eng, 0)
        self.dma_rr[eng] = rr + 1
        key = ("dma", eng, rr % self.n_dma_sems[eng])
        o.semkey = key
        prev = self.dma_last.get(key)
        if prev is not None:
            o.deps.append(prev)
        self.dma_last[key] = o
        c = self.dma_cnt.get(key, 0) + 16
        self.dma_cnt[key] = c
        o.count = c
        self._add_deps(o, reads, writes)
        self.ops[eng].append(o)
        return o

    def barrier(self):
        lasts = [o for o in self.last_real.values()] + list(self.dma_last.values())
        for e in self.ENGS:
            o = Op(e, None)
            for d in lasts:
                if (not d.is_dma) and d.eng == e:
                    continue
                o.deps.append(d)
                d.signal = True
            self.ops[e].append(o)

    def emit(self, final_wait_eng="sp"):
        nc = self.nc
        self.barrier()
        for e in self.ENGS:
            c = 0
            for o in self.ops[e]:
                if o.is_dma or o.fn is None:
                    continue
                o.semkey = ("eng", e)
                if o.signal:
                    c += 1
                    o.count = c
        keys = set()
        for e in self.ENGS:
            for o in self.ops[e]:
                if o.signal and o.fn is not None:
                    keys.add(o.semkey)
        sems = {}
        for k in sorted(keys, key=str):
            sems[k] = nc.alloc_semaphore(name="s_" + "_".join(str(x) for x in k))
        stats = {"ins": 0, "wait": 0}
        with nc.Block() as block:
            deco = {"pe": block.tensor, "dve": block.vector, "act": block.scalar,
                    "pool": block.gpsimd, "sp": block.sync}
            for e in self.ENGS:
                ops = self.ops[e]

                def body(eng, ops=ops):
                    seen = {}
                    for o in ops:
                        for d in o.deps:
                            if seen.get(d.semkey, 0) < d.count:
                                eng.wait_ge(sems[d.semkey], d.count)
                                seen[d.semkey] = d.count
                                stats["wait"] += 1
                        if o.fn is None:
                            continue
                        ins = o.fn(eng)
                        stats["ins"] += 1
                        if o.signal:
                            ins.then_inc(sems[o.semkey], 16 if o.is_dma else 1)

                deco[e](body)
        self.stats = stats


class Arena:
    def __init__(self, handle, nwords):
        self.h = handle
        self.n = nwords
        self.off = 0

    def alloc(self, free_shape, dtype=F32, align=16):
        n = 1
        for s in free_shape:
            n *= s
        size = 4 if dtype in (F32, I32, U32) else 2
        words = (n * size + 3) // 4
        self.off = (self.off + align - 1) // align * align
        assert self.off + words <= self.n, ("arena overflow", self.off, words, self.n)
        ap = self.h[:, self.off:self.off + words]
        self.off += words
        if dtype != F32:
            ap = ap.bitcast(dtype)
            if ap.shape[1] != n:
                ap = ap[:, 0:n]
        if len(free_shape) == 2:
            ap = ap.rearrange("p (a b) -> p a b", a=free_shape[0])
        elif len(free_shape) == 3:
            ap = ap.rearrange("p (a b c) -> p a b c", a=free_shape[0], b=free_shape[1])
        return ap


def phase_b(S, nc, sb, ps, C, x1_tiles, y_tiles, NB=6):
    ident = sb.alloc([128], BF16); b_ident = Buf()
    iota16 = sb.alloc([16], F32); b_iota = Buf()
    cst = sb.alloc([144], F32); b_cst = Buf()
    wq = sb.alloc([8, 2048], BF16); b_wq = Buf()
    skT = sb.alloc([16, 128], BF16); b_skT = Buf()
    gffn = sb.alloc([1024], F32); b_gffn = Buf()
    S.dma("sp", lambda e: e.dma_start(out=cst, in_=C["consts"]), writes=[b_cst])
    S.op("dve", lambda e: e.tensor_copy(out=ident, in_=cst[:, 0:128]), reads=[b_cst], writes=[b_ident])
    S.op("dve", lambda e: e.tensor_copy(out=iota16, in_=cst[:, 128:144]), reads=[b_cst], writes=[b_iota])
    S.dma("sp", lambda e: e.dma_start(out=gffn, in_=C["g_ffn"].partition_broadcast(128)), writes=[b_gffn])
    stg = [sb.alloc([2048], F32) for _ in range(2)]
    b_stg = [Buf(), Buf()]
    wqv = C["w_query"].rearrange("(c p) n -> c p n", p=128)
    for c in range(8):
        S.dma("sp", lambda e, c=c: e.dma_start(out=stg[c % 2], in_=wqv[c]), writes=[b_stg[c % 2]])
        S.op("act" if c % 2 else "dve",
             (lambda e, c=c: e.copy(out=wq[:, c, :], in_=stg[c % 2])) if c % 2 else
             (lambda e, c=c: e.tensor_copy(out=wq[:, c, :], in_=stg[c % 2])),
             reads=[b_stg[c % 2]], writes=[b_wq])
    skv = C["sub_keys"].rearrange("g k d -> k g d")
    skn = stg[0].rearrange("p (g d) -> p g d", g=16)
    skb = sb.alloc([16, 128], BF16); b_skb = Buf()
    pbig = ps.alloc([16, 128], F32, align=512); b_pbig = Buf()
    pT = ps.alloc([1024], BF16, align=512); b_pT = Buf()
    pTv = pT.rearrange("p (c n) -> p c n", c=8)
    S.dma("sp", lambda e: e.dma_start(out=skn, in_=skv), writes=[b_stg[0]])
    S.op("dve", lambda e: e.tensor_copy(out=skb, in_=skn), reads=[b_stg[0]], writes=[b_skb])
    for half in range(2):
        for j in range(8):
            g = half * 8 + j
            S.op("pe", lambda e, g=g, j=j: e.transpose(out=pTv[:, j, :], in_=skb[:, g, :], identity=ident),
                 reads=[b_skb, b_ident], writes=[b_pT])
        S.op("dve", lambda e, half=half: e.tensor_copy(out=skT[:, half * 8:(half + 1) * 8, :], in_=pTv),
             reads=[b_pT], writes=[b_skT])

    x1t = [sb.alloc([1024], F32) for _ in range(2)]; b_x1t = [Buf(), Buf()]
    junk = sb.alloc([1024], F32); b_junk = Buf()
    hn = sb.alloc([1024], F32); b_hn = Buf()
    hb = sb.alloc([1024], BF16); b_hb = Buf()
    hT = sb.alloc([8, 128], BF16); b_hT = Buf()
    qT = sb.alloc([16, 128], BF16); b_qT = Buf()
    s = sb.alloc([16, 128], F32); b_s = Buf()
    s2 = sb.alloc([16, 128], F32); b_s2 = Buf()
    sv = sb.alloc([16, 16], F32); b_sv = Buf()
    siu = sb.alloc([16, 16], U32); b_siu = Buf()
    sif = sb.alloc([16, 16], F32); b_sif = Buf()
    cand = sb.alloc([8, 256], F32); b_cand = Buf()
    cand2 = sb.alloc([8, 256], F32); b_cand2 = Buf()
    cv = sb.alloc([8, 16], F32); b_cv = Buf()
    ciu = sb.alloc([8, 16], U32); b_ciu = Buf()
    rcu = sb.alloc([2, 128], U32); b_rcu = Buf()
    rcf = sb.alloc([2, 128], F32); b_rcf = Buf()
    oh = sb.alloc([128, 16], F32); b_oh = Buf()
    i12 = sb.alloc([2, 128], F32); b_i12 = Buf()
    ef = sb.alloc([128], F32); b_ef = Buf()
    ei = [sb.alloc([128], I32) for _ in range(2)]; b_ei = [Buf(), Buf()]
    araw = sb.alloc([128], F32); b_araw = Buf()
    aw = sb.alloc([128], F32); b_aw = Buf()
    gsm = sb.alloc([8, 16], F32); b_gsm = Buf()
    small = sb.alloc([32], F32); b_small = Buf()
    acc = [sb.alloc([1024], F32) for _ in range(2)]; b_acc = [Buf(), Buf()]
    ub = [sb.alloc([1024], F32) for _ in range(NB)]; b_ub = [Buf() for _ in range(NB)]
    vb = [sb.alloc([1024], F32) for _ in range(NB)]; b_vb = [Buf() for _ in range(NB)]
    gcount = [0, 0]

    sv4 = sv.rearrange("p (h t) k -> p h t k", t=2)
    sif4 = sif.rearrange("p (h t) k -> p h t k", t=2)
    cand4 = cand.rearrange("p h (i j) -> p h i j", i=16)
    oh4 = oh.rearrange("p (h k) i -> p h k i", h=8)

    for ti in range(len(x1_tiles)):
        xb = x1t[ti % 2]; bxb = b_x1t[ti % 2]
        eib = ei[ti % 2]; beib = b_ei[ti % 2]
        ac = acc[ti % 2]; bac = b_acc[ti % 2]
        S.dma("sp", lambda e, xb=xb, ti=ti: e.dma_start(out=xb, in_=x1_tiles[ti]), writes=[bxb])
        S.op("act", lambda e, xb=xb: e.activation(out=junk, in_=xb, func=AF.Square, accum_out=small[:, 0:1]),
             reads=[bxb], writes=[b_junk, b_small])
        S.op("act", lambda e: e.activation(out=small[:, 1:2], in_=small[:, 0:1], func=AF.Sqrt, bias=EPS, scale=1.0 / D),
             reads=[b_small], writes=[b_small])
        S.op("dve", lambda e: e.reciprocal(out=small[:, 2:3], in_=small[:, 1:2]), reads=[b_small], writes=[b_small])
        S.op("dve", lambda e, xb=xb: e.scalar_tensor_tensor(out=hn, in0=xb, scalar=small[:, 2:3], in1=gffn,
                                                            op0=ALU.mult, op1=ALU.mult),
             reads=[bxb, b_small, b_gffn], writes=[b_hn])
        S.op("act", lambda e: e.copy(out=hb, in_=hn), reads=[b_hn], writes=[b_hb])
        for c in range(8):
            S.op("pe", lambda e, c=c: e.transpose(out=pTv[:, c, :], in_=hb[:, c * 128:(c + 1) * 128], identity=ident),
                 reads=[b_hb, b_ident], writes=[b_pT])
        S.op("act", lambda e: e.copy(out=hT, in_=pTv), reads=[b_pT], writes=[b_hT])
        for g in range(16):
            for c in range(8):
                S.op("pe", lambda e, g=g, c=c: e.matmul(pbig[:, g, :], lhsT=wq[:, c, g * 128:(g + 1) * 128],
                                                        rhs=hT[:, c, :], start=(c == 0), stop=(c == 7)),
                     reads=[b_wq, b_hT], writes=[b_pbig])
        S.op("act", lambda e: e.copy(out=qT, in_=pbig), reads=[b_pbig], writes=[b_qT])
        for g in range(16):
            S.op("pe", lambda e, g=g: e.matmul(pbig[:, g, :], lhsT=qT[:, g, :], rhs=skT[:, g, :], start=True, stop=True),
                 reads=[b_qT, b_skT], writes=[b_pbig])
        S.op("dve", lambda e: e.tensor_copy(out=s, in_=pbig), reads=[b_pbig], writes=[b_s])
        for g in range(16):
            S.op("dve", lambda e, g=g: e.max(out=sv[:, g, 0:8], in_=s[:, g, :]), reads=[b_s], writes=[b_sv])
            S.op("dve", lambda e, g=g: e.match_replace(out=s2[:, g, :], in_to_replace=sv[:, g, 0:8],
                                                       in_values=s[:, g, :], imm_value=-1e30),
                 reads=[b_s, b_sv], writes=[b_s2])
            S.op("dve", lambda e, g=g: e.max(out=sv[:, g, 8:16], in_=s2[:, g, :]), reads=[b_s2], writes=[b_sv])
            S.op("dve", lambda e, g=g: e.max_index(out=siu[:, g, 0:8], in_max=sv[:, g, 0:8], in_values=s[:, g, :]),
                 reads=[b_s, b_sv], writes=[b_siu])
            S.op("dve", lambda e, g=g: e.max_index(out=siu[:, g, 8:16], in_max=sv[:, g, 8:16], in_values=s[:, g, :]),
                 reads=[b_s, b_sv], writes=[b_siu])
        S.op("dve", lambda e: e.tensor_copy(out=sif, in_=siu), reads=[b_siu], writes=[b_sif])
        for h in range(8):
            S.op("dve", lambda e, h=h: e.tensor_tensor(
                out=cand4[:, h], in0=sv4[:, h, 0, :].unsqueeze(2).broadcast_to([128, 16, 16]),
                in1=sv4[:, h, 1, :].unsqueeze(1).broadcast_to([128, 16, 16]), op=ALU.add),
                 reads=[b_sv], writes=[b_cand])
        for h in range(8):
            S.op("dve", lambda e, h=h: e.max(out=cv[:, h, 0:8], in_=cand[:, h, :]), reads=[b_cand], writes=[b_cv])
            S.op("dve", lambda e, h=h: e.match_replace(out=cand2[:, h, :], in_to_replace=cv[:, h, 0:8],
                                                       in_values=cand[:, h, :], imm_value=-1e30),
                 reads=[b_cand, b_cv], writes=[b_cand2])
            S.op("dve", lambda e, h=h: e.max(out=cv[:, h, 8:16], in_=cand2[:, h, :]), reads=[b_cand2], writes=[b_cv])
            S.op("dve", lambda e, h=h: e.max_index(out=ciu[:, h, 0:8], in_max=cv[:, h, 0:8], in_values=cand[:, h, :]),
                 reads=[b_cand, b_cv], writes=[b_ciu])
            S.op("dve", lambda e, h=h: e.max_index(out=ciu[:, h, 8:16], in_max=cv[:, h, 8:16], in_values=cand[:, h, :]),
                 reads=[b_cand, b_cv], writes=[b_ciu])
        ciu_f = ciu.rearrange("p h k -> p (h k)")
        S.op("dve", lambda e: e.tensor_single_scalar(out=rcu[:, 0, :], in_=ciu_f, scalar=4, op=ALU.logical_shift_right),
             reads=[b_ciu], writes=[b_rcu])
        S.op("dve", lambda e: e.tensor_single_scalar(out=rcu[:, 1, :], in_=ciu_f, scalar=15, op=ALU.bitwise_and),
             reads=[b_ciu], writes=[b_rcu])
        S.op("dve", lambda e: e.tensor_copy(out=rcf, in_=rcu), reads=[b_rcu], writes=[b_rcf])
        for t in range(2):
            S.op("dve", lambda e, t=t: e.tensor_tensor(
                out=oh, in0=rcf[:, t, :].unsqueeze(2).broadcast_to([128, 128, 16]),
                in1=iota16.unsqueeze(1).broadcast_to([128, 128, 16]), op=ALU.is_equal),
                 reads=[b_rcf, b_iota], writes=[b_oh])
            for h in range(8):
                S.op("dve", lambda e, h=h, t=t: e.tensor_tensor(
                    out=oh4[:, h], in0=oh4[:, h], in1=sif4[:, h, t, :].unsqueeze(1).broadcast_to([128, 16, 16]),
                    op=ALU.mult), reads=[b_oh, b_sif], writes=[b_oh])
            S.op("dve", lambda e, t=t: e.tensor_reduce(out=i12[:, t, :], in_=oh, axis=AX.X, op=ALU.add),
                 reads=[b_oh], writes=[b_i12])
        S.op("dve", lambda e: e.scalar_tensor_tensor(out=ef, in0=i12[:, 0, :], scalar=128.0, in1=i12[:, 1, :],
                                                     op0=ALU.mult, op1=ALU.add), reads=[b_i12], writes=[b_ef])
        S.op("dve", lambda e, eib=eib: e.tensor_copy(out=eib, in_=ef), reads=[b_ef], writes=[beib])
        S.op("dve", lambda e: e.tensor_tensor(out=gsm, in0=cv, in1=cv[:, :, 0:1].broadcast_to([128, 8, 16]),
                                              op=ALU.subtract), reads=[b_cv], writes=[b_gsm])
        S.op("act", lambda e: e.activation(out=gsm, in_=gsm, func=AF.Exp), reads=[b_gsm], writes=[b_gsm])
        S.op("dve", lambda e: e.tensor_reduce(out=small[:, 8:16], in_=gsm, axis=AX.X, op=ALU.add),
             reads=[b_gsm], writes=[b_small])
        S.op("dve", lambda e: e.reciprocal(out=small[:, 16:24], in_=small[:, 8:16]), reads=[b_small], writes=[b_small])
        S.op("dve", lambda e: e.tensor_tensor(out=gsm, in0=gsm,
                                              in1=small[:, 16:24].unsqueeze(2).broadcast_to([128, 8, 16]),
                                              op=ALU.mult), reads=[b_gsm, b_small], writes=[b_gsm])
        for k in range(128):
            j = gcount[0] % NB; gcount[0] += 1
            S.dma("pool", lambda e, k=k, j=j, eib=eib: e.indirect_dma_start(
                out=ub[j], out_offset=None, in_=C["expert_u"],
                in_offset=bass.IndirectOffsetOnAxis(ap=eib[:, k:k + 1], axis=0)),
                  reads=[beib], writes=[b_ub[j]])
            S.op("dve", lambda e, k=k, j=j: e.scalar_tensor_tensor(
                out=ub[j], in0=ub[j], scalar=1.0, in1=hn, op0=ALU.mult, op1=ALU.mult, accum_out=araw[:, k:k + 1]),
                 reads=[b_hn], writes=[b_ub[j]] + ([b_araw] if k in (0, 127) else []))
        S.op("act", lambda e: e.activation(out=aw, in_=araw, func=AF.Gelu), reads=[b_araw], writes=[b_aw])
        S.op("dve", lambda e: e.tensor_tensor(out=aw, in0=aw, in1=gsm.rearrange("p h k -> p (h k)"), op=ALU.mult),
             reads=[b_aw, b_gsm], writes=[b_aw])
        for k in range(128):
            j = gcount[1] % NB; gcount[1] += 1
            S.dma("pool", lambda e, k=k, j=j, eib=eib: e.indirect_dma_start(
                out=vb[j], out_offset=None, in_=C["expert_v"],
                in_offset=bass.IndirectOffsetOnAxis(ap=eib[:, k:k + 1], axis=0)),
                  reads=[beib], writes=[b_vb[j]])
            src = xb if k == 0 else ac
            S.op("dve", lambda e, k=k, j=j, src=src, ac=ac: e.scalar_tensor_tensor(
                out=ac, in0=vb[j], scalar=aw[:, k:k + 1], in1=src, op0=ALU.mult, op1=ALU.add),
                 reads=[b_vb[j], b_aw] + ([bxb] if k == 0 else []), writes=[bac])
        S.dma("sp", lambda e, ac=ac, ti=ti: e.dma_start(out=y_tiles[ti], in_=ac), reads=[bac])


def phase_b2(S, nc, sb, ps, C, x1_tiles, y_tiles, uv16, b_uv16, NB=8):
    ident = sb.alloc([128], BF16); b_ident = Buf()
    iota16 = sb.alloc([16], F32); b_iota = Buf()
    cst = sb.alloc([144], F32); b_cst = Buf()
    wq = sb.alloc([8, 2048], BF16); b_wq = Buf()
    skT = sb.alloc([16, 128], BF16); b_skT = Buf()
    gffn = sb.alloc([1024], F32); b_gffn = Buf()
    S.dma("sp", lambda e: e.dma_start(out=cst, in_=C["consts"]), writes=[b_cst])
    S.op("dve", lambda e: e.tensor_copy(out=ident, in_=cst[:, 0:128]), reads=[b_cst], writes=[b_ident])
    S.op("dve", lambda e: e.tensor_copy(out=iota16, in_=cst[:, 128:144]), reads=[b_cst], writes=[b_iota])
    S.dma("sp", lambda e: e.dma_start(out=gffn, in_=C["g_ffn"].partition_broadcast(128)), writes=[b_gffn])
    stg = [sb.alloc([2048], F32) for _ in range(2)]
    b_stg = [Buf(), Buf()]
    wqv = C["w_query"].rearrange("(c p) n -> c p n", p=128)
    for c in range(8):
        S.dma("sp", lambda e, c=c: e.dma_start(out=stg[c % 2], in_=wqv[c]), writes=[b_stg[c % 2]])
        S.op("act" if c % 2 else "dve",
             (lambda e, c=c: e.copy(out=wq[:, c, :], in_=stg[c % 2])) if c % 2 else
             (lambda e, c=c: e.tensor_copy(out=wq[:, c, :], in_=stg[c % 2])),
             reads=[b_stg[c % 2]], writes=[b_wq])
    skv = C["sub_keys"].rearrange("g k d -> k g d")
    skn = stg[0].rearrange("p (g d) -> p g d", g=16)
    skb = sb.alloc([16, 128], BF16); b_skb = Buf()
    pbig = ps.alloc([16, 128], F32, align=512); b_pbig = Buf()
    pT = ps.alloc([1024], BF16, align=512); b_pT = Buf()
    pTv = pT.rearrange("p (c n) -> p c n", c=8)
    S.dma("sp", lambda e: e.dma_start(out=skn, in_=skv), writes=[b_stg[0]])
    S.op("dve", lambda e: e.tensor_copy(out=skb, in_=skn), reads=[b_stg[0]], writes=[b_skb])
    for half in range(2):
        for j in range(8):
            g = half * 8 + j
            S.op("pe", lambda e, g=g, j=j: e.transpose(out=pTv[:, j, :], in_=skb[:, g, :], identity=ident),
                 reads=[b_skb, b_ident], writes=[b_pT])
        S.op("dve", lambda e, half=half: e.tensor_copy(out=skT[:, half * 8:(half + 1) * 8, :], in_=pTv),
             reads=[b_pT], writes=[b_skT])

    x1t = [sb.alloc([1024], F32) for _ in range(2)]; b_x1t = [Buf(), Buf()]
    junk = sb.alloc([1024], F32); b_junk = Buf()
    hb = sb.alloc([1024], BF16); b_hb = Buf()
    hT = sb.alloc([8, 128], BF16); b_hT = Buf()
    qT = sb.alloc([16, 128], BF16); b_qT = Buf()
    s = sb.alloc([16, 128], F32); b_s = Buf()
    s2 = sb.alloc([16, 128], F32); b_s2 = Buf()
    sv = sb.alloc([16, 16], F32); b_sv = Buf()
    siu = sb.alloc([16, 16], U32); b_siu = Buf()
    sif = sb.alloc([16, 16], F32); b_sif = Buf()
    cand = sb.alloc([8, 256], F32); b_cand = Buf()
    cand2 = sb.alloc([8, 256], F32); b_cand2 = Buf()
    cv = sb.alloc([8, 16], F32); b_cv = Buf()
    ciu = sb.alloc([8, 16], U32); b_ciu = Buf()
    rcu = sb.alloc([2, 128], U32); b_rcu = Buf()
    rcf = sb.alloc([2, 128], F32); b_rcf = Buf()
    oh = sb.alloc([128, 16], F32); b_oh = Buf()
    i12 = sb.alloc([2, 128], F32); b_i12 = Buf()
    ef = sb.alloc([128], F32); b_ef = Buf()
    ei = [sb.alloc([128], I32) for _ in range(2)]; b_ei = [Buf(), Buf()]
    araw = sb.alloc([128], F32)
    gsm = sb.alloc([8, 16], F32); b_gsm = Buf()
    small = sb.alloc([32], F32); b_small = Buf()
    yo = sb.alloc([1024], F32); b_yo = Buf()
    uvb = [sb.alloc([2048], BF16) for _ in range(NB)]; b_uvb = [Buf() for _ in range(NB)]
    dg = [sb.alloc([128], BF16) for _ in range(4)]; b_dg = [Buf() for _ in range(4)]
    gl = sb.alloc([128], F32); b_gl = [Buf() for _ in range(8)]
    b_ar = [Buf() for _ in range(8)]
    pacc = ps.alloc([1024], F32, align=512); b_pacc = Buf()
    gcount = [0, 0]

    sv4 = sv.rearrange("p (h t) k -> p h t k", t=2)
    sif4 = sif.rearrange("p (h t) k -> p h t k", t=2)
    cand4 = cand.rearrange("p h (i j) -> p h i j", i=16)
    oh4 = oh.rearrange("p (h k) i -> p h k i", h=8)

    for ti in range(len(x1_tiles)):
        xb = x1t[ti % 2]; bxb = b_x1t[ti % 2]
        eib = ei[ti % 2]; beib = b_ei[ti % 2]
        S.dma("sp", lambda e, xb=xb, ti=ti: e.dma_start(out=xb, in_=x1_tiles[ti]), writes=[bxb])
        S.op("act", lambda e, xb=xb: e.activation(out=junk, in_=xb, func=AF.Square, accum_out=small[:, 0:1]),
             reads=[bxb], writes=[b_junk, b_small])
        S.op("act", lambda e: e.activation(out=small[:, 1:2], in_=small[:, 0:1], func=AF.Sqrt, bias=EPS, scale=1.0 / D),
             reads=[b_small], writes=[b_small])
        S.op("dve", lambda e: e.reciprocal(out=small[:, 2:3], in_=small[:, 1:2]), reads=[b_small], writes=[b_small])
        S.op("dve", lambda e, xb=xb: e.scalar_tensor_tensor(out=hb, in0=xb, scalar=small[:, 2:3], in1=gffn,
                                                            op0=ALU.mult, op1=ALU.mult),
             reads=[bxb, b_small, b_gffn], writes=[b_hb])
        for c in range(8):
            S.op("pe", lambda e, c=c: e.transpose(out=pTv[:, c, :], in_=hb[:, c * 128:(c + 1) * 128], identity=ident),
                 reads=[b_hb, b_ident], writes=[b_pT])
        S.op("act", lambda e: e.copy(out=hT, in_=pTv), reads=[b_pT], writes=[b_hT])
        for g in range(16):
            for c in range(8):
                S.op("pe", lambda e, g=g, c=c: e.matmul(pbig[:, g, :], lhsT=wq[:, c, g * 128:(g + 1) * 128],
                                                        rhs=hT[:, c, :], start=(c == 0), stop=(c == 7)),
                     reads=[b_wq, b_hT], writes=[b_pbig])
        S.op("act", lambda e: e.copy(out=qT, in_=pbig), reads=[b_pbig], writes=[b_qT])
        for g in range(16):
            S.op("pe", lambda e, g=g: e.matmul(pbig[:, g, :], lhsT=qT[:, g, :], rhs=skT[:, g, :], start=True, stop=True),
                 reads=[b_qT, b_skT], writes=[b_pbig])
        S.op("dve", lambda e: e.tensor_copy(out=s, in_=pbig), reads=[b_pbig], writes=[b_s])
        for g in range(16):
            S.op("dve", lambda e, g=g: e.max(out=sv[:, g, 0:8], in_=s[:, g, :]), reads=[b_s], writes=[b_sv])
            S.op("dve", lambda e, g=g: e.match_replace(out=s2[:, g, :], in_to_replace=sv[:, g, 0:8],
                                                       in_values=s[:, g, :], imm_value=-1e30),
                 reads=[b_s, b_sv], writes=[b_s2])
            S.op("dve", lambda e, g=g: e.max(out=sv[:, g, 8:16], in_=s2[:, g, :]), reads=[b_s2], writes=[b_sv])
            S.op("dve", lambda e, g=g: e.max_index(out=siu[:, g, 0:8], in_max=sv[:, g, 0:8], in_values=s[:, g, :]),
                 reads=[b_s, b_sv], writes=[b_siu])
            S.op("dve", lambda e, g=g: e.max_index(out=siu[:, g, 8:16], in_max=sv[:, g, 8:16], in_values=s[:, g, :]),
                 reads=[b_s, b_sv], writes=[b_siu])
        S.op("dve", lambda e: e.tensor_copy(out=sif, in_=siu), reads=[b_siu], writes=[b_sif])
        for h in range(8):
            S.op("dve", lambda e, h=h: e.tensor_tensor(
                out=cand4[:, h], in0=sv4[:, h, 0, :].unsqueeze(2).broadcast_to([128, 16, 16]),
                in1=sv4[:, h, 1, :].unsqueeze(1).broadcast_to([128, 16, 16]), op=ALU.add),
                 reads=[b_sv], writes=[b_cand])
        for h in range(8):
            S.op("dve", lambda e, h=h: e.max(out=cv[:, h, 0:8], in_=cand[:, h, :]), reads=[b_cand], writes=[b_cv])
            S.op("dve", lambda e, h=h: e.match_replace(out=cand2[:, h, :], in_to_replace=cv[:, h, 0:8],
                                                       in_values=cand[:, h, :], imm_value=-1e30),
                 reads=[b_cand, b_cv], writes=[b_cand2])
            S.op("dve", lambda e, h=h: e.max(out=cv[:, h, 8:16], in_=cand2[:, h, :]), reads=[b_cand2], writes=[b_cv])
            S.op("dve", lambda e, h=h: e.max_index(out=ciu[:, h, 0:8], in_max=cv[:, h, 0:8], in_values=cand[:, h, :]),
                 reads=[b_cand, b_cv], writes=[b_ciu])
            S.op("dve", lambda e, h=h: e.max_index(out=ciu[:, h, 8:16], in_max=cv[:, h, 8:16], in_values=cand[:, h, :]),
                 reads=[b_cand, b_cv], writes=[b_ciu])
        ciu_f = ciu.rearrange("p h k -> p (h k)")
        S.op("dve", lambda e: e.tensor_single_scalar(out=rcu[:, 0, :], in_=ciu_f, scalar=4, op=ALU.logical_shift_right),
             reads=[b_ciu], writes=[b_rcu])
        S.op("dve", lambda e: e.tensor_single_scalar(out=rcu[:, 1, :], in_=ciu_f, scalar=15, op=ALU.bitwise_and),
             reads=[b_ciu], writes=[b_rcu])
        S.op("dve", lambda e: e.tensor_copy(out=rcf, in_=rcu), reads=[b_rcu], writes=[b_rcf])
        for t in range(2):
            S.op("dve", lambda e, t=t: e.tensor_tensor(
                out=oh, in0=rcf[:, t, :].unsqueeze(2).broadcast_to([128, 128, 16]),
                in1=iota16.unsqueeze(1).broadcast_to([128, 128, 16]), op=ALU.is_equal),
                 reads=[b_rcf, b_iota], writes=[b_oh])
            for h in range(8):
                S.op("dve", lambda e, h=h, t=t: e.tensor_tensor(
                    out=oh4[:, h], in0=oh4[:, h], in1=sif4[:, h, t, :].unsqueeze(1).broadcast_to([128, 16, 16]),
                    op=ALU.mult), reads=[b_oh, b_sif], writes=[b_oh])
            S.op("dve", lambda e, t=t: e.tensor_reduce(out=i12[:, t, :], in_=oh, axis=AX.X, op=ALU.add),
                 reads=[b_oh], writes=[b_i12])
        S.op("dve", lambda e: e.scalar_tensor_tensor(out=ef, in0=i12[:, 0, :], scalar=128.0, in1=i12[:, 1, :],
                                                     op0=ALU.mult, op1=ALU.add), reads=[b_i12], writes=[b_ef])
        S.op("dve", lambda e, eib=eib: e.tensor_copy(out=eib, in_=ef), reads=[b_ef], writes=[beib])
        S.op("dve", lambda e: e.tensor_tensor(out=gsm, in0=cv, in1=cv[:, :, 0:1].broadcast_to([128, 8, 16]),
                                              op=ALU.subtract), reads=[b_cv], writes=[b_gsm])
        S.op("act", lambda e: e.activation(out=gsm, in_=gsm, func=AF.Exp), reads=[b_gsm], writes=[b_gsm])
        S.op("dve", lambda e: e.tensor_reduce(out=small[:, 8:16], in_=gsm, axis=AX.X, op=ALU.add),
             reads=[b_gsm], writes=[b_small])
        S.op("dve", lambda e: e.reciprocal(out=small[:, 16:24], in_=small[:, 8:16]), reads=[b_small], writes=[b_small])
        S.op("dve", lambda e: e.tensor_tensor(out=gsm, in0=gsm,
                                              in1=small[:, 16:24].unsqueeze(2).broadcast_to([128, 8, 16]),
                                              op=ALU.mult), reads=[b_gsm, b_small], writes=[b_gsm])
        gsf = gsm.rearrange("p h k -> p (h k)")
        for k in range(128):
            j = gcount[0] % NB; gcount[0] += 1
            r8 = k % 8; r4 = k % 4
            S.dma("pool", lambda e, k=k, j=j, eib=eib: e.indirect_dma_start(
                out=uvb[j], out_offset=None, in_=uv16,
                in_offset=bass.IndirectOffsetOnAxis(ap=eib[:, k:k + 1], axis=0)),
                  reads=[beib, b_uv16], writes=[b_uvb[j]])
            S.op("dve", lambda e, k=k, j=j: e.scalar_tensor_tensor(
                out=uvb[j][:, 0:1024], in0=uvb[j][:, 0:1024], scalar=1.0, in1=hb, op0=ALU.mult, op1=ALU.mult,
                accum_out=araw[:, k:k + 1]), reads=[b_hb], writes=[b_uvb[j], b_ar[r8]])
            S.op("act", lambda e, k=k: e.activation(out=gl[:, k:k + 1], in_=araw[:, k:k + 1], func=AF.Gelu),
                 reads=[b_ar[r8]], writes=[b_gl[r8]])
            S.op("dve", lambda e, k=k, r4=r4: e.tensor_scalar(out=dg[r4], in0=ident, scalar1=gl[:, k:k + 1],
                                                            scalar2=gsf[:, k:k + 1], op0=ALU.mult, op1=ALU.mult),
                 reads=[b_ident, b_gl[r8], b_gsm], writes=[b_dg[r4]])
            for hf in range(2):
                S.op("pe", lambda e, k=k, j=j, r4=r4, hf=hf: e.matmul(
                    pacc[:, hf * 512:(hf + 1) * 512], lhsT=dg[r4], rhs=uvb[j][:, 1024 + hf * 512:1024 + (hf + 1) * 512],
                    start=(k == 0), stop=(k == 127)), reads=[b_dg[r4], b_uvb[j]], writes=[b_pacc])
        S.op("dve", lambda e, xb=xb: e.tensor_tensor(out=yo, in0=pacc, in1=xb, op=ALU.add),
             reads=[b_pacc, bxb], writes=[b_yo])
        S.dma("sp", lambda e, ti=ti: e.dma_start(out=y_tiles[ti], in_=yo), reads=[b_yo])


def phase_b3(S, nc, sb, ps, C, x1_tiles, y_tiles, uv16, b_uv16, NB=10):
    ident = sb.alloc([128], BF16); b_ident = Buf()
    iota16 = sb.alloc([16], F32); b_iota = Buf()
    cst = sb.alloc([144], F32); b_cst = Buf()
    wq = sb.alloc([8, 2048], BF16); b_wq = Buf()
    skT = sb.alloc([16, 128], BF16); b_skT = Buf()
    gffn = sb.alloc([1024], F32); b_gffn = Buf()
    S.dma("sp", lambda e: e.dma_start(out=cst, in_=C["consts"]), writes=[b_cst])
    S.op("dve", lambda e: e.tensor_copy(out=ident, in_=cst[:, 0:128]), reads=[b_cst], writes=[b_ident])
    S.op("dve", lambda e: e.tensor_copy(out=iota16, in_=cst[:, 128:144]), reads=[b_cst], writes=[b_iota])
    S.dma("sp", lambda e: e.dma_start(out=gffn, in_=C["g_ffn"].partition_broadcast(128)), writes=[b_gffn])
    stg = [sb.alloc([2048], F32) for _ in range(2)]
    b_stg = [Buf(), Buf()]
    wqv = C["w_query"].rearrange("(c p) n -> c p n", p=128)
    for c in range(8):
        S.dma("sp", lambda e, c=c: e.dma_start(out=stg[c % 2], in_=wqv[c]), writes=[b_stg[c % 2]])
        S.op("act" if c % 2 else "dve",
             (lambda e, c=c: e.copy(out=wq[:, c, :], in_=stg[c % 2])) if c % 2 else
             (lambda e, c=c: e.tensor_copy(out=wq[:, c, :], in_=stg[c % 2])),
             reads=[b_stg[c % 2]], writes=[b_wq])
    skv = C["sub_keys"].rearrange("g k d -> k g d")
    skn = stg[0].rearrange("p (g d) -> p g d", g=16)
    skb = sb.alloc([16, 128], BF16); b_skb = Buf()
    pbig = ps.alloc([16, 128], F32, align=512); b_pbig = Buf()
    pT = ps.alloc([1024], BF16, align=512); b_pT = Buf()
    pTv = pT.rearrange("p (c n) -> p c n", c=8)
    S.dma("sp", lambda e: e.dma_start(out=skn, in_=skv), writes=[b_stg[0]])
    S.op("dve", lambda e: e.tensor_copy(out=skb, in_=skn), reads=[b_stg[0]], writes=[b_skb])
    for half in range(2):
        for j in range(8):
            g = half * 8 + j
            S.op("pe", lambda e, g=g, j=j: e.transpose(out=pTv[:, j, :], in_=skb[:, g, :], identity=ident),
                 reads=[b_skb, b_ident], writes=[b_pT])
        S.op("dve", lambda e, half=half: e.tensor_copy(out=skT[:, half * 8:(half + 1) * 8, :], in_=pTv),
             reads=[b_pT], writes=[b_skT])

    x1t = [sb.alloc([1024], F32) for _ in range(2)]; b_x1t = [Buf(), Buf()]
    junk = sb.alloc([1024], F32); b_junk = Buf()
    hb2 = [sb.alloc([1024], BF16) for _ in range(2)]; b_hb2 = [Buf(), Buf()]
    hT = sb.alloc([8, 128], BF16); b_hT = Buf()
    qT = sb.alloc([16, 128], BF16); b_qT = Buf()
    s = sb.alloc([16, 128], F32); b_s = Buf()
    s2 = sb.alloc([16, 128], F32); b_s2 = Buf()
    sv = sb.alloc([16, 16], F32); b_sv = Buf()
    siu = sb.alloc([16, 16], U32); b_siu = Buf()
    sif = sb.alloc([16, 16], F32); b_sif = Buf()
    cand = sb.alloc([8, 256], F32); b_cand = Buf()
    cand2 = sb.alloc([8, 256], F32); b_cand2 = Buf()
    cv = sb.alloc([8, 16], F32); b_cv = Buf()
    ciu = sb.alloc([8, 16], U32); b_ciu = Buf()
    rcu = sb.alloc([2, 128], U32); b_rcu = Buf()
    rcf = sb.alloc([2, 128], F32); b_rcf = Buf()
    oh = sb.alloc([128, 16], F32); b_oh = Buf()
    i12 = sb.alloc([2, 128], F32); b_i12 = Buf()
    ef = sb.alloc([128], F32); b_ef = Buf()
    ei = [sb.alloc([128], I32) for _ in range(2)]; b_ei = [Buf(), Buf()]
    araw = sb.alloc([128], F32)
    gsm2 = [sb.alloc([8, 16], F32) for _ in range(2)]; b_gsm2 = [Buf(), Buf()]
    small = sb.alloc([32], F32); b_small = Buf()
    yo = sb.alloc([1024], F32); b_yo = Buf()
    uvb = [sb.alloc([2048], BF16) for _ in range(NB)]; b_uvb = [Buf() for _ in range(NB)]
    dg = [sb.alloc([128], BF16) for _ in range(4)]; b_dg = [Buf() for _ in range(4)]
    gl = sb.alloc([128], F32); b_gl = [Buf() for _ in range(8)]
    b_ar = [Buf() for _ in range(8)]
    pacc = ps.alloc([1024], F32, align=512); b_pacc = Buf()
    gcount = [0, 0]

    sv4 = sv.rearrange("p (h t) k -> p h t k", t=2)
    sif4 = sif.rearrange("p (h t) k -> p h t k", t=2)
    cand4 = cand.rearrange("p h (i j) -> p h i j", i=16)
    oh4 = oh.rearrange("p (h k) i -> p h k i", h=8)

    def route_gen(ti):
        xb = x1t[ti % 2]; bxb = b_x1t[ti % 2]
        eib = ei[ti % 2]; beib = b_ei[ti % 2]
        hb = hb2[ti % 2]; b_hb = b_hb2[ti % 2]
        gsm = gsm2[ti % 2]; b_gsm = b_gsm2[ti % 2]
        S.dma("sp", lambda e, xb=xb, ti=ti: e.dma_start(out=xb, in_=x1_tiles[ti]), writes=[bxb])
        yield
        S.op("act", lambda e, xb=xb: e.activation(out=junk, in_=xb, func=AF.Square, accum_out=small[:, 0:1]),
             reads=[bxb], writes=[b_junk, b_small])
        yield
        S.op("act", lambda e: e.activation(out=small[:, 1:2], in_=small[:, 0:1], func=AF.Sqrt, bias=EPS, scale=1.0 / D),
             reads=[b_small], writes=[b_small])
        yield
        S.op("dve", lambda e: e.reciprocal(out=small[:, 2:3], in_=small[:, 1:2]), reads=[b_small], writes=[b_small])
        yield
        S.op("dve", lambda e, xb=xb: e.scalar_tensor_tensor(out=hb, in0=xb, scalar=small[:, 2:3], in1=gffn,
                                                            op0=ALU.mult, op1=ALU.mult),
             reads=[bxb, b_small, b_gffn], writes=[b_hb])
        yield
        for c in range(8):
            S.op("pe", lambda e, c=c: e.transpose(out=pTv[:, c, :], in_=hb[:, c * 128:(c + 1) * 128], identity=ident),
                 reads=[b_hb, b_ident], writes=[b_pT])
            yield
        S.op("act", lambda e: e.copy(out=hT, in_=pTv), reads=[b_pT], writes=[b_hT])
        yield
        for g in range(16):
            for c in range(8):
                S.op("pe", lambda e, g=g, c=c: e.matmul(pbig[:, g, :], lhsT=wq[:, c, g * 128:(g + 1) * 128],
                                                        rhs=hT[:, c, :], start=(c == 0), stop=(c == 7)),
                     reads=[b_wq, b_hT], writes=[b_pbig])
                yield
        S.op("act", lambda e: e.copy(out=qT, in_=pbig), reads=[b_pbig], writes=[b_qT])
        yield
        for g in range(16):
            S.op("pe", lambda e, g=g: e.matmul(pbig[:, g, :], lhsT=qT[:, g, :], rhs=skT[:, g, :], start=True, stop=True),
                 reads=[b_qT, b_skT], writes=[b_pbig])
            yield
        S.op("dve", lambda e: e.tensor_copy(out=s, in_=pbig), reads=[b_pbig], writes=[b_s])
        yield
        for g in range(16):
            S.op("dve", lambda e, g=g: e.max(out=sv[:, g, 0:8], in_=s[:, g, :]), reads=[b_s], writes=[b_sv])
            yield
            S.op("dve", lambda e, g=g: e.match_replace(out=s2[:, g, :], in_to_replace=sv[:, g, 0:8],
                                                       in_values=s[:, g, :], imm_value=-1e30),
                 reads=[b_s, b_sv], writes=[b_s2])
            yield
            S.op("dve", lambda e, g=g: e.max(out=sv[:, g, 8:16], in_=s2[:, g, :]), reads=[b_s2], writes=[b_sv])
            yield
            S.op("dve", lambda e, g=g: e.max_index(out=siu[:, g, 0:8], in_max=sv[:, g, 0:8], in_values=s[:, g, :]),
                 reads=[b_s, b_sv], writes=[b_siu])
            yield
            S.op("dve", lambda e, g=g: e.max_index(out=siu[:, g, 8:16], in_max=sv[:, g, 8:16], in_values=s[:, g, :]),
                 reads=[b_s, b_sv], writes=[b_siu])
            yield
        S.op("dve", lambda e: e.tensor_copy(out=sif, in_=siu), reads=[b_siu], writes=[b_sif])
        yield
        for h in range(8):
            S.op("dve", lambda e, h=h: e.tensor_tensor(
                out=cand4[:, h], in0=sv4[:, h, 0, :].unsqueeze(2).broadcast_to([128, 16, 16]),
                in1=sv4[:, h, 1, :].unsqueeze(1).broadcast_to([128, 16, 16]), op=ALU.add),
                 reads=[b_sv], writes=[b_cand])
            yield
        for h in range(8):
            S.op("dve", lambda e, h=h: e.max(out=cv[:, h, 0:8], in_=cand[:, h, :]), reads=[b_cand], writes=[b_cv])
            yield
            S.op("dve", lambda e, h=h: e.match_replace(out=cand2[:, h, :], in_to_replace=cv[:, h, 0:8],
                                                       in_values=cand[:, h, :], imm_value=-1e30),
                 reads=[b_cand, b_cv], writes=[b_cand2])
            yield
            S.op("dve", lambda e, h=h: e.max(out=cv[:, h, 8:16], in_=cand2[:, h, :]), reads=[b_cand2], writes=[b_cv])
            yield
            S.op("dve", lambda e, h=h: e.max_index(out=ciu[:, h, 0:8], in_max=cv[:, h, 0:8], in_values=cand[:, h, :]),
                 reads=[b_cand, b_cv], writes=[b_ciu])
            yield
            S.op("dve", lambda e, h=h: e.max_index(out=ciu[:, h, 8:16], in_max=cv[:, h, 8:16], in_values=cand[:, h, :]),
                 reads=[b_cand, b_cv], writes=[b_ciu])
            yield
        ciu_f = ciu.rearrange("p h k -> p (h k)")
        S.op("dve", lambda e: e.tensor_single_scalar(out=rcu[:, 0, :], in_=ciu_f, scalar=4, op=ALU.logical_shift_right),
             reads=[b_ciu], writes=[b_rcu])
        yield
        S.op("dve", lambda e: e.tensor_single_scalar(out=rcu[:, 1, :], in_=ciu_f, scalar=15, op=ALU.bitwise_and),
             reads=[b_ciu], writes=[b_rcu])
        yield
        S.op("dve", lambda e: e.tensor_copy(out=rcf, in_=rcu), reads=[b_rcu], writes=[b_rcf])
        yield
        for t in range(2):
            S.op("dve", lambda e, t=t: e.tensor_tensor(
                out=oh, in0=rcf[:, t, :].unsqueeze(2).broadcast_to([128, 128, 16]),
                in1=iota16.unsqueeze(1).broadcast_to([128, 128, 16]), op=ALU.is_equal),
                 reads=[b_rcf, b_iota], writes=[b_oh])
            yield
            for h in range(8):
                S.op("dve", lambda e, h=h, t=t: e.tensor_tensor(
                    out=oh4[:, h], in0=oh4[:, h], in1=sif4[:, h, t, :].unsqueeze(1).broadcast_to([128, 16, 16]),
                    op=ALU.mult), reads=[b_oh, b_sif], writes=[b_oh])
                yield
            S.op("dve", lambda e, t=t: e.tensor_reduce(out=i12[:, t, :], in_=oh, axis=AX.X, op=ALU.add),
                 reads=[b_oh], writes=[b_i12])
            yield
        S.op("dve", lambda e: e.scalar_tensor_tensor(out=ef, in0=i12[:, 0, :], scalar=128.0, in1=i12[:, 1, :],
                                                     op0=ALU.mult, op1=ALU.add), reads=[b_i12], writes=[b_ef])
        yield
        S.op("dve", lambda e, eib=eib: e.tensor_copy(out=eib, in_=ef), reads=[b_ef], writes=[beib])
        yield
        S.op("dve", lambda e: e.tensor_tensor(out=gsm, in0=cv, in1=cv[:, :, 0:1].broadcast_to([128, 8, 16]),
                                              op=ALU.subtract), reads=[b_cv], writes=[b_gsm])
        yield
        S.op("act", lambda e: e.activation(out=gsm, in_=gsm, func=AF.Exp), reads=[b_gsm], writes=[b_gsm])
        yield
        S.op("dve", lambda e: e.tensor_reduce(out=small[:, 8:16], in_=gsm, axis=AX.X, op=ALU.add),
             reads=[b_gsm], writes=[b_small])
        yield
        S.op("dve", lambda e: e.reciprocal(out=small[:, 16:24], in_=small[:, 8:16]), reads=[b_small], writes=[b_small])
        yield
        S.op("dve", lambda e: e.tensor_tensor(out=gsm, in0=gsm,
                                              in1=small[:, 16:24].unsqueeze(2).broadcast_to([128, 8, 16]),
                                              op=ALU.mult), reads=[b_gsm, b_small], writes=[b_gsm])
        yield

    def slots(ti, filler, rate):
        xb = x1t[ti % 2]; bxb = b_x1t[ti % 2]
        eib = ei[ti % 2]; beib = b_ei[ti % 2]
        hb = hb2[ti % 2]; b_hb = b_hb2[ti % 2]
        gsm = gsm2[ti % 2]; b_gsm = b_gsm2[ti % 2]
        gsf = gsm.rearrange("p h k -> p (h k)")
        LAG = 3
        jj = {}
        for kx in range(128 + LAG):
            if kx < 128:
                k = kx
                j = gcount[0] % NB; gcount[0] += 1
                jj[k] = j
                r8 = k % 8
                S.dma("pool", lambda e, k=k, j=j, eib=eib: e.indirect_dma_start(
                    out=uvb[j], out_offset=None, in_=uv16,
                    in_offset=bass.IndirectOffsetOnAxis(ap=eib[:, k:k + 1], axis=0)),
                      reads=[beib, b_uv16], writes=[b_uvb[j]])
                S.op("dve", lambda e, k=k, j=j: e.scalar_tensor_tensor(
                    out=uvb[j][:, 0:1024], in0=uvb[j][:, 0:1024], scalar=1.0, in1=hb, op0=ALU.mult, op1=ALU.mult,
                    accum_out=araw[:, k:k + 1]), reads=[b_hb], writes=[b_uvb[j], b_ar[r8]])
                S.op("act", lambda e, k=k: e.activation(out=gl[:, k:k + 1], in_=araw[:, k:k + 1], func=AF.Gelu),
                     reads=[b_ar[r8]], writes=[b_gl[r8]])
            if kx >= LAG:
                k = kx - LAG
                j = jj[k]
                r8 = k % 8; r4 = k % 4
                S.op("dve", lambda e, k=k, r4=r4: e.tensor_scalar(out=dg[r4], in0=ident, scalar1=gl[:, k:k + 1],
                                                                scalar2=gsf[:, k:k + 1], op0=ALU.mult, op1=ALU.mult),
                     reads=[b_ident, b_gl[r8], b_gsm], writes=[b_dg[r4]])
                for hf in range(2):
                    S.op("pe", lambda e, k=k, j=j, r4=r4, hf=hf: e.matmul(
                        pacc[:, hf * 512:(hf + 1) * 512], lhsT=dg[r4], rhs=uvb[j][:, 1024 + hf * 512:1024 + (hf + 1) * 512],
                        start=(k == 0), stop=(k == 127)), reads=[b_dg[r4], b_uvb[j]], writes=[b_pacc])
            if filler is not None:
                for _ in range(rate):
                    next(filler, None)
        S.op("dve", lambda e, xb=xb: e.tensor_tensor(out=yo, in0=pacc, in1=xb, op=ALU.add),
             reads=[b_pacc, bxb], writes=[b_yo])
        S.dma("sp", lambda e, ti=ti: e.dma_start(out=y_tiles[ti], in_=yo), reads=[b_yo])


    NT = len(x1_tiles)
    g = route_gen(0)
    for _ in g:
        pass
    for ti in range(NT):
        nxt = route_gen(ti + 1) if ti + 1 < NT else None
        slots(ti, nxt, 4)
        if nxt is not None:
            for _ in nxt:
                pass


def convert_tables(S, C, uv16, b_uv16, nsplit=8):
    rows = NEXP // nsplit
    for t, name in enumerate(("expert_u", "expert_v")):
        for i in range(nsplit):
            S.dma("pool", lambda e, t=t, i=i, name=name: e.dma_start(
                out=uv16[i * rows:(i + 1) * rows, t * 1024:(t + 1) * 1024], in_=C[name][i * rows:(i + 1) * rows, :]),
                  writes=[b_uv16])


def make_consts():
    c = np.zeros((128, 144), np.float32)
    c[:, :128] = np.eye(128, dtype=np.float32)
    c[:, 128:144] = np.arange(16, dtype=np.float32)[None, :]
    return c


PP_GMIX, PP_GOUT, PP_CB, PP_LNG, PP_LNB, PP_GQ, PP_GK, PP_CW, PP_N = 0, 8, 16, 20, 24, 28, 29, 32, 160
RING = 8
URING = 6
QRING = 5


def key_tiles(s, NQ):
    kts = list(range(s - 2, s + 3))
    if s == 2:
        kts.append(5)
    if s == NQ + 1:
        kts.insert(0, NQ - 2)
    return kts


def phase_a(S, nc, sb, ps, C, NQ, xs_tiles, x1_tiles):
    NS = NQ + 4
    nslab = len(xs_tiles)
    cst = sb.alloc([144], F32); b_cst = Buf()
    ident = sb.alloc([128], BF16); b_ident = Buf()
    onesf = sb.alloc([128], F32); b_onesf = Buf()
    blk = sb.alloc([128], BF16); b_blk = Buf()
    pp = sb.alloc([PP_N], F32); b_pp = Buf()
    gq8 = sb.alloc([1], F32); b_gq8 = Buf()
    win = sb.alloc([8, 2560], BF16); b_win = Buf()
    wout = sb.alloc([8, 1024], BF16); b_wout = Buf()
    T = sb.alloc([8, 16, 64], F32); b_T = Buf()
    ntile_idx = {}
    for s in range(2, NQ + 2):
        for kt in key_tiles(s, NQ):
            ntile_idx[(s, kt)] = len(ntile_idx)
    NTI = len(ntile_idx)
    rmask = sb.alloc([nslab, NTI, 2], F32); b_rmask = Buf()
    S.dma("sp", lambda e: e.dma_start(out=cst, in_=C["consts"]), writes=[b_cst])
    S.dma("sp", lambda e: e.dma_start(out=pp, in_=C["pp"]), writes=[b_pp])
    S.dma("sp", lambda e: e.dma_start(out=T.rearrange("p h j c -> p (h j c)"), in_=C["ttab"]), writes=[b_T])
    S.dma("sp", lambda e: e.dma_start(out=rmask.rearrange("p a b c -> p (a b c)"), in_=C["rmask"]), writes=[b_rmask])
    S.op("dve", lambda e: e.tensor_copy(out=ident, in_=cst[:, 0:128]), reads=[b_cst], writes=[b_ident])
    S.op("dve", lambda e: e.memset(onesf, 1.0), writes=[b_onesf])
    S.op("dve", lambda e: e.memset(blk, 0.0), writes=[b_blk])
    S.op("dve", lambda e: e.memset(blk[0:64, 0:64], 1.0), writes=[b_blk])
    S.op("dve", lambda e: e.memset(blk[64:128, 64:128], 1.0), writes=[b_blk])
    S.op("dve", lambda e: e.tensor_scalar(out=gq8, in0=pp[:, PP_GQ:PP_GQ + 1], scalar1=0.125, scalar2=None, op0=ALU.mult),
         reads=[b_pp], writes=[b_gq8])
    stg = [sb.alloc([1280], F32) for _ in range(2)]; b_stg = [Buf(), Buf()]
    winv = C["w_in"].rearrange("(c p) n -> c p n", p=128)
    woutv = C["w_out"].rearrange("(c p) n -> c p n", p=128)
    n = 0
    for c in range(8):
        for hf in range(2):
            j = n % 2; n += 1
            S.dma("sp", lambda e, c=c, hf=hf, j=j: e.dma_start(out=stg[j], in_=winv[c][:, hf * 1280:(hf + 1) * 1280]),
                  writes=[b_stg[j]])
            S.op("dve", lambda e, c=c, hf=hf, j=j: e.tensor_scalar(
                out=win[:, c, hf * 1280:(hf + 1) * 1280], in0=stg[j], scalar1=pp[:, PP_GMIX + c:PP_GMIX + c + 1],
                scalar2=None, op0=ALU.mult), reads=[b_stg[j], b_pp], writes=[b_win])
    for c in range(8):
        j = n % 2; n += 1
        S.dma("sp", lambda e, c=c, j=j: e.dma_start(out=stg[j][:, 0:1024], in_=woutv[c]), writes=[b_stg[j]])
        S.op("dve", lambda e, c=c, j=j: e.tensor_scalar(
            out=wout[:, c, :], in0=stg[j][:, 0:1024], scalar1=pp[:, PP_GOUT + c:PP_GOUT + c + 1],
            scalar2=None, op0=ALU.mult), reads=[b_stg[j], b_pp], writes=[b_wout])
    cw = pp[:, PP_CW:PP_CW + 124].rearrange("p (c k) -> p c k", c=4)

    kT = [sb.alloc([4, 128], BF16) for _ in range(RING)]; b_kT = [Buf() for _ in range(RING)]
    va = [sb.alloc([8, 65], BF16) for _ in range(RING)]; b_va = [Buf() for _ in range(RING)]
    uT = [sb.alloc([4, 160], F32) for _ in range(URING)]; b_uT = [Buf() for _ in range(URING)]
    qA = [sb.alloc([4, 128], BF16) for _ in range(QRING)]; b_qA = [Buf() for _ in range(QRING)]
    qB = [sb.alloc([4, 128], BF16) for _ in range(QRING)]; b_qB = [Buf() for _ in range(QRING)]
    for r in range(RING):
        S.op("pool", lambda e, r=r: e.memset(va[r], 1.0), writes=[b_va[r]])
    for r in range(QRING):
        S.op("pool", lambda e, r=r: e.memset(qA[r], 0.0), writes=[b_qA[r]])
        S.op("pool", lambda e, r=r: e.memset(qB[r], 0.0), writes=[b_qB[r]])
    for r in range(URING):
        S.op("pool", lambda e, r=r: e.memset(uT[r], 0.0), writes=[b_uT[r]])
    xt = [sb.alloc([1024], F32) for _ in range(2)]; b_xt = [Buf(), Buf()]
    xr = [sb.alloc([1024], F32) for _ in range(2)]; b_xr = [Buf(), Buf()]
    junk = sb.alloc([1024], BF16); b_junk = Buf()
    small = sb.alloc([16], F32); b_small = Buf()
    xn = sb.alloc([1024], BF16); b_xn = Buf()
    hT = sb.alloc([8, 128], BF16); b_hT = Buf()
    qkr = sb.alloc([8, 128], F32); b_qkr = Buf()
    sq = sb.alloc([8, 128], BF16); b_sq = Buf()
    qks = sb.alloc([8, 128], F32); b_qks = Buf()
    sg = sb.alloc([4, 128], F32); b_sg = Buf()
    NTMP = 3
    tmp = [sb.alloc([128], F32) for _ in range(NTMP)]; b_tmp = [Buf() for _ in range(NTMP)]
    PT = [sb.alloc([128], BF16) for _ in range(NTMP)]; b_PT = [Buf() for _ in range(NTMP)]
    ya = sb.alloc([8, 64], F32); b_ya = Buf()
    yan = sb.alloc([512], BF16); b_yan = Buf()
    yaT = sb.alloc([4, 128], BF16); b_yaT = Buf()
    yconv = sb.alloc([4, 128], F32); b_yc = [Buf() for _ in range(4)]
    csq = sb.alloc([4, 128], F32); b_csq = Buf()
    ctmp = sb.alloc([128], F32); b_ctmp = Buf()
    st = sb.alloc([4, 128], F32); b_st = Buf()
    z = sb.alloc([4, 128], F32); b_z = Buf()
    ycn = sb.alloc([4, 128], BF16); b_ycn = Buf()
    bank0 = ps.alloc([512], F32, align=512); b_b0 = Buf()
    pT = bank0.bitcast(BF16); b_pT = b_b0
    pTv = pT.rearrange("p (c n) -> p c n", c=8)
    pV = bank0; b_pV = b_b0
    pX = ps.alloc([8, 128], F32, align=512); b_pX = Buf()
    pS = ps.alloc([8, 128], F32, align=512); b_pS = Buf()
    pY = pS.rearrange("p a b -> p (a b)")
    pMa = ps.alloc([4, 128], F32, align=512)
    pMb = ps.alloc([4, 128], F32, align=512)
    pMs = [pMa[:, 0, :], pMb[:, 0, :]]; b_pMs = [Buf(), Buf()]
    pO = ps.alloc([4, 128], F32, align=512); b_pO = Buf()
    pM = pO; b_pM = [b_pO, b_pO, b_pO, b_pO]

    def stage_p(sl, s):
        x = xt[s % 2]; bx = b_xt[s % 2]
        r = s % RING
        S.dma("sp", lambda e: e.dma_start(out=x, in_=xs_tiles[sl][s]), writes=[bx])
        S.op("act", lambda e: e.activation(out=junk, in_=x, func=AF.Square, accum_out=small[:, 0:1]),
             reads=[bx], writes=[b_junk, b_small])
        S.op("act", lambda e: e.activation(out=small[:, 1:2], in_=small[:, 0:1], func=AF.Sqrt, bias=EPS, scale=1.0 / D),
             reads=[b_small], writes=[b_small])
        S.op("dve", lambda e: e.reciprocal(out=small[:, 2:3], in_=small[:, 1:2]), reads=[b_small], writes=[b_small])
        S.op("dve", lambda e: e.tensor_scalar(out=xn, in0=x, scalar1=small[:, 2:3], scalar2=None, op0=ALU.mult),
             reads=[bx, b_small], writes=[b_xn])
        for c in range(8):
            S.op("pe", lambda e, c=c: e.transpose(out=pTv[:, c, :], in_=xn[:, c * 128:(c + 1) * 128], identity=ident),
                 reads=[b_xn, b_ident], writes=[b_pT])
        S.op("act", lambda e: e.copy(out=hT, in_=pTv), reads=[b_pT], writes=[b_hT])
        for j in range(8):
            for c in range(8):
                S.op("pe", lambda e, j=j, c=c: e.matmul(pX[:, j, :], lhsT=win[:, c, j * 128:(j + 1) * 128], rhs=hT[:, c, :],
                                                        start=(c == 0), stop=(c == 7)),
                     reads=[b_win, b_hT], writes=[b_pX])
        ur = s % URING
        S.op("act", lambda e: e.activation(out=sg, in_=pX[:, 4:8, :], func=AF.Sigmoid), reads=[b_pX], writes=[b_sg])
        S.op("dve", lambda e: e.tensor_tensor(out=uT[ur][:, :, 16:144], in0=pX[:, 0:4, :], in1=sg, op=ALU.mult),
             reads=[b_pX, b_sg], writes=[b_uT[ur]])
        if s >= 1:
            up = (s - 1) % URING
            S.op("pool", lambda e: e.tensor_copy(out=uT[up][:, :, 144:159], in_=uT[ur][:, :, 16:31]),
                 reads=[b_uT[ur]], writes=[b_uT[up]])
        if s + 1 < NS:
            un = (s + 1) % URING
            S.op("pool", lambda e: e.tensor_copy(out=uT[un][:, :, 1:16], in_=uT[ur][:, :, 129:144]),
                 reads=[b_uT[ur]], writes=[b_uT[un]])
        for j in range(8):
            for c in range(8):
                S.op("pe", lambda e, j=j, c=c: e.matmul(pX[:, j, :], lhsT=win[:, c, 1024 + j * 128:1024 + (j + 1) * 128],
                                                        rhs=hT[:, c, :], start=(c == 0), stop=(c == 7)),
                     reads=[b_win, b_hT], writes=[b_pX])
        S.op("act", lambda e: e.copy(out=qkr, in_=pX), reads=[b_pX], writes=[b_qkr])
        S.op("act", lambda e: e.activation(out=sq, in_=qkr, func=AF.Square), reads=[b_qkr], writes=[b_sq])
        for j in range(8):
            S.op("pe", lambda e, j=j: e.matmul(pS[:, j, :], lhsT=blk, rhs=sq[:, j, :], start=True, stop=True),
                 reads=[b_blk, b_sq], writes=[b_pS])
        S.op("act", lambda e: e.activation(out=qks, in_=pS, func=AF.Sqrt, bias=EPS, scale=1.0 / 64), reads=[b_pS], writes=[b_qks])
        S.op("dve", lambda e: e.reciprocal(out=qks, in_=qks), reads=[b_qks], writes=[b_qks])
        S.op("dve", lambda e: e.tensor_tensor(out=qkr, in0=qkr, in1=qks, op=ALU.mult), reads=[b_qkr, b_qks], writes=[b_qkr])
        S.op("dve", lambda e: e.tensor_scalar(out=kT[r], in0=qkr[:, 4:8, :], scalar1=pp[:, PP_GK:PP_GK + 1], scalar2=None,
                                              op0=ALU.mult), reads=[b_qkr, b_pp], writes=[b_kT[r]])
        if 2 <= s < NQ + 2:
            qr = s % QRING
            S.op("dve", lambda e: e.tensor_scalar(out=qA[qr][0:64], in0=qkr[0:64, 0:4, :], scalar1=gq8[0:64], scalar2=None,
                                                  op0=ALU.mult), reads=[b_qkr, b_gq8], writes=[b_qA[qr]])
            S.op("dve", lambda e: e.tensor_scalar(out=qB[qr][64:128], in0=qkr[64:128, 0:4, :], scalar1=gq8[64:128],
                                                  scalar2=None, op0=ALU.mult), reads=[b_qkr, b_gq8], writes=[b_qB[qr]])
        for c in range(8):
            S.op("pe", lambda e, c=c: e.matmul(pV, lhsT=hT[:, c, :], rhs=win[:, c, 2048:2560], start=(c == 0), stop=(c == 7)),
                 reads=[b_win, b_hT], writes=[b_pV])
        S.op("act", lambda e: e.copy(out=va[r][:, :, 0:64], in_=pV.rearrange("p (h d) -> p h d", h=8)),
             reads=[b_pV], writes=[b_va[r]])

    cnt = [0]

    def stage_m(sl, s):
        i = s - 2
        qr = s % QRING
        x = xr[i % 2]; bx = b_xr[i % 2]
        S.dma("sp", lambda e: e.dma_start(out=x, in_=xs_tiles[sl][s]), writes=[bx])
        kts = key_tiles(s, NQ)
        for grp in range(2):
            for hh in range(4):
                h = grp * 4 + hh
                c = h // 2
                qm = (qA if h % 2 == 0 else qB)[qr]
                bqm = (b_qA if h % 2 == 0 else b_qB)[qr]
                for n_, kt in enumerate(kts):
                    kr = kt % RING
                    m = cnt[0] % 2; cnt[0] += 1
                    t_ = cnt[0] % NTMP
                    j0 = 8 - 2 * (kt - s)
                    ti = ntile_idx[(s, kt)]
                    S.op("pe", lambda e, kr=kr, c=c, qm=qm, m=m: e.matmul(pMs[m], lhsT=kT[kr][:, c, :], rhs=qm[:, c, :],
                                                                        start=True, stop=True),
                         reads=[b_kT[kr], bqm], writes=[b_pMs[m]])
                    S.op("dve", lambda e, m=m, t_=t_, h=h, j0=j0: e.tensor_tensor(
                        out=tmp[t_], in0=pMs[m], in1=T[:, h, j0:j0 + 2, :].rearrange("p a b -> p (a b)"), op=ALU.add),
                         reads=[b_pMs[m], b_T], writes=[b_tmp[t_]])
                    for qh in range(2):
                        S.op("act", lambda e, t_=t_, qh=qh, ti=ti: e.activation(
                            out=PT[t_][:, qh * 64:(qh + 1) * 64], in_=tmp[t_][:, qh * 64:(qh + 1) * 64], func=AF.Exp,
                            bias=rmask[:, sl, ti, qh:qh + 1]), reads=[b_tmp[t_], b_rmask], writes=[b_PT[t_]])
                    S.op("pe", lambda e, t_=t_, kr=kr, h=h, hh=hh, n_=n_: e.matmul(
                        pO[:, hh, 0:65], lhsT=PT[t_], rhs=va[kr][:, h, :], start=(n_ == 0), stop=(n_ == len(kts) - 1)),
                         reads=[b_PT[t_], b_va[kr]], writes=[b_pO])
            S.op("dve", lambda e: e.reciprocal(out=small[:, 4:8], in_=pO[:, :, 64]), reads=[b_pO], writes=[b_small])
            S.op("dve", lambda e, grp=grp: e.tensor_tensor(
                out=ya[:, grp * 4:(grp + 1) * 4, :], in0=pO[:, :, 0:64],
                in1=small[:, 4:8].unsqueeze(2).broadcast_to([128, 4, 64]), op=ALU.mult),
                 reads=[b_pO, b_small], writes=[b_ya])
        yaf = ya.rearrange("p h d -> p (h d)")
        S.op("act", lambda e: e.activation(out=junk[:, 0:512], in_=yaf, func=AF.Square, accum_out=small[:, 8:9]),
             reads=[b_ya], writes=[b_junk, b_small])
        S.op("act", lambda e: e.activation(out=small[:, 9:10], in_=small[:, 8:9], func=AF.Sqrt, bias=EPS, scale=1.0 / 512),
             reads=[b_small], writes=[b_small])
        S.op("dve", lambda e: e.reciprocal(out=small[:, 10:11], in_=small[:, 9:10]), reads=[b_small], writes=[b_small])
        S.op("dve", lambda e: e.tensor_scalar(out=yan, in0=yaf, scalar1=small[:, 10:11], scalar2=None, op0=ALU.mult),
             reads=[b_ya, b_small], writes=[b_yan])
        for c in range(4):
            S.op("pe", lambda e, c=c: e.transpose(out=pTv[:, c, :], in_=yan[:, c * 128:(c + 1) * 128], identity=ident),
                 reads=[b_yan, b_ident], writes=[b_pT])
        S.op("act", lambda e: e.copy(out=yaT, in_=pTv[:, 0:4, :]), reads=[b_pT], writes=[b_yaT])
        ur = s % URING
        for c in range(4):
            eng = "pool" if c < 1 else "dve"
            S.op(eng, lambda e, c=c: e.tensor_scalar(out=yconv[:, c, :], in0=uT[ur][:, c, 1:129], scalar1=cw[:, c, 0:1],
                                                     scalar2=pp[:, PP_CB + c:PP_CB + c + 1], op0=ALU.mult, op1=ALU.add),
                 reads=[b_uT[ur], b_pp], writes=[b_yc[c]])
            for k in range(1, 31):
                if eng == "dve":
                    S.op(eng, lambda e, c=c, k=k: e.scalar_tensor_tensor(
                        out=yconv[:, c, :], in0=uT[ur][:, c, k + 1:k + 129], scalar=cw[:, c, k:k + 1], in1=yconv[:, c, :],
                        op0=ALU.mult, op1=ALU.add), reads=[b_uT[ur], b_pp], writes=[b_yc[c]])
                else:
                    S.op(eng, lambda e, c=c, k=k: e.tensor_scalar(
                        out=ctmp, in0=uT[ur][:, c, k + 1:k + 129], scalar1=cw[:, c, k:k + 1], scalar2=None, op0=ALU.mult),
                         reads=[b_uT[ur], b_pp], writes=[b_ctmp])
                    S.op(eng, lambda e, c=c: e.tensor_tensor(out=yconv[:, c, :], in0=yconv[:, c, :], in1=ctmp, op=ALU.add),
                         reads=[b_ctmp], writes=[b_yc[c]])
        S.op("act", lambda e: e.activation(out=csq, in_=yconv, func=AF.Square), reads=b_yc, writes=[b_csq])
        for c in range(4):
            S.op("pe", lambda e, c=c: e.matmul(pM[:, 2, :], lhsT=onesf, rhs=yconv[:, c, :], start=(c == 0), stop=(c == 3)),
                 reads=[b_onesf] + b_yc, writes=[b_pM[2]])
        for c in range(4):
            S.op("pe", lambda e, c=c: e.matmul(pM[:, 3, :], lhsT=onesf, rhs=csq[:, c, :], start=(c == 0), stop=(c == 3)),
                 reads=[b_onesf, b_csq], writes=[b_pM[3]])
        S.op("dve", lambda e: e.tensor_scalar(out=st[:, 0, :], in0=pM[:, 2, :], scalar1=1.0 / 512, scalar2=None, op0=ALU.mult),
             reads=[b_pM[2]], writes=[b_st])
        S.op("dve", lambda e: e.tensor_tensor(out=st[:, 1, :], in0=st[:, 0, :], in1=st[:, 0, :], op=ALU.mult),
             reads=[b_st], writes=[b_st])
        S.op("dve", lambda e: e.scalar_tensor_tensor(out=st[:, 2, :], in0=pM[:, 3, :], scalar=1.0 / 512, in1=st[:, 1, :],
                                                     op0=ALU.mult, op1=ALU.subtract), reads=[b_pM[3], b_st], writes=[b_st])
        S.op("act", lambda e: e.activation(out=st[:, 3, :], in_=st[:, 2, :], func=AF.Sqrt, bias=EPS, scale=1.0),
             reads=[b_st], writes=[b_st])
        S.op("dve", lambda e: e.reciprocal(out=st[:, 3, :], in_=st[:, 3, :]), reads=[b_st], writes=[b_st])
        S.op("dve", lambda e: e.tensor_tensor(out=z, in0=yconv, in1=st[:, 0, :].unsqueeze(1).broadcast_to([128, 4, 128]),
                                              op=ALU.subtract), reads=b_yc + [b_st], writes=[b_z])
        S.op("dve", lambda e: e.tensor_tensor(out=z, in0=z, in1=st[:, 3, :].unsqueeze(1).broadcast_to([128, 4, 128]),
                                              op=ALU.mult), reads=[b_z, b_st], writes=[b_z])
        for c in range(4):
            S.op("dve", lambda e, c=c: e.tensor_scalar(out=z[:, c, :], in0=z[:, c, :], scalar1=pp[:, PP_LNG + c:PP_LNG + c + 1],
                                                       scalar2=pp[:, PP_LNB + c:PP_LNB + c + 1], op0=ALU.mult, op1=ALU.add),
                 reads=[b_z, b_pp], writes=[b_z])
        S.op("act", lambda e: e.activation(out=z, in_=z, func=AF.Silu), reads=[b_z], writes=[b_z])
        S.op("act", lambda e: e.activation(out=csq, in_=z, func=AF.Square), reads=[b_z], writes=[b_csq])
        for c in range(4):
            S.op("pe", lambda e, c=c: e.matmul(pM[:, 2, :], lhsT=onesf, rhs=csq[:, c, :], start=(c == 0), stop=(c == 3)),
                 reads=[b_onesf, b_csq], writes=[b_pM[2]])
        S.op("act", lambda e: e.activation(out=st[:, 0, :], in_=pM[:, 2, :], func=AF.Sqrt, bias=EPS, scale=1.0 / 512),
             reads=[b_pM[2]], writes=[b_st])
        S.op("dve", lambda e: e.reciprocal(out=st[:, 0, :], in_=st[:, 0, :]), reads=[b_st], writes=[b_st])
        S.op("dve", lambda e: e.tensor_tensor(out=ycn, in0=z, in1=st[:, 0, :].unsqueeze(1).broadcast_to([128, 4, 128]),
                                              op=ALU.mult), reads=[b_z, b_st], writes=[b_ycn])
        for hf in range(2):
            for c in range(8):
                lhs = ycn[:, c, :] if c < 4 else yaT[:, c - 4, :]
                S.op("pe", lambda e, c=c, hf=hf, lhs=lhs: e.matmul(pY[:, hf * 512:(hf + 1) * 512], lhsT=lhs,
                                                                   rhs=wout[:, c, hf * 512:(hf + 1) * 512],
                                                                   start=(c == 0), stop=(c == 7)),
                     reads=[b_ycn, b_yaT, b_wout], writes=[b_pS])
        S.op("dve", lambda e: e.tensor_tensor(out=x, in0=pY, in1=x, op=ALU.add), reads=[b_pS, bx], writes=[bx])
        S.dma("sp", lambda e: e.dma_start(out=x1_tiles[sl][i], in_=x), reads=[bx])

    for sl in range(nslab):
        for step in range(NS + 3):
            if step < NS:
                stage_p(sl, step)
            s = step - 3
            if 2 <= s < NQ + 2:
                stage_m(sl, s)


def row_mask_table(NQ, kind, q=0, R=None):
    idx = {}
    for s in range(2, NQ + 2):
        for kt in key_tiles(s, NQ):
            idx[(s, kt)] = len(idx)
    out = np.full((128, len(idx), 2), NEGM, np.float32)
    nrows = 2 * NQ
    if kind == "full":
        R = nrows; base = 0
    else:
        base = q
    for (s, kt), ti in idx.items():
        for qh in range(2):
            r = base + 2 * (s - 2) + qh
            rs = min(max(r - 4, 0), R - 8)
            for kh in range(2):
                rk = base + 2 * kt + kh - 4
                if rs <= rk < rs + 8:
                    out[kh * 64:(kh + 1) * 64, ti, qh] = 0.0
    return out


def bias_table(rpb):
    T = np.full((128, 8, 16, 64), NEGM, np.float32)
    cq = np.arange(64)
    cs = np.clip(cq - 8, 0, 48)
    for kh in range(2):
        for j in range(16):
            dr = kh - j + 8
            if abs(dr) > 7:
                continue
            for cp in range(64):
                ok = (cp >= cs) & (cp < cs + 16)
                off = np.clip(cp - cq + 15, 0, 30)
                vals = rpb[:, dr + 7, :][:, off]
                T[kh * 64 + cp, :, j, :] = np.where(ok[None, :], vals, NEGM)
    return T


def small_params(g_mix, g_out_conv, g_out_attn, conv_w, conv_b, ln_g, ln_b, q_g, k_g):
    pp = np.zeros((128, PP_N), np.float32)
    pp[:, PP_GMIX:PP_GMIX + 8] = g_mix.reshape(8, 128).T
    pp[:, PP_GOUT:PP_GOUT + 4] = g_out_conv.reshape(4, 128).T
    pp[:, PP_GOUT + 4:PP_GOUT + 8] = g_out_attn.reshape(4, 128).T
    pp[:, PP_CB:PP_CB + 4] = conv_b.reshape(4, 128).T
    pp[:, PP_LNG:PP_LNG + 4] = ln_g.reshape(4, 128).T
    pp[:, PP_LNB:PP_LNB + 4] = ln_b.reshape(4, 128).T
    pp[:, PP_GQ] = np.tile(q_g, 2)
    pp[:, PP_GK] = np.tile(k_g, 2)
    pp[:, PP_CW:PP_CW + 124] = conv_w.T.reshape(4, 128, 31).transpose(1, 0, 2).reshape(128, 124)
    return pp


NQ_FULL = 32
N_CORES = 8
ARENA_WORDS = 51200


def build_program(NQ=NQ_FULL):
    NS = NQ + 4
    nc = bass.Bass("TRN2", target_bir_lowering=False)
    xs = nc.dram_tensor("xs", [2, NS * 128, D], F32, kind="ExternalInput").ap()
    y = nc.dram_tensor("y", [2, NQ * 128, D], F32, kind="ExternalOutput").ap()
    nti = sum(len(key_tiles(s, NQ)) for s in range(2, NQ + 2))
    C = {}
    for name, shape in (("consts", [128, 144]), ("pp", [128, PP_N]), ("ttab", [128, 8 * 16 * 64]),
                        ("rmask", [128, 2 * nti * 2]), ("w_in", [D, 2560]), ("w_out", [D, D]),
                        ("g_ffn", [D]), ("w_query", [D, 2048]), ("sub_keys", [16, 128, 128]),
                        ("expert_u", [NEXP, D]), ("expert_v", [NEXP, D])):
        C[name] = nc.dram_tensor(name, shape, F32, kind="ExternalInput").ap()
    S = Sched(nc)
    sbh = nc.alloc_sbuf_tensor("arena", [128, ARENA_WORDS], F32)
    psh = nc.alloc_psum_tensor("parena", [128, 4096], F32)
    uv16 = nc.dram_tensor("uv16", [NEXP, 2048], BF16, kind="Internal").ap()
    b_uv16 = Buf()
    convert_tables(S, C, uv16, b_uv16)
    xst = xs.rearrange("a (n p) d -> a n p d", p=128)
    yt = y.rearrange("a (n p) d -> a n p d", p=128)
    sb = Arena(sbh, ARENA_WORDS); ps = Arena(psh, 4096)
    phase_a(S, nc, sb, ps, C, NQ, [[xst[a, i] for i in range(NS)] for a in range(2)],
            [[yt[a, i] for i in range(NQ)] for a in range(2)])
    S.barrier()
    sb = Arena(sbh, ARENA_WORDS); ps = Arena(psh, 4096)
    ytl = [yt[a, i] for a in range(2) for i in range(NQ)]
    phase_b3(S, nc, sb, ps, C, ytl, ytl, uv16, b_uv16)
    S.emit()
    return nc, S


def kernel(x_prompt, x_sample, g_mix, w_in, conv_w, conv_b, conv_ln_g, conv_ln_b,
           q_norm_g, k_norm_g, rpb, g_out_conv, g_out_attn, w_out, g_ffn,
           w_query, sub_keys, expert_u, expert_v):
    f = lambda a: np.ascontiguousarray(np.asarray(a, dtype=np.float32))
    x_prompt, x_sample = f(x_prompt), f(x_sample)
    NQ = NQ_FULL
    NS = NQ + 4
    T = NQ * 128
    H = 256
    shared = {
        "consts": make_consts(),
        "pp": small_params(f(g_mix)[0], f(g_out_conv)[0], f(g_out_attn)[0], f(conv_w)[0], f(conv_b)[0],
                           f(conv_ln_g)[0], f(conv_ln_b)[0], f(q_norm_g)[0], f(k_norm_g)[0]),
        "ttab": bias_table(f(rpb)[0]).reshape(128, -1),
        "w_in": f(w_in)[0], "w_out": f(w_out)[0], "g_ffn": f(g_ffn)[0], "w_query": f(w_query)[0],
        "sub_keys": f(sub_keys)[0].reshape(16, 128, 128),
        "expert_u": f(expert_u)[0], "expert_v": f(expert_v)[0],
    }
    rm_full = row_mask_table(NQ, "full")
    in_maps = []
    for c in range(N_CORES):
        b, q = c // 4, c % 4
        xs = np.zeros((2, NS * 128, D), np.float32)
        xs[0, H:H + T] = x_sample[c]
        lo, hi = q * T - H, q * T + T + H
        clo, chi = max(lo, 0), min(hi, x_prompt.shape[1])
        xs[1, clo - lo:clo - lo + (chi - clo)] = x_prompt[b, clo:chi]
        rm = np.stack([rm_full, row_mask_table(NQ, "chunk", q=2 * NQ * q, R=x_prompt.shape[1] // 64)], axis=1)
        m = dict(shared)
        m["xs"] = xs
        m["rmask"] = np.ascontiguousarray(rm.reshape(128, -1))
        in_maps.append(m)
    nc, _ = build_program(NQ)
    res = run_bass_kernel_spmd(nc, in_maps, core_ids=list(range(N_CORES)))
    y_prompt = np.zeros_like(x_prompt)
    y_sample = np.zeros_like(x_sample)
    for c in range(N_CORES):
        yc = np.asarray(res.results[c]["y"], dtype=np.float32)
        y_sample[c] = yc[0]
        y_prompt[c // 4, (c % 4) * T:(c % 4 + 1) * T] = yc[1]
    return (y_prompt, y_sample)
```

```python
import numpy as np
import concourse.bass as bass
import concourse.mybir as mybir
from concourse.bass_utils import run_bass_kernel_spmd

F32 = mybir.dt.float32
BF16 = mybir.dt.bfloat16
I32 = mybir.dt.int32
U32 = mybir.dt.uint32
ALU = mybir.AluOpType
AF = mybir.ActivationFunctionType
AX = mybir.AxisListType

D = 1024
NEXP = 16384
EPS = 1e-6
NEGM = -30000.0


class Buf:
    __slots__ = ("name", "w", "r")

    def __init__(self, name=""):
        self.name = name
        self.w = None
        self.r = []


class Op:
    __slots__ = ("eng", "fn", "deps", "signal", "semkey", "count", "is_dma")

    def __init__(self, eng, fn):
        self.eng = eng
        self.fn = fn
        self.deps = []
        self.signal = False
        self.semkey = None
        self.count = 0
        self.is_dma = False


class Sched:
    ENGS = ("pe", "dve", "act", "pool", "sp")
    SELF_SYNC = {"pe": False, "dve": True, "act": True, "pool": True, "sp": False}

    def __init__(self, nc, n_dma_sems=None):
        self.nc = nc
        self.ops = {e: [] for e in self.ENGS}
        self.n_dma_sems = n_dma_sems or {"sp": 16, "act": 8, "pool": 24}
        self.dma_rr = {}
        self.dma_last = {}
        self.dma_cnt = {}
        self.last_real = {}

    def _add_deps(self, op, reads, writes):
        deps = []
        for b in reads:
            if b.w is not None:
                deps.append(b.w)
        for b in writes:
            if b.w is not None:
                deps.append(b.w)
            deps.extend(b.r)
        for d in deps:
            if d is op:
                continue
            if (not d.is_dma) and d.eng == op.eng and not self.SELF_SYNC[op.eng]:
                continue
            op.deps.append(d)
            d.signal = True
        for b in reads:
            b.r.append(op)
        for b in writes:
            b.w = op
            b.r = []

    def op(self, eng, fn, reads=(), writes=()):
        o = Op(eng, fn)
        self._add_deps(o, reads, writes)
        self.ops[eng].append(o)
        self.last_real[eng] = o
        return o

    def dma(self, eng, fn, reads=(), writes=()):
        o = Op(eng, fn)
        o.is_dma = True
        o.signal = True
        rr = self.dma_rr.get(eng, 0)
        self.dma_rr[eng] = rr + 1
        key = ("dma", eng, rr % self.n_dma_sems[eng])
        o.semkey = key
        prev = self.dma_last.get(key)
        if prev is not None:
            o.deps.append(prev)
        self.dma_last[key] = o
        c = self.dma_cnt.get(key, 0) + 16
        self.dma_cnt[key] = c
        o.count = c
        self._add_deps(o, reads, writes)
        self.ops[eng].append(o)
        return o

    def barrier(self):
        lasts = [o for o in self.last_real.values()] + list(self.dma_last.values())
        for e in self.ENGS:
            o = Op(e, None)
            for d in lasts:
                if (not d.is_dma) and d.eng == e:
                    continue
                o.deps.append(d)
                d.signal = True
            self.ops[e].append(o)

    def emit(self, final_wait_eng="sp"):
        nc = self.nc
        self.barrier()
        for e in self.ENGS:
            c = 0
            for o in self.ops[e]:
                if o.is_dma or o.fn is None:
                    continue
                o.semkey = ("eng", e)
                if o.signal:
                    c += 1
                    o.count = c
        keys = set()
        for e in self.ENGS:
            for o in self.ops[e]:
                if o.signal and o.fn is not None:
                    keys.add(o.semkey)
        sems = {}
        for k in sorted(keys, key=str):
            sems[k] = nc.alloc_semaphore(name="s_" + "_".join(str(x) for x in k))
        stats = {"ins": 0, "wait": 0}
        with nc.Block() as block:
            deco = {"pe": block.tensor, "dve": block.vector, "act": block.scalar,
                    "pool": block.gpsimd, "sp": block.sync}
            for e in self.ENGS:
                ops = self.ops[e]

                def body(eng, ops=ops):
                    seen = {}
                    for o in ops:
                        for d in o.deps:
                            if seen.get(d.semkey, 0) < d.count:
                                eng.wait_ge(sems[d.semkey], d.count)
                                seen[d.semkey] = d.count
                                stats["wait"] += 1
                        if o.fn is None:
                            continue
                        ins = o.fn(eng)
                        stats["ins"] += 1
                        if o.signal:
                            ins.then_inc(sems[o.semkey], 16 if o.is_dma else 1)

                deco[e](body)
        self.stats = stats


class Arena:
    def __init__(self, handle, nwords):
        self.h = handle
        self.n = nwords
        self.off = 0

    def alloc(self, free_shape, dtype=F32, align=16):
        n = 1
        for s in free_shape:
            n *= s
        size = 4 if dtype in (F32, I32, U32) else 2
        words = (n * size + 3) // 4
        self.off = (self.off + align - 1) // align * align
        assert self.off + words <= self.n, ("arena overflow", self.off, words, self.n)
        ap = self.h[:, self.off:self.off + words]
        self.off += words
        if dtype != F32:
            ap = ap.bitcast(dtype)
            if ap.shape[1] != n:
                ap = ap[:, 0:n]
        if len(free_shape) == 2:
            ap = ap.rearrange("p (a b) -> p a b", a=free_shape[0])
        elif len(free_shape) == 3:
            ap = ap.rearrange("p (a b c) -> p a b c", a=free_shape[0], b=free_shape[1])
        return ap


def phase_b(S, nc, sb, ps, C, x1_tiles, y_tiles, NB=6):
    ident = sb.alloc([128], BF16); b_ident = Buf()
    iota16 = sb.alloc([16], F32); b_iota = Buf()
    cst = sb.alloc([144], F32); b_cst = Buf()
    wq = sb.alloc([8, 2048], BF16); b_wq = Buf()
    skT = sb.alloc([16, 128], BF16); b_skT = Buf()
    gffn = sb.alloc([1024], F32); b_gffn = Buf()
    S.dma("sp", lambda e: e.dma_start(out=cst, in_=C["consts"]), writes=[b_cst])
    S.op("dve", lambda e: e.tensor_copy(out=ident, in_=cst[:, 0:128]), reads=[b_cst], writes=[b_ident])
    S.op("dve", lambda e: e.tensor_copy(out=iota16, in_=cst[:, 128:144]), reads=[b_cst], writes=[b_iota])
    S.dma("sp", lambda e: e.dma_start(out=gffn, in_=C["g_ffn"].partition_broadcast(128)), writes=[b_gffn])
    stg = [sb.alloc([2048], F32) for _ in range(2)]
    b_stg = [Buf(), Buf()]
    wqv = C["w_query"].rearrange("(c p) n -> c p n", p=128)
    for c in range(8):
        S.dma("sp", lambda e, c=c: e.dma_start(out=stg[c % 2], in_=wqv[c]), writes=[b_stg[c % 2]])
        S.op("act" if c % 2 else "dve",
             (lambda e, c=c: e.copy(out=wq[:, c, :], in_=stg[c % 2])) if c % 2 else
             (lambda e, c=c: e.tensor_copy(out=wq[:, c, :], in_=stg[c % 2])),
             reads=[b_stg[c % 2]], writes=[b_wq])
    skv = C["sub_keys"].rearrange("g k d -> k g d")
    skn = stg[0].rearrange("p (g d) -> p g d", g=16)
    skb = sb.alloc([16, 128], BF16); b_skb = Buf()
    pbig = ps.alloc([16, 128], F32, align=512); b_pbig = Buf()
    pT = ps.alloc([1024], BF16, align=512); b_pT = Buf()
    pTv = pT.rearrange("p (c n) -> p c n", c=8)
    S.dma("sp", lambda e: e.dma_start(out=skn, in_=skv), writes=[b_stg[0]])
    S.op("dve", lambda e: e.tensor_copy(out=skb, in_=skn), reads=[b_stg[0]], writes=[b_skb])
    for half in range(2):
        for j in range(8):
            g = half * 8 + j
            S.op("pe", lambda e, g=g, j=j: e.transpose(out=pTv[:, j, :], in_=skb[:, g, :], identity=ident),
                 reads=[b_skb, b_ident], writes=[b_pT])
        S.op("dve", lambda e, half=half: e.tensor_copy(out=skT[:, half * 8:(half + 1) * 8, :], in_=pTv),
             reads=[b_pT], writes=[b_skT])

    x1t = [sb.alloc([1024], F32) for _ in range(2)]; b_x1t = [Buf(), Buf()]
    junk = sb.alloc([1024], F32); b_junk = Buf()
    hn = sb.alloc([1024], F32); b_hn = Buf()
    hb = sb.alloc([1024], BF16); b_hb = Buf()
    hT = sb.alloc([8, 128], BF16); b_hT = Buf()
    qT = sb.alloc([16, 128], BF16); b_qT = Buf()
    s = sb.alloc([16, 128], F32); b_s = Buf()
    s2 = sb.alloc([16, 128], F32); b_s2 = Buf()
    sv = sb.alloc([16, 16], F32); b_sv = Buf()
    siu = sb.alloc([16, 16], U32); b_siu = Buf()
    sif = sb.alloc([16, 16], F32); b_sif = Buf()
    cand = sb.alloc([8, 256], F32); b_cand = Buf()
    cand2 = sb.alloc([8, 256], F32); b_cand2 = Buf()
    cv = sb.alloc([8, 16], F32); b_cv = Buf()
    ciu = sb.alloc([8, 16], U32); b_ciu = Buf()
    rcu = sb.alloc([2, 128], U32); b_rcu = Buf()
    rcf = sb.alloc([2, 128], F32); b_rcf = Buf()
    oh = sb.alloc([128, 16], F32); b_oh = Buf()
    i12 = sb.alloc([2, 128], F32); b_i12 = Buf()
    ef = sb.alloc([128], F32); b_ef = Buf()
    ei = [sb.alloc([128], I32) for _ in range(2)]; b_ei = [Buf(), Buf()]
    araw = sb.alloc([128], F32); b_araw = Buf()
    aw = sb.alloc([128], F32); b_aw = Buf()
    gsm = sb.alloc([8, 16], F32); b_gsm = Buf()
    small = sb.alloc([32], F32); b_small = Buf()
    acc = [sb.alloc([1024], F32) for _ in range(2)]; b_acc = [Buf(), Buf()]
    ub = [sb.alloc([1024], F32) for _ in range(NB)]; b_ub = [Buf() for _ in range(NB)]
    vb = [sb.alloc([1024], F32) for _ in range(NB)]; b_vb = [Buf() for _ in range(NB)]
    gcount = [0, 0]

    sv4 = sv.rearrange("p (h t) k -> p h t k", t=2)
    sif4 = sif.rearrange("p (h t) k -> p h t k", t=2)
    cand4 = cand.rearrange("p h (i j) -> p h i j", i=16)
    oh4 = oh.rearrange("p (h k) i -> p h k i", h=8)

    for ti in range(len(x1_tiles)):
        xb = x1t[ti % 2]; bxb = b_x1t[ti % 2]
        eib = ei[ti % 2]; beib = b_ei[ti % 2]
        ac = acc[ti % 2]; bac = b_acc[ti % 2]
        S.dma("sp", lambda e, xb=xb, ti=ti: e.dma_start(out=xb, in_=x1_tiles[ti]), writes=[bxb])
        S.op("act", lambda e, xb=xb: e.activation(out=junk, in_=xb, func=AF.Square, accum_out=small[:, 0:1]),
             reads=[bxb], writes=[b_junk, b_small])
        S.op("act", lambda e: e.activation(out=small[:, 1:2], in_=small[:, 0:1], func=AF.Sqrt, bias=EPS, scale=1.0 / D),
             reads=[b_small], writes=[b_small])
        S.op("dve", lambda e: e.reciprocal(out=small[:, 2:3], in_=small[:, 1:2]), reads=[b_small], writes=[b_small])
        S.op("dve", lambda e, xb=xb: e.scalar_tensor_tensor(out=hn, in0=xb, scalar=small[:, 2:3], in1=gffn,
                                                            op0=ALU.mult, op1=ALU.mult),
             reads=[bxb, b_small, b_gffn], writes=[b_hn])
        S.op("act", lambda e: e.copy(out=hb, in_=hn), reads=[b_hn], writes=[b_hb])
        for c in range(8):
            S.op("pe", lambda e, c=c: e.transpose(out=pTv[:, c, :], in_=hb[:, c * 128:(c + 1) * 128], identity=ident),
                 reads=[b_hb, b_ident], writes=[b_pT])
        S.op("act", lambda e: e.copy(out=hT, in_=pTv), reads=[b_pT], writes=[b_hT])
        for g in range(16):
            for c in range(8):
                S.op("pe", lambda e, g=g, c=c: e.matmul(pbig[:, g, :], lhsT=wq[:, c, g * 128:(g + 1) * 128],
                                                        rhs=hT[:, c, :], start=(c == 0), stop=(c == 7)),
                     reads=[b_wq, b_hT], writes=[b_pbig])
        S.op("act", lambda e: e.copy(out=qT, in_=pbig), reads=[b_pbig], writes=[b_qT])
        for g in range(16):
            S.op("pe", lambda e, g=g: e.matmul(pbig[:, g, :], lhsT=qT[:, g, :], rhs=skT[:, g, :], start=True, stop=True),
                 reads=[b_qT, b_skT], writes=[b_pbig])
        S.op("dve", lambda e: e.tensor_copy(out=s, in_=pbig), reads=[b_pbig], writes=[b_s])
        for g in range(16):
            S.op("dve", lambda e, g=g: e.max(out=sv[:, g, 0:8], in_=s[:, g, :]), reads=[b_s], writes=[b_sv])
            S.op("dve", lambda e, g=g: e.match_replace(out=s2[:, g, :], in_to_replace=sv[:, g, 0:8],
                                                       in_values=s[:, g, :], imm_value=-1e30),
                 reads=[b_s, b_sv], writes=[b_s2])
            S.op("dve", lambda e, g=g: e.max(out=sv[:, g, 8:16], in_=s2[:, g, :]), reads=[b_s2], writes=[b_sv])
            S.op("dve", lambda e, g=g: e.max_index(out=siu[:, g, 0:8], in_max=sv[:, g, 0:8], in_values=s[:, g, :]),
                 reads=[b_s, b_sv], writes=[b_siu])
            S.op("dve", lambda e, g=g: e.max_index(out=siu[:, g, 8:16], in_max=sv[:, g, 8:16], in_values=s[:, g, :]),
                 reads=[b_s, b_sv], writes=[b_siu])
        S.op("dve", lambda e: e.tensor_copy(out=sif, in_=siu), reads=[b_siu], writes=[b_sif])
        for h in range(8):
            S.op("dve", lambda e, h=h: e.tensor_tensor(
                out=cand4[:, h], in0=sv4[:, h, 0, :].unsqueeze(2).broadcast_to([128, 16, 16]),
                in1=sv4[:, h, 1, :].unsqueeze(1).broadcast_to([128, 16, 16]), op=ALU.add),
                 reads=[b_sv], writes=[b_cand])
        for h in range(8):
            S.op("dve", lambda e, h=h: e.max(out=cv[:, h, 0:8], in_=cand[:, h, :]), reads=[b_cand], writes=[b_cv])
            S.op("dve", lambda e, h=h: e.match_replace(out=cand2[:, h, :], in_to_replace=cv[:, h, 0:8],
                                                       in_values=cand[:, h, :], imm_value=-1e30),
                 reads=[b_cand, b_cv], writes=[b_cand2])
            S.op("dve", lambda e, h=h: e.max(out=cv[:, h, 8:16], in_=cand2[:, h, :]), reads=[b_cand2], writes=[b_cv])
            S.op("dve", lambda e, h=h: e.max_index(out=ciu[:, h, 0:8], in_max=cv[:, h, 0:8], in_values=cand[:, h, :]),
                 reads=[b_cand, b_cv], writes=[b_ciu])
            S.op("dve", lambda e, h=h: e.max_index(out=ciu[:, h, 8:16], in_max=cv[:, h, 8:16], in_values=cand[:, h, :]),
                 reads=[b_cand, b_cv], writes=[b_ciu])
        ciu_f = ciu.rearrange("p h k -> p (h k)")
        S.op("dve", lambda e: e.tensor_single_scalar(out=rcu[:, 0, :], in_=ciu_f, scalar=4, op=ALU.logical_shift_right),
             reads=[b_ciu], writes=[b_rcu])
        S.op("dve", lambda e: e.tensor_single_scalar(out=rcu[:, 1, :], in_=ciu_f, scalar=15, op=ALU.bitwise_and),
             reads=[b_ciu], writes=[b_rcu])
        S.op("dve", lambda e: e.tensor_copy(out=rcf, in_=rcu), reads=[b_rcu], writes=[b_rcf])
        for t in range(2):
            S.op("dve", lambda e, t=t: e.tensor_tensor(
                out=oh, in0=rcf[:, t, :].unsqueeze(2).broadcast_to([128, 128, 16]),
                in1=iota16.unsqueeze(1).broadcast_to([128, 128, 16]), op=ALU.is_equal),
                 reads=[b_rcf, b_iota], writes=[b_oh])
            for h in range(8):
                S.op("dve", lambda e, h=h, t=t: e.tensor_tensor(
                    out=oh4[:, h], in0=oh4[:, h], in1=sif4[:, h, t, :].unsqueeze(1).broadcast_to([128, 16, 16]),
                    op=ALU.mult), reads=[b_oh, b_sif], writes=[b_oh])
            S.op("dve", lambda e, t=t: e.tensor_reduce(out=i12[:, t, :], in_=oh, axis=AX.X, op=ALU.add),
                 reads=[b_oh], writes=[b_i12])
        S.op("dve", lambda e: e.scalar_tensor_tensor(out=ef, in0=i12[:, 0, :], scalar=128.0, in1=i12[:, 1, :],
                                                     op0=ALU.mult, op1=ALU.add), reads=[b_i12], writes=[b_ef])
        S.op("dve", lambda e, eib=eib: e.tensor_copy(out=eib, in_=ef), reads=[b_ef], writes=[beib])
        S.op("dve", lambda e: e.tensor_tensor(out=gsm, in0=cv, in1=cv[:, :, 0:1].broadcast_to([128, 8, 16]),
                                              op=ALU.subtract), reads=[b_cv], writes=[b_gsm])
        S.op("act", lambda e: e.activation(out=gsm, in_=gsm, func=AF.Exp), reads=[b_gsm], writes=[b_gsm])
        S.op("dve", lambda e: e.tensor_reduce(out=small[:, 8:16], in_=gsm, axis=AX.X, op=ALU.add),
             reads=[b_gsm], writes=[b_small])
        S.op("dve", lambda e: e.reciprocal(out=small[:, 16:24], in_=small[:, 8:16]), reads=[b_small], writes=[b_small])
        S.op("dve", lambda e: e.tensor_tensor(out=gsm, in0=gsm,
                                              in1=small[:, 16:24].unsqueeze(2).broadcast_to([128, 8, 16]),
                                              op=ALU.mult), reads=[b_gsm, b_small], writes=[b_gsm])
        for k in range(128):
            j = gcount[0] % NB; gcount[0] += 1
            S.dma("pool", lambda e, k=k, j=j, eib=eib: e.indirect_dma_start(
                out=ub[j], out_offset=None, in_=C["expert_u"],
                in_offset=bass.IndirectOffsetOnAxis(ap=eib[:, k:k + 1], axis=0)),
                  reads=[beib], writes=[b_ub[j]])
            S.op("dve", lambda e, k=k, j=j: e.scalar_tensor_tensor(
                out=ub[j], in0=ub[j], scalar=1.0, in1=hn, op0=ALU.mult, op1=ALU.mult, accum_out=araw[:, k:k + 1]),
                 reads=[b_hn], writes=[b_ub[j]] + ([b_araw] if k in (0, 127) else []))
        S.op("act", lambda e: e.activation(out=aw, in_=araw, func=AF.Gelu), reads=[b_araw], writes=[b_aw])
        S.op("dve", lambda e: e.tensor_tensor(out=aw, in0=aw, in1=gsm.rearrange("p h k -> p (h k)"), op=ALU.mult),
             reads=[b_aw, b_gsm], writes=[b_aw])
        for k in range(128):
            j = gcount[1] % NB; gcount[1] += 1
            S.dma("pool", lambda e, k=k, j=j, eib=eib: e.indirect_dma_start(
                out=vb[j], out_offset=None, in_=C["expert_v"],
                in_offset=bass.IndirectOffsetOnAxis(ap=eib[:, k:k + 1], axis=0)),
                  reads=[beib], writes=[b_vb[j]])
            src = xb if k == 0 else ac
            S.op("dve", lambda e, k=k, j=j, src=src, ac=ac: e.scalar_tensor_tensor(
                out=ac, in0=vb[j], scalar=aw[:, k:k + 1], in1=src, op0=ALU.mult, op1=ALU.add),
                 reads=[b_vb[j], b_aw] + ([bxb] if k == 0 else []), writes=[bac])
        S.dma("sp", lambda e, ac=ac, ti=ti: e.dma_start(out=y_tiles[ti], in_=ac), reads=[bac])


def phase_b2(S, nc, sb, ps, C, x1_tiles, y_tiles, uv16, b_uv16, NB=8):
    ident = sb.alloc([128], BF16); b_ident = Buf()
    iota16 = sb.alloc([16], F32); b_iota = Buf()
    cst = sb.alloc([144], F32); b_cst = Buf()
    wq = sb.alloc([8, 2048], BF16); b_wq = Buf()
    skT = sb.alloc([16, 128], BF16); b_skT = Buf()
    gffn = sb.alloc([1024], F32); b_gffn = Buf()
    S.dma("sp", lambda e: e.dma_start(out=cst, in_=C["consts"]), writes=[b_cst])
    S.op("dve", lambda e: e.tensor_copy(out=ident, in_=cst[:, 0:128]), reads=[b_cst], writes=[b_ident])
    S.op("dve", lambda e: e.tensor_copy(out=iota16, in_=cst[:, 128:144]), reads=[b_cst], writes=[b_iota])
    S.dma("sp", lambda e: e.dma_start(out=gffn, in_=C["g_ffn"].partition_broadcast(128)), writes=[b_gffn])
    stg = [sb.alloc([2048], F32) for _ in range(2)]
    b_stg = [Buf(), Buf()]
    wqv = C["w_query"].rearrange("(c p) n -> c p n", p=128)
    for c in range(8):
        S.dma("sp", lambda e, c=c: e.dma_start(out=stg[c % 2], in_=wqv[c]), writes=[b_stg[c % 2]])
        S.op("act" if c % 2 else "dve",
             (lambda e, c=c: e.copy(out=wq[:, c, :], in_=stg[c % 2])) if c % 2 else
             (lambda e, c=c: e.tensor_copy(out=wq[:, c, :], in_=stg[c % 2])),
             reads=[b_stg[c % 2]], writes=[b_wq])
    skv = C["sub_keys"].rearrange("g k d -> k g d")
    skn = stg[0].rearrange("p (g d) -> p g d", g=16)
    skb = sb.alloc([16, 128], BF16); b_skb = Buf()
    pbig = ps.alloc([16, 128], F32, align=512); b_pbig = Buf()
    pT = ps.alloc([1024], BF16, align=512); b_pT = Buf()
    pTv = pT.rearrange("p (c n) -> p c n", c=8)
    S.dma("sp", lambda e: e.dma_start(out=skn, in_=skv), writes=[b_stg[0]])
    S.op("dve", lambda e: e.tensor_copy(out=skb, in_=skn), reads=[b_stg[0]], writes=[b_skb])
    for half in range(2):
        for j in range(8):
            g = half * 8 + j
            S.op("pe", lambda e, g=g, j=j: e.transpose(out=pTv[:, j, :], in_=skb[:, g, :], identity=ident),
                 reads=[b_skb, b_ident], writes=[b_pT])
        S.op("dve", lambda e, half=half: e.tensor_copy(out=skT[:, half * 8:(half + 1) * 8, :], in_=pTv),
             reads=[b_pT], writes=[b_skT])

    x1t = [sb.alloc([1024], F32) for _ in range(2)]; b_x1t = [Buf(), Buf()]
    junk = sb.alloc([1024], F32); b_junk = Buf()
    hb = sb.alloc([1024], BF16); b_hb = Buf()
    hT = sb.alloc([8, 128], BF16); b_hT = Buf()
    qT = sb.alloc([16, 128], BF16); b_qT = Buf()
    s = sb.alloc([16, 128], F32); b_s = Buf()
    s2 = sb.alloc([16, 128], F32); b_s2 = Buf()
    sv = sb.alloc([16, 16], F32); b_sv = Buf()
    siu = sb.alloc([16, 16], U32); b_siu = Buf()
    sif = sb.alloc([16, 16], F32); b_sif = Buf()
    cand = sb.alloc([8, 256], F32); b_cand = Buf()
    cand2 = sb.alloc([8, 256], F32); b_cand2 = Buf()
    cv = sb.alloc([8, 16], F32); b_cv = Buf()
    ciu = sb.alloc([8, 16], U32); b_ciu = Buf()
    rcu = sb.alloc([2, 128], U32); b_rcu = Buf()
    rcf = sb.alloc([2, 128], F32); b_rcf = Buf()
    oh = sb.alloc([128, 16], F32); b_oh = Buf()
    i12 = sb.alloc([2, 128], F32); b_i12 = Buf()
    ef = sb.alloc([128], F32); b_ef = Buf()
    ei = [sb.alloc([128], I32) for _ in range(2)]; b_ei = [Buf(), Buf()]
    araw = sb.alloc([128], F32)
    gsm = sb.alloc([8, 16], F32); b_gsm = Buf()
    small = sb.alloc([32], F32); b_small = Buf()
    yo = sb.alloc([1024], F32); b_yo = Buf()
    uvb = [sb.alloc([2048], BF16) for _ in range(NB)]; b_uvb = [Buf() for _ in range(NB)]
    dg = [sb.alloc([128], BF16) for _ in range(4)]; b_dg = [Buf() for _ in range(4)]
    gl = sb.alloc([128], F32); b_gl = [Buf() for _ in range(8)]
    b_ar = [Buf() for _ in range(8)]
    pacc = ps.alloc([1024], F32, align=512); b_pacc = Buf()
    gcount = [0, 0]

    sv4 = sv.rearrange("p (h t) k -> p h t k", t=2)
    sif4 = sif.rearrange("p (h t) k -> p h t k", t=2)
    cand4 = cand.rearrange("p h (i j) -> p h i j", i=16)
    oh4 = oh.rearrange("p (h k) i -> p h k i", h=8)

    for ti in range(len(x1_tiles)):
        xb = x1t[ti % 2]; bxb = b_x1t[ti % 2]
        eib = ei[ti % 2]; beib = b_ei[ti % 2]
        S.dma("sp", lambda e, xb=xb, ti=ti: e.dma_start(out=xb, in_=x1_tiles[ti]), writes=[bxb])
        S.op("act", lambda e, xb=xb: e.activation(out=junk, in_=xb, func=AF.Square, accum_out=small[:, 0:1]),
             reads=[bxb], writes=[b_junk, b_small])
        S.op("act", lambda e: e.activation(out=small[:, 1:2], in_=small[:, 0:1], func=AF.Sqrt, bias=EPS, scale=1.0 / D),
             reads=[b_small], writes=[b_small])
        S.op("dve", lambda e: e.reciprocal(out=small[:, 2:3], in_=small[:, 1:2]), reads=[b_small], writes=[b_small])
        S.op("dve", lambda e, xb=xb: e.scalar_tensor_tensor(out=hb, in0=xb, scalar=small[:, 2:3], in1=gffn,
                                                            op0=ALU.mult, op1=ALU.mult),
             reads=[bxb, b_small, b_gffn], writes=[b_hb])
        for c in range(8):
            S.op("pe", lambda e, c=c: e.transpose(out=pTv[:, c, :], in_=hb[:, c * 128:(c + 1) * 128], identity=ident),
                 reads=[b_hb, b_ident], writes=[b_pT])
        S.op("act", lambda e: e.copy(out=hT, in_=pTv), reads=[b_pT], writes=[b_hT])
        for g in range(16):
            for c in range(8):
                S.op("pe", lambda e, g=g, c=c: e.matmul(pbig[:, g, :], lhsT=wq[:, c, g * 128:(g + 1) * 128],
                                                        rhs=hT[:, c, :], start=(c == 0), stop=(c == 7)),
                     reads=[b_wq, b_hT], writes=[b_pbig])
        S.op("act", lambda e: e.copy(out=qT, in_=pbig), reads=[b_pbig], writes=[b_qT])
        for g in range(16):
            S.op("pe", lambda e, g=g: e.matmul(pbig[:, g, :], lhsT=qT[:, g, :], rhs=skT[:, g, :], start=True, stop=True),
                 reads=[b_qT, b_skT], writes=[b_pbig])
        S.op("dve", lambda e: e.tensor_copy(out=s, in_=pbig), reads=[b_pbig], writes=[b_s])
        for g in range(16):
            S.op("dve", lambda e, g=g: e.max(out=sv[:, g, 0:8], in_=s[:, g, :]), reads=[b_s], writes=[b_sv])
            S.op("dve", lambda e, g=g: e.match_replace(out=s2[:, g, :], in_to_replace=sv[:, g, 0:8],
                                                       in_values=s[:, g, :], imm_value=-1e30),
                 reads=[b_s, b_sv], writes=[b_s2])
            S.op("dve", lambda e, g=g: e.max(out=sv[:, g, 8:16], in_=s2[:, g, :]), reads=[b_s2], writes=[b_sv])
            S.op("dve", lambda e, g=g: e.max_index(out=siu[:, g, 0:8], in_max=sv[:, g, 0:8], in_values=s[:, g, :]),
                 reads=[b_s, b_sv], writes=[b_siu])
            S.op("dve", lambda e, g=g: e.max_index(out=siu[:, g, 8:16], in_max=sv[:, g, 8:16], in_values=s[:, g, :]),
                 reads=[b_s, b_sv], writes=[b_siu])
        S.op("dve", lambda e: e.tensor_copy(out=sif, in_=siu), reads=[b_siu], writes=[b_sif])
        for h in range(8):
            S.op("dve", lambda e, h=h: e.tensor_tensor(
                out=cand4[:, h], in0=sv4[:, h, 0, :].unsqueeze(2).broadcast_to([128, 16, 16]),
                in1=sv4[:, h, 1, :].unsqueeze(1).broadcast_to([128, 16, 16]), op=ALU.add),
                 reads=[b_sv], writes=[b_cand])
        for h in range(8):
            S.op("dve", lambda e, h=h: e.max(out=cv[:, h, 0:8], in_=cand[:, h, :]), reads=[b_cand], writes=[b_cv])
            S.op("dve", lambda e, h=h: e.match_replace(out=cand2[:, h, :], in_to_replace=cv[:, h, 0:8],
                                                       in_values=cand[:, h, :], imm_value=-1e30),
                 reads=[b_cand, b_cv], writes=[b_cand2])
            S.op("dve", lambda e, h=h: e.max(out=cv[:, h, 8:16], in_=cand2[:, h, :]), reads=[b_cand2], writes=[b_cv])
            S.op("dve", lambda e, h=h: e.max_index(out=ciu[:, h, 0:8], in_max=cv[:, h, 0:8], in_values=cand[:, h, :]),
                 reads=[b_cand, b_cv], writes=[b_ciu])
            S.op("dve", lambda e, h=h: e.max_index(out=ciu[:, h, 8:16], in_max=cv[:, h, 8:16], in_values=cand[:, h, :]),
                 reads=[b_cand, b_cv], writes=[b_ciu])
        ciu_f = ciu.rearrange("p h k -> p (h k)")
        S.op("dve", lambda e: e.tensor_single_scalar(out=rcu[:, 0, :], in_=ciu_f, scalar=4, op=ALU.logical_shift_right),
             reads=[b_ciu], writes=[b_rcu])
        S.op("dve", lambda e: e.tensor_single_scalar(out=rcu[:, 1, :], in_=ciu_f, scalar=15, op=ALU.bitwise_and),
             reads=[b_ciu], writes=[b_rcu])
        S.op("dve", lambda e: e.tensor_copy(out=rcf, in_=rcu), reads=[b_rcu], writes=[b_rcf])
        for t in range(2):
            S.op("dve", lambda e, t=t: e.tensor_tensor(
                out=oh, in0=rcf[:, t, :].unsqueeze(2).broadcast_to([128, 128, 16]),
                in1=iota16.unsqueeze(1).broadcast_to([128, 128, 16]), op=ALU.is_equal),
                 reads=[b_rcf, b_iota], writes=[b_oh])
            for h in range(8):
                S.op("dve", lambda e, h=h, t=t: e.tensor_tensor(
                    out=oh4[:, h], in0=oh4[:, h], in1=sif4[:, h, t, :].unsqueeze(1).broadcast_to([128, 16, 16]),
                    op=ALU.mult), reads=[b_oh, b_sif], writes=[b_oh])
            S.op("dve", lambda e, t=t: e.tensor_reduce(out=i12[:, t, :], in_=oh, axis=AX.X, op=ALU.add),
                 reads=[b_oh], writes=[b_i12])
        S.op("dve", lambda e: e.scalar_tensor_tensor(out=ef, in0=i12[:, 0, :], scalar=128.0, in1=i12[:, 1, :],
                                                     op0=ALU.mult, op1=ALU.add), reads=[b_i12], writes=[b_ef])
        S.op("dve", lambda e, eib=eib: e.tensor_copy(out=eib, in_=ef), reads=[b_ef], writes=[beib])
        S.op("dve", lambda e: e.tensor_tensor(out=gsm, in0=cv, in1=cv[:, :, 0:1].broadcast_to([128, 8, 16]),
                                              op=ALU.subtract), reads=[b_cv], writes=[b_gsm])
        S.op("act", lambda e: e.activation(out=gsm, in_=gsm, func=AF.Exp), reads=[b_gsm], writes=[b_gsm])
        S.op("dve", lambda e: e.tensor_reduce(out=small[:, 8:16], in_=gsm, axis=AX.X, op=ALU.add),
             reads=[b_gsm], writes=[b_small])
        S.op("dve", lambda e: e.reciprocal(out=small[:, 16:24], in_=small[:, 8:16]), reads=[b_small], writes=[b_small])
        S.op("dve", lambda e: e.tensor_tensor(out=gsm, in0=gsm,
                                              in1=small[:, 16:24].unsqueeze(2).broadcast_to([128, 8, 16]),
                                              op=ALU.mult), reads=[b_gsm, b_small], writes=[b_gsm])
        gsf = gsm.rearrange("p h k -> p (h k)")
        for k in range(128):
            j = gcount[0] % NB; gcount[0] += 1
            r8 = k % 8; r4 = k % 4
            S.dma("pool", lambda e, k=k, j=j, eib=eib: e.indirect_dma_start(
                out=uvb[j], out_offset=None, in_=uv16,
                in_offset=bass.IndirectOffsetOnAxis(ap=eib[:, k:k + 1], axis=0)),
                  reads=[beib, b_uv16], writes=[b_uvb[j]])
            S.op("dve", lambda e, k=k, j=j: e.scalar_tensor_tensor(
                out=uvb[j][:, 0:1024], in0=uvb[j][:, 0:1024], scalar=1.0, in1=hb, op0=ALU.mult, op1=ALU.mult,
                accum_out=araw[:, k:k + 1]), reads=[b_hb], writes=[b_uvb[j], b_ar[r8]])
            S.op("act", lambda e, k=k: e.activation(out=gl[:, k:k + 1], in_=araw[:, k:k + 1], func=AF.Gelu),
                 reads=[b_ar[r8]], writes=[b_gl[r8]])
            S.op("dve", lambda e, k=k, r4=r4: e.tensor_scalar(out=dg[r4], in0=ident, scalar1=gl[:, k:k + 1],
                                                            scalar2=gsf[:, k:k + 1], op0=ALU.mult, op1=ALU.mult),
                 reads=[b_ident, b_gl[r8], b_gsm], writes=[b_dg[r4]])
            for hf in range(2):
                S.op("pe", lambda e, k=k, j=j, r4=r4, hf=hf: e.matmul(
                    pacc[:, hf * 512:(hf + 1) * 512], lhsT=dg[r4], rhs=uvb[j][:, 1024 + hf * 512:1024 + (hf + 1) * 512],
                    start=(k == 0), stop=(k == 127)), reads=[b_dg[r4], b_uvb[j]], writes=[b_pacc])
        S.op("dve", lambda e, xb=xb: e.tensor_tensor(out=yo, in0=pacc, in1=xb, op=ALU.add),
             reads=[b_pacc, bxb], writes=[b_yo])
        S.dma("sp", lambda e, ti=ti: e.dma_start(out=y_tiles[ti], in_=yo), reads=[b_yo])


def phase_b3(S, nc, sb, ps, C, x1_tiles, y_tiles, uv16, b_uv16, NB=10):
    ident = sb.alloc([128], BF16); b_ident = Buf()
    iota16 = sb.alloc([16], F32); b_iota = Buf()
    cst = sb.alloc([144], F32); b_cst = Buf()
    wq = sb.alloc([8, 2048], BF16); b_wq = Buf()
    skT = sb.alloc([16, 128], BF16); b_skT = Buf()
    gffn = sb.alloc([1024], F32); b_gffn = Buf()
    S.dma("sp", lambda e: e.dma_start(out=cst, in_=C["consts"]), writes=[b_cst])
    S.op("dve", lambda e: e.tensor_copy(out=ident, in_=cst[:, 0:128]), reads=[b_cst], writes=[b_ident])
    S.op("dve", lambda e: e.tensor_copy(out=iota16, in_=cst[:, 128:144]), reads=[b_cst], writes=[b_iota])
    S.dma("sp", lambda e: e.dma_start(out=gffn, in_=C["g_ffn"].partition_broadcast(128)), writes=[b_gffn])
    stg = [sb.alloc([2048], F32) for _ in range(2)]
    b_stg = [Buf(), Buf()]
    wqv = C["w_query"].rearrange("(c p) n -> c p n", p=128)
    for c in range(8):
        S.dma("sp", lambda e, c=c: e.dma_start(out=stg[c % 2], in_=wqv[c]), writes=[b_stg[c % 2]])
        S.op("act" if c % 2 else "dve",
             (lambda e, c=c: e.copy(out=wq[:, c, :], in_=stg[c % 2])) if c % 2 else
             (lambda e, c=c: e.tensor_copy(out=wq[:, c, :], in_=stg[c % 2])),
             reads=[b_stg[c % 2]], writes=[b_wq])
    skv = C["sub_keys"].rearrange("g k d -> k g d")
    skn = stg[0].rearrange("p (g d) -> p g d", g=16)
    skb = sb.alloc([16, 128], BF16); b_skb = Buf()
    pbig = ps.alloc([16, 128], F32, align=512); b_pbig = Buf()
    pT = ps.alloc([1024], BF16, align=512); b_pT = Buf()
    pTv = pT.rearrange("p (c n) -> p c n", c=8)
    S.dma("sp", lambda e: e.dma_start(out=skn, in_=skv), writes=[b_stg[0]])
    S.op("dve", lambda e: e.tensor_copy(out=skb, in_=skn), reads=[b_stg[0]], writes=[b_skb])
    for half in range(2):
        for j in range(8):
            g = half * 8 + j
            S.op("pe", lambda e, g=g, j=j: e.transpose(out=pTv[:, j, :], in_=skb[:, g, :], identity=ident),
                 reads=[b_skb, b_ident], writes=[b_pT])
        S.op("dve", lambda e, half=half: e.tensor_copy(out=skT[:, half * 8:(half + 1) * 8, :], in_=pTv),
             reads=[b_pT], writes=[b_skT])

    x1t = [sb.alloc([1024], F32) for _ in range(2)]; b_x1t = [Buf(), Buf()]
    junk = sb.alloc([1024], F32); b_junk = Buf()
    hb2 = [sb.alloc([1024], BF16) for _ in range(2)]; b_hb2 = [Buf(), Buf()]
    hT = sb.alloc([8, 128], BF16); b_hT = Buf()
    qT = sb.alloc([16, 128], BF16); b_qT = Buf()
    s = sb.alloc([16, 128], F32); b_s = Buf()
    s2 = sb.alloc([16, 128], F32); b_s2 = Buf()
    sv = sb.alloc([16, 16], F32); b_sv = Buf()
    siu = sb.alloc([16, 16], U32); b_siu = Buf()
    sif = sb.alloc([16, 16], F32); b_sif = Buf()
    cand = sb.alloc([8, 256], F32); b_cand = Buf()
    cand2 = sb.alloc([8, 256], F32); b_cand2 = Buf()
    cv = sb.alloc([8, 16], F32); b_cv = Buf()
    ciu = sb.alloc([8, 16], U32); b_ciu = Buf()
    rcu = sb.alloc([2, 128], U32); b_rcu = Buf()
    rcf = sb.alloc([2, 128], F32); b_rcf = Buf()
    oh = sb.alloc([128, 16], F32); b_oh = Buf()
    i12 = sb.alloc([2, 128], F32); b_i12 = Buf()
    ef = sb.alloc([128], F32); b_ef = Buf()
    ei = [sb.alloc([128], I32) for _ in range(2)]; b_ei = [Buf(), Buf()]
    araw = sb.alloc([128], F32)
    gsm2 = [sb.alloc([8, 16], F32) for _ in range(2)]; b_gsm2 = [Buf(), Buf()]
    small = sb.alloc([32], F32); b_small = Buf()
    yo = sb.alloc([1024], F32); b_yo = Buf()
    uvb = [sb.alloc([2048], BF16) for _ in range(NB)]; b_uvb = [Buf() for _ in range(NB)]
    dg = [sb.alloc([128], BF16) for _ in range(4)]; b_dg = [Buf() for _ in range(4)]
    gl = sb.alloc([128], F32); b_gl = [Buf() for _ in range(8)]
    b_ar = [Buf() for _ in range(8)]
    pacc = ps.alloc([1024], F32, align=512); b_pacc = Buf()
    gcount = [0, 0]

    sv4 = sv.rearrange("p (h t) k -> p h t k", t=2)
    sif4 = sif.rearrange("p (h t) k -> p h t k", t=2)
    cand4 = cand.rearrange("p h (i j) -> p h i j", i=16)
    oh4 = oh.rearrange("p (h k) i -> p h k i", h=8)

    def route_gen(ti):
        xb = x1t[ti % 2]; bxb = b_x1t[ti % 2]
        eib = ei[ti % 2]; beib = b_ei[ti % 2]
        hb = hb2[ti % 2]; b_hb = b_hb2[ti % 2]
        gsm = gsm2[ti % 2]; b_gsm = b_gsm2[ti % 2]
        S.dma("sp", lambda e, xb=xb, ti=ti: e.dma_start(out=xb, in_=x1_tiles[ti]), writes=[bxb])
        yield
        S.op("act", lambda e, xb=xb: e.activation(out=junk, in_=xb, func=AF.Square, accum_out=small[:, 0:1]),
             reads=[bxb], writes=[b_junk, b_small])
        yield
        S.op("act", lambda e: e.activation(out=small[:, 1:2], in_=small[:, 0:1], func=AF.Sqrt, bias=EPS, scale=1.0 / D),
             reads=[b_small], writes=[b_small])
        yield
        S.op("dve", lambda e: e.reciprocal(out=small[:, 2:3], in_=small[:, 1:2]), reads=[b_small], writes=[b_small])
        yield
        S.op("dve", lambda e, xb=xb: e.scalar_tensor_tensor(out=hb, in0=xb, scalar=small[:, 2:3], in1=gffn,
                                                            op0=ALU.mult, op1=ALU.mult),
             reads=[bxb, b_small, b_gffn], writes=[b_hb])
        yield
        for c in range(8):
            S.op("pe", lambda e, c=c: e.transpose(out=pTv[:, c, :], in_=hb[:, c * 128:(c + 1) * 128], identity=ident),
                 reads=[b_hb, b_ident], writes=[b_pT])
            yield
        S.op("act", lambda e: e.copy(out=hT, in_=pTv), reads=[b_pT], writes=[b_hT])
        yield
        for g in range(16):
            for c in range(8):
                S.op("pe", lambda e, g=g, c=c: e.matmul(pbig[:, g, :], lhsT=wq[:, c, g * 128:(g + 1) * 128],
                                                        rhs=hT[:, c, :], start=(c == 0), stop=(c == 7)),
                     reads=[b_wq, b_hT], writes=[b_pbig])
                yield
        S.op("act", lambda e: e.copy(out=qT, in_=pbig), reads=[b_pbig], writes=[b_qT])
        yield
        for g in range(16):
            S.op("pe", lambda e, g=g: e.matmul(pbig[:, g, :], lhsT=qT[:, g, :], rhs=skT[:, g, :], start=True, stop=True),
                 reads=[b_qT, b_skT], writes=[b_pbig])
            yield
        S.op("dve", lambda e: e.tensor_copy(out=s, in_=pbig), reads=[b_pbig], writes=[b_s])
        yield
        for g in range(16):
            S.op("dve", lambda e, g=g: e.max(out=sv[:, g, 0:8], in_=s[:, g, :]), reads=[b_s], writes=[b_sv])
            yield
            S.op("dve", lambda e, g=g: e.match_replace(out=s2[:, g, :], in_to_replace=sv[:, g, 0:8],
                                                       in_values=s[:, g, :], imm_value=-1e30),
                 reads=[b_s, b_sv], writes=[b_s2])
            yield
            S.op("dve", lambda e, g=g: e.max(out=sv[:, g, 8:16], in_=s2[:, g, :]), reads=[b_s2], writes=[b_sv])
            yield
            S.op("dve", lambda e, g=g: e.max_index(out=siu[:, g, 0:8], in_max=sv[:, g, 0:8], in_values=s[:, g, :]),
                 reads=[b_s, b_sv], writes=[b_siu])
            yield
            S.op("dve", lambda e, g=g: e.max_index(out=siu[:, g, 8:16], in_max=sv[:, g, 8:16], in_values=s[:, g, :]),
                 reads=[b_s, b_sv], writes=[b_siu])
            yield
        S.op("dve", lambda e: e.tensor_copy(out=sif, in_=siu), reads=[b_siu], writes=[b_sif])
        yield
        for h in range(8):
            S.op("dve", lambda e, h=h: e.tensor_tensor(
                out=cand4[:, h], in0=sv4[:, h, 0, :].unsqueeze(2).broadcast_to([128, 16, 16]),
                in1=sv4[:, h, 1, :].unsqueeze(1).broadcast_to([128, 16, 16]), op=ALU.add),
                 reads=[b_sv], writes=[b_cand])
            yield
        for h in range(8):
            S.op("dve", lambda e, h=h: e.max(out=cv[:, h, 0:8], in_=cand[:, h, :]), reads=[b_cand], writes=[b_cv])
            yield
            S.op("dve", lambda e, h=h: e.match_replace(out=cand2[:, h, :], in_to_replace=cv[:, h, 0:8],
                                                       in_values=cand[:, h, :], imm_value=-1e30),
                 reads=[b_cand, b_cv], writes=[b_cand2])
            yield
            S.op("dve", lambda e, h=h: e.max(out=cv[:, h, 8:16], in_=cand2[:, h, :]), reads=[b_cand2], writes=[b_cv])
            yield
            S.op("dve", lambda e, h=h: e.max_index(out=ciu[:, h, 0:8], in_max=cv[:, h, 0:8], in_values=cand[:, h, :]),
                 reads=[b_cand, b_cv], writes=[b_ciu])
            yield
            S.op("dve", lambda e, h=h: e.max_index(out=ciu[:, h, 8:16], in_max=cv[:, h, 8:16], in_values=cand[:, h, :]),
                 reads=[b_cand, b_cv], writes=[b_ciu])
            yield
        ciu_f = ciu.rearrange("p h k -> p (h k)")
        S.op("dve", lambda e: e.tensor_single_scalar(out=rcu[:, 0, :], in_=ciu_f, scalar=4, op=ALU.logical_shift_right),
             reads=[b_ciu], writes=[b_rcu])
        yield
        S.op("dve", lambda e: e.tensor_single_scalar(out=rcu[:, 1, :], in_=ciu_f, scalar=15, op=ALU.bitwise_and),
             reads=[b_ciu], writes=[b_rcu])
        yield
        S.op("dve", lambda e: e.tensor_copy(out=rcf, in_=rcu), reads=[b_rcu], writes=[b_rcf])
        yield
        for t in range(2):
            S.op("dve", lambda e, t=t: e.tensor_tensor(
                out=oh, in0=rcf[:, t, :].unsqueeze(2).broadcast_to([128, 128, 16]),
                in1=iota16.unsqueeze(1).broadcast_to([128, 128, 16]), op=ALU.is_equal),
                 reads=[b_rcf, b_iota], writes=[b_oh])
            yield
            for h in range(8):
                S.op("dve", lambda e, h=h, t=t: e.tensor_tensor(
                    out=oh4[:, h], in0=oh4[:, h], in1=sif4[:, h, t, :].unsqueeze(1).broadcast_to([128, 16, 16]),
                    op=ALU.mult), reads=[b_oh, b_sif], writes=[b_oh])
                yield
            S.op("dve", lambda e, t=t: e.tensor_reduce(out=i12[:, t, :], in_=oh, axis=AX.X, op=ALU.add),
                 reads=[b_oh], writes=[b_i12])
            yield
        S.op("dve", lambda e: e.scalar_tensor_tensor(out=ef, in0=i12[:, 0, :], scalar=128.0, in1=i12[:, 1, :],
                                                     op0=ALU.mult, op1=ALU.add), reads=[b_i12], writes=[b_ef])
        yield
        S.op("dve", lambda e, eib=eib: e.tensor_copy(out=eib, in_=ef), reads=[b_ef], writes=[beib])
        yield
        S.op("dve", lambda e: e.tensor_tensor(out=gsm, in0=cv, in1=cv[:, :, 0:1].broadcast_to([128, 8, 16]),
                                              op=ALU.subtract), reads=[b_cv], writes=[b_gsm])
        yield
        S.op("act", lambda e: e.activation(out=gsm, in_=gsm, func=AF.Exp), reads=[b_gsm], writes=[b_gsm])
        yield
        S.op("dve", lambda e: e.tensor_reduce(out=small[:, 8:16], in_=gsm, axis=AX.X, op=ALU.add),
             reads=[b_gsm], writes=[b_small])
        yield
        S.op("dve", lambda e: e.reciprocal(out=small[:, 16:24], in_=small[:, 8:16]), reads=[b_small], writes=[b_small])
        yield
        S.op("dve", lambda e: e.tensor_tensor(out=gsm, in0=gsm,
                                              in1=small[:, 16:24].unsqueeze(2).broadcast_to([128, 8, 16]),
                                              op=ALU.mult), reads=[b_gsm, b_small], writes=[b_gsm])
        yield

    def slots(ti, filler, rate):
        xb = x1t[ti % 2]; bxb = b_x1t[ti % 2]
        eib = ei[ti % 2]; beib = b_ei[ti % 2]
        hb = hb2[ti % 2]; b_hb = b_hb2[ti % 2]
        gsm = gsm2[ti % 2]; b_gsm = b_gsm2[ti % 2]
        gsf = gsm.rearrange("p h k -> p (h k)")
        LAG = 3
        jj = {}
        for kx in range(128 + LAG):
            if kx < 128:
                k = kx
                j = gcount[0] % NB; gcount[0] += 1
                jj[k] = j
                r8 = k % 8
                S.dma("pool", lambda e, k=k, j=j, eib=eib: e.indirect_dma_start(
                    out=uvb[j], out_offset=None, in_=uv16,
                    in_offset=bass.IndirectOffsetOnAxis(ap=eib[:, k:k + 1], axis=0)),
                      reads=[beib, b_uv16], writes=[b_uvb[j]])
                S.op("dve", lambda e, k=k, j=j: e.scalar_tensor_tensor(
                    out=uvb[j][:, 0:1024], in0=uvb[j][:, 0:1024], scalar=1.0, in1=hb, op0=ALU.mult, op1=ALU.mult,
                    accum_out=araw[:, k:k + 1]), reads=[b_hb], writes=[b_uvb[j], b_ar[r8]])
                S.op("act", lambda e, k=k: e.activation(out=gl[:, k:k + 1], in_=araw[:, k:k + 1], func=AF.Gelu),
                     reads=[b_ar[r8]], writes=[b_gl[r8]])
            if kx >= LAG:
                k = kx - LAG
                j = jj[k]
                r8 = k % 8; r4 = k % 4
                S.op("dve", lambda e, k=k, r4=r4: e.tensor_scalar(out=dg[r4], in0=ident, scalar1=gl[:, k:k + 1],
                                                                scalar2=gsf[:, k:k + 1], op0=ALU.mult, op1=ALU.mult),
                     reads=[b_ident, b_gl[r8], b_gsm], writes=[b_dg[r4]])
                for hf in range(2):
                    S.op("pe", lambda e, k=k, j=j, r4=r4, hf=hf: e.matmul(
                        pacc[:, hf * 512:(hf + 1) * 512], lhsT=dg[r4], rhs=uvb[j][:, 1024 + hf * 512:1024 + (hf + 1) * 512],
                        start=(k == 0), stop=(k == 127)), reads=[b_dg[r4], b_uvb[j]], writes=[b_pacc])
            if filler is not None:
                for _ in range(rate):
                    next(filler, None)
        S.op("dve", lambda e, xb=xb: e.tensor_tensor(out=yo, in0=pacc, in1=xb, op=ALU.add),
             reads=[b_pacc, bxb], writes=[b_yo])
        S.dma("sp", lambda e, ti=ti: e.dma_start(out=y_tiles[ti], in_=yo), reads=[b_yo])


    NT = len(x1_tiles)
    g = route_gen(0)
    for _ in g:
        pass
    for ti in range(NT):
        nxt = route_gen(ti + 1) if ti + 1 < NT else None
        slots(ti, nxt, 4)
        if nxt is not None:
            for _ in nxt:
                pass


def convert_tables(S, C, uv16, b_uv16, nsplit=8):
    rows = NEXP // nsplit
    for t, name in enumerate(("expert_u", "expert_v")):
        for i in range(nsplit):
            S.dma("pool", lambda e, t=t, i=i, name=name: e.dma_start(
                out=uv16[i * rows:(i + 1) * rows, t * 1024:(t + 1) * 1024], in_=C[name][i * rows:(i + 1) * rows, :]),
                  writes=[b_uv16])


def make_consts():
    c = np.zeros((128, 144), np.float32)
    c[:, :128] = np.eye(128, dtype=np.float32)
    c[:, 128:144] = np.arange(16, dtype=np.float32)[None, :]
    return c


PP_GMIX, PP_GOUT, PP_CB, PP_LNG, PP_LNB, PP_GQ, PP_GK, PP_CW, PP_N = 0, 8, 16, 20, 24, 28, 29, 32, 160
RING = 8
URING = 6
QRING = 5


def key_tiles(s, NQ):
    kts = list(range(s - 2, s + 3))
    if s == 2:
        kts.append(5)
    if s == NQ + 1:
        kts.insert(0, NQ - 2)
    return kts


def phase_a(S, nc, sb, ps, C, NQ, xs_tiles, x1_tiles):
    NS = NQ + 4
    nslab = len(xs_tiles)
    cst = sb.alloc([144], F32); b_cst = Buf()
    ident = sb.alloc([128], BF16); b_ident = Buf()
    onesf = sb.alloc([128], F32); b_onesf = Buf()
    blk = sb.alloc([128], BF16); b_blk = Buf()
    pp = sb.alloc([PP_N], F32); b_pp = Buf()
    gq8 = sb.alloc([1], F32); b_gq8 = Buf()
    win = sb.alloc([8, 2560], BF16); b_win = Buf()
    wout = sb.alloc([8, 1024], BF16); b_wout = Buf()
    T = sb.alloc([8, 16, 64], F32); b_T = Buf()
    ntile_idx = {}
    for s in range(2, NQ + 2):
        for kt in key_tiles(s, NQ):
            ntile_idx[(s, kt)] = len(ntile_idx)
    NTI = len(ntile_idx)
    rmask = sb.alloc([nslab, NTI, 2], F32); b_rmask = Buf()
    S.dma("sp", lambda e: e.dma_start(out=cst, in_=C["consts"]), writes=[b_cst])
    S.dma("sp", lambda e: e.dma_start(out=pp, in_=C["pp"]), writes=[b_pp])
    S.dma("sp", lambda e: e.dma_start(out=T.rearrange("p h j c -> p (h j c)"), in_=C["ttab"]), writes=[b_T])
    S.dma("sp", lambda e: e.dma_start(out=rmask.rearrange("p a b c -> p (a b c)"), in_=C["rmask"]), writes=[b_rmask])
    S.op("dve", lambda e: e.tensor_copy(out=ident, in_=cst[:, 0:128]), reads=[b_cst], writes=[b_ident])
    S.op("dve", lambda e: e.memset(onesf, 1.0), writes=[b_onesf])
    S.op("dve", lambda e: e.memset(blk, 0.0), writes=[b_blk])
    S.op("dve", lambda e: e.memset(blk[0:64, 0:64], 1.0), writes=[b_blk])
    S.op("dve", lambda e: e.memset(blk[64:128, 64:128], 1.0), writes=[b_blk])
    S.op("dve", lambda e: e.tensor_scalar(out=gq8, in0=pp[:, PP_GQ:PP_GQ + 1], scalar1=0.125, scalar2=None, op0=ALU.mult),
         reads=[b_pp], writes=[b_gq8])
    stg = [sb.alloc([1280], F32) for _ in range(2)]; b_stg = [Buf(), Buf()]
    winv = C["w_in"].rearrange("(c p) n -> c p n", p=128)
    woutv = C["w_out"].rearrange("(c p) n -> c p n", p=128)
    n = 0
    for c in range(8):
        for hf in range(2):
            j = n % 2; n += 1
            S.dma("sp", lambda e, c=c, hf=hf, j=j: e.dma_start(out=stg[j], in_=winv[c][:, hf * 1280:(hf + 1) * 1280]),
                  writes=[b_stg[j]])
            S.op("dve", lambda e, c=c, hf=hf, j=j: e.tensor_scalar(
                out=win[:, c, hf * 1280:(hf + 1) * 1280], in0=stg[j], scalar1=pp[:, PP_GMIX + c:PP_GMIX + c + 1],
                scalar2=None, op0=ALU.mult), reads=[b_stg[j], b_pp], writes=[b_win])
    for c in range(8):
        j = n % 2; n += 1
        S.dma("sp", lambda e, c=c, j=j: e.dma_start(out=stg[j][:, 0:1024], in_=woutv[c]), writes=[b_stg[j]])
        S.op("dve", lambda e, c=c, j=j: e.tensor_scalar(
            out=wout[:, c, :], in0=stg[j][:, 0:1024], scalar1=pp[:, PP_GOUT + c:PP_GOUT + c + 1],
            scalar2=None, op0=ALU.mult), reads=[b_stg[j], b_pp], writes=[b_wout])
    cw = pp[:, PP_CW:PP_CW + 124].rearrange("p (c k) -> p c k", c=4)

    kT = [sb.alloc([4, 128], BF16) for _ in range(RING)]; b_kT = [Buf() for _ in range(RING)]
    va = [sb.alloc([8, 65], BF16) for _ in range(RING)]; b_va = [Buf() for _ in range(RING)]
    uT = [sb.alloc([4, 160], F32) for _ in range(URING)]; b_uT = [Buf() for _ in range(URING)]
    qA = [sb.alloc([4, 128], BF16) for _ in range(QRING)]; b_qA = [Buf() for _ in range(QRING)]
    qB = [sb.alloc([4, 128], BF16) for _ in range(QRING)]; b_qB = [Buf() for _ in range(QRING)]
    for r in range(RING):
        S.op("pool", lambda e, r=r: e.memset(va[r], 1.0), writes=[b_va[r]])
    for r in range(QRING):
        S.op("pool", lambda e, r=r: e.memset(qA[r], 0.0), writes=[b_qA[r]])
        S.op("pool", lambda e, r=r: e.memset(qB[r], 0.0), writes=[b_qB[r]])
    for r in range(URING):
        S.op("pool", lambda e, r=r: e.memset(uT[r], 0.0), writes=[b_uT[r]])
    xt = [sb.alloc([1024], F32) for _ in range(2)]; b_xt = [Buf(), Buf()]
    xr = [sb.alloc([1024], F32) for _ in range(2)]; b_xr = [Buf(), Buf()]
    junk = sb.alloc([1024], BF16); b_junk = Buf()
    small = sb.alloc([16], F32); b_small = Buf()
    xn = sb.alloc([1024], BF16); b_xn = Buf()
    hT = sb.alloc([8, 128], BF16); b_hT = Buf()
    qkr = sb.alloc([8, 128], F32); b_qkr = Buf()
    sq = sb.alloc([8, 128], BF16); b_sq = Buf()
    qks = sb.alloc([8, 128], F32); b_qks = Buf()
    sg = sb.alloc([4, 128], F32); b_sg = Buf()
    NTMP = 4
    tmp = [sb.alloc([128], F32) for _ in range(NTMP)]; b_tmp = [Buf() for _ in range(NTMP)]
    PT = [sb.alloc([128], BF16) for _ in range(NTMP)]; b_PT = [Buf() for _ in range(NTMP)]
    ya = sb.alloc([8, 64], F32); b_ya = Buf()
    yan = sb.alloc([512], BF16); b_yan = Buf()
    yaT = sb.alloc([4, 128], BF16); b_yaT = Buf()
    yconv = sb.alloc([4, 128], F32); b_yc = [Buf() for _ in range(4)]
    csq = sb.alloc([4, 128], F32); b_csq = Buf()
    ctmp = sb.alloc([128], F32); b_ctmp = Buf()
    st = sb.alloc([4, 128], F32); b_st = Buf()
    z = sb.alloc([4, 128], F32); b_z = Buf()
    ycn2 = [sb.alloc([4, 128], BF16) for _ in range(2)]; b_ycn2 = [Buf(), Buf()]
    bank0 = ps.alloc([512], F32, align=512); b_b0 = Buf()
    pT = bank0.bitcast(BF16); b_pT = b_b0
    pTv = pT.rearrange("p (c n) -> p c n", c=8)
    pV = bank0; b_pV = b_b0
    pX = ps.alloc([8, 128], F32, align=512); b_pX = Buf()
    pS = ps.alloc([8, 128], F32, align=512); b_pS = Buf()
    pY = pS.rearrange("p a b -> p (a b)")
    pMa = ps.alloc([4, 128], F32, align=512)
    pMb = ps.alloc([4, 128], F32, align=512)
    b_bk5 = Buf(); b_bk6 = Buf()
    pMs = [pMa[:, 0, :], pMb[:, 0, :], pMa[:, 2, :]]; b_pMs = [b_bk5, b_bk6, b_bk5]
    pO = ps.alloc([4, 128], F32, align=512); b_pO = Buf()
    pM = pO; b_pM = [b_pO, b_pO, b_pO, b_pO]

    def stage_p_gen(sl, s):
        x = xt[s % 2]; bx = b_xt[s % 2]
        r = s % RING
        S.dma("sp", lambda e: e.dma_start(out=x, in_=xs_tiles[sl][s]), writes=[bx])
        yield
        S.op("act", lambda e: e.activation(out=junk, in_=x, func=AF.Square, accum_out=small[:, 0:1]),
             reads=[bx], writes=[b_junk, b_small])
        yield
        S.op("act", lambda e: e.activation(out=small[:, 1:2], in_=small[:, 0:1], func=AF.Sqrt, bias=EPS, scale=1.0 / D),
             reads=[b_small], writes=[b_small])
        yield
        S.op("dve", lambda e: e.reciprocal(out=small[:, 2:3], in_=small[:, 1:2]), reads=[b_small], writes=[b_small])
        yield
        S.op("dve", lambda e: e.tensor_scalar(out=xn, in0=x, scalar1=small[:, 2:3], scalar2=None, op0=ALU.mult),
             reads=[bx, b_small], writes=[b_xn])
        yield
        for c in range(8):
            S.op("pe", lambda e, c=c: e.transpose(out=pTv[:, c, :], in_=xn[:, c * 128:(c + 1) * 128], identity=ident),
                 reads=[b_xn, b_ident], writes=[b_pT])
            yield
        S.op("act", lambda e: e.copy(out=hT, in_=pTv), reads=[b_pT], writes=[b_hT])
        yield
        for j in range(8):
            for c in range(8):
                S.op("pe", lambda e, j=j, c=c: e.matmul(pX[:, j, :], lhsT=win[:, c, j * 128:(j + 1) * 128], rhs=hT[:, c, :],
                                                        start=(c == 0), stop=(c == 7)),
                     reads=[b_win, b_hT], writes=[b_pX])
                yield
        ur = s % URING
        S.op("act", lambda e: e.activation(out=sg, in_=pX[:, 4:8, :], func=AF.Sigmoid), reads=[b_pX], writes=[b_sg])
        yield
        S.op("dve", lambda e: e.tensor_tensor(out=uT[ur][:, :, 16:144], in0=pX[:, 0:4, :], in1=sg, op=ALU.mult),
             reads=[b_pX, b_sg], writes=[b_uT[ur]])
        yield
        if s >= 1:
            up = (s - 1) % URING
            S.op("pool", lambda e: e.tensor_copy(out=uT[up][:, :, 144:159], in_=uT[ur][:, :, 16:31]),
                 reads=[b_uT[ur]], writes=[b_uT[up]])
            yield
        if s + 1 < NS:
            un = (s + 1) % URING
            S.op("pool", lambda e: e.tensor_copy(out=uT[un][:, :, 1:16], in_=uT[ur][:, :, 129:144]),
                 reads=[b_uT[ur]], writes=[b_uT[un]])
            yield
        for j in range(8):
            for c in range(8):
                S.op("pe", lambda e, j=j, c=c: e.matmul(pX[:, j, :], lhsT=win[:, c, 1024 + j * 128:1024 + (j + 1) * 128],
                                                        rhs=hT[:, c, :], start=(c == 0), stop=(c == 7)),
                     reads=[b_win, b_hT], writes=[b_pX])
                yield
        S.op("act", lambda e: e.copy(out=qkr, in_=pX), reads=[b_pX], writes=[b_qkr])
        yield
        S.op("act", lambda e: e.activation(out=sq, in_=qkr, func=AF.Square), reads=[b_qkr], writes=[b_sq])
        yield
        for j in range(8):
            S.op("pe", lambda e, j=j: e.matmul(pS[:, j, :], lhsT=blk, rhs=sq[:, j, :], start=True, stop=True),
                 reads=[b_blk, b_sq], writes=[b_pS])
            yield
        S.op("act", lambda e: e.activation(out=qks, in_=pS, func=AF.Sqrt, bias=EPS, scale=1.0 / 64), reads=[b_pS], writes=[b_qks])
        yield
        S.op("dve", lambda e: e.reciprocal(out=qks, in_=qks), reads=[b_qks], writes=[b_qks])
        yield
        S.op("dve", lambda e: e.tensor_tensor(out=qkr, in0=qkr, in1=qks, op=ALU.mult), reads=[b_qkr, b_qks], writes=[b_qkr])
        yield
        S.op("dve", lambda e: e.tensor_scalar(out=kT[r], in0=qkr[:, 4:8, :], scalar1=pp[:, PP_GK:PP_GK + 1], scalar2=None,
                                              op0=ALU.mult), reads=[b_qkr, b_pp], writes=[b_kT[r]])
        yield
        if 2 <= s < NQ + 2:
            qr = s % QRING
            S.op("dve", lambda e: e.tensor_scalar(out=qA[qr][0:64], in0=qkr[0:64, 0:4, :], scalar1=gq8[0:64], scalar2=None,
                                                  op0=ALU.mult), reads=[b_qkr, b_gq8], writes=[b_qA[qr]])
            yield
            S.op("dve", lambda e: e.tensor_scalar(out=qB[qr][64:128], in0=qkr[64:128, 0:4, :], scalar1=gq8[64:128],
                                                  scalar2=None, op0=ALU.mult), reads=[b_qkr, b_gq8], writes=[b_qB[qr]])
            yield
        for c in range(8):
            S.op("pe", lambda e, c=c: e.matmul(pV, lhsT=hT[:, c, :], rhs=win[:, c, 2048:2560], start=(c == 0), stop=(c == 7)),
                 reads=[b_win, b_hT], writes=[b_pV])
            yield
        S.op("act", lambda e: e.copy(out=va[r][:, :, 0:64], in_=pV.rearrange("p (h d) -> p h d", h=8)),
             reads=[b_pV], writes=[b_va[r]])
        yield

    cnt = [0]

    def m_conv_gen(sl, s):
        ur = s % URING
        ycn_s = ycn2[s % 2]; b_ycn_s = b_ycn2[s % 2]
        for c in range(4):
            S.op("dve", lambda e, c=c: e.tensor_scalar(out=yconv[:, c, :], in0=uT[ur][:, c, 1:129], scalar1=cw[:, c, 0:1],
                                                       scalar2=pp[:, PP_CB + c:PP_CB + c + 1], op0=ALU.mult, op1=ALU.add),
                 reads=[b_uT[ur], b_pp], writes=[b_yc[c]])
            yield
        for k in range(1, 31):
            for c in range(4):
                S.op("dve", lambda e, c=c, k=k: e.scalar_tensor_tensor(
                    out=yconv[:, c, :], in0=uT[ur][:, c, k + 1:k + 129], scalar=cw[:, c, k:k + 1], in1=yconv[:, c, :],
                    op0=ALU.mult, op1=ALU.add), reads=[b_uT[ur], b_pp], writes=[b_yc[c]])
                yield
        S.op("act", lambda e: e.activation(out=csq, in_=yconv, func=AF.Square), reads=b_yc, writes=[b_csq])
        yield
        for c in range(4):
            S.op("pe", lambda e, c=c: e.matmul(pS[:, 0, :], lhsT=onesf, rhs=yconv[:, c, :], start=(c == 0), stop=(c == 3)),
                 reads=[b_onesf] + b_yc, writes=[b_pS])
            yield
        for c in range(4):
            S.op("pe", lambda e, c=c: e.matmul(pS[:, 1, :], lhsT=onesf, rhs=csq[:, c, :], start=(c == 0), stop=(c == 3)),
                 reads=[b_onesf, b_csq], writes=[b_pS])
            yield
        S.op("dve", lambda e: e.tensor_scalar(out=st[:, 0, :], in0=pS[:, 0, :], scalar1=1.0 / 512, scalar2=None, op0=ALU.mult),
             reads=[b_pS], writes=[b_st])
        yield
        S.op("dve", lambda e: e.tensor_tensor(out=st[:, 1, :], in0=st[:, 0, :], in1=st[:, 0, :], op=ALU.mult),
             reads=[b_st], writes=[b_st])
        yield
        S.op("dve", lambda e: e.scalar_tensor_tensor(out=st[:, 2, :], in0=pS[:, 1, :], scalar=1.0 / 512, in1=st[:, 1, :],
                                                     op0=ALU.mult, op1=ALU.subtract), reads=[b_pS, b_st], writes=[b_st])
        yield
        S.op("act", lambda e: e.activation(out=st[:, 3, :], in_=st[:, 2, :], func=AF.Sqrt, bias=EPS, scale=1.0),
             reads=[b_st], writes=[b_st])
        yield
        S.op("dve", lambda e: e.reciprocal(out=st[:, 3, :], in_=st[:, 3, :]), reads=[b_st], writes=[b_st])
        yield
        S.op("dve", lambda e: e.tensor_tensor(out=z, in0=yconv, in1=st[:, 0, :].unsqueeze(1).broadcast_to([128, 4, 128]),
                                              op=ALU.subtract), reads=b_yc + [b_st], writes=[b_z])
        yield
        S.op("dve", lambda e: e.tensor_tensor(out=z, in0=z, in1=st[:, 3, :].unsqueeze(1).broadcast_to([128, 4, 128]),
                                              op=ALU.mult), reads=[b_z, b_st], writes=[b_z])
        yield
        for c in range(4):
            S.op("dve", lambda e, c=c: e.tensor_scalar(out=z[:, c, :], in0=z[:, c, :], scalar1=pp[:, PP_LNG + c:PP_LNG + c + 1],
                                                       scalar2=pp[:, PP_LNB + c:PP_LNB + c + 1], op0=ALU.mult, op1=ALU.add),
                 reads=[b_z, b_pp], writes=[b_z])
            yield
        S.op("act", lambda e: e.activation(out=z, in_=z, func=AF.Silu), reads=[b_z], writes=[b_z])
        yield
        S.op("act", lambda e: e.activation(out=csq, in_=z, func=AF.Square), reads=[b_z], writes=[b_csq])
        yield
        for c in range(4):
            S.op("pe", lambda e, c=c: e.matmul(pS[:, 0, :], lhsT=onesf, rhs=csq[:, c, :], start=(c == 0), stop=(c == 3)),
                 reads=[b_onesf, b_csq], writes=[b_pS])
            yield
        S.op("act", lambda e: e.activation(out=st[:, 0, :], in_=pS[:, 0, :], func=AF.Sqrt, bias=EPS, scale=1.0 / 512),
             reads=[b_pS], writes=[b_st])
        yield
        S.op("dve", lambda e: e.reciprocal(out=st[:, 0, :], in_=st[:, 0, :]), reads=[b_st], writes=[b_st])
        yield
        S.op("dve", lambda e: e.tensor_tensor(out=ycn_s, in0=z, in1=st[:, 0, :].unsqueeze(1).broadcast_to([128, 4, 128]),
                                              op=ALU.mult), reads=[b_z, b_st], writes=[b_ycn_s])
        yield

    def m_attn(sl, s, filler, rate):
        i = s - 2
        qr = s % QRING
        x = xr[i % 2]; bx = b_xr[i % 2]
        S.dma("sp", lambda e: e.dma_start(out=x, in_=xs_tiles[sl][s]), writes=[bx])
        kts = key_tiles(s, NQ)
        units = []
        for grp in range(2):
            for hh in range(4):
                for n_, kt in enumerate(kts):
                    units.append((grp, hh, n_, kt))
        LOOK = 2
        NSLOT = 3
        info = {}

        def emit_st(u):
            grp, hh, n_, kt = units[u]
            h = grp * 4 + hh
            c = h // 2
            qm = (qA if h % 2 == 0 else qB)[qr]
            bqm = (b_qA if h % 2 == 0 else b_qB)[qr]
            kr = kt % RING
            m = cnt[0] % NSLOT; cnt[0] += 1
            info[u] = m
            S.op("pe", lambda e: e.matmul(pMs[m], lhsT=kT[kr][:, c, :], rhs=qm[:, c, :], start=True, stop=True),
                 reads=[b_kT[kr], bqm], writes=[b_pMs[m]])

        def emit_rest(u):
            grp, hh, n_, kt = units[u]
            h = grp * 4 + hh
            kr = kt % RING
            m = info[u]
            t_ = u % NTMP
            j0 = 8 - 2 * (kt - s)
            ti = ntile_idx[(s, kt)]
            S.op("dve", lambda e: e.tensor_tensor(
                out=tmp[t_], in0=pMs[m], in1=T[:, h, j0:j0 + 2, :].rearrange("p a b -> p (a b)"), op=ALU.add),
                 reads=[b_pMs[m], b_T], writes=[b_tmp[t_]])
            for qh in range(2):
                S.op("act", lambda e, qh=qh: e.activation(
                    out=PT[t_][:, qh * 64:(qh + 1) * 64], in_=tmp[t_][:, qh * 64:(qh + 1) * 64], func=AF.Exp,
                    bias=rmask[:, sl, ti, qh:qh + 1]), reads=[b_tmp[t_], b_rmask], writes=[b_PT[t_]])
            S.op("pe", lambda e: e.matmul(pO[:, hh, 0:65], lhsT=PT[t_], rhs=va[kr][:, h, :],
                                          start=(n_ == 0), stop=(n_ == len(kts) - 1)),
                 reads=[b_PT[t_], b_va[kr]], writes=[b_pO])
            if hh == 3 and n_ == len(kts) - 1:
                S.op("dve", lambda e: e.reciprocal(out=small[:, 4:8], in_=pO[:, :, 64]), reads=[b_pO], writes=[b_small])
                S.op("dve", lambda e: e.tensor_tensor(
                    out=ya[:, grp * 4:(grp + 1) * 4, :], in0=pO[:, :, 0:64],
                    in1=small[:, 4:8].unsqueeze(2).broadcast_to([128, 4, 64]), op=ALU.mult),
                     reads=[b_pO, b_small], writes=[b_ya])

        for u in range(min(LOOK, len(units))):
            emit_st(u)
        for u in range(len(units)):
            emit_rest(u)
            if u + LOOK < len(units):
                emit_st(u + LOOK)
            if filler is not None:
                for _ in range(rate):
                    next(filler, None)
        if filler is not None:
            for _ in filler:
                pass
        yaf = ya.rearrange("p h d -> p (h d)")
        S.op("act", lambda e: e.activation(out=junk[:, 0:512], in_=yaf, func=AF.Square, accum_out=small[:, 8:9]),
             reads=[b_ya], writes=[b_junk, b_small])
        S.op("act", lambda e: e.activation(out=small[:, 9:10], in_=small[:, 8:9], func=AF.Sqrt, bias=EPS, scale=1.0 / 512),
             reads=[b_small], writes=[b_small])
        S.op("dve", lambda e: e.reciprocal(out=small[:, 10:11], in_=small[:, 9:10]), reads=[b_small], writes=[b_small])
        S.op("dve", lambda e: e.tensor_scalar(out=yan, in0=yaf, scalar1=small[:, 10:11], scalar2=None, op0=ALU.mult),
             reads=[b_ya, b_small], writes=[b_yan])
        for c in range(4):
            S.op("pe", lambda e, c=c: e.transpose(out=pTv[:, c, :], in_=yan[:, c * 128:(c + 1) * 128], identity=ident),
                 reads=[b_yan, b_ident], writes=[b_pT])
        S.op("act", lambda e: e.copy(out=yaT, in_=pTv[:, 0:4, :]), reads=[b_pT], writes=[b_yaT])

    def m_out(sl, s):
        i = s - 2
        x = xr[i % 2]; bx = b_xr[i % 2]
        ycn_s = ycn2[s % 2]; b_ycn_s = b_ycn2[s % 2]
        for hf in range(2):
            for c in range(8):
                lhs = ycn_s[:, c, :] if c < 4 else yaT[:, c - 4, :]
                S.op("pe", lambda e, c=c, hf=hf, lhs=lhs: e.matmul(pY[:, hf * 512:(hf + 1) * 512], lhsT=lhs,
                                                                   rhs=wout[:, c, hf * 512:(hf + 1) * 512],
                                                                   start=(c == 0), stop=(c == 7)),
                     reads=[b_ycn_s, b_yaT, b_wout], writes=[b_pS])
        S.op("dve", lambda e: e.tensor_tensor(out=x, in0=pY, in1=x, op=ALU.add), reads=[b_pS, bx], writes=[bx])
        S.dma("sp", lambda e: e.dma_start(out=x1_tiles[sl][i], in_=x), reads=[bx])


    def chain(*gens):
        for g in gens:
            if g is not None:
                for _ in g:
                    yield

    for sl in range(nslab):
        for _ in stage_p_gen(sl, 0):
            pass
        for step in range(NS + 3):
            gp = stage_p_gen(sl, step + 1) if step + 1 < NS else None
            sc = step - 2
            gc = m_conv_gen(sl, sc) if 2 <= sc < NQ + 2 else None
            g = chain(gc, gp)
            s = step - 3
            if 2 <= s < NQ + 2:
                m_attn(sl, s, g, 8)
            for _ in g:
                pass
            if 2 <= s < NQ + 2:
                m_out(sl, s)


def row_mask_table(NQ, kind, q=0, R=None):
    idx = {}
    for s in range(2, NQ + 2):
        for kt in key_tiles(s, NQ):
            idx[(s, kt)] = len(idx)
    out = np.full((128, len(idx), 2), NEGM, np.float32)
    nrows = 2 * NQ
    if kind == "full":
        R = nrows; base = 0
    else:
        base = q
    for (s, kt), ti in idx.items():
        for qh in range(2):
            r = base + 2 * (s - 2) + qh
            rs = min(max(r - 4, 0), R - 8)
            for kh in range(2):
                rk = base + 2 * kt + kh - 4
                if rs <= rk < rs + 8:
                    out[kh * 64:(kh + 1) * 64, ti, qh] = 0.0
    return out


def bias_table(rpb):
    T = np.full((128, 8, 16, 64), NEGM, np.float32)
    cq = np.arange(64)
    cs = np.clip(cq - 8, 0, 48)
    for kh in range(2):
        for j in range(16):
            dr = kh - j + 8
            if abs(dr) > 7:
                continue
            for cp in range(64):
                ok = (cp >= cs) & (cp < cs + 16)
                off = np.clip(cp - cq + 15, 0, 30)
                vals = rpb[:, dr + 7, :][:, off]
                T[kh * 64 + cp, :, j, :] = np.where(ok[None, :], vals, NEGM)
    return T


def small_params(g_mix, g_out_conv, g_out_attn, conv_w, conv_b, ln_g, ln_b, q_g, k_g):
    pp = np.zeros((128, PP_N), np.float32)
    pp[:, PP_GMIX:PP_GMIX + 8] = g_mix.reshape(8, 128).T
    pp[:, PP_GOUT:PP_GOUT + 4] = g_out_conv.reshape(4, 128).T
    pp[:, PP_GOUT + 4:PP_GOUT + 8] = g_out_attn.reshape(4, 128).T
    pp[:, PP_CB:PP_CB + 4] = conv_b.reshape(4, 128).T
    pp[:, PP_LNG:PP_LNG + 4] = ln_g.reshape(4, 128).T
    pp[:, PP_LNB:PP_LNB + 4] = ln_b.reshape(4, 128).T
    pp[:, PP_GQ] = np.tile(q_g, 2)
    pp[:, PP_GK] = np.tile(k_g, 2)
    pp[:, PP_CW:PP_CW + 124] = conv_w.T.reshape(4, 128, 31).transpose(1, 0, 2).reshape(128, 124)
    return pp


NQ_FULL = 32
N_CORES = 8
ARENA_WORDS = 51200


def build_program(NQ=NQ_FULL):
    NS = NQ + 4
    nc = bass.Bass("TRN2", target_bir_lowering=False)
    xs = nc.dram_tensor("xs", [2, NS * 128, D], F32, kind="ExternalInput").ap()
    y = nc.dram_tensor("y", [2, NQ * 128, D], F32, kind="ExternalOutput").ap()
    nti = sum(len(key_tiles(s, NQ)) for s in range(2, NQ + 2))
    C = {}
    for name, shape in (("consts", [128, 144]), ("pp", [128, PP_N]), ("ttab", [128, 8 * 16 * 64]),
                        ("rmask", [128, 2 * nti * 2]), ("w_in", [D, 2560]), ("w_out", [D, D]),
                        ("g_ffn", [D]), ("w_query", [D, 2048]), ("sub_keys", [16, 128, 128]),
                        ("expert_u", [NEXP, D]), ("expert_v", [NEXP, D])):
        C[name] = nc.dram_tensor(name, shape, F32, kind="ExternalInput").ap()
    S = Sched(nc)
    sbh = nc.alloc_sbuf_tensor("arena", [128, ARENA_WORDS], F32)
    psh = nc.alloc_psum_tensor("parena", [128, 4096], F32)
    uv16 = nc.dram_tensor("uv16", [NEXP, 2048], BF16, kind="Internal").ap()
    b_uv16 = Buf()
    convert_tables(S, C, uv16, b_uv16)
    xst = xs.rearrange("a (n p) d -> a n p d", p=128)
    yt = y.rearrange("a (n p) d -> a n p d", p=128)
    sb = Arena(sbh, ARENA_WORDS); ps = Arena(psh, 4096)
    phase_a(S, nc, sb, ps, C, NQ, [[xst[a, i] for i in range(NS)] for a in range(2)],
            [[yt[a, i] for i in range(NQ)] for a in range(2)])
    S.barrier()
    sb = Arena(sbh, ARENA_WORDS); ps = Arena(psh, 4096)
    ytl = [yt[a, i] for a in range(2) for i in range(NQ)]
    phase_b3(S, nc, sb, ps, C, ytl, ytl, uv16, b_uv16)
    S.emit()
    return nc, S


def kernel(x_prompt, x_sample, g_mix, w_in, conv_w, conv_b, conv_ln_g, conv_ln_b,
           q_norm_g, k_norm_g, rpb, g_out_conv, g_out_attn, w_out, g_ffn,
           w_query, sub_keys, expert_u, expert_v):
    f = lambda a: np.ascontiguousarray(np.asarray(a, dtype=np.float32))
    x_prompt, x_sample = f(x_prompt), f(x_sample)
    NQ = NQ_FULL
    NS = NQ + 4
    T = NQ * 128
    H = 256
    shared = {
        "consts": make_consts(),
        "pp": small_params(f(g_mix)[0], f(g_out_conv)[0], f(g_out_attn)[0], f(conv_w)[0], f(conv_b)[0],
                           f(conv_ln_g)[0], f(conv_ln_b)[0], f(q_norm_g)[0], f(k_norm_g)[0]),
        "ttab": bias_table(f(rpb)[0]).reshape(128, -1),
        "w_in": f(w_in)[0], "w_out": f(w_out)[0], "g_ffn": f(g_ffn)[0], "w_query": f(w_query)[0],
        "sub_keys": f(sub_keys)[0].reshape(16, 128, 128),
        "expert_u": f(expert_u)[0], "expert_v": f(expert_v)[0],
    }
    rm_full = row_mask_table(NQ, "full")
    in_maps = []
    for c in range(N_CORES):
        b, q = c // 4, c % 4
        xs = np.zeros((2, NS * 128, D), np.float32)
        xs[0, H:H + T] = x_sample[c]
        lo, hi = q * T - H, q * T + T + H
        clo, chi = max(lo, 0), min(hi, x_prompt.shape[1])
        xs[1, clo - lo:clo - lo + (chi - clo)] = x_prompt[b, clo:chi]
        rm = np.stack([rm_full, row_mask_table(NQ, "chunk", q=2 * NQ * q, R=x_prompt.shape[1] // 64)], axis=1)
        m = dict(shared)
        m["xs"] = xs
        m["rmask"] = np.ascontiguousarray(rm.reshape(128, -1))
        in_maps.append(m)
    nc, _ = build_program(NQ)
    res = run_bass_kernel_spmd(nc, in_maps, core_ids=list(range(N_CORES)))
    y_prompt = np.zeros_like(x_prompt)
    y_sample = np.zeros_like(x_sample)
    for c in range(N_CORES):
        yc = np.asarray(res.results[c]["y"], dtype=np.float32)
        y_sample[c] = yc[0]
        y_prompt[c // 4, (c % 4) * T:(c % 4 + 1) * T] = yc[1]
    return (y_prompt, y_sample)
```

```python
import numpy as np
import concourse.bass as bass
import concourse.mybir as mybir
from concourse.bass_utils import run_bass_kernel_spmd

F32 = mybir.dt.float32
BF16 = mybir.dt.bfloat16
I32 = mybir.dt.int32
U32 = mybir.dt.uint32
ALU = mybir.AluOpType
AF = mybir.ActivationFunctionType
AX = mybir.AxisListType

D = 1024
NEXP = 16384
EPS = 1e-6
NEGM = -30000.0


class Buf:
    __slots__ = ("name", "w", "r")

    def __init__(self, name=""):
        self.name = name
        self.w = None
        self.r = []


class Op:
    __slots__ = ("eng", "fn", "deps", "signal", "semkey", "count", "is_dma")

    def __init__(self, eng, fn):
        self.eng = eng
        self.fn = fn
        self.deps = []
        self.signal = False
        self.semkey = None
        self.count = 0
        self.is_dma = False


class Sched:
    ENGS = ("pe", "dve", "act", "pool", "sp")
    SELF_SYNC = {"pe": False, "dve": True, "act": True, "pool": True, "sp": False}

    def __init__(self, nc, n_dma_sems=None):
        self.nc = nc
        self.ops = {e: [] for e in self.ENGS}
        self.n_dma_sems = n_dma_sems or {"sp": 16, "act": 8, "pool": 24}
        self.dma_rr = {}
        self.dma_last = {}
        self.dma_cnt = {}
        self.last_real = {}

    def _add_deps(self, op, reads, writes):
        deps = []
        for b in reads:
            if b.w is not None:
                deps.append(b.w)
        for b in writes:
            if b.w is not None:
                deps.append(b.w)
            deps.extend(b.r)
        for d in deps:
            if d is op:
                continue
            if (not d.is_dma) and d.eng == op.eng and not self.SELF_SYNC[op.eng]:
                continue
            op.deps.append(d)
            d.signal = True
        for b in reads:
            b.r.append(op)
        for b in writes:
            b.w = op
            b.r = []

    def op(self, eng, fn, reads=(), writes=()):
        o = Op(eng, fn)
        self._add_deps(o, reads, writes)
        self.ops[eng].append(o)
        self.last_real[eng] = o
        return o

    def dma(self, eng, fn, reads=(), writes=()):
        o = Op(eng, fn)
        o.is_dma = True
        o.signal = True
        rr = self.dma_rr.get(eng, 0)
        self.dma_rr[eng] = rr + 1
        key = ("dma", eng, rr % self.n_dma_sems[eng])
        o.semkey = key
        prev = self.dma_last.get(key)
        if prev is not None:
            o.deps.append(prev)
        self.dma_last[key] = o
        c = self.dma_cnt.get(key, 0) + 16
        self.dma_cnt[key] = c
        o.count = c
        self._add_deps(o, reads, writes)
        self.ops[eng].append(o)
        return o

    def barrier(self):
        lasts = [o for o in self.last_real.values()] + list(self.dma_last.values())
        for e in self.ENGS:
            o = Op(e, None)
            for d in lasts:
                if (not d.is_dma) and d.eng == e:
                    continue
                o.deps.append(d)
                d.signal = True
            self.ops[e].append(o)

    def emit(self, final_wait_eng="sp"):
        nc = self.nc
        self.barrier()
        for e in self.ENGS:
            c = 0
            for o in self.ops[e]:
                if o.is_dma or o.fn is None:
                    continue
                o.semkey = ("eng", e)
                if o.signal:
                    c += 1
                    o.count = c
        keys = set()
        for e in self.ENGS:
            for o in self.ops[e]:
                if o.signal and o.fn is not None:
                    keys.add(o.semkey)
        sems = {}
        for k in sorted(keys, key=str):
            sems[k] = nc.alloc_semaphore(name="s_" + "_".join(str(x) for x in k))
        stats = {"ins": 0, "wait": 0}
        with nc.Block() as block:
            deco = {"pe": block.tensor, "dve": block.vector, "act": block.scalar,
                    "pool": block.gpsimd, "sp": block.sync}
            for e in self.ENGS:
                ops = self.ops[e]

                def body(eng, ops=ops):
                    seen = {}
                    for o in ops:
                        for d in o.deps:
                            if seen.get(d.semkey, 0) < d.count:
                                eng.wait_ge(sems[d.semkey], d.count)
                                seen[d.semkey] = d.count
                                stats["wait"] += 1
                        if o.fn is None:
                            continue
                        ins = o.fn(eng)
                        stats["ins"] += 1
                        if o.signal:
                            ins.then_inc(sems[o.semkey], 16 if o.is_dma else 1)

                deco[e](body)
        self.stats = stats


class Arena:
    def __init__(self, handle, nwords):
        self.h = handle
        self.n = nwords
        self.off = 0

    def alloc(self, free_shape, dtype=F32, align=16):
        n = 1
        for s in free_shape:
            n *= s
        size = 4 if dtype in (F32, I32, U32) else 2
        words = (n * size + 3) // 4
        self.off = (self.off + align - 1) // align * align
        assert self.off + words <= self.n, ("arena overflow", self.off, words, self.n)
        ap = self.h[:, self.off:self.off + words]
        self.off += words
        if dtype != F32:
            ap = ap.bitcast(dtype)
            if ap.shape[1] != n:
                ap = ap[:, 0:n]
        if len(free_shape) == 2:
            ap = ap.rearrange("p (a b) -> p a b", a=free_shape[0])
        elif len(free_shape) == 3:
            ap = ap.rearrange("p (a b c) -> p a b c", a=free_shape[0], b=free_shape[1])
        return ap


def phase_b(S, nc, sb, ps, C, x1_tiles, y_tiles, NB=6):
    ident = sb.alloc([128], BF16); b_ident = Buf()
    iota16 = sb.alloc([16], F32); b_iota = Buf()
    cst = sb.alloc([144], F32); b_cst = Buf()
    wq = sb.alloc([8, 2048], BF16); b_wq = Buf()
    skT = sb.alloc([16, 128], BF16); b_skT = Buf()
    gffn = sb.alloc([1024], F32); b_gffn = Buf()
    S.dma("sp", lambda e: e.dma_start(out=cst, in_=C["consts"]), writes=[b_cst])
    S.op("dve", lambda e: e.tensor_copy(out=ident, in_=cst[:, 0:128]), reads=[b_cst], writes=[b_ident])
    S.op("dve", lambda e: e.tensor_copy(out=iota16, in_=cst[:, 128:144]), reads=[b_cst], writes=[b_iota])
    S.dma("sp", lambda e: e.dma_start(out=gffn, in_=C["g_ffn"].partition_broadcast(128)), writes=[b_gffn])
    stg = [sb.alloc([2048], F32) for _ in range(2)]
    b_stg = [Buf(), Buf()]
    wqv = C["w_query"].rearrange("(c p) n -> c p n", p=128)
    for c in range(8):
        S.dma("sp", lambda e, c=c: e.dma_start(out=stg[c % 2], in_=wqv[c]), writes=[b_stg[c % 2]])
        S.op("act" if c % 2 else "dve",
             (lambda e, c=c: e.copy(out=wq[:, c, :], in_=stg[c % 2])) if c % 2 else
             (lambda e, c=c: e.tensor_copy(out=wq[:, c, :], in_=stg[c % 2])),
             reads=[b_stg[c % 2]], writes=[b_wq])
    skv = C["sub_keys"].rearrange("g k d -> k g d")
    skn = stg[0].rearrange("p (g d) -> p g d", g=16)
    skb = sb.alloc([16, 128], BF16); b_skb = Buf()
    pbig = ps.alloc([16, 128], F32, align=512); b_pbig = Buf()
    pT = ps.alloc([1024], BF16, align=512); b_pT = Buf()
    pTv = pT.rearrange("p (c n) -> p c n", c=8)
    S.dma("sp", lambda e: e.dma_start(out=skn, in_=skv), writes=[b_stg[0]])
    S.op("dve", lambda e: e.tensor_copy(out=skb, in_=skn), reads=[b_stg[0]], writes=[b_skb])
    for half in range(2):
        for j in range(8):
            g = half * 8 + j
            S.op("pe", lambda e, g=g, j=j: e.transpose(out=pTv[:, j, :], in_=skb[:, g, :], identity=ident),
                 reads=[b_skb, b_ident], writes=[b_pT])
        S.op("dve", lambda e, half=half: e.tensor_copy(out=skT[:, half * 8:(half + 1) * 8, :], in_=pTv),
             reads=[b_pT], writes=[b_skT])

    x1t = [sb.alloc([1024], F32) for _ in range(2)]; b_x1t = [Buf(), Buf()]
    junk = sb.alloc([1024], F32); b_junk = Buf()
    hn = sb.alloc([1024], F32); b_hn = Buf()
    hb = sb.alloc([1024], BF16); b_hb = Buf()
    hT = sb.alloc([8, 128], BF16); b_hT = Buf()
    qT = sb.alloc([16, 128], BF16); b_qT = Buf()
    s = sb.alloc([16, 128], F32); b_s = Buf()
    s2 = sb.alloc([16, 128], F32); b_s2 = Buf()
    sv = sb.alloc([16, 16], F32); b_sv = Buf()
    siu = sb.alloc([16, 16], U32); b_siu = Buf()
    sif = sb.alloc([16, 16], F32); b_sif = Buf()
    cand = sb.alloc([8, 256], F32); b_cand = Buf()
    cand2 = sb.alloc([8, 256], F32); b_cand2 = Buf()
    cv = sb.alloc([8, 16], F32); b_cv = Buf()
    ciu = sb.alloc([8, 16], U32); b_ciu = Buf()
    rcu = sb.alloc([2, 128], U32); b_rcu = Buf()
    rcf = sb.alloc([2, 128], F32); b_rcf = Buf()
    oh = sb.alloc([128, 16], F32); b_oh = Buf()
    i12 = sb.alloc([2, 128], F32); b_i12 = Buf()
    ef = sb.alloc([128], F32); b_ef = Buf()
    ei = [sb.alloc([128], I32) for _ in range(2)]; b_ei = [Buf(), Buf()]
    araw = sb.alloc([128], F32); b_araw = Buf()
    aw = sb.alloc([128], F32); b_aw = Buf()
    gsm = sb.alloc([8, 16], F32); b_gsm = Buf()
    small = sb.alloc([32], F32); b_small = Buf()
    acc = [sb.alloc([1024], F32) for _ in range(2)]; b_acc = [Buf(), Buf()]
    ub = [sb.alloc([1024], F32) for _ in range(NB)]; b_ub = [Buf() for _ in range(NB)]
    vb = [sb.alloc([1024], F32) for _ in range(NB)]; b_vb = [Buf() for _ in range(NB)]
    gcount = [0, 0]

    sv4 = sv.rearrange("p (h t) k -> p h t k", t=2)
    sif4 = sif.rearrange("p (h t) k -> p h t k", t=2)
    cand4 = cand.rearrange("p h (i j) -> p h i j", i=16)
    oh4 = oh.rearrange("p (h k) i -> p h k i", h=8)

    for ti in range(len(x1_tiles)):
        xb = x1t[ti % 2]; bxb = b_x1t[ti % 2]
        eib = ei[ti % 2]; beib = b_ei[ti % 2]
        ac = acc[ti % 2]; bac = b_acc[ti % 2]
        S.dma("sp", lambda e, xb=xb, ti=ti: e.dma_start(out=xb, in_=x1_tiles[ti]), writes=[bxb])
        S.op("act", lambda e, xb=xb: e.activation(out=junk, in_=xb, func=AF.Square, accum_out=small[:, 0:1]),
             reads=[bxb], writes=[b_junk, b_small])
        S.op("act", lambda e: e.activation(out=small[:, 1:2], in_=small[:, 0:1], func=AF.Sqrt, bias=EPS, scale=1.0 / D),
             reads=[b_small], writes=[b_small])
        S.op("dve", lambda e: e.reciprocal(out=small[:, 2:3], in_=small[:, 1:2]), reads=[b_small], writes=[b_small])
        S.op("dve", lambda e, xb=xb: e.scalar_tensor_tensor(out=hn, in0=xb, scalar=small[:, 2:3], in1=gffn,
                                                            op0=ALU.mult, op1=ALU.mult),
             reads=[bxb, b_small, b_gffn], writes=[b_hn])
        S.op("act", lambda e: e.copy(out=hb, in_=hn), reads=[b_hn], writes=[b_hb])
        for c in range(8):
            S.op("pe", lambda e, c=c: e.transpose(out=pTv[:, c, :], in_=hb[:, c * 128:(c + 1) * 128], identity=ident),
                 reads=[b_hb, b_ident], writes=[b_pT])
        S.op("act", lambda e: e.copy(out=hT, in_=pTv), reads=[b_pT], writes=[b_hT])
        for g in range(16):
            for c in range(8):
                S.op("pe", lambda e, g=g, c=c: e.matmul(pbig[:, g, :], lhsT=wq[:, c, g * 128:(g + 1) * 128],
                                                        rhs=hT[:, c, :], start=(c == 0), stop=(c == 7)),
                     reads=[b_wq, b_hT], writes=[b_pbig])
        S.op("act", lambda e: e.copy(out=qT, in_=pbig), reads=[b_pbig], writes=[b_qT])
        for g in range(16):
            S.op("pe", lambda e, g=g: e.matmul(pbig[:, g, :], lhsT=qT[:, g, :], rhs=skT[:, g, :], start=True, stop=True),
                 reads=[b_qT, b_skT], writes=[b_pbig])
        S.op("dve", lambda e: e.tensor_copy(out=s, in_=pbig), reads=[b_pbig], writes=[b_s])
        for g in range(16):
            S.op("dve", lambda e, g=g: e.max(out=sv[:, g, 0:8], in_=s[:, g, :]), reads=[b_s], writes=[b_sv])
            S.op("dve", lambda e, g=g: e.match_replace(out=s2[:, g, :], in_to_replace=sv[:, g, 0:8],
                                                       in_values=s[:, g, :], imm_value=-1e30),
                 reads=[b_s, b_sv], writes=[b_s2])
            S.op("dve", lambda e, g=g: e.max(out=sv[:, g, 8:16], in_=s2[:, g, :]), reads=[b_s2], writes=[b_sv])
            S.op("dve", lambda e, g=g: e.max_index(out=siu[:, g, 0:8], in_max=sv[:, g, 0:8], in_values=s[:, g, :]),
                 reads=[b_s, b_sv], writes=[b_siu])
            S.op("dve", lambda e, g=g: e.max_index(out=siu[:, g, 8:16], in_max=sv[:, g, 8:16], in_values=s[:, g, :]),
                 reads=[b_s, b_sv], writes=[b_siu])
        S.op("dve", lambda e: e.tensor_copy(out=sif, in_=siu), reads=[b_siu], writes=[b_sif])
        for h in range(8):
            S.op("dve", lambda e, h=h: e.tensor_tensor(
                out=cand4[:, h], in0=sv4[:, h, 0, :].unsqueeze(2).broadcast_to([128, 16, 16]),
                in1=sv4[:, h, 1, :].unsqueeze(1).broadcast_to([128, 16, 16]), op=ALU.add),
                 reads=[b_sv], writes=[b_cand])
        for h in range(8):
            S.op("dve", lambda e, h=h: e.max(out=cv[:, h, 0:8], in_=cand[:, h, :]), reads=[b_cand], writes=[b_cv])
            S.op("dve", lambda e, h=h: e.match_replace(out=cand2[:, h, :], in_to_replace=cv[:, h, 0:8],
                                                       in_values=cand[:, h, :], imm_value=-1e30),
                 reads=[b_cand, b_cv], writes=[b_cand2])
            S.op("dve", lambda e, h=h: e.max(out=cv[:, h, 8:16], in_=cand2[:, h, :]), reads=[b_cand2], writes=[b_cv])
            S.op("dve", lambda e, h=h: e.max_index(out=ciu[:, h, 0:8], in_max=cv[:, h, 0:8], in_values=cand[:, h, :]),
                 reads=[b_cand, b_cv], writes=[b_ciu])
            S.op("dve", lambda e, h=h: e.max_index(out=ciu[:, h, 8:16], in_max=cv[:, h, 8:16], in_values=cand[:, h, :]),
                 reads=[b_cand, b_cv], writes=[b_ciu])
        ciu_f = ciu.rearrange("p h k -> p (h k)")
        S.op("dve", lambda e: e.tensor_single_scalar(out=rcu[:, 0, :], in_=ciu_f, scalar=4, op=ALU.logical_shift_right),
             reads=[b_ciu], writes=[b_rcu])
        S.op("dve", lambda e: e.tensor_single_scalar(out=rcu[:, 1, :], in_=ciu_f, scalar=15, op=ALU.bitwise_and),
             reads=[b_ciu], writes=[b_rcu])
        S.op("dve", lambda e: e.tensor_copy(out=rcf, in_=rcu), reads=[b_rcu], writes=[b_rcf])
        for t in range(2):
            S.op("dve", lambda e, t=t: e.tensor_tensor(
                out=oh, in0=rcf[:, t, :].unsqueeze(2).broadcast_to([128, 128, 16]),
                in1=iota16.unsqueeze(1).broadcast_to([128, 128, 16]), op=ALU.is_equal),
                 reads=[b_rcf, b_iota], writes=[b_oh])
            for h in range(8):
                S.op("dve", lambda e, h=h, t=t: e.tensor_tensor(
                    out=oh4[:, h], in0=oh4[:, h], in1=sif4[:, h, t, :].unsqueeze(1).broadcast_to([128, 16, 16]),
                    op=ALU.mult), reads=[b_oh, b_sif], writes=[b_oh])
            S.op("dve", lambda e, t=t: e.tensor_reduce(out=i12[:, t, :], in_=oh, axis=AX.X, op=ALU.add),
                 reads=[b_oh], writes=[b_i12])
        S.op("dve", lambda e: e.scalar_tensor_tensor(out=ef, in0=i12[:, 0, :], scalar=128.0, in1=i12[:, 1, :],
                                                     op0=ALU.mult, op1=ALU.add), reads=[b_i12], writes=[b_ef])
        S.op("dve", lambda e, eib=eib: e.tensor_copy(out=eib, in_=ef), reads=[b_ef], writes=[beib])
        S.op("dve", lambda e: e.tensor_tensor(out=gsm, in0=cv, in1=cv[:, :, 0:1].broadcast_to([128, 8, 16]),
                                              op=ALU.subtract), reads=[b_cv], writes=[b_gsm])
        S.op("act", lambda e: e.activation(out=gsm, in_=gsm, func=AF.Exp), reads=[b_gsm], writes=[b_gsm])
        S.op("dve", lambda e: e.tensor_reduce(out=small[:, 8:16], in_=gsm, axis=AX.X, op=ALU.add),
             reads=[b_gsm], writes=[b_small])
        S.op("dve", lambda e: e.reciprocal(out=small[:, 16:24], in_=small[:, 8:16]), reads=[b_small], writes=[b_small])
        S.op("dve", lambda e: e.tensor_tensor(out=gsm, in0=gsm,
                                              in1=small[:, 16:24].unsqueeze(2).broadcast_to([128, 8, 16]),
                                              op=ALU.mult), reads=[b_gsm, b_small], writes=[b_gsm])
        for k in range(128):
            j = gcount[0] % NB; gcount[0] += 1
            S.dma("pool", lambda e, k=k, j=j, eib=eib: e.indirect_dma_start(
                out=ub[j], out_offset=None, in_=C["expert_u"],
                in_offset=bass.IndirectOffsetOnAxis(ap=eib[:, k:k + 1], axis=0)),
                  reads=[beib], writes=[b_ub[j]])
            S.op("dve", lambda e, k=k, j=j: e.scalar_tensor_tensor(
                out=ub[j], in0=ub[j], scalar=1.0, in1=hn, op0=ALU.mult, op1=ALU.mult, accum_out=araw[:, k:k + 1]),
                 reads=[b_hn], writes=[b_ub[j]] + ([b_araw] if k in (0, 127) else []))
        S.op("act", lambda e: e.activation(out=aw, in_=araw, func=AF.Gelu), reads=[b_araw], writes=[b_aw])
        S.op("dve", lambda e: e.tensor_tensor(out=aw, in0=aw, in1=gsm.rearrange("p h k -> p (h k)"), op=ALU.mult),
             reads=[b_aw, b_gsm], writes=[b_aw])
        for k in range(128):
            j = gcount[1] % NB; gcount[1] += 1
            S.dma("pool", lambda e, k=k, j=j, eib=eib: e.indirect_dma_start(
                out=vb[j], out_offset=None, in_=C["expert_v"],
                in_offset=bass.IndirectOffsetOnAxis(ap=eib[:, k:k + 1], axis=0)),
                  reads=[beib], writes=[b_vb[j]])
            src = xb if k == 0 else ac
            S.op("dve", lambda e, k=k, j=j, src=src, ac=ac: e.scalar_tensor_tensor(
                out=ac, in0=vb[j], scalar=aw[:, k:k + 1], in1=src, op0=ALU.mult, op1=ALU.add),
                 reads=[b_vb[j], b_aw] + ([bxb] if k == 0 else []), writes=[bac])
        S.dma("sp", lambda e, ac=ac, ti=ti: e.dma_start(out=y_tiles[ti], in_=ac), reads=[bac])


def phase_b2(S, nc, sb, ps, C, x1_tiles, y_tiles, uv16, b_uv16, NB=8):
    ident = sb.alloc([128], BF16); b_ident = Buf()
    iota16 = sb.alloc([16], F32); b_iota = Buf()
    cst = sb.alloc([144], F32); b_cst = Buf()
    wq = sb.alloc([8, 2048], BF16); b_wq = Buf()
    skT = sb.alloc([16, 128], BF16); b_skT = Buf()
    gffn = sb.alloc([1024], F32); b_gffn = Buf()
    S.dma("sp", lambda e: e.dma_start(out=cst, in_=C["consts"]), writes=[b_cst])
    S.op("dve", lambda e: e.tensor_copy(out=ident, in_=cst[:, 0:128]), reads=[b_cst], writes=[b_ident])
    S.op("dve", lambda e: e.tensor_copy(out=iota16, in_=cst[:, 128:144]), reads=[b_cst], writes=[b_iota])
    S.dma("sp", lambda e: e.dma_start(out=gffn, in_=C["g_ffn"].partition_broadcast(128)), writes=[b_gffn])
    stg = [sb.alloc([2048], F32) for _ in range(2)]
    b_stg = [Buf(), Buf()]
    wqv = C["w_query"].rearrange("(c p) n -> c p n", p=128)
    for c in range(8):
        S.dma("sp", lambda e, c=c: e.dma_start(out=stg[c % 2], in_=wqv[c]), writes=[b_stg[c % 2]])
        S.op("act" if c % 2 else "dve",
             (lambda e, c=c: e.copy(out=wq[:, c, :], in_=stg[c % 2])) if c % 2 else
             (lambda e, c=c: e.tensor_copy(out=wq[:, c, :], in_=stg[c % 2])),
             reads=[b_stg[c % 2]], writes=[b_wq])
    skv = C["sub_keys"].rearrange("g k d -> k g d")
    skn = stg[0].rearrange("p (g d) -> p g d", g=16)
    skb = sb.alloc([16, 128], BF16); b_skb = Buf()
    pbig = ps.alloc([16, 128], F32, align=512); b_pbig = Buf()
    pT = ps.alloc([1024], BF16, align=512); b_pT = Buf()
    pTv = pT.rearrange("p (c n) -> p c n", c=8)
    S.dma("sp", lambda e: e.dma_start(out=skn, in_=skv), writes=[b_stg[0]])
    S.op("dve", lambda e: e.tensor_copy(out=skb, in_=skn), reads=[b_stg[0]], writes=[b_skb])
    for half in range(2):
        for j in range(8):
            g = half * 8 + j
            S.op("pe", lambda e, g=g, j=j: e.transpose(out=pTv[:, j, :], in_=skb[:, g, :], identity=ident),
                 reads=[b_skb, b_ident], writes=[b_pT])
        S.op("dve", lambda e, half=half: e.tensor_copy(out=skT[:, half * 8:(half + 1) * 8, :], in_=pTv),
             reads=[b_pT], writes=[b_skT])

    x1t = [sb.alloc([1024], F32) for _ in range(2)]; b_x1t = [Buf(), Buf()]
    junk = sb.alloc([1024], F32); b_junk = Buf()
    hb = sb.alloc([1024], BF16); b_hb = Buf()
    hT = sb.alloc([8, 128], BF16); b_hT = Buf()
    qT = sb.alloc([16, 128], BF16); b_qT = Buf()
    s = sb.alloc([16, 128], F32); b_s = Buf()
    s2 = sb.alloc([16, 128], F32); b_s2 = Buf()
    sv = sb.alloc([16, 16], F32); b_sv = Buf()
    siu = sb.alloc([16, 16], U32); b_siu = Buf()
    sif = sb.alloc([16, 16], F32); b_sif = Buf()
    cand = sb.alloc([8, 256], F32); b_cand = Buf()
    cand2 = sb.alloc([8, 256], F32); b_cand2 = Buf()
    cv = sb.alloc([8, 16], F32); b_cv = Buf()
    ciu = sb.alloc([8, 16], U32); b_ciu = Buf()
    rcu = sb.alloc([2, 128], U32); b_rcu = Buf()
    rcf = sb.alloc([2, 128], F32); b_rcf = Buf()
    oh = sb.alloc([128, 16], F32); b_oh = Buf()
    i12 = sb.alloc([2, 128], F32); b_i12 = Buf()
    ef = sb.alloc([128], F32); b_ef = Buf()
    ei = [sb.alloc([128], I32) for _ in range(2)]; b_ei = [Buf(), Buf()]
    araw = sb.alloc([128], F32)
    gsm = sb.alloc([8, 16], F32); b_gsm = Buf()
    small = sb.alloc([32], F32); b_small = Buf()
    yo = sb.alloc([1024], F32); b_yo = Buf()
    uvb = [sb.alloc([2048], BF16) for _ in range(NB)]; b_uvb = [Buf() for _ in range(NB)]
    dg = [sb.alloc([128], BF16) for _ in range(4)]; b_dg = [Buf() for _ in range(4)]
    gl = sb.alloc([128], F32); b_gl = [Buf() for _ in range(8)]
    b_ar = [Buf() for _ in range(8)]
    pacc = ps.alloc([1024], F32, align=512); b_pacc = Buf()
    gcount = [0, 0]

    sv4 = sv.rearrange("p (h t) k -> p h t k", t=2)
    sif4 = sif.rearrange("p (h t) k -> p h t k", t=2)
    cand4 = cand.rearrange("p h (i j) -> p h i j", i=16)
    oh4 = oh.rearrange("p (h k) i -> p h k i", h=8)

    for ti in range(len(x1_tiles)):
        xb = x1t[ti % 2]; bxb = b_x1t[ti % 2]
        eib = ei[ti % 2]; beib = b_ei[ti % 2]
        S.dma("sp", lambda e, xb=xb, ti=ti: e.dma_start(out=xb, in_=x1_tiles[ti]), writes=[bxb])
        S.op("act", lambda e, xb=xb: e.activation(out=junk, in_=xb, func=AF.Square, accum_out=small[:, 0:1]),
             reads=[bxb], writes=[b_junk, b_small])
        S.op("act", lambda e: e.activation(out=small[:, 1:2], in_=small[:, 0:1], func=AF.Sqrt, bias=EPS, scale=1.0 / D),
             reads=[b_small], writes=[b_small])
        S.op("dve", lambda e: e.reciprocal(out=small[:, 2:3], in_=small[:, 1:2]), reads=[b_small], writes=[b_small])
        S.op("dve", lambda e, xb=xb: e.scalar_tensor_tensor(out=hb, in0=xb, scalar=small[:, 2:3], in1=gffn,
                                                            op0=ALU.mult, op1=ALU.mult),
             reads=[bxb, b_small, b_gffn], writes=[b_hb])
        for c in range(8):
            S.op("pe", lambda e, c=c: e.transpose(out=pTv[:, c, :], in_=hb[:, c * 128:(c + 1) * 128], identity=ident),
                 reads=[b_hb, b_ident], writes=[b_pT])
        S.op("act", lambda e: e.copy(out=hT, in_=pTv), reads=[b_pT], writes=[b_hT])
        for g in range(16):
            for c in range(8):
                S.op("pe", lambda e, g=g, c=c: e.matmul(pbig[:, g, :], lhsT=wq[:, c, g * 128:(g + 1) * 128],
                                                        rhs=hT[:, c, :], start=(c == 0), stop=(c == 7)),
                     reads=[b_wq, b_hT], writes=[b_pbig])
        S.op("act", lambda e: e.copy(out=qT, in_=pbig), reads=[b_pbig], writes=[b_qT])
        for g in range(16):
            S.op("pe", lambda e, g=g: e.matmul(pbig[:, g, :], lhsT=qT[:, g, :], rhs=skT[:, g, :], start=True, stop=True),
                 reads=[b_qT, b_skT], writes=[b_pbig])
        S.op("dve", lambda e: e.tensor_copy(out=s, in_=pbig), reads=[b_pbig], writes=[b_s])
        for g in range(16):
            S.op("dve", lambda e, g=g: e.max(out=sv[:, g, 0:8], in_=s[:, g, :]), reads=[b_s], writes=[b_sv])
            S.op("dve", lambda e, g=g: e.match_replace(out=s2[:, g, :], in_to_replace=sv[:, g, 0:8],
                                                       in_values=s[:, g, :], imm_value=-1e30),
                 reads=[b_s, b_sv], writes=[b_s2])
            S.op("dve", lambda e, g=g: e.max(out=sv[:, g, 8:16], in_=s2[:, g, :]), reads=[b_s2], writes=[b_sv])
            S.op("dve", lambda e, g=g: e.max_index(out=siu[:, g, 0:8], in_max=sv[:, g, 0:8], in_values=s[:, g, :]),
                 reads=[b_s, b_sv], writes=[b_siu])
            S.op("dve", lambda e, g=g: e.max_index(out=siu[:, g, 8:16], in_max=sv[:, g, 8:16], in_values=s[:, g, :]),
                 reads=[b_s, b_sv], writes=[b_siu])
        S.op("dve", lambda e: e.tensor_copy(out=sif, in_=siu), reads=[b_siu], writes=[b_sif])
        for h in range(8):
            S.op("dve", lambda e, h=h: e.tensor_tensor(
                out=cand4[:, h], in0=sv4[:, h, 0, :].unsqueeze(2).broadcast_to([128, 16, 16]),
                in1=sv4[:, h, 1, :].unsqueeze(1).broadcast_to([128, 16, 16]), op=ALU.add),
                 reads=[b_sv], writes=[b_cand])
        for h in range(8):
            S.op("dve", lambda e, h=h: e.max(out=cv[:, h, 0:8], in_=cand[:, h, :]), reads=[b_cand], writes=[b_cv])
            S.op("dve", lambda e, h=h: e.match_replace(out=cand2[:, h, :], in_to_replace=cv[:, h, 0:8],
                                                       in_values=cand[:, h, :], imm_value=-1e30),
                 reads=[b_cand, b_cv], writes=[b_cand2])
            S.op("dve", lambda e, h=h: e.max(out=cv[:, h, 8:16], in_=cand2[:, h, :]), reads=[b_cand2], writes=[b_cv])
            S.op("dve", lambda e, h=h: e.max_index(out=ciu[:, h, 0:8], in_max=cv[:, h, 0:8], in_values=cand[:, h, :]),
                 reads=[b_cand, b_cv], writes=[b_ciu])
            S.op("dve", lambda e, h=h: e.max_index(out=ciu[:, h, 8:16], in_max=cv[:, h, 8:16], in_values=cand[:, h, :]),
                 reads=[b_cand, b_cv], writes=[b_ciu])
        ciu_f = ciu.rearrange("p h k -> p (h k)")
        S.op("dve", lambda e: e.tensor_single_scalar(out=rcu[:, 0, :], in_=ciu_f, scalar=4, op=ALU.logical_shift_right),
             reads=[b_ciu], writes=[b_rcu])
        S.op("dve", lambda e: e.tensor_single_scalar(out=rcu[:, 1, :], in_=ciu_f, scalar=15, op=ALU.bitwise_and),
             reads=[b_ciu], writes=[b_rcu])
        S.op("dve", lambda e: e.tensor_copy(out=rcf, in_=rcu), reads=[b_rcu], writes=[b_rcf])
        for t in range(2):
            S.op("dve", lambda e, t=t: e.tensor_tensor(
                out=oh, in0=rcf[:, t, :].unsqueeze(2).broadcast_to([128, 128, 16]),
                in1=iota16.unsqueeze(1).broadcast_to([128, 128, 16]), op=ALU.is_equal),
                 reads=[b_rcf, b_iota], writes=[b_oh])
            for h in range(8):
                S.op("dve", lambda e, h=h, t=t: e.tensor_tensor(
                    out=oh4[:, h], in0=oh4[:, h], in1=sif4[:, h, t, :].unsqueeze(1).broadcast_to([128, 16, 16]),
                    op=ALU.mult), reads=[b_oh, b_sif], writes=[b_oh])
            S.op("dve", lambda e, t=t: e.tensor_reduce(out=i12[:, t, :], in_=oh, axis=AX.X, op=ALU.add),
                 reads=[b_oh], writes=[b_i12])
        S.op("dve", lambda e: e.scalar_tensor_tensor(out=ef, in0=i12[:, 0, :], scalar=128.0, in1=i12[:, 1, :],
                                                     op0=ALU.mult, op1=ALU.add), reads=[b_i12], writes=[b_ef])
        S.op("dve", lambda e, eib=eib: e.tensor_copy(out=eib, in_=ef), reads=[b_ef], writes=[beib])
        S.op("dve", lambda e: e.tensor_tensor(out=gsm, in0=cv, in1=cv[:, :, 0:1].broadcast_to([128, 8, 16]),
                                              op=ALU.subtract), reads=[b_cv], writes=[b_gsm])
        S.op("act", lambda e: e.activation(out=gsm, in_=gsm, func=AF.Exp), reads=[b_gsm], writes=[b_gsm])
        S.op("dve", lambda e: e.tensor_reduce(out=small[:, 8:16], in_=gsm, axis=AX.X, op=ALU.add),
             reads=[b_gsm], writes=[b_small])
        S.op("dve", lambda e: e.reciprocal(out=small[:, 16:24], in_=small[:, 8:16]), reads=[b_small], writes=[b_small])
        S.op("dve", lambda e: e.tensor_tensor(out=gsm, in0=gsm,
                                              in1=small[:, 16:24].unsqueeze(2).broadcast_to([128, 8, 16]),
                                              op=ALU.mult), reads=[b_gsm, b_small], writes=[b_gsm])
        gsf = gsm.rearrange("p h k -> p (h k)")
        for k in range(128):
            j = gcount[0] % NB; gcount[0] += 1
            r8 = k % 8; r4 = k % 4
            S.dma("pool", lambda e, k=k, j=j, eib=eib: e.indirect_dma_start(
                out=uvb[j], out_offset=None, in_=uv16,
                in_offset=bass.IndirectOffsetOnAxis(ap=eib[:, k:k + 1], axis=0)),
                  reads=[beib, b_uv16], writes=[b_uvb[j]])
            S.op("dve", lambda e, k=k, j=j: e.scalar_tensor_tensor(
                out=uvb[j][:, 0:1024], in0=uvb[j][:, 0:1024], scalar=1.0, in1=hb, op0=ALU.mult, op1=ALU.mult,
                accum_out=araw[:, k:k + 1]), reads=[b_hb], writes=[b_uvb[j], b_ar[r8]])
            S.op("act", lambda e, k=k: e.activation(out=gl[:, k:k + 1], in_=araw[:, k:k + 1], func=AF.Gelu),
                 reads=[b_ar[r8]], writes=[b_gl[r8]])
            S.op("dve", lambda e, k=k, r4=r4: e.tensor_scalar(out=dg[r4], in0=ident, scalar1=gl[:, k:k + 1],
                                                            scalar2=gsf[:, k:k + 1], op0=ALU.mult, op1=ALU.mult),
                 reads=[b_ident, b_gl[r8], b_gsm], writes=[b_dg[r4]])
            for hf in range(2):
                S.op("pe", lambda e, k=k, j=j, r4=r4, hf=hf: e.matmul(
                    pacc[:, hf * 512:(hf + 1) * 512], lhsT=dg[r4], rhs=uvb[j][:, 1024 + hf * 512:1024 + (hf + 1) * 512],
                    start=(k == 0), stop=(k == 127)), reads=[b_dg[r4], b_uvb[j]], writes=[b_pacc])
        S.op("dve", lambda e, xb=xb: e.tensor_tensor(out=yo, in0=pacc, in1=xb, op=ALU.add),
             reads=[b_pacc, bxb], writes=[b_yo])
        S.dma("sp", lambda e, ti=ti: e.dma_start(out=y_tiles[ti], in_=yo), reads=[b_yo])


def phase_b3(S, nc, sb, ps, C, x1_tiles, y_tiles, uv16, b_uv16, NB=10):
    ident = sb.alloc([128], BF16); b_ident = Buf()
    iota16 = sb.alloc([16], F32); b_iota = Buf()
    cst = sb.alloc([144], F32); b_cst = Buf()
    wq = sb.alloc([8, 2048], BF16); b_wq = Buf()
    skT = sb.alloc([16, 128], BF16); b_skT = Buf()
    gffn = sb.alloc([1024], F32); b_gffn = Buf()
    S.dma("sp", lambda e: e.dma_start(out=cst, in_=C["consts"]), writes=[b_cst])
    S.op("dve", lambda e: e.tensor_copy(out=ident, in_=cst[:, 0:128]), reads=[b_cst], writes=[b_ident])
    S.op("dve", lambda e: e.tensor_copy(out=iota16, in_=cst[:, 128:144]), reads=[b_cst], writes=[b_iota])
    S.dma("sp", lambda e: e.dma_start(out=gffn, in_=C["g_ffn"].partition_broadcast(128)), writes=[b_gffn])
    stg = [sb.alloc([2048], F32) for _ in range(2)]
    b_stg = [Buf(), Buf()]
    wqv = C["w_query"].rearrange("(c p) n -> c p n", p=128)
    for c in range(8):
        S.dma("sp", lambda e, c=c: e.dma_start(out=stg[c % 2], in_=wqv[c]), writes=[b_stg[c % 2]])
        S.op("act" if c % 2 else "dve",
             (lambda e, c=c: e.copy(out=wq[:, c, :], in_=stg[c % 2])) if c % 2 else
             (lambda e, c=c: e.tensor_copy(out=wq[:, c, :], in_=stg[c % 2])),
             reads=[b_stg[c % 2]], writes=[b_wq])
    skv = C["sub_keys"].rearrange("g k d -> k g d")
    skn = stg[0].rearrange("p (g d) -> p g d", g=16)
    skb = sb.alloc([16, 128], BF16); b_skb = Buf()
    pbig = ps.alloc([16, 128], F32, align=512); b_pbig = Buf()
    pT = ps.alloc([1024], BF16, align=512); b_pT = Buf()
    pTv = pT.rearrange("p (c n) -> p c n", c=8)
    S.dma("sp", lambda e: e.dma_start(out=skn, in_=skv), writes=[b_stg[0]])
    S.op("dve", lambda e: e.tensor_copy(out=skb, in_=skn), reads=[b_stg[0]], writes=[b_skb])
    for half in range(2):
        for j in range(8):
            g = half * 8 + j
            S.op("pe", lambda e, g=g, j=j: e.transpose(out=pTv[:, j, :], in_=skb[:, g, :], identity=ident),
                 reads=[b_skb, b_ident], writes=[b_pT])
        S.op("dve", lambda e, half=half: e.tensor_copy(out=skT[:, half * 8:(half + 1) * 8, :], in_=pTv),
             reads=[b_pT], writes=[b_skT])

    x1t = [sb.alloc([1024], F32) for _ in range(2)]; b_x1t = [Buf(), Buf()]
    junk = sb.alloc([1024], F32); b_junk = Buf()
    hb2 = [sb.alloc([1024], BF16) for _ in range(2)]; b_hb2 = [Buf(), Buf()]
    hT = sb.alloc([8, 128], BF16); b_hT = Buf()
    qT = sb.alloc([16, 128], BF16); b_qT = Buf()
    s = sb.alloc([16, 128], F32); b_s = Buf()
    s2 = sb.alloc([16, 128], F32); b_s2 = Buf()
    sv = sb.alloc([16, 16], F32); b_sv = Buf()
    siu = sb.alloc([16, 16], U32); b_siu = Buf()
    sif = sb.alloc([16, 16], F32); b_sif = Buf()
    cand = sb.alloc([8, 256], F32); b_cand = Buf()
    cand2 = sb.alloc([8, 256], F32); b_cand2 = Buf()
    cv = sb.alloc([8, 16], F32); b_cv = Buf()
    ciu = sb.alloc([8, 16], U32); b_ciu = Buf()
    rcu = sb.alloc([2, 128], U32); b_rcu = Buf()
    rcf = sb.alloc([2, 128], F32); b_rcf = Buf()
    oh = sb.alloc([128, 16], F32); b_oh = Buf()
    i12 = sb.alloc([2, 128], F32); b_i12 = Buf()
    ef = sb.alloc([128], F32); b_ef = Buf()
    ei = [sb.alloc([128], I32) for _ in range(2)]; b_ei = [Buf(), Buf()]
    araw = sb.alloc([128], F32)
    gsm2 = [sb.alloc([8, 16], F32) for _ in range(2)]; b_gsm2 = [Buf(), Buf()]
    small = sb.alloc([32], F32); b_small = Buf()
    yo = sb.alloc([1024], F32); b_yo = Buf()
    uvb = [sb.alloc([2048], BF16) for _ in range(NB)]; b_uvb = [Buf() for _ in range(NB)]
    dg = [sb.alloc([128], BF16) for _ in range(4)]; b_dg = [Buf() for _ in range(4)]
    gl = sb.alloc([128], F32); b_gl = [Buf() for _ in range(8)]
    wk = sb.alloc([128], F32); b_wk = [Buf() for _ in range(8)]
    b_ar = [Buf() for _ in range(8)]
    pacc = ps.alloc([1024], F32, align=512); b_pacc = Buf()
    gcount = [0, 0]

    sv4 = sv.rearrange("p (h t) k -> p h t k", t=2)
    sif4 = sif.rearrange("p (h t) k -> p h t k", t=2)
    cand4 = cand.rearrange("p h (i j) -> p h i j", i=16)
    oh4 = oh.rearrange("p (h k) i -> p h k i", h=8)

    def route_gen(ti):
        xb = x1t[ti % 2]; bxb = b_x1t[ti % 2]
        eib = ei[ti % 2]; beib = b_ei[ti % 2]
        hb = hb2[ti % 2]; b_hb = b_hb2[ti % 2]
        gsm = gsm2[ti % 2]; b_gsm = b_gsm2[ti % 2]
        S.dma("sp", lambda e, xb=xb, ti=ti: e.dma_start(out=xb, in_=x1_tiles[ti]), writes=[bxb])
        yield
        S.op("act", lambda e, xb=xb: e.activation(out=junk, in_=xb, func=AF.Square, accum_out=small[:, 0:1]),
             reads=[bxb], writes=[b_junk, b_small])
        yield
        S.op("act", lambda e: e.activation(out=small[:, 1:2], in_=small[:, 0:1], func=AF.Sqrt, bias=EPS, scale=1.0 / D),
             reads=[b_small], writes=[b_small])
        yield
        S.op("dve", lambda e: e.reciprocal(out=small[:, 2:3], in_=small[:, 1:2]), reads=[b_small], writes=[b_small])
        yield
        S.op("dve", lambda e, xb=xb: e.scalar_tensor_tensor(out=hb, in0=xb, scalar=small[:, 2:3], in1=gffn,
                                                            op0=ALU.mult, op1=ALU.mult),
             reads=[bxb, b_small, b_gffn], writes=[b_hb])
        yield
        for c in range(8):
            S.op("pe", lambda e, c=c: e.transpose(out=pTv[:, c, :], in_=hb[:, c * 128:(c + 1) * 128], identity=ident),
                 reads=[b_hb, b_ident], writes=[b_pT])
            yield
        S.op("act", lambda e: e.copy(out=hT, in_=pTv), reads=[b_pT], writes=[b_hT])
        yield
        for g in range(16):
            for c in range(8):
                S.op("pe", lambda e, g=g, c=c: e.matmul(pbig[:, g, :], lhsT=wq[:, c, g * 128:(g + 1) * 128],
                                                        rhs=hT[:, c, :], start=(c == 0), stop=(c == 7)),
                     reads=[b_wq, b_hT], writes=[b_pbig])
                yield
        S.op("act", lambda e: e.copy(out=qT, in_=pbig), reads=[b_pbig], writes=[b_qT])
        yield
        for g in range(16):
            S.op("pe", lambda e, g=g: e.matmul(pbig[:, g, :], lhsT=qT[:, g, :], rhs=skT[:, g, :], start=True, stop=True),
                 reads=[b_qT, b_skT], writes=[b_pbig])
            yield
        S.op("dve", lambda e: e.tensor_copy(out=s, in_=pbig), reads=[b_pbig], writes=[b_s])
        yield
        for g in range(16):
            S.op("dve", lambda e, g=g: e.max(out=sv[:, g, 0:8], in_=s[:, g, :]), reads=[b_s], writes=[b_sv])
            yield
            S.op("dve", lambda e, g=g: e.match_replace(out=s2[:, g, :], in_to_replace=sv[:, g, 0:8],
                                                       in_values=s[:, g, :], imm_value=-1e30),
                 reads=[b_s, b_sv], writes=[b_s2])
            yield
            S.op("dve", lambda e, g=g: e.max(out=sv[:, g, 8:16], in_=s2[:, g, :]), reads=[b_s2], writes=[b_sv])
            yield
            S.op("dve", lambda e, g=g: e.max_index(out=siu[:, g, 0:8], in_max=sv[:, g, 0:8], in_values=s[:, g, :]),
                 reads=[b_s, b_sv], writes=[b_siu])
            yield
            S.op("dve", lambda e, g=g: e.max_index(out=siu[:, g, 8:16], in_max=sv[:, g, 8:16], in_values=s[:, g, :]),
                 reads=[b_s, b_sv], writes=[b_siu])
            yield
        S.op("dve", lambda e: e.tensor_copy(out=sif, in_=siu), reads=[b_siu], writes=[b_sif])
        yield
        for h in range(8):
            S.op("dve", lambda e, h=h: e.tensor_tensor(
                out=cand4[:, h], in0=sv4[:, h, 0, :].unsqueeze(2).broadcast_to([128, 16, 16]),
                in1=sv4[:, h, 1, :].unsqueeze(1).broadcast_to([128, 16, 16]), op=ALU.add),
                 reads=[b_sv], writes=[b_cand])
            yield
        for h in range(8):
            S.op("dve", lambda e, h=h: e.max(out=cv[:, h, 0:8], in_=cand[:, h, :]), reads=[b_cand], writes=[b_cv])
            yield
            S.op("dve", lambda e, h=h: e.match_replace(out=cand2[:, h, :], in_to_replace=cv[:, h, 0:8],
                                                       in_values=cand[:, h, :], imm_value=-1e30),
                 reads=[b_cand, b_cv], writes=[b_cand2])
            yield
            S.op("dve", lambda e, h=h: e.max(out=cv[:, h, 8:16], in_=cand2[:, h, :]), reads=[b_cand2], writes=[b_cv])
            yield
            S.op("dve", lambda e, h=h: e.max_index(out=ciu[:, h, 0:8], in_max=cv[:, h, 0:8], in_values=cand[:, h, :]),
                 reads=[b_cand, b_cv], writes=[b_ciu])
            yield
            S.op("dve", lambda e, h=h: e.max_index(out=ciu[:, h, 8:16], in_max=cv[:, h, 8:16], in_values=cand[:, h, :]),
                 reads=[b_cand, b_cv], writes=[b_ciu])
            yield
        ciu_f = ciu.rearrange("p h k -> p (h k)")
        S.op("dve", lambda e: e.tensor_single_scalar(out=rcu[:, 0, :], in_=ciu_f, scalar=4, op=ALU.logical_shift_right),
             reads=[b_ciu], writes=[b_rcu])
        yield
        S.op("dve", lambda e: e.tensor_single_scalar(out=rcu[:, 1, :], in_=ciu_f, scalar=15, op=ALU.bitwise_and),
             reads=[b_ciu], writes=[b_rcu])
        yield
        S.op("dve", lambda e: e.tensor_copy(out=rcf, in_=rcu), reads=[b_rcu], writes=[b_rcf])
        yield
        for t in range(2):
            S.op("dve", lambda e, t=t: e.tensor_tensor(
                out=oh, in0=rcf[:, t, :].unsqueeze(2).broadcast_to([128, 128, 16]),
                in1=iota16.unsqueeze(1).broadcast_to([128, 128, 16]), op=ALU.is_equal),
                 reads=[b_rcf, b_iota], writes=[b_oh])
            yield
            for h in range(8):
                S.op("dve", lambda e, h=h, t=t: e.tensor_tensor(
                    out=oh4[:, h], in0=oh4[:, h], in1=sif4[:, h, t, :].unsqueeze(1).broadcast_to([128, 16, 16]),
                    op=ALU.mult), reads=[b_oh, b_sif], writes=[b_oh])
                yield
            S.op("dve", lambda e, t=t: e.tensor_reduce(out=i12[:, t, :], in_=oh, axis=AX.X, op=ALU.add),
                 reads=[b_oh], writes=[b_i12])
            yield
        S.op("dve", lambda e: e.scalar_tensor_tensor(out=ef, in0=i12[:, 0, :], scalar=128.0, in1=i12[:, 1, :],
                                                     op0=ALU.mult, op1=ALU.add), reads=[b_i12], writes=[b_ef])
        yield
        S.op("dve", lambda e, eib=eib: e.tensor_copy(out=eib, in_=ef), reads=[b_ef], writes=[beib])
        yield
        S.op("dve", lambda e: e.tensor_tensor(out=gsm, in0=cv, in1=cv[:, :, 0:1].broadcast_to([128, 8, 16]),
                                              op=ALU.subtract), reads=[b_cv], writes=[b_gsm])
        yield
        S.op("act", lambda e: e.activation(out=gsm, in_=gsm, func=AF.Exp), reads=[b_gsm], writes=[b_gsm])
        yield
        S.op("dve", lambda e: e.tensor_reduce(out=small[:, 8:16], in_=gsm, axis=AX.X, op=ALU.add),
             reads=[b_gsm], writes=[b_small])
        yield
        S.op("dve", lambda e: e.reciprocal(out=small[:, 16:24], in_=small[:, 8:16]), reads=[b_small], writes=[b_small])
        yield
        S.op("dve", lambda e: e.tensor_tensor(out=gsm, in0=gsm,
                                              in1=small[:, 16:24].unsqueeze(2).broadcast_to([128, 8, 16]),
                                              op=ALU.mult), reads=[b_gsm, b_small], writes=[b_gsm])
        yield

    def slots(ti, filler, rate):
        xb = x1t[ti % 2]; bxb = b_x1t[ti % 2]
        eib = ei[ti % 2]; beib = b_ei[ti % 2]
        hb = hb2[ti % 2]; b_hb = b_hb2[ti % 2]
        gsm = gsm2[ti % 2]; b_gsm = b_gsm2[ti % 2]
        gsf = gsm.rearrange("p h k -> p (h k)")
        LAG = 3
        jj = {}
        for kx in range(128 + LAG):
            if kx < 128:
                k = kx
                j = gcount[0] % NB; gcount[0] += 1
                jj[k] = j
                r8 = k % 8
                S.dma("pool", lambda e, k=k, j=j, eib=eib: e.indirect_dma_start(
                    out=uvb[j], out_offset=None, in_=uv16,
                    in_offset=bass.IndirectOffsetOnAxis(ap=eib[:, k:k + 1], axis=0)),
                      reads=[beib, b_uv16], writes=[b_uvb[j]])
                S.op("dve", lambda e, k=k, j=j: e.scalar_tensor_tensor(
                    out=uvb[j][:, 0:1024], in0=uvb[j][:, 0:1024], scalar=1.0, in1=hb, op0=ALU.mult, op1=ALU.mult,
                    accum_out=araw[:, k:k + 1]), reads=[b_hb], writes=[b_uvb[j], b_ar[r8]])
                S.op("act", lambda e, k=k: e.activation(out=gl[:, k:k + 1], in_=araw[:, k:k + 1], func=AF.Gelu),
                     reads=[b_ar[r8]], writes=[b_gl[r8]])
                if filler is not None:
                    next(filler, None)
            if kx >= LAG:
                k = kx - LAG
                j = jj[k]
                r8 = k % 8; r4 = k % 4
                S.op("act", lambda e, k=k: e.activation(out=wk[:, k:k + 1], in_=gl[:, k:k + 1], func=AF.Copy,
                                                        scale=gsf[:, k:k + 1]),
                     reads=[b_gl[r8], b_gsm], writes=[b_wk[r8]])
                S.op("act", lambda e, k=k, r4=r4: e.activation(out=dg[r4], in_=ident, func=AF.Copy, scale=wk[:, k:k + 1]),
                     reads=[b_ident, b_wk[r8]], writes=[b_dg[r4]])
                for hf in range(2):
                    S.op("pe", lambda e, k=k, j=j, r4=r4, hf=hf: e.matmul(
                        pacc[:, hf * 512:(hf + 1) * 512], lhsT=dg[r4], rhs=uvb[j][:, 1024 + hf * 512:1024 + (hf + 1) * 512],
                        start=(k == 0), stop=(k == 127)), reads=[b_dg[r4], b_uvb[j]], writes=[b_pacc])
            if filler is not None:
                for _ in range(rate - 1):
                    next(filler, None)
        S.op("dve", lambda e, xb=xb: e.tensor_tensor(out=yo, in0=pacc, in1=xb, op=ALU.add),
             reads=[b_pacc, bxb], writes=[b_yo])
        S.dma("sp", lambda e, ti=ti: e.dma_start(out=y_tiles[ti], in_=yo), reads=[b_yo])


    NT = len(x1_tiles)
    g = route_gen(0)
    for _ in g:
        pass
    for ti in range(NT):
        nxt = route_gen(ti + 1) if ti + 1 < NT else None
        slots(ti, nxt, 3)
        if nxt is not None:
            for _ in nxt:
                pass


def convert_tables(S, C, uv16, b_uv16, nsplit=8):
    rows = NEXP // nsplit
    for t, name in enumerate(("expert_u", "expert_v")):
        for i in range(nsplit):
            S.dma("pool", lambda e, t=t, i=i, name=name: e.dma_start(
                out=uv16[i * rows:(i + 1) * rows, t * 1024:(t + 1) * 1024], in_=C[name][i * rows:(i + 1) * rows, :]),
                  writes=[b_uv16])


def make_consts():
    c = np.zeros((128, 144), np.float32)
    c[:, :128] = np.eye(128, dtype=np.float32)
    c[:, 128:144] = np.arange(16, dtype=np.float32)[None, :]
    return c


PP_GMIX, PP_GOUT, PP_CB, PP_LNG, PP_LNB, PP_GQ, PP_GK, PP_CW, PP_N = 0, 8, 16, 20, 24, 28, 29, 32, 160
RING = 8
URING = 6
QRING = 5


def key_tiles(s, NQ):
    kts = list(range(s - 2, s + 3))
    if s == 2:
        kts.append(5)
    if s == NQ + 1:
        kts.insert(0, NQ - 2)
    return kts


def phase_a(S, nc, sb, ps, C, NQ, xs_tiles, x1_tiles):
    NS = NQ + 4
    nslab = len(xs_tiles)
    cst = sb.alloc([144], F32); b_cst = Buf()
    ident = sb.alloc([128], BF16); b_ident = Buf()
    onesf = sb.alloc([128], F32); b_onesf = Buf()
    blk = sb.alloc([128], BF16); b_blk = Buf()
    pp = sb.alloc([PP_N], F32); b_pp = Buf()
    gq8 = sb.alloc([1], F32); b_gq8 = Buf()
    win = sb.alloc([8, 2560], BF16); b_win = Buf()
    wout = sb.alloc([8, 1024], BF16); b_wout = Buf()
    T = sb.alloc([8, 16, 64], F32); b_T = Buf()
    ntile_idx = {}
    for s in range(2, NQ + 2):
        for kt in key_tiles(s, NQ):
            ntile_idx[(s, kt)] = len(ntile_idx)
    NTI = len(ntile_idx)
    rmask = sb.alloc([nslab, NTI, 2], F32); b_rmask = Buf()
    S.dma("sp", lambda e: e.dma_start(out=cst, in_=C["consts"]), writes=[b_cst])
    S.dma("sp", lambda e: e.dma_start(out=pp, in_=C["pp"]), writes=[b_pp])
    S.dma("sp", lambda e: e.dma_start(out=T.rearrange("p h j c -> p (h j c)"), in_=C["ttab"]), writes=[b_T])
    S.dma("sp", lambda e: e.dma_start(out=rmask.rearrange("p a b c -> p (a b c)"), in_=C["rmask"]), writes=[b_rmask])
    S.op("dve", lambda e: e.tensor_copy(out=ident, in_=cst[:, 0:128]), reads=[b_cst], writes=[b_ident])
    S.op("dve", lambda e: e.memset(onesf, 1.0), writes=[b_onesf])
    S.op("dve", lambda e: e.memset(blk, 0.0), writes=[b_blk])
    S.op("dve", lambda e: e.memset(blk[0:64, 0:64], 1.0), writes=[b_blk])
    S.op("dve", lambda e: e.memset(blk[64:128, 64:128], 1.0), writes=[b_blk])
    S.op("dve", lambda e: e.tensor_scalar(out=gq8, in0=pp[:, PP_GQ:PP_GQ + 1], scalar1=0.125, scalar2=None, op0=ALU.mult),
         reads=[b_pp], writes=[b_gq8])
    stg = [sb.alloc([1280], F32) for _ in range(2)]; b_stg = [Buf(), Buf()]
    winv = C["w_in"].rearrange("(c p) n -> c p n", p=128)
    woutv = C["w_out"].rearrange("(c p) n -> c p n", p=128)
    n = 0
    for c in range(8):
        for hf in range(2):
            j = n % 2; n += 1
            S.dma("sp", lambda e, c=c, hf=hf, j=j: e.dma_start(out=stg[j], in_=winv[c][:, hf * 1280:(hf + 1) * 1280]),
                  writes=[b_stg[j]])
            S.op("dve", lambda e, c=c, hf=hf, j=j: e.tensor_scalar(
                out=win[:, c, hf * 1280:(hf + 1) * 1280], in0=stg[j], scalar1=pp[:, PP_GMIX + c:PP_GMIX + c + 1],
                scalar2=None, op0=ALU.mult), reads=[b_stg[j], b_pp], writes=[b_win])
    for c in range(8):
        j = n % 2; n += 1
        S.dma("sp", lambda e, c=c, j=j: e.dma_start(out=stg[j][:, 0:1024], in_=woutv[c]), writes=[b_stg[j]])
        S.op("dve", lambda e, c=c, j=j: e.tensor_scalar(
            out=wout[:, c, :], in0=stg[j][:, 0:1024], scalar1=pp[:, PP_GOUT + c:PP_GOUT + c + 1],
            scalar2=None, op0=ALU.mult), reads=[b_stg[j], b_pp], writes=[b_wout])
    cw = pp[:, PP_CW:PP_CW + 124].rearrange("p (c k) -> p c k", c=4)

    kT = [sb.alloc([4, 128], BF16) for _ in range(RING)]; b_kT = [Buf() for _ in range(RING)]
    va = [sb.alloc([8, 65], BF16) for _ in range(RING)]; b_va = [Buf() for _ in range(RING)]
    uT = [sb.alloc([4, 160], F32) for _ in range(URING)]; b_uT = [Buf() for _ in range(URING)]
    qA = [sb.alloc([4, 128], BF16) for _ in range(QRING)]; b_qA = [Buf() for _ in range(QRING)]
    qB = [sb.alloc([4, 128], BF16) for _ in range(QRING)]; b_qB = [Buf() for _ in range(QRING)]
    for r in range(RING):
        S.op("pool", lambda e, r=r: e.memset(va[r], 1.0), writes=[b_va[r]])
    for r in range(QRING):
        S.op("pool", lambda e, r=r: e.memset(qA[r], 0.0), writes=[b_qA[r]])
        S.op("pool", lambda e, r=r: e.memset(qB[r], 0.0), writes=[b_qB[r]])
    for r in range(URING):
        S.op("pool", lambda e, r=r: e.memset(uT[r], 0.0), writes=[b_uT[r]])
    xt = [sb.alloc([1024], F32) for _ in range(2)]; b_xt = [Buf(), Buf()]
    xr = [sb.alloc([1024], F32) for _ in range(2)]; b_xr = [Buf(), Buf()]
    junk = sb.alloc([1024], BF16); b_junk = Buf()
    small = sb.alloc([16], F32); b_small = Buf()
    xn = sb.alloc([1024], BF16); b_xn = Buf()
    hT = sb.alloc([8, 128], BF16); b_hT = Buf()
    qkr = sb.alloc([8, 128], F32); b_qkr = Buf()
    sq = sb.alloc([8, 128], BF16); b_sq = Buf()
    qks = sb.alloc([8, 128], F32); b_qks = Buf()
    sg = sb.alloc([4, 128], F32); b_sg = Buf()
    NTMP = 4
    tmp = [sb.alloc([128], F32) for _ in range(NTMP)]; b_tmp = [Buf() for _ in range(NTMP)]
    PT = [sb.alloc([128], BF16) for _ in range(NTMP)]; b_PT = [Buf() for _ in range(NTMP)]
    ya = sb.alloc([8, 64], F32); b_ya = Buf()
    yan = sb.alloc([512], BF16); b_yan = Buf()
    yaT = sb.alloc([4, 128], BF16); b_yaT = Buf()
    yconv = sb.alloc([4, 128], F32); b_yc = [Buf() for _ in range(4)]
    csq = sb.alloc([4, 128], F32); b_csq = Buf()
    ctmp = sb.alloc([128], F32); b_ctmp = Buf()
    st = sb.alloc([4, 128], F32); b_st = Buf()
    z = sb.alloc([4, 128], F32); b_z = Buf()
    ycn2 = [sb.alloc([4, 128], BF16) for _ in range(2)]; b_ycn2 = [Buf(), Buf()]
    bank0 = ps.alloc([512], F32, align=512); b_b0 = Buf()
    pT = bank0.bitcast(BF16); b_pT = b_b0
    pTv = pT.rearrange("p (c n) -> p c n", c=8)
    pV = bank0; b_pV = b_b0
    pX = ps.alloc([8, 128], F32, align=512); b_pX = Buf()
    pS = ps.alloc([8, 128], F32, align=512); b_pS = Buf()
    pY = pS.rearrange("p a b -> p (a b)")
    pMa = ps.alloc([4, 128], F32, align=512)
    pMb = ps.alloc([4, 128], F32, align=512)
    b_bk5 = Buf(); b_bk6 = Buf()
    pMs = [pMa[:, 0, :], pMb[:, 0, :], pMa[:, 2, :]]; b_pMs = [b_bk5, b_bk6, b_bk5]
    pO = ps.alloc([4, 128], F32, align=512); b_pO = Buf()
    pM = pO; b_pM = [b_pO, b_pO, b_pO, b_pO]

    def stage_p_gen(sl, s):
        x = xt[s % 2]; bx = b_xt[s % 2]
        r = s % RING
        S.dma("sp", lambda e: e.dma_start(out=x, in_=xs_tiles[sl][s]), writes=[bx])
        yield
        S.op("act", lambda e: e.activation(out=junk, in_=x, func=AF.Square, accum_out=small[:, 0:1]),
             reads=[bx], writes=[b_junk, b_small])
        yield
        S.op("act", lambda e: e.activation(out=small[:, 1:2], in_=small[:, 0:1], func=AF.Sqrt, bias=EPS, scale=1.0 / D),
             reads=[b_small], writes=[b_small])
        yield
        S.op("dve", lambda e: e.reciprocal(out=small[:, 2:3], in_=small[:, 1:2]), reads=[b_small], writes=[b_small])
        yield
        S.op("dve", lambda e: e.tensor_scalar(out=xn, in0=x, scalar1=small[:, 2:3], scalar2=None, op0=ALU.mult),
             reads=[bx, b_small], writes=[b_xn])
        yield
        for c in range(8):
            S.op("pe", lambda e, c=c: e.transpose(out=pTv[:, c, :], in_=xn[:, c * 128:(c + 1) * 128], identity=ident),
                 reads=[b_xn, b_ident], writes=[b_pT])
            yield
        S.op("act", lambda e: e.copy(out=hT, in_=pTv), reads=[b_pT], writes=[b_hT])
        yield
        for j in range(8):
            for c in range(8):
                S.op("pe", lambda e, j=j, c=c: e.matmul(pX[:, j, :], lhsT=win[:, c, j * 128:(j + 1) * 128], rhs=hT[:, c, :],
                                                        start=(c == 0), stop=(c == 7)),
                     reads=[b_win, b_hT], writes=[b_pX])
                yield
        ur = s % URING
        S.op("act", lambda e: e.activation(out=sg, in_=pX[:, 4:8, :], func=AF.Sigmoid), reads=[b_pX], writes=[b_sg])
        yield
        S.op("dve", lambda e: e.tensor_tensor(out=uT[ur][:, :, 16:144], in0=pX[:, 0:4, :], in1=sg, op=ALU.mult),
             reads=[b_pX, b_sg], writes=[b_uT[ur]])
        yield
        if s >= 1:
            up = (s - 1) % URING
            S.op("pool", lambda e: e.tensor_copy(out=uT[up][:, :, 144:159], in_=uT[ur][:, :, 16:31]),
                 reads=[b_uT[ur]], writes=[b_uT[up]])
            yield
        if s + 1 < NS:
            un = (s + 1) % URING
            S.op("pool", lambda e: e.tensor_copy(out=uT[un][:, :, 1:16], in_=uT[ur][:, :, 129:144]),
                 reads=[b_uT[ur]], writes=[b_uT[un]])
            yield
        for j in range(8):
            for c in range(8):
                S.op("pe", lambda e, j=j, c=c: e.matmul(pX[:, j, :], lhsT=win[:, c, 1024 + j * 128:1024 + (j + 1) * 128],
                                                        rhs=hT[:, c, :], start=(c == 0), stop=(c == 7)),
                     reads=[b_win, b_hT], writes=[b_pX])
                yield
        S.op("act", lambda e: e.copy(out=qkr, in_=pX), reads=[b_pX], writes=[b_qkr])
        yield
        S.op("act", lambda e: e.activation(out=sq, in_=qkr, func=AF.Square), reads=[b_qkr], writes=[b_sq])
        yield
        for j in range(8):
            S.op("pe", lambda e, j=j: e.matmul(pS[:, j, :], lhsT=blk, rhs=sq[:, j, :], start=True, stop=True),
                 reads=[b_blk, b_sq], writes=[b_pS])
            yield
        S.op("act", lambda e: e.activation(out=qks, in_=pS, func=AF.Sqrt, bias=EPS, scale=1.0 / 64), reads=[b_pS], writes=[b_qks])
        yield
        S.op("dve", lambda e: e.reciprocal(out=qks, in_=qks), reads=[b_qks], writes=[b_qks])
        yield
        S.op("dve", lambda e: e.tensor_tensor(out=qkr, in0=qkr, in1=qks, op=ALU.mult), reads=[b_qkr, b_qks], writes=[b_qkr])
        yield
        S.op("dve", lambda e: e.tensor_scalar(out=kT[r], in0=qkr[:, 4:8, :], scalar1=pp[:, PP_GK:PP_GK + 1], scalar2=None,
                                              op0=ALU.mult), reads=[b_qkr, b_pp], writes=[b_kT[r]])
        yield
        if 2 <= s < NQ + 2:
            qr = s % QRING
            S.op("dve", lambda e: e.tensor_scalar(out=qA[qr][0:64], in0=qkr[0:64, 0:4, :], scalar1=gq8[0:64], scalar2=None,
                                                  op0=ALU.mult), reads=[b_qkr, b_gq8], writes=[b_qA[qr]])
            yield
            S.op("dve", lambda e: e.tensor_scalar(out=qB[qr][64:128], in0=qkr[64:128, 0:4, :], scalar1=gq8[64:128],
                                                  scalar2=None, op0=ALU.mult), reads=[b_qkr, b_gq8], writes=[b_qB[qr]])
            yield
        for c in range(8):
            S.op("pe", lambda e, c=c: e.matmul(pV, lhsT=hT[:, c, :], rhs=win[:, c, 2048:2560], start=(c == 0), stop=(c == 7)),
                 reads=[b_win, b_hT], writes=[b_pV])
            yield
        S.op("act", lambda e: e.copy(out=va[r][:, :, 0:64], in_=pV.rearrange("p (h d) -> p h d", h=8)),
             reads=[b_pV], writes=[b_va[r]])
        yield

    cnt = [0]

    def m_conv_gen(sl, s):
        ur = s % URING
        ycn_s = ycn2[s % 2]; b_ycn_s = b_ycn2[s % 2]
        for c in range(4):
            S.op("dve", lambda e, c=c: e.tensor_scalar(out=yconv[:, c, :], in0=uT[ur][:, c, 1:129], scalar1=cw[:, c, 0:1],
                                                       scalar2=pp[:, PP_CB + c:PP_CB + c + 1], op0=ALU.mult, op1=ALU.add),
                 reads=[b_uT[ur], b_pp], writes=[b_yc[c]])
            yield
        for k in range(1, 31):
            for c in range(4):
                S.op("dve", lambda e, c=c, k=k: e.scalar_tensor_tensor(
                    out=yconv[:, c, :], in0=uT[ur][:, c, k + 1:k + 129], scalar=cw[:, c, k:k + 1], in1=yconv[:, c, :],
                    op0=ALU.mult, op1=ALU.add), reads=[b_uT[ur], b_pp], writes=[b_yc[c]])
                yield
        S.op("act", lambda e: e.activation(out=csq, in_=yconv, func=AF.Square), reads=b_yc, writes=[b_csq])
        yield
        for c in range(4):
            S.op("pe", lambda e, c=c: e.matmul(pS[:, 0, :], lhsT=onesf, rhs=yconv[:, c, :], start=(c == 0), stop=(c == 3)),
                 reads=[b_onesf] + b_yc, writes=[b_pS])
            yield
        for c in range(4):
            S.op("pe", lambda e, c=c: e.matmul(pS[:, 1, :], lhsT=onesf, rhs=csq[:, c, :], start=(c == 0), stop=(c == 3)),
                 reads=[b_onesf, b_csq], writes=[b_pS])
            yield
        S.op("dve", lambda e: e.tensor_scalar(out=st[:, 0, :], in0=pS[:, 0, :], scalar1=1.0 / 512, scalar2=None, op0=ALU.mult),
             reads=[b_pS], writes=[b_st])
        yield
        S.op("dve", lambda e: e.tensor_tensor(out=st[:, 1, :], in0=st[:, 0, :], in1=st[:, 0, :], op=ALU.mult),
             reads=[b_st], writes=[b_st])
        yield
        S.op("dve", lambda e: e.scalar_tensor_tensor(out=st[:, 2, :], in0=pS[:, 1, :], scalar=1.0 / 512, in1=st[:, 1, :],
                                                     op0=ALU.mult, op1=ALU.subtract), reads=[b_pS, b_st], writes=[b_st])
        yield
        S.op("act", lambda e: e.activation(out=st[:, 3, :], in_=st[:, 2, :], func=AF.Sqrt, bias=EPS, scale=1.0),
             reads=[b_st], writes=[b_st])
        yield
        S.op("dve", lambda e: e.reciprocal(out=st[:, 3, :], in_=st[:, 3, :]), reads=[b_st], writes=[b_st])
        yield
        S.op("dve", lambda e: e.tensor_tensor(out=z, in0=yconv, in1=st[:, 0, :].unsqueeze(1).broadcast_to([128, 4, 128]),
                                              op=ALU.subtract), reads=b_yc + [b_st], writes=[b_z])
        yield
        S.op("dve", lambda e: e.tensor_tensor(out=z, in0=z, in1=st[:, 3, :].unsqueeze(1).broadcast_to([128, 4, 128]),
                                              op=ALU.mult), reads=[b_z, b_st], writes=[b_z])
        yield
        for c in range(4):
            S.op("dve", lambda e, c=c: e.tensor_scalar(out=z[:, c, :], in0=z[:, c, :], scalar1=pp[:, PP_LNG + c:PP_LNG + c + 1],
                                                       scalar2=pp[:, PP_LNB + c:PP_LNB + c + 1], op0=ALU.mult, op1=ALU.add),
                 reads=[b_z, b_pp], writes=[b_z])
            yield
        S.op("act", lambda e: e.activation(out=z, in_=z, func=AF.Silu), reads=[b_z], writes=[b_z])
        yield
        S.op("act", lambda e: e.activation(out=csq, in_=z, func=AF.Square), reads=[b_z], writes=[b_csq])
        yield
        for c in range(4):
            S.op("pe", lambda e, c=c: e.matmul(pS[:, 0, :], lhsT=onesf, rhs=csq[:, c, :], start=(c == 0), stop=(c == 3)),
                 reads=[b_onesf, b_csq], writes=[b_pS])
            yield
        S.op("act", lambda e: e.activation(out=st[:, 0, :], in_=pS[:, 0, :], func=AF.Sqrt, bias=EPS, scale=1.0 / 512),
             reads=[b_pS], writes=[b_st])
        yield
        S.op("dve", lambda e: e.reciprocal(out=st[:, 0, :], in_=st[:, 0, :]), reads=[b_st], writes=[b_st])
        yield
        S.op("dve", lambda e: e.tensor_tensor(out=ycn_s, in0=z, in1=st[:, 0, :].unsqueeze(1).broadcast_to([128, 4, 128]),
                                              op=ALU.mult), reads=[b_z, b_st], writes=[b_ycn_s])
        yield

    def m_attn(sl, s, filler, rate):
        i = s - 2
        qr = s % QRING
        x = xr[i % 2]; bx = b_xr[i % 2]
        S.dma("sp", lambda e: e.dma_start(out=x, in_=xs_tiles[sl][s]), writes=[bx])
        kts = key_tiles(s, NQ)
        units = []
        for grp in range(2):
            for hh in range(4):
                for n_, kt in enumerate(kts):
                    units.append((grp, hh, n_, kt))
        LOOK = 2
        NSLOT = 3
        info = {}

        def emit_st(u):
            grp, hh, n_, kt = units[u]
            h = grp * 4 + hh
            c = h // 2
            qm = (qA if h % 2 == 0 else qB)[qr]
            bqm = (b_qA if h % 2 == 0 else b_qB)[qr]
            kr = kt % RING
            m = cnt[0] % NSLOT; cnt[0] += 1
            info[u] = m
            S.op("pe", lambda e: e.matmul(pMs[m], lhsT=kT[kr][:, c, :], rhs=qm[:, c, :], start=True, stop=True),
                 reads=[b_kT[kr], bqm], writes=[b_pMs[m]])

        def emit_rest(u):
            grp, hh, n_, kt = units[u]
            h = grp * 4 + hh
            kr = kt % RING
            m = info[u]
            t_ = u % NTMP
            j0 = 8 - 2 * (kt - s)
            ti = ntile_idx[(s, kt)]
            S.op("dve", lambda e: e.tensor_tensor(
                out=tmp[t_], in0=pMs[m], in1=T[:, h, j0:j0 + 2, :].rearrange("p a b -> p (a b)"), op=ALU.add),
                 reads=[b_pMs[m], b_T], writes=[b_tmp[t_]])
            for qh in range(2):
                S.op("act", lambda e, qh=qh: e.activation(
                    out=PT[t_][:, qh * 64:(qh + 1) * 64], in_=tmp[t_][:, qh * 64:(qh + 1) * 64], func=AF.Exp,
                    bias=rmask[:, sl, ti, qh:qh + 1]), reads=[b_tmp[t_], b_rmask], writes=[b_PT[t_]])
            S.op("pe", lambda e: e.matmul(pO[:, hh, 0:65], lhsT=PT[t_], rhs=va[kr][:, h, :],
                                          start=(n_ == 0), stop=(n_ == len(kts) - 1)),
                 reads=[b_PT[t_], b_va[kr]], writes=[b_pO])
            if hh == 3 and n_ == len(kts) - 1:
                S.op("dve", lambda e: e.reciprocal(out=small[:, 4:8], in_=pO[:, :, 64]), reads=[b_pO], writes=[b_small])
                S.op("dve", lambda e: e.tensor_tensor(
                    out=ya[:, grp * 4:(grp + 1) * 4, :], in0=pO[:, :, 0:64],
                    in1=small[:, 4:8].unsqueeze(2).broadcast_to([128, 4, 64]), op=ALU.mult),
                     reads=[b_pO, b_small], writes=[b_ya])

        for u in range(min(LOOK, len(units))):
            emit_st(u)
        for u in range(len(units)):
            emit_rest(u)
            if u + LOOK < len(units):
                emit_st(u + LOOK)
            if filler is not None:
                for _ in range(rate):
                    next(filler, None)
        if filler is not None:
            for _ in filler:
                pass
        yaf = ya.rearrange("p h d -> p (h d)")
        S.op("act", lambda e: e.activation(out=junk[:, 0:512], in_=yaf, func=AF.Square, accum_out=small[:, 8:9]),
             reads=[b_ya], writes=[b_junk, b_small])
        S.op("act", lambda e: e.activation(out=small[:, 9:10], in_=small[:, 8:9], func=AF.Sqrt, bias=EPS, scale=1.0 / 512),
             reads=[b_small], writes=[b_small])
        S.op("dve", lambda e: e.reciprocal(out=small[:, 10:11], in_=small[:, 9:10]), reads=[b_small], writes=[b_small])
        S.op("dve", lambda e: e.tensor_scalar(out=yan, in0=yaf, scalar1=small[:, 10:11], scalar2=None, op0=ALU.mult),
             reads=[b_ya, b_small], writes=[b_yan])
        for c in range(4):
            S.op("pe", lambda e, c=c: e.transpose(out=pTv[:, c, :], in_=yan[:, c * 128:(c + 1) * 128], identity=ident),
                 reads=[b_yan, b_ident], writes=[b_pT])
        S.op("act", lambda e: e.copy(out=yaT, in_=pTv[:, 0:4, :]), reads=[b_pT], writes=[b_yaT])

    def m_out(sl, s):
        i = s - 2
        x = xr[i % 2]; bx = b_xr[i % 2]
        ycn_s = ycn2[s % 2]; b_ycn_s = b_ycn2[s % 2]
        for hf in range(2):
            for c in range(8):
                lhs = ycn_s[:, c, :] if c < 4 else yaT[:, c - 4, :]
                S.op("pe", lambda e, c=c, hf=hf, lhs=lhs: e.matmul(pY[:, hf * 512:(hf + 1) * 512], lhsT=lhs,
                                                                   rhs=wout[:, c, hf * 512:(hf + 1) * 512],
                                                                   start=(c == 0), stop=(c == 7)),
                     reads=[b_ycn_s, b_yaT, b_wout], writes=[b_pS])
        S.op("dve", lambda e: e.tensor_tensor(out=x, in0=pY, in1=x, op=ALU.add), reads=[b_pS, bx], writes=[bx])
        S.dma("sp", lambda e: e.dma_start(out=x1_tiles[sl][i], in_=x), reads=[bx])


    def chain(*gens):
        for g in gens:
            if g is not None:
                for _ in g:
                    yield

    for sl in range(nslab):
        for _ in stage_p_gen(sl, 0):
            pass
        for step in range(NS + 3):
            gp = stage_p_gen(sl, step + 1) if step + 1 < NS else None
            sc = step - 2
            gc = m_conv_gen(sl, sc) if 2 <= sc < NQ + 2 else None
            g = chain(gc, gp)
            s = step - 3
            if 2 <= s < NQ + 2:
                m_attn(sl, s, g, 8)
            for _ in g:
                pass
            if 2 <= s < NQ + 2:
                m_out(sl, s)


def row_mask_table(NQ, kind, q=0, R=None):
    idx = {}
    for s in range(2, NQ + 2):
        for kt in key_tiles(s, NQ):
            idx[(s, kt)] = len(idx)
    out = np.full((128, len(idx), 2), NEGM, np.float32)
    nrows = 2 * NQ
    if kind == "full":
        R = nrows; base = 0
    else:
        base = q
    for (s, kt), ti in idx.items():
        for qh in range(2):
            r = base + 2 * (s - 2) + qh
            rs = min(max(r - 4, 0), R - 8)
            for kh in range(2):
                rk = base + 2 * kt + kh - 4
                if rs <= rk < rs + 8:
                    out[kh * 64:(kh + 1) * 64, ti, qh] = 0.0
    return out


def bias_table(rpb):
    T = np.full((128, 8, 16, 64), NEGM, np.float32)
    cq = np.arange(64)
    cs = np.clip(cq - 8, 0, 48)
    for kh in range(2):
        for j in range(16):
            dr = kh - j + 8
            if abs(dr) > 7:
                continue
            for cp in range(64):
                ok = (cp >= cs) & (cp < cs + 16)
                off = np.clip(cp - cq + 15, 0, 30)
                vals = rpb[:, dr + 7, :][:, off]
                T[kh * 64 + cp, :, j, :] = np.where(ok[None, :], vals, NEGM)
    return T


def small_params(g_mix, g_out_conv, g_out_attn, conv_w, conv_b, ln_g, ln_b, q_g, k_g):
    pp = np.zeros((128, PP_N), np.float32)
    pp[:, PP_GMIX:PP_GMIX + 8] = g_mix.reshape(8, 128).T
    pp[:, PP_GOUT:PP_GOUT + 4] = g_out_conv.reshape(4, 128).T
    pp[:, PP_GOUT + 4:PP_GOUT + 8] = g_out_attn.reshape(4, 128).T
    pp[:, PP_CB:PP_CB + 4] = conv_b.reshape(4, 128).T
    pp[:, PP_LNG:PP_LNG + 4] = ln_g.reshape(4, 128).T
    pp[:, PP_LNB:PP_LNB + 4] = ln_b.reshape(4, 128).T
    pp[:, PP_GQ] = np.tile(q_g, 2)
    pp[:, PP_GK] = np.tile(k_g, 2)
    pp[:, PP_CW:PP_CW + 124] = conv_w.T.reshape(4, 128, 31).transpose(1, 0, 2).reshape(128, 124)
    return pp


NQ_FULL = 32
N_CORES = 8
ARENA_WORDS = 51200


def build_program(NQ=NQ_FULL):
    NS = NQ + 4
    nc = bass.Bass("TRN2", target_bir_lowering=False)
    xs = nc.dram_tensor("xs", [2, NS * 128, D], F32, kind="ExternalInput").ap()
    y = nc.dram_tensor("y", [2, NQ * 128, D], F32, kind="ExternalOutput").ap()
    nti = sum(len(key_tiles(s, NQ)) for s in range(2, NQ + 2))
    C = {}
    for name, shape in (("consts", [128, 144]), ("pp", [128, PP_N]), ("ttab", [128, 8 * 16 * 64]),
                        ("rmask", [128, 2 * nti * 2]), ("w_in", [D, 2560]), ("w_out", [D, D]),
                        ("g_ffn", [D]), ("w_query", [D, 2048]), ("sub_keys", [16, 128, 128]),
                        ("expert_u", [NEXP, D]), ("expert_v", [NEXP, D])):
        C[name] = nc.dram_tensor(name, shape, F32, kind="ExternalInput").ap()
    S = Sched(nc)
    sbh = nc.alloc_sbuf_tensor("arena", [128, ARENA_WORDS], F32)
    psh = nc.alloc_psum_tensor("parena", [128, 4096], F32)
    uv16 = nc.dram_tensor("uv16", [NEXP, 2048], BF16, kind="Internal").ap()
    b_uv16 = Buf()
    convert_tables(S, C, uv16, b_uv16)
    xst = xs.rearrange("a (n p) d -> a n p d", p=128)
    yt = y.rearrange("a (n p) d -> a n p d", p=128)
    sb = Arena(sbh, ARENA_WORDS); ps = Arena(psh, 4096)
    phase_a(S, nc, sb, ps, C, NQ, [[xst[a, i] for i in range(NS)] for a in range(2)],
            [[yt[a, i] for i in range(NQ)] for a in range(2)])
    S.barrier()
    sb = Arena(sbh, ARENA_WORDS); ps = Arena(psh, 4096)
    ytl = [yt[a, i] for a in range(2) for i in range(NQ)]
    phase_b3(S, nc, sb, ps, C, ytl, ytl, uv16, b_uv16)
    S.emit()
    return nc, S


def kernel(x_prompt, x_sample, g_mix, w_in, conv_w, conv_b, conv_ln_g, conv_ln_b,
           q_norm_g, k_norm_g, rpb, g_out_conv, g_out_attn, w_out, g_ffn,
           w_query, sub_keys, expert_u, expert_v):
    f = lambda a: np.ascontiguousarray(np.asarray(a, dtype=np.float32))
    x_prompt, x_sample = f(x_prompt), f(x_sample)
    NQ = NQ_FULL
    NS = NQ + 4
    T = NQ * 128
    H = 256
    shared = {
        "consts": make_consts(),
        "pp": small_params(f(g_mix)[0], f(g_out_conv)[0], f(g_out_attn)[0], f(conv_w)[0], f(conv_b)[0],
                           f(conv_ln_g)[0], f(conv_ln_b)[0], f(q_norm_g)[0], f(k_norm_g)[0]),
        "ttab": bias_table(f(rpb)[0]).reshape(128, -1),
        "w_in": f(w_in)[0], "w_out": f(w_out)[0], "g_ffn": f(g_ffn)[0], "w_query": f(w_query)[0],
        "sub_keys": f(sub_keys)[0].reshape(16, 128, 128),
        "expert_u": f(expert_u)[0], "expert_v": f(expert_v)[0],
    }
    rm_full = row_mask_table(NQ, "full")
    in_maps = []
    for c in range(N_CORES):
        b, q = c // 4, c % 4
        xs = np.zeros((2, NS * 128, D), np.float32)
        xs[0, H:H + T] = x_sample[c]
        lo, hi = q * T - H, q * T + T + H
        clo, chi = max(lo, 0), min(hi, x_prompt.shape[1])
        xs[1, clo - lo:clo - lo + (chi - clo)] = x_prompt[b, clo:chi]
        rm = np.stack([rm_full, row_mask_table(NQ, "chunk", q=2 * NQ * q, R=x_prompt.shape[1] // 64)], axis=1)
        m = dict(shared)
        m["xs"] = xs
        m["rmask"] = np.ascontiguousarray(rm.reshape(128, -1))
        in_maps.append(m)
    nc, _ = build_program(NQ)
    res = run_bass_kernel_spmd(nc, in_maps, core_ids=list(range(N_CORES)))
    y_prompt = np.zeros_like(x_prompt)
    y_sample = np.zeros_like(x_sample)
    for c in range(N_CORES):
        yc = np.asarray(res.results[c]["y"], dtype=np.float32)
        y_sample[c] = yc[0]
        y_prompt[c // 4, (c % 4) * T:(c % 4 + 1) * T] = yc[1]
    return (y_prompt, y_sample)
```

```python
import numpy as np
import concourse.bass as bass
import concourse.mybir as mybir
from concourse.bass_utils import run_bass_kernel_spmd

F32 = mybir.dt.float32
BF16 = mybir.dt.bfloat16
I32 = mybir.dt.int32
U32 = mybir.dt.uint32
ALU = mybir.AluOpType
AF = mybir.ActivationFunctionType
AX = mybir.AxisListType

D = 1024
NEXP = 16384
EPS = 1e-6
NEGM = -30000.0


class Buf:
    __slots__ = ("name", "w", "r")

    def __init__(self, name=""):
        self.name = name
        self.w = None
        self.r = []


class Op:
    __slots__ = ("eng", "fn", "deps", "signal", "semkey", "count", "is_dma")

    def __init__(self, eng, fn):
        self.eng = eng
        self.fn = fn
        self.deps = []
        self.signal = False
        self.semkey = None
        self.count = 0
        self.is_dma = False


class Sched:
    ENGS = ("pe", "dve", "act", "pool", "sp")
    SELF_SYNC = {"pe": False, "dve": True, "act": True, "pool": True, "sp": False}

    def __init__(self, nc, n_dma_sems=None):
        self.nc = nc
        self.ops = {e: [] for e in self.ENGS}
        self.n_dma_sems = n_dma_sems or {"sp": 16, "act": 8, "pool": 24}
        self.dma_rr = {}
        self.dma_last = {}
        self.dma_cnt = {}
        self.last_real = {}

    def _add_deps(self, op, reads, writes):
        deps = []
        for b in reads:
            if b.w is not None:
                deps.append(b.w)
        for b in writes:
            if b.w is not None:
                deps.append(b.w)
            deps.extend(b.r)
        for d in deps:
            if d is op:
                continue
            if (not d.is_dma) and d.eng == op.eng and not self.SELF_SYNC[op.eng]:
                continue
            op.deps.append(d)
            d.signal = True
        for b in reads:
            b.r.append(op)
        for b in writes:
            b.w = op
            b.r = []

    def op(self, eng, fn, reads=(), writes=()):
        o = Op(eng, fn)
        self._add_deps(o, reads, writes)
        self.ops[eng].append(o)
        self.last_real[eng] = o
        return o

    def dma(self, eng, fn, reads=(), writes=()):
        o = Op(eng, fn)
        o.is_dma = True
        o.signal = True
        rr = self.dma_rr.get(eng, 0)
        self.dma_rr[eng] = rr + 1
        key = ("dma", eng, rr % self.n_dma_sems[eng])
        o.semkey = key
        prev = self.dma_last.get(key)
        if prev is not None:
            o.deps.append(prev)
        self.dma_last[key] = o
        c = self.dma_cnt.get(key, 0) + 16
        self.dma_cnt[key] = c
        o.count = c
        self._add_deps(o, reads, writes)
        self.ops[eng].append(o)
        return o

    def barrier(self):
        lasts = [o for o in self.last_real.values()] + list(self.dma_last.values())
        for e in self.ENGS:
            o = Op(e, None)
            for d in lasts:
                if (not d.is_dma) and d.eng == e:
                    continue
                o.deps.append(d)
                d.signal = True
            self.ops[e].append(o)

    def emit(self, final_wait_eng="sp"):
        nc = self.nc
        self.barrier()
        for e in self.ENGS:
            c = 0
            for o in self.ops[e]:
                if o.is_dma or o.fn is None:
                    continue
                o.semkey = ("eng", e)
                if o.signal:
                    c += 1
                    o.count = c
        keys = set()
        for e in self.ENGS:
            for o in self.ops[e]:
                if o.signal and o.fn is not None:
                    keys.add(o.semkey)
        sems = {}
        for k in sorted(keys, key=str):
            sems[k] = nc.alloc_semaphore(name="s_" + "_".join(str(x) for x in k))
        stats = {"ins": 0, "wait": 0}
        with nc.Block() as block:
            deco = {"pe": block.tensor, "dve": block.vector, "act": block.scalar,
                    "pool": block.gpsimd, "sp": block.sync}
            for e in self.ENGS:
                ops = self.ops[e]

                def body(eng, ops=ops):
                    seen = {}
                    for o in ops:
                        for d in o.deps:
                            if seen.get(d.semkey, 0) < d.count:
                                eng.wait_ge(sems[d.semkey], d.count)
                                seen[d.semkey] = d.count
                                stats["wait"] += 1
                        if o.fn is None:
                            continue
                        ins = o.fn(eng)
                        stats["ins"] += 1
                        if o.signal:
                            ins.then_inc(sems[o.semkey], 16 if o.is_dma else 1)

                deco[e](body)
        self.stats = stats


class Arena:
    def __init__(self, handle, nwords):
        self.h = handle
        self.n = nwords
        self.off = 0

    def alloc(self, free_shape, dtype=F32, align=16):
        n = 1
        for s in free_shape:
            n *= s
        size = 4 if dtype in (F32, I32, U32) else 2
        words = (n * size + 3) // 4
        self.off = (self.off + align - 1) // align * align
        assert self.off + words <= self.n, ("arena overflow", self.off, words, self.n)
        ap = self.h[:, self.off:self.off + words]
        self.off += words
        if dtype != F32:
            ap = ap.bitcast(dtype)
            if ap.shape[1] != n:
                ap = ap[:, 0:n]
        if len(free_shape) == 2:
            ap = ap.rearrange("p (a b) -> p a b", a=free_shape[0])
        elif len(free_shape) == 3:
            ap = ap.rearrange("p (a b c) -> p a b c", a=free_shape[0], b=free_shape[1])
        return ap


def phase_b(S, nc, sb, ps, C, x1_tiles, y_tiles, NB=6):
    ident = sb.alloc([128], BF16); b_ident = Buf()
    iota16 = sb.alloc([16], F32); b_iota = Buf()
    cst = sb.alloc([144], F32); b_cst = Buf()
    wq = sb.alloc([8, 2048], BF16); b_wq = Buf()
    skT = sb.alloc([16, 128], BF16); b_skT = Buf()
    gffn = sb.alloc([1024], F32); b_gffn = Buf()
    S.dma("sp", lambda e: e.dma_start(out=cst, in_=C["consts"]), writes=[b_cst])
    S.op("dve", lambda e: e.tensor_copy(out=ident, in_=cst[:, 0:128]), reads=[b_cst], writes=[b_ident])
    S.op("dve", lambda e: e.tensor_copy(out=iota16, in_=cst[:, 128:144]), reads=[b_cst], writes=[b_iota])
    S.dma("sp", lambda e: e.dma_start(out=gffn, in_=C["g_ffn"].partition_broadcast(128)), writes=[b_gffn])
    stg = [sb.alloc([2048], F32) for _ in range(2)]
    b_stg = [Buf(), Buf()]
    wqv = C["w_query"].rearrange("(c p) n -> c p n", p=128)
    for c in range(8):
        S.dma("sp", lambda e, c=c: e.dma_start(out=stg[c % 2], in_=wqv[c]), writes=[b_stg[c % 2]])
        S.op("act" if c % 2 else "dve",
             (lambda e, c=c: e.copy(out=wq[:, c, :], in_=stg[c % 2])) if c % 2 else
             (lambda e, c=c: e.tensor_copy(out=wq[:, c, :], in_=stg[c % 2])),
             reads=[b_stg[c % 2]], writes=[b_wq])
    skv = C["sub_keys"].rearrange("g k d -> k g d")
    skn = stg[0].rearrange("p (g d) -> p g d", g=16)
    skb = sb.alloc([16, 128], BF16); b_skb = Buf()
    pbig = ps.alloc([16, 128], F32, align=512); b_pbig = Buf()
    pT = ps.alloc([1024], BF16, align=512); b_pT = Buf()
    pTv = pT.rearrange("p (c n) -> p c n", c=8)
    S.dma("sp", lambda e: e.dma_start(out=skn, in_=skv), writes=[b_stg[0]])
    S.op("dve", lambda e: e.tensor_copy(out=skb, in_=skn), reads=[b_stg[0]], writes=[b_skb])
    for half in range(2):
        for j in range(8):
            g = half * 8 + j
            S.op("pe", lambda e, g=g, j=j: e.transpose(out=pTv[:, j, :], in_=skb[:, g, :], identity=ident),
                 reads=[b_skb, b_ident], writes=[b_pT])
        S.op("dve", lambda e, half=half: e.tensor_copy(out=skT[:, half * 8:(half + 1) * 8, :], in_=pTv),
             reads=[b_pT], writes=[b_skT])

    x1t = [sb.alloc([1024], F32) for _ in range(2)]; b_x1t = [Buf(), Buf()]
    junk = sb.alloc([1024], F32); b_junk = Buf()
    hn = sb.alloc([1024], F32); b_hn = Buf()
    hb = sb.alloc([1024], BF16); b_hb = Buf()
    hT = sb.alloc([8, 128], BF16); b_hT = Buf()
    qT = sb.alloc([16, 128], BF16); b_qT = Buf()
    s = sb.alloc([16, 128], F32); b_s = Buf()
    s2 = sb.alloc([16, 128], F32); b_s2 = Buf()
    sv = sb.alloc([16, 16], F32); b_sv = Buf()
    siu = sb.alloc([16, 16], U32); b_siu = Buf()
    sif = sb.alloc([16, 16], F32); b_sif = Buf()
    cand = sb.alloc([8, 256], F32); b_cand = Buf()
    cand2 = sb.alloc([8, 256], F32); b_cand2 = Buf()
    cv = sb.alloc([8, 16], F32); b_cv = Buf()
    ciu = sb.alloc([8, 16], U32); b_ciu = Buf()
    rcu = sb.alloc([2, 128], U32); b_rcu = Buf()
    rcf = sb.alloc([2, 128], F32); b_rcf = Buf()
    oh = sb.alloc([128, 16], F32); b_oh = Buf()
    i12 = sb.alloc([2, 128], F32); b_i12 = Buf()
    ef = sb.alloc([128], F32); b_ef = Buf()
    ei = [sb.alloc([128], I32) for _ in range(2)]; b_ei = [Buf(), Buf()]
    araw = sb.alloc([128], F32); b_araw = Buf()
    aw = sb.alloc([128], F32); b_aw = Buf()
    gsm = sb.alloc([8, 16], F32); b_gsm = Buf()
    small = sb.alloc([32], F32); b_small = Buf()
    acc = [sb.alloc([1024], F32) for _ in range(2)]; b_acc = [Buf(), Buf()]
    ub = [sb.alloc([1024], F32) for _ in range(NB)]; b_ub = [Buf() for _ in range(NB)]
    vb = [sb.alloc([1024], F32) for _ in range(NB)]; b_vb = [Buf() for _ in range(NB)]
    gcount = [0, 0]

    sv4 = sv.rearrange("p (h t) k -> p h t k", t=2)
    sif4 = sif.rearrange("p (h t) k -> p h t k", t=2)
    cand4 = cand.rearrange("p h (i j) -> p h i j", i=16)
    oh4 = oh.rearrange("p (h k) i -> p h k i", h=8)

    for ti in range(len(x1_tiles)):
        xb = x1t[ti % 2]; bxb = b_x1t[ti % 2]
        eib = ei[ti % 2]; beib = b_ei[ti % 2]
        ac = acc[ti % 2]; bac = b_acc[ti % 2]
        S.dma("sp", lambda e, xb=xb, ti=ti: e.dma_start(out=xb, in_=x1_tiles[ti]), writes=[bxb])
        S.op("act", lambda e, xb=xb: e.activation(out=junk, in_=xb, func=AF.Square, accum_out=small[:, 0:1]),
             reads=[bxb], writes=[b_junk, b_small])
        S.op("act", lambda e: e.activation(out=small[:, 1:2], in_=small[:, 0:1], func=AF.Sqrt, bias=EPS, scale=1.0 / D),
             reads=[b_small], writes=[b_small])
        S.op("dve", lambda e: e.reciprocal(out=small[:, 2:3], in_=small[:, 1:2]), reads=[b_small], writes=[b_small])
        S.op("dve", lambda e, xb=xb: e.scalar_tensor_tensor(out=hn, in0=xb, scalar=small[:, 2:3], in1=gffn,
                                                            op0=ALU.mult, op1=ALU.mult),
             reads=[bxb, b_small, b_gffn], writes=[b_hn])
        S.op("act", lambda e: e.copy(out=hb, in_=hn), reads=[b_hn], writes=[b_hb])
        for c in range(8):
            S.op("pe", lambda e, c=c: e.transpose(out=pTv[:, c, :], in_=hb[:, c * 128:(c + 1) * 128], identity=ident),
                 reads=[b_hb, b_ident], writes=[b_pT])
        S.op("act", lambda e: e.copy(out=hT, in_=pTv), reads=[b_pT], writes=[b_hT])
        for g in range(16):
            for c in range(8):
                S.op("pe", lambda e, g=g, c=c: e.matmul(pbig[:, g, :], lhsT=wq[:, c, g * 128:(g + 1) * 128],
                                                        rhs=hT[:, c, :], start=(c == 0), stop=(c == 7)),
                     reads=[b_wq, b_hT], writes=[b_pbig])
        S.op("act", lambda e: e.copy(out=qT, in_=pbig), reads=[b_pbig], writes=[b_qT])
        for g in range(16):
            S.op("pe", lambda e, g=g: e.matmul(pbig[:, g, :], lhsT=qT[:, g, :], rhs=skT[:, g, :], start=True, stop=True),
                 reads=[b_qT, b_skT], writes=[b_pbig])
        S.op("dve", lambda e: e.tensor_copy(out=s, in_=pbig), reads=[b_pbig], writes=[b_s])
        for g in range(16):
            S.op("dve", lambda e, g=g: e.max(out=sv[:, g, 0:8], in_=s[:, g, :]), reads=[b_s], writes=[b_sv])
            S.op("dve", lambda e, g=g: e.match_replace(out=s2[:, g, :], in_to_replace=sv[:, g, 0:8],
                                                       in_values=s[:, g, :], imm_value=-1e30),
                 reads=[b_s, b_sv], writes=[b_s2])
            S.op("dve", lambda e, g=g: e.max(out=sv[:, g, 8:16], in_=s2[:, g, :]), reads=[b_s2], writes=[b_sv])
            S.op("dve", lambda e, g=g: e.max_index(out=siu[:, g, 0:8], in_max=sv[:, g, 0:8], in_values=s[:, g, :]),
                 reads=[b_s, b_sv], writes=[b_siu])
            S.op("dve", lambda e, g=g: e.max_index(out=siu[:, g, 8:16], in_max=sv[:, g, 8:16], in_values=s[:, g, :]),
                 reads=[b_s, b_sv], writes=[b_siu])
        S.op("dve", lambda e: e.tensor_copy(out=sif, in_=siu), reads=[b_siu], writes=[b_sif])
        for h in range(8):
            S.op("dve", lambda e, h=h: e.tensor_tensor(
                out=cand4[:, h], in0=sv4[:, h, 0, :].unsqueeze(2).broadcast_to([128, 16, 16]),
                in1=sv4[:, h, 1, :].unsqueeze(1).broadcast_to([128, 16, 16]), op=ALU.add),
                 reads=[b_sv], writes=[b_cand])
        for h in range(8):
            S.op("dve", lambda e, h=h: e.max(out=cv[:, h, 0:8], in_=cand[:, h, :]), reads=[b_cand], writes=[b_cv])
            S.op("dve", lambda e, h=h: e.match_replace(out=cand2[:, h, :], in_to_replace=cv[:, h, 0:8],
                                                       in_values=cand[:, h, :], imm_value=-1e30),
                 reads=[b_cand, b_cv], writes=[b_cand2])
            S.op("dve", lambda e, h=h: e.max(out=cv[:, h, 8:16], in_=cand2[:, h, :]), reads=[b_cand2], writes=[b_cv])
            S.op("dve", lambda e, h=h: e.max_index(out=ciu[:, h, 0:8], in_max=cv[:, h, 0:8], in_values=cand[:, h, :]),
                 reads=[b_cand, b_cv], writes=[b_ciu])
            S.op("dve", lambda e, h=h: e.max_index(out=ciu[:, h, 8:16], in_max=cv[:, h, 8:16], in_values=cand[:, h, :]),
                 reads=[b_cand, b_cv], writes=[b_ciu])
        ciu_f = ciu.rearrange("p h k -> p (h k)")
        S.op("dve", lambda e: e.tensor_single_scalar(out=rcu[:, 0, :], in_=ciu_f, scalar=4, op=ALU.logical_shift_right),
             reads=[b_ciu], writes=[b_rcu])
        S.op("dve", lambda e: e.tensor_single_scalar(out=rcu[:, 1, :], in_=ciu_f, scalar=15, op=ALU.bitwise_and),
             reads=[b_ciu], writes=[b_rcu])
        S.op("dve", lambda e: e.tensor_copy(out=rcf, in_=rcu), reads=[b_rcu], writes=[b_rcf])
        for t in range(2):
            S.op("dve", lambda e, t=t: e.tensor_tensor(
                out=oh, in0=rcf[:, t, :].unsqueeze(2).broadcast_to([128, 128, 16]),
                in1=iota16.unsqueeze(1).broadcast_to([128, 128, 16]), op=ALU.is_equal),
                 reads=[b_rcf, b_iota], writes=[b_oh])
            for h in range(8):
                S.op("dve", lambda e, h=h, t=t: e.tensor_tensor(
                    out=oh4[:, h], in0=oh4[:, h], in1=sif4[:, h, t, :].unsqueeze(1).broadcast_to([128, 16, 16]),
                    op=ALU.mult), reads=[b_oh, b_sif], writes=[b_oh])
            S.op("dve", lambda e, t=t: e.tensor_reduce(out=i12[:, t, :], in_=oh, axis=AX.X, op=ALU.add),
                 reads=[b_oh], writes=[b_i12])
        S.op("dve", lambda e: e.scalar_tensor_tensor(out=ef, in0=i12[:, 0, :], scalar=128.0, in1=i12[:, 1, :],
                                                     op0=ALU.mult, op1=ALU.add), reads=[b_i12], writes=[b_ef])
        S.op("dve", lambda e, eib=eib: e.tensor_copy(out=eib, in_=ef), reads=[b_ef], writes=[beib])
        S.op("dve", lambda e: e.tensor_tensor(out=gsm, in0=cv, in1=cv[:, :, 0:1].broadcast_to([128, 8, 16]),
                                              op=ALU.subtract), reads=[b_cv], writes=[b_gsm])
        S.op("act", lambda e: e.activation(out=gsm, in_=gsm, func=AF.Exp), reads=[b_gsm], writes=[b_gsm])
        S.op("dve", lambda e: e.tensor_reduce(out=small[:, 8:16], in_=gsm, axis=AX.X, op=ALU.add),
             reads=[b_gsm], writes=[b_small])
        S.op("dve", lambda e: e.reciprocal(out=small[:, 16:24], in_=small[:, 8:16]), reads=[b_small], writes=[b_small])
        S.op("dve", lambda e: e.tensor_tensor(out=gsm, in0=gsm,
                                              in1=small[:, 16:24].unsqueeze(2).broadcast_to([128, 8, 16]),
                                              op=ALU.mult), reads=[b_gsm, b_small], writes=[b_gsm])
        for k in range(128):
            j = gcount[0] % NB; gcount[0] += 1
            S.dma("pool", lambda e, k=k, j=j, eib=eib: e.indirect_dma_start(
                out=ub[j], out_offset=None, in_=C["expert_u"],
                in_offset=bass.IndirectOffsetOnAxis(ap=eib[:, k:k + 1], axis=0)),
                  reads=[beib], writes=[b_ub[j]])
            S.op("dve", lambda e, k=k, j=j: e.scalar_tensor_tensor(
                out=ub[j], in0=ub[j], scalar=1.0, in1=hn, op0=ALU.mult, op1=ALU.mult, accum_out=araw[:, k:k + 1]),
                 reads=[b_hn], writes=[b_ub[j]] + ([b_araw] if k in (0, 127) else []))
        S.op("act", lambda e: e.activation(out=aw, in_=araw, func=AF.Gelu), reads=[b_araw], writes=[b_aw])
        S.op("dve", lambda e: e.tensor_tensor(out=aw, in0=aw, in1=gsm.rearrange("p h k -> p (h k)"), op=ALU.mult),
             reads=[b_aw, b_gsm], writes=[b_aw])
        for k in range(128):
            j = gcount[1] % NB; gcount[1] += 1
            S.dma("pool", lambda e, k=k, j=j, eib=eib: e.indirect_dma_start(
                out=vb[j], out_offset=None, in_=C["expert_v"],
                in_offset=bass.IndirectOffsetOnAxis(ap=eib[:, k:k + 1], axis=0)),
                  reads=[beib], writes=[b_vb[j]])
            src = xb if k == 0 else ac
            S.op("dve", lambda e, k=k, j=j, src=src, ac=ac: e.scalar_tensor_tensor(
                out=ac, in0=vb[j], scalar=aw[:, k:k + 1], in1=src, op0=ALU.mult, op1=ALU.add),
                 reads=[b_vb[j], b_aw] + ([bxb] if k == 0 else []), writes=[bac])
        S.dma("sp", lambda e, ac=ac, ti=ti: e.dma_start(out=y_tiles[ti], in_=ac), reads=[bac])


def phase_b2(S, nc, sb, ps, C, x1_tiles, y_tiles, uv16, b_uv16, NB=8):
    ident = sb.alloc([128], BF16); b_ident = Buf()
    iota16 = sb.alloc([16], F32); b_iota = Buf()
    cst = sb.alloc([144], F32); b_cst = Buf()
    wq = sb.alloc([8, 2048], BF16); b_wq = Buf()
    skT = sb.alloc([16, 128], BF16); b_skT = Buf()
    gffn = sb.alloc([1024], F32); b_gffn = Buf()
    S.dma("sp", lambda e: e.dma_start(out=cst, in_=C["consts"]), writes=[b_cst])
    S.op("dve", lambda e: e.tensor_copy(out=ident, in_=cst[:, 0:128]), reads=[b_cst], writes=[b_ident])
    S.op("dve", lambda e: e.tensor_copy(out=iota16, in_=cst[:, 128:144]), reads=[b_cst], writes=[b_iota])
    S.dma("sp", lambda e: e.dma_start(out=gffn, in_=C["g_ffn"].partition_broadcast(128)), writes=[b_gffn])
    stg = [sb.alloc([2048], F32) for _ in range(2)]
    b_stg = [Buf(), Buf()]
    wqv = C["w_query"].rearrange("(c p) n -> c p n", p=128)
    for c in range(8):
        S.dma("sp", lambda e, c=c: e.dma_start(out=stg[c % 2], in_=wqv[c]), writes=[b_stg[c % 2]])
        S.op("act" if c % 2 else "dve",
             (lambda e, c=c: e.copy(out=wq[:, c, :], in_=stg[c % 2])) if c % 2 else
             (lambda e, c=c: e.tensor_copy(out=wq[:, c, :], in_=stg[c % 2])),
             reads=[b_stg[c % 2]], writes=[b_wq])
    skv = C["sub_keys"].rearrange("g k d -> k g d")
    skn = stg[0].rearrange("p (g d) -> p g d", g=16)
    skb = sb.alloc([16, 128], BF16); b_skb = Buf()
    pbig = ps.alloc([16, 128], F32, align=512); b_pbig = Buf()
    pT = ps.alloc([1024], BF16, align=512); b_pT = Buf()
    pTv = pT.rearrange("p (c n) -> p c n", c=8)
    S.dma("sp", lambda e: e.dma_start(out=skn, in_=skv), writes=[b_stg[0]])
    S.op("dve", lambda e: e.tensor_copy(out=skb, in_=skn), reads=[b_stg[0]], writes=[b_skb])
    for half in range(2):
        for j in range(8):
            g = half * 8 + j
            S.op("pe", lambda e, g=g, j=j: e.transpose(out=pTv[:, j, :], in_=skb[:, g, :], identity=ident),
                 reads=[b_skb, b_ident], writes=[b_pT])
        S.op("dve", lambda e, half=half: e.tensor_copy(out=skT[:, half * 8:(half + 1) * 8, :], in_=pTv),
             reads=[b_pT], writes=[b_skT])

    x1t = [sb.alloc([1024], F32) for _ in range(2)]; b_x1t = [Buf(), Buf()]
    junk = sb.alloc([1024], F32); b_junk = Buf()
    hb = sb.alloc([1024], BF16); b_hb = Buf()
    hT = sb.alloc([8, 128], BF16); b_hT = Buf()
    qT = sb.alloc([16, 128], BF16); b_qT = Buf()
    s = sb.alloc([16, 128], F32); b_s = Buf()
    s2 = sb.alloc([16, 128], F32); b_s2 = Buf()
    sv = sb.alloc([16, 16], F32); b_sv = Buf()
    siu = sb.alloc([16, 16], U32); b_siu = Buf()
    sif = sb.alloc([16, 16], F32); b_sif = Buf()
    cand = sb.alloc([8, 256], F32); b_cand = Buf()
    cand2 = sb.alloc([8, 256], F32); b_cand2 = Buf()
    cv = sb.alloc([8, 16], F32); b_cv = Buf()
    ciu = sb.alloc([8, 16], U32); b_ciu = Buf()
    rcu = sb.alloc([2, 128], U32); b_rcu = Buf()
    rcf = sb.alloc([2, 128], F32); b_rcf = Buf()
    oh = sb.alloc([128, 16], F32); b_oh = Buf()
    i12 = sb.alloc([2, 128], F32); b_i12 = Buf()
    ef = sb.alloc([128], F32); b_ef = Buf()
    ei = [sb.alloc([128], I32) for _ in range(2)]; b_ei = [Buf(), Buf()]
    araw = sb.alloc([128], F32)
    gsm = sb.alloc([8, 16], F32); b_gsm = Buf()
    small = sb.alloc([32], F32); b_small = Buf()
    yo = sb.alloc([1024], F32); b_yo = Buf()
    uvb = [sb.alloc([2048], BF16) for _ in range(NB)]; b_uvb = [Buf() for _ in range(NB)]
    dg = [sb.alloc([128], BF16) for _ in range(4)]; b_dg = [Buf() for _ in range(4)]
    gl = sb.alloc([128], F32); b_gl = [Buf() for _ in range(8)]
    b_ar = [Buf() for _ in range(8)]
    pacc = ps.alloc([1024], F32, align=512); b_pacc = Buf()
    gcount = [0, 0]

    sv4 = sv.rearrange("p (h t) k -> p h t k", t=2)
    sif4 = sif.rearrange("p (h t) k -> p h t k", t=2)
    cand4 = cand.rearrange("p h (i j) -> p h i j", i=16)
    oh4 = oh.rearrange("p (h k) i -> p h k i", h=8)

    for ti in range(len(x1_tiles)):
        xb = x1t[ti % 2]; bxb = b_x1t[ti % 2]
        eib = ei[ti % 2]; beib = b_ei[ti % 2]
        S.dma("sp", lambda e, xb=xb, ti=ti: e.dma_start(out=xb, in_=x1_tiles[ti]), writes=[bxb])
        S.op("act", lambda e, xb=xb: e.activation(out=junk, in_=xb, func=AF.Square, accum_out=small[:, 0:1]),
             reads=[bxb], writes=[b_junk, b_small])
        S.op("act", lambda e: e.activation(out=small[:, 1:2], in_=small[:, 0:1], func=AF.Sqrt, bias=EPS, scale=1.0 / D),
             reads=[b_small], writes=[b_small])
        S.op("dve", lambda e: e.reciprocal(out=small[:, 2:3], in_=small[:, 1:2]), reads=[b_small], writes=[b_small])
        S.op("dve", lambda e, xb=xb: e.scalar_tensor_tensor(out=hb, in0=xb, scalar=small[:, 2:3], in1=gffn,
                                                            op0=ALU.mult, op1=ALU.mult),
             reads=[bxb, b_small, b_gffn], writes=[b_hb])
        for c in range(8):
            S.op("pe", lambda e, c=c: e.transpose(out=pTv[:, c, :], in_=hb[:, c * 128:(c + 1) * 128], identity=ident),
                 reads=[b_hb, b_ident], writes=[b_pT])
        S.op("act", lambda e: e.copy(out=hT, in_=pTv), reads=[b_pT], writes=[b_hT])
        for g in range(16):
            for c in range(8):
                S.op("pe", lambda e, g=g, c=c: e.matmul(pbig[:, g, :], lhsT=wq[:, c, g * 128:(g + 1) * 128],
                                                        rhs=hT[:, c, :], start=(c == 0), stop=(c == 7)),
                     reads=[b_wq, b_hT], writes=[b_pbig])
        S.op("act", lambda e: e.copy(out=qT, in_=pbig), reads=[b_pbig], writes=[b_qT])
        for g in range(16):
            S.op("pe", lambda e, g=g: e.matmul(pbig[:, g, :], lhsT=qT[:, g, :], rhs=skT[:, g, :], start=True, stop=True),
                 reads=[b_qT, b_skT], writes=[b_pbig])
        S.op("dve", lambda e: e.tensor_copy(out=s, in_=pbig), reads=[b_pbig], writes=[b_s])
        for g in range(16):
            S.op("dve", lambda e, g=g: e.max(out=sv[:, g, 0:8], in_=s[:, g, :]), reads=[b_s], writes=[b_sv])
            S.op("dve", lambda e, g=g: e.match_replace(out=s2[:, g, :], in_to_replace=sv[:, g, 0:8],
                                                       in_values=s[:, g, :], imm_value=-1e30),
                 reads=[b_s, b_sv], writes=[b_s2])
            S.op("dve", lambda e, g=g: e.max(out=sv[:, g, 8:16], in_=s2[:, g, :]), reads=[b_s2], writes=[b_sv])
            S.op("dve", lambda e, g=g: e.max_index(out=siu[:, g, 0:8], in_max=sv[:, g, 0:8], in_values=s[:, g, :]),
                 reads=[b_s, b_sv], writes=[b_siu])
            S.op("dve", lambda e, g=g: e.max_index(out=siu[:, g, 8:16], in_max=sv[:, g, 8:16], in_values=s[:, g, :]),
                 reads=[b_s, b_sv], writes=[b_siu])
        S.op("dve", lambda e: e.tensor_copy(out=sif, in_=siu), reads=[b_siu], writes=[b_sif])
        for h in range(8):
            S.op("dve", lambda e, h=h: e.tensor_tensor(
                out=cand4[:, h], in0=sv4[:, h, 0, :].unsqueeze(2).broadcast_to([128, 16, 16]),
                in1=sv4[:, h, 1, :].unsqueeze(1).broadcast_to([128, 16, 16]), op=ALU.add),
                 reads=[b_sv], writes=[b_cand])
        for h in range(8):
            S.op("dve", lambda e, h=h: e.max(out=cv[:, h, 0:8], in_=cand[:, h, :]), reads=[b_cand], writes=[b_cv])
            S.op("dve", lambda e, h=h: e.match_replace(out=cand2[:, h, :], in_to_replace=cv[:, h, 0:8],
                                                       in_values=cand[:, h, :], imm_value=-1e30),
                 reads=[b_cand, b_cv], writes=[b_cand2])
            S.op("dve", lambda e, h=h: e.max(out=cv[:, h, 8:16], in_=cand2[:, h, :]), reads=[b_cand2], writes=[b_cv])
            S.op("dve", lambda e, h=h: e.max_index(out=ciu[:, h, 0:8], in_max=cv[:, h, 0:8], in_values=cand[:, h, :]),
                 reads=[b_cand, b_cv], writes=[b_ciu])
            S.op("dve", lambda e, h=h: e.max_index(out=ciu[:, h, 8:16], in_max=cv[:, h, 8:16], in_values=cand[:, h, :]),
                 reads=[b_cand, b_cv], writes=[b_ciu])
        ciu_f = ciu.rearrange("p h k -> p (h k)")
        S.op("dve", lambda e: e.tensor_single_scalar(out=rcu[:, 0, :], in_=ciu_f, scalar=4, op=ALU.logical_shift_right),
             reads=[b_ciu], writes=[b_rcu])
        S.op("dve", lambda e: e.tensor_single_scalar(out=rcu[:, 1, :], in_=ciu_f, scalar=15, op=ALU.bitwise_and),
             reads=[b_ciu], writes=[b_rcu])
        S.op("dve", lambda e: e.tensor_copy(out=rcf, in_=rcu), reads=[b_rcu], writes=[b_rcf])
        for t in range(2):
            S.op("dve", lambda e, t=t: e.tensor_tensor(
                out=oh, in0=rcf[:, t, :].unsqueeze(2).broadcast_to([128, 128, 16]),
                in1=iota16.unsqueeze(1).broadcast_to([128, 128, 16]), op=ALU.is_equal),
                 reads=[b_rcf, b_iota], writes=[b_oh])
            for h in range(8):
                S.op("dve", lambda e, h=h, t=t: e.tensor_tensor(
                    out=oh4[:, h], in0=oh4[:, h], in1=sif4[:, h, t, :].unsqueeze(1).broadcast_to([128, 16, 16]),
                    op=ALU.mult), reads=[b_oh, b_sif], writes=[b_oh])
            S.op("dve", lambda e, t=t: e.tensor_reduce(out=i12[:, t, :], in_=oh, axis=AX.X, op=ALU.add),
                 reads=[b_oh], writes=[b_i12])
        S.op("dve", lambda e: e.scalar_tensor_tensor(out=ef, in0=i12[:, 0, :], scalar=128.0, in1=i12[:, 1, :],
                                                     op0=ALU.mult, op1=ALU.add), reads=[b_i12], writes=[b_ef])
        S.op("dve", lambda e, eib=eib: e.tensor_copy(out=eib, in_=ef), reads=[b_ef], writes=[beib])
        S.op("dve", lambda e: e.tensor_tensor(out=gsm, in0=cv, in1=cv[:, :, 0:1].broadcast_to([128, 8, 16]),
                                              op=ALU.subtract), reads=[b_cv], writes=[b_gsm])
        S.op("act", lambda e: e.activation(out=gsm, in_=gsm, func=AF.Exp), reads=[b_gsm], writes=[b_gsm])
        S.op("dve", lambda e: e.tensor_reduce(out=small[:, 8:16], in_=gsm, axis=AX.X, op=ALU.add),
             reads=[b_gsm], writes=[b_small])
        S.op("dve", lambda e: e.reciprocal(out=small[:, 16:24], in_=small[:, 8:16]), reads=[b_small], writes=[b_small])
        S.op("dve", lambda e: e.tensor_tensor(out=gsm, in0=gsm,
                                              in1=small[:, 16:24].unsqueeze(2).broadcast_to([128, 8, 16]),
                                              op=ALU.mult), reads=[b_gsm, b_small], writes=[b_gsm])
        gsf = gsm.rearrange("p h k -> p (h k)")
        for k in range(128):
            j = gcount[0] % NB; gcount[0] += 1
            r8 = k % 8; r4 = k % 4
            S.dma("pool", lambda e, k=k, j=j, eib=eib: e.indirect_dma_start(
                out=uvb[j], out_offset=None, in_=uv16,
                in_offset=bass.IndirectOffsetOnAxis(ap=eib[:, k:k + 1], axis=0)),
                  reads=[beib, b_uv16], writes=[b_uvb[j]])
            S.op("dve", lambda e, k=k, j=j: e.scalar_tensor_tensor(
                out=uvb[j][:, 0:1024], in0=uvb[j][:, 0:1024], scalar=1.0, in1=hb, op0=ALU.mult, op1=ALU.mult,
                accum_out=araw[:, k:k + 1]), reads=[b_hb], writes=[b_uvb[j], b_ar[r8]])
            S.op("act", lambda e, k=k: e.activation(out=gl[:, k:k + 1], in_=araw[:, k:k + 1], func=AF.Gelu),
                 reads=[b_ar[r8]], writes=[b_gl[r8]])
            S.op("dve", lambda e, k=k, r4=r4: e.tensor_scalar(out=dg[r4], in0=ident, scalar1=gl[:, k:k + 1],
                                                            scalar2=gsf[:, k:k + 1], op0=ALU.mult, op1=ALU.mult),
                 reads=[b_ident, b_gl[r8], b_gsm], writes=[b_dg[r4]])
            for hf in range(2):
                S.op("pe", lambda e, k=k, j=j, r4=r4, hf=hf: e.matmul(
                    pacc[:, hf * 512:(hf + 1) * 512], lhsT=dg[r4], rhs=uvb[j][:, 1024 + hf * 512:1024 + (hf + 1) * 512],
                    start=(k == 0), stop=(k == 127)), reads=[b_dg[r4], b_uvb[j]], writes=[b_pacc])
        S.op("dve", lambda e, xb=xb: e.tensor_tensor(out=yo, in0=pacc, in1=xb, op=ALU.add),
             reads=[b_pacc, bxb], writes=[b_yo])
        S.dma("sp", lambda e, ti=ti: e.dma_start(out=y_tiles[ti], in_=yo), reads=[b_yo])


def phase_b3(S, nc, sb, ps, C, x1_tiles, y_tiles, uv16, b_uv16, NB=14):
    ident = sb.alloc([128], BF16); b_ident = Buf()
    iota16 = sb.alloc([16], F32); b_iota = Buf()
    cst = sb.alloc([144], F32); b_cst = Buf()
    wq = sb.alloc([8, 2048], BF16); b_wq = Buf()
    skT = sb.alloc([16, 128], BF16); b_skT = Buf()
    gffn = sb.alloc([1024], F32); b_gffn = Buf()
    S.dma("sp", lambda e: e.dma_start(out=cst, in_=C["consts"]), writes=[b_cst])
    S.op("dve", lambda e: e.tensor_copy(out=ident, in_=cst[:, 0:128]), reads=[b_cst], writes=[b_ident])
    S.op("dve", lambda e: e.tensor_copy(out=iota16, in_=cst[:, 128:144]), reads=[b_cst], writes=[b_iota])
    S.dma("sp", lambda e: e.dma_start(out=gffn, in_=C["g_ffn"].partition_broadcast(128)), writes=[b_gffn])
    stg = [sb.alloc([2048], F32) for _ in range(2)]
    b_stg = [Buf(), Buf()]
    wqv = C["w_query"].rearrange("(c p) n -> c p n", p=128)
    for c in range(8):
        S.dma("sp", lambda e, c=c: e.dma_start(out=stg[c % 2], in_=wqv[c]), writes=[b_stg[c % 2]])
        S.op("act" if c % 2 else "dve",
             (lambda e, c=c: e.copy(out=wq[:, c, :], in_=stg[c % 2])) if c % 2 else
             (lambda e, c=c: e.tensor_copy(out=wq[:, c, :], in_=stg[c % 2])),
             reads=[b_stg[c % 2]], writes=[b_wq])
    skv = C["sub_keys"].rearrange("g k d -> k g d")
    skn = stg[0].rearrange("p (g d) -> p g d", g=16)
    skb = sb.alloc([16, 128], BF16); b_skb = Buf()
    pbig = ps.alloc([16, 128], F32, align=512); b_pbig = Buf()
    pT = ps.alloc([1024], BF16, align=512); b_pT = Buf()
    pTv = pT.rearrange("p (c n) -> p c n", c=8)
    S.dma("sp", lambda e: e.dma_start(out=skn, in_=skv), writes=[b_stg[0]])
    S.op("dve", lambda e: e.tensor_copy(out=skb, in_=skn), reads=[b_stg[0]], writes=[b_skb])
    for half in range(2):
        for j in range(8):
            g = half * 8 + j
            S.op("pe", lambda e, g=g, j=j: e.transpose(out=pTv[:, j, :], in_=skb[:, g, :], identity=ident),
                 reads=[b_skb, b_ident], writes=[b_pT])
        S.op("dve", lambda e, half=half: e.tensor_copy(out=skT[:, half * 8:(half + 1) * 8, :], in_=pTv),
             reads=[b_pT], writes=[b_skT])

    x1t = [sb.alloc([1024], F32) for _ in range(2)]; b_x1t = [Buf(), Buf()]
    junk = sb.alloc([1024], F32); b_junk = Buf()
    hb2 = [sb.alloc([1024], BF16) for _ in range(2)]; b_hb2 = [Buf(), Buf()]
    hT = sb.alloc([8, 128], BF16); b_hT = Buf()
    qT = sb.alloc([16, 128], BF16); b_qT = Buf()
    s = sb.alloc([16, 128], F32); b_s = Buf()
    s2 = sb.alloc([16, 128], F32); b_s2 = Buf()
    sv = sb.alloc([16, 16], F32); b_sv = Buf()
    siu = sb.alloc([16, 16], U32); b_siu = Buf()
    sif = sb.alloc([16, 16], F32); b_sif = Buf()
    cand = sb.alloc([8, 256], F32); b_cand = Buf()
    cand2 = sb.alloc([8, 256], F32); b_cand2 = Buf()
    cv = sb.alloc([8, 16], F32); b_cv = Buf()
    ciu = sb.alloc([8, 16], U32); b_ciu = Buf()
    rcu = sb.alloc([2, 128], U32); b_rcu = Buf()
    rcf = sb.alloc([2, 128], F32); b_rcf = Buf()
    oh = sb.alloc([128, 16], F32); b_oh = Buf()
    i12 = sb.alloc([2, 128], F32); b_i12 = Buf()
    ef = sb.alloc([128], F32); b_ef = Buf()
    ei = [sb.alloc([128], I32) for _ in range(2)]; b_ei = [Buf(), Buf()]
    araw = sb.alloc([128], F32)
    gsm2 = [sb.alloc([8, 16], F32) for _ in range(2)]; b_gsm2 = [Buf(), Buf()]
    small = sb.alloc([32], F32); b_small = Buf()
    yo = sb.alloc([1024], F32); b_yo = Buf()
    uvb = [sb.alloc([2048], BF16) for _ in range(NB)]; b_uvb = [Buf() for _ in range(NB)]
    dg = [sb.alloc([128], BF16) for _ in range(4)]; b_dg = [Buf() for _ in range(4)]
    gl = sb.alloc([128], F32); b_gl = [Buf() for _ in range(8)]
    b_ar = [Buf() for _ in range(8)]
    pacc = ps.alloc([1024], F32, align=512); b_pacc = Buf()
    gcount = [0, 0]

    sv4 = sv.rearrange("p (h t) k -> p h t k", t=2)
    sif4 = sif.rearrange("p (h t) k -> p h t k", t=2)
    cand4 = cand.rearrange("p h (i j) -> p h i j", i=16)
    oh4 = oh.rearrange("p (h k) i -> p h k i", h=8)

    def route_gen(ti):
        xb = x1t[ti % 2]; bxb = b_x1t[ti % 2]
        eib = ei[ti % 2]; beib = b_ei[ti % 2]
        hb = hb2[ti % 2]; b_hb = b_hb2[ti % 2]
        gsm = gsm2[ti % 2]; b_gsm = b_gsm2[ti % 2]
        S.dma("sp", lambda e, xb=xb, ti=ti: e.dma_start(out=xb, in_=x1_tiles[ti]), writes=[bxb])
        yield
        S.op("act", lambda e, xb=xb: e.activation(out=junk, in_=xb, func=AF.Square, accum_out=small[:, 0:1]),
             reads=[bxb], writes=[b_junk, b_small])
        yield
        S.op("act", lambda e: e.activation(out=small[:, 1:2], in_=small[:, 0:1], func=AF.Sqrt, bias=EPS, scale=1.0 / D),
             reads=[b_small], writes=[b_small])
        yield
        S.op("dve", lambda e: e.reciprocal(out=small[:, 2:3], in_=small[:, 1:2]), reads=[b_small], writes=[b_small])
        yield
        S.op("dve", lambda e, xb=xb: e.scalar_tensor_tensor(out=hb, in0=xb, scalar=small[:, 2:3], in1=gffn,
                                                            op0=ALU.mult, op1=ALU.mult),
             reads=[bxb, b_small, b_gffn], writes=[b_hb])
        yield
        for c in range(8):
            S.op("pe", lambda e, c=c: e.transpose(out=pTv[:, c, :], in_=hb[:, c * 128:(c + 1) * 128], identity=ident),
                 reads=[b_hb, b_ident], writes=[b_pT])
            yield
        S.op("act", lambda e: e.copy(out=hT, in_=pTv), reads=[b_pT], writes=[b_hT])
        yield
        for g in range(16):
            for c in range(8):
                S.op("pe", lambda e, g=g, c=c: e.matmul(pbig[:, g, :], lhsT=wq[:, c, g * 128:(g + 1) * 128],
                                                        rhs=hT[:, c, :], start=(c == 0), stop=(c == 7)),
                     reads=[b_wq, b_hT], writes=[b_pbig])
                yield
        S.op("act", lambda e: e.copy(out=qT, in_=pbig), reads=[b_pbig], writes=[b_qT])
        yield
        for g in range(16):
            S.op("pe", lambda e, g=g: e.matmul(pbig[:, g, :], lhsT=qT[:, g, :], rhs=skT[:, g, :], start=True, stop=True),
                 reads=[b_qT, b_skT], writes=[b_pbig])
            yield
        S.op("dve", lambda e: e.tensor_copy(out=s, in_=pbig), reads=[b_pbig], writes=[b_s])
        yield
        for g in range(16):
            S.op("dve", lambda e, g=g: e.max(out=sv[:, g, 0:8], in_=s[:, g, :]), reads=[b_s], writes=[b_sv])
            yield
            S.op("dve", lambda e, g=g: e.match_replace(out=s2[:, g, :], in_to_replace=sv[:, g, 0:8],
                                                       in_values=s[:, g, :], imm_value=-1e30),
                 reads=[b_s, b_sv], writes=[b_s2])
            yield
            S.op("dve", lambda e, g=g: e.max(out=sv[:, g, 8:16], in_=s2[:, g, :]), reads=[b_s2], writes=[b_sv])
            yield
            S.op("dve", lambda e, g=g: e.max_index(out=siu[:, g, 0:8], in_max=sv[:, g, 0:8], in_values=s[:, g, :]),
                 reads=[b_s, b_sv], writes=[b_siu])
            yield
            S.op("dve", lambda e, g=g: e.max_index(out=siu[:, g, 8:16], in_max=sv[:, g, 8:16], in_values=s[:, g, :]),
                 reads=[b_s, b_sv], writes=[b_siu])
            yield
        S.op("dve", lambda e: e.tensor_copy(out=sif, in_=siu), reads=[b_siu], writes=[b_sif])
        yield
        for h in range(8):
            S.op("dve", lambda e, h=h: e.tensor_tensor(
                out=cand4[:, h], in0=sv4[:, h, 0, :].unsqueeze(2).broadcast_to([128, 16, 16]),
                in1=sv4[:, h, 1, :].unsqueeze(1).broadcast_to([128, 16, 16]), op=ALU.add),
                 reads=[b_sv], writes=[b_cand])
            yield
        for h in range(8):
            S.op("dve", lambda e, h=h: e.max(out=cv[:, h, 0:8], in_=cand[:, h, :]), reads=[b_cand], writes=[b_cv])
            yield
            S.op("dve", lambda e, h=h: e.match_replace(out=cand2[:, h, :], in_to_replace=cv[:, h, 0:8],
                                                       in_values=cand[:, h, :], imm_value=-1e30),
                 reads=[b_cand, b_cv], writes=[b_cand2])
            yield
            S.op("dve", lambda e, h=h: e.max(out=cv[:, h, 8:16], in_=cand2[:, h, :]), reads=[b_cand2], writes=[b_cv])
            yield
            S.op("dve", lambda e, h=h: e.max_index(out=ciu[:, h, 0:8], in_max=cv[:, h, 0:8], in_values=cand[:, h, :]),
                 reads=[b_cand, b_cv], writes=[b_ciu])
            yield
            S.op("dve", lambda e, h=h: e.max_index(out=ciu[:, h, 8:16], in_max=cv[:, h, 8:16], in_values=cand[:, h, :]),
                 reads=[b_cand, b_cv], writes=[b_ciu])
            yield
        ciu_f = ciu.rearrange("p h k -> p (h k)")
        S.op("dve", lambda e: e.tensor_single_scalar(out=rcu[:, 0, :], in_=ciu_f, scalar=4, op=ALU.logical_shift_right),
             reads=[b_ciu], writes=[b_rcu])
        yield
        S.op("dve", lambda e: e.tensor_single_scalar(out=rcu[:, 1, :], in_=ciu_f, scalar=15, op=ALU.bitwise_and),
             reads=[b_ciu], writes=[b_rcu])
        yield
        S.op("dve", lambda e: e.tensor_copy(out=rcf, in_=rcu), reads=[b_rcu], writes=[b_rcf])
        yield
        for t in range(2):
            S.op("dve", lambda e, t=t: e.tensor_tensor(
                out=oh, in0=rcf[:, t, :].unsqueeze(2).broadcast_to([128, 128, 16]),
                in1=iota16.unsqueeze(1).broadcast_to([128, 128, 16]), op=ALU.is_equal),
                 reads=[b_rcf, b_iota], writes=[b_oh])
            yield
            for h in range(8):
                S.op("dve", lambda e, h=h, t=t: e.tensor_tensor(
                    out=oh4[:, h], in0=oh4[:, h], in1=sif4[:, h, t, :].unsqueeze(1).broadcast_to([128, 16, 16]),
                    op=ALU.mult), reads=[b_oh, b_sif], writes=[b_oh])
                yield
            S.op("dve", lambda e, t=t: e.tensor_reduce(out=i12[:, t, :], in_=oh, axis=AX.X, op=ALU.add),
                 reads=[b_oh], writes=[b_i12])
            yield
        S.op("dve", lambda e: e.scalar_tensor_tensor(out=ef, in0=i12[:, 0, :], scalar=128.0, in1=i12[:, 1, :],
                                                     op0=ALU.mult, op1=ALU.add), reads=[b_i12], writes=[b_ef])
        yield
        S.op("dve", lambda e, eib=eib: e.tensor_copy(out=eib, in_=ef), reads=[b_ef], writes=[beib])
        yield
        S.op("dve", lambda e: e.tensor_tensor(out=gsm, in0=cv, in1=cv[:, :, 0:1].broadcast_to([128, 8, 16]),
                                              op=ALU.subtract), reads=[b_cv], writes=[b_gsm])
        yield
        S.op("act", lambda e: e.activation(out=gsm, in_=gsm, func=AF.Exp), reads=[b_gsm], writes=[b_gsm])
        yield
        S.op("dve", lambda e: e.tensor_reduce(out=small[:, 8:16], in_=gsm, axis=AX.X, op=ALU.add),
             reads=[b_gsm], writes=[b_small])
        yield
        S.op("dve", lambda e: e.reciprocal(out=small[:, 16:24], in_=small[:, 8:16]), reads=[b_small], writes=[b_small])
        yield
        S.op("dve", lambda e: e.tensor_tensor(out=gsm, in0=gsm,
                                              in1=small[:, 16:24].unsqueeze(2).broadcast_to([128, 8, 16]),
                                              op=ALU.mult), reads=[b_gsm, b_small], writes=[b_gsm])
        yield

    def slots(ti, filler, rate):
        xb = x1t[ti % 2]; bxb = b_x1t[ti % 2]
        eib = ei[ti % 2]; beib = b_ei[ti % 2]
        hb = hb2[ti % 2]; b_hb = b_hb2[ti % 2]
        gsm = gsm2[ti % 2]; b_gsm = b_gsm2[ti % 2]
        gsf = gsm.rearrange("p h k -> p (h k)")
        LAG = 3
        jj = {}
        for kx in range(128 + LAG):
            if kx < 128:
                k = kx
                j = gcount[0] % NB; gcount[0] += 1
                jj[k] = j
                r8 = k % 8
                S.dma("pool", lambda e, k=k, j=j, eib=eib: e.indirect_dma_start(
                    out=uvb[j], out_offset=None, in_=uv16,
                    in_offset=bass.IndirectOffsetOnAxis(ap=eib[:, k:k + 1], axis=0)),
                      reads=[beib, b_uv16], writes=[b_uvb[j]])
                S.op("dve", lambda e, k=k, j=j: e.scalar_tensor_tensor(
                    out=uvb[j][:, 0:1024], in0=uvb[j][:, 0:1024], scalar=1.0, in1=hb, op0=ALU.mult, op1=ALU.mult,
                    accum_out=araw[:, k:k + 1]), reads=[b_hb], writes=[b_uvb[j], b_ar[r8]])
                S.op("act", lambda e, k=k: e.activation(out=gl[:, k:k + 1], in_=araw[:, k:k + 1], func=AF.Gelu),
                     reads=[b_ar[r8]], writes=[b_gl[r8]])
                if filler is not None:
                    next(filler, None)
            if kx >= LAG:
                k = kx - LAG
                j = jj[k]
                r8 = k % 8; r4 = k % 4
                S.op("dve", lambda e, k=k, r4=r4: e.tensor_scalar(out=dg[r4], in0=ident, scalar1=gl[:, k:k + 1],
                                                                scalar2=gsf[:, k:k + 1], op0=ALU.mult, op1=ALU.mult),
                     reads=[b_ident, b_gl[r8], b_gsm], writes=[b_dg[r4]])
                for hf in range(2):
                    S.op("pe", lambda e, k=k, j=j, r4=r4, hf=hf: e.matmul(
                        pacc[:, hf * 512:(hf + 1) * 512], lhsT=dg[r4], rhs=uvb[j][:, 1024 + hf * 512:1024 + (hf + 1) * 512],
                        start=(k == 0), stop=(k == 127)), reads=[b_dg[r4], b_uvb[j]], writes=[b_pacc])
            if filler is not None:
                for _ in range(rate - 1):
                    next(filler, None)
        S.op("dve", lambda e, xb=xb: e.tensor_tensor(out=yo, in0=pacc, in1=xb, op=ALU.add),
             reads=[b_pacc, bxb], writes=[b_yo])
        S.dma("sp", lambda e, ti=ti: e.dma_start(out=y_tiles[ti], in_=yo), reads=[b_yo])


    NT = len(x1_tiles)
    g = route_gen(0)
    for _ in g:
        pass
    for ti in range(NT):
        nxt = route_gen(ti + 1) if ti + 1 < NT else None
        slots(ti, nxt, 3)
        if nxt is not None:
            for _ in nxt:
                pass


def convert_tables(S, C, uv16, b_uv16, nsplit=8):
    rows = NEXP // nsplit
    for t, name in enumerate(("expert_u", "expert_v")):
        for i in range(nsplit):
            S.dma("pool", lambda e, t=t, i=i, name=name: e.dma_start(
                out=uv16[i * rows:(i + 1) * rows, t * 1024:(t + 1) * 1024], in_=C[name][i * rows:(i + 1) * rows, :]),
                  writes=[b_uv16])


def make_consts():
    c = np.zeros((128, 144), np.float32)
    c[:, :128] = np.eye(128, dtype=np.float32)
    c[:, 128:144] = np.arange(16, dtype=np.float32)[None, :]
    return c


PP_GMIX, PP_GOUT, PP_CB, PP_LNG, PP_LNB, PP_GQ, PP_GK, PP_CW, PP_N = 0, 8, 16, 20, 24, 28, 29, 32, 160
RING = 8
URING = 6
QRING = 5


def key_tiles(s, NQ):
    kts = list(range(s - 2, s + 3))
    if s == 2:
        kts.append(5)
    if s == NQ + 1:
        kts.insert(0, NQ - 2)
    return kts


def phase_a(S, nc, sb, ps, C, NQ, xs_tiles, x1_tiles):
    NS = NQ + 4
    nslab = len(xs_tiles)
    cst = sb.alloc([144], F32); b_cst = Buf()
    ident = sb.alloc([128], BF16); b_ident = Buf()
    onesf = sb.alloc([128], F32); b_onesf = Buf()
    blk = sb.alloc([128], BF16); b_blk = Buf()
    pp = sb.alloc([PP_N], F32); b_pp = Buf()
    gq8 = sb.alloc([1], F32); b_gq8 = Buf()
    win = sb.alloc([8, 2560], BF16); b_win = Buf()
    wout = sb.alloc([8, 1024], BF16); b_wout = Buf()
    T = sb.alloc([8, 16, 64], F32); b_T = Buf()
    ntile_idx = {}
    for s in range(2, NQ + 2):
        for kt in key_tiles(s, NQ):
            ntile_idx[(s, kt)] = len(ntile_idx)
    NTI = len(ntile_idx)
    rmask = sb.alloc([nslab, NTI, 2], F32); b_rmask = Buf()
    S.dma("sp", lambda e: e.dma_start(out=cst, in_=C["consts"]), writes=[b_cst])
    S.dma("sp", lambda e: e.dma_start(out=pp, in_=C["pp"]), writes=[b_pp])
    S.dma("sp", lambda e: e.dma_start(out=T.rearrange("p h j c -> p (h j c)"), in_=C["ttab"]), writes=[b_T])
    S.dma("sp", lambda e: e.dma_start(out=rmask.rearrange("p a b c -> p (a b c)"), in_=C["rmask"]), writes=[b_rmask])
    S.op("dve", lambda e: e.tensor_copy(out=ident, in_=cst[:, 0:128]), reads=[b_cst], writes=[b_ident])
    S.op("dve", lambda e: e.memset(onesf, 1.0), writes=[b_onesf])
    S.op("dve", lambda e: e.memset(blk, 0.0), writes=[b_blk])
    S.op("dve", lambda e: e.memset(blk[0:64, 0:64], 1.0), writes=[b_blk])
    S.op("dve", lambda e: e.memset(blk[64:128, 64:128], 1.0), writes=[b_blk])
    S.op("dve", lambda e: e.tensor_scalar(out=gq8, in0=pp[:, PP_GQ:PP_GQ + 1], scalar1=0.125, scalar2=None, op0=ALU.mult),
         reads=[b_pp], writes=[b_gq8])
    stg = [sb.alloc([1280], F32) for _ in range(2)]; b_stg = [Buf(), Buf()]
    winv = C["w_in"].rearrange("(c p) n -> c p n", p=128)
    woutv = C["w_out"].rearrange("(c p) n -> c p n", p=128)
    n = 0
    for c in range(8):
        for hf in range(2):
            j = n % 2; n += 1
            S.dma("sp", lambda e, c=c, hf=hf, j=j: e.dma_start(out=stg[j], in_=winv[c][:, hf * 1280:(hf + 1) * 1280]),
                  writes=[b_stg[j]])
            S.op("dve", lambda e, c=c, hf=hf, j=j: e.tensor_scalar(
                out=win[:, c, hf * 1280:(hf + 1) * 1280], in0=stg[j], scalar1=pp[:, PP_GMIX + c:PP_GMIX + c + 1],
                scalar2=None, op0=ALU.mult), reads=[b_stg[j], b_pp], writes=[b_win])
    for c in range(8):
        j = n % 2; n += 1
        S.dma("sp", lambda e, c=c, j=j: e.dma_start(out=stg[j][:, 0:1024], in_=woutv[c]), writes=[b_stg[j]])
        S.op("dve", lambda e, c=c, j=j: e.tensor_scalar(
            out=wout[:, c, :], in0=stg[j][:, 0:1024], scalar1=pp[:, PP_GOUT + c:PP_GOUT + c + 1],
            scalar2=None, op0=ALU.mult), reads=[b_stg[j], b_pp], writes=[b_wout])
    cw = pp[:, PP_CW:PP_CW + 124].rearrange("p (c k) -> p c k", c=4)

    kT = [sb.alloc([4, 128], BF16) for _ in range(RING)]; b_kT = [Buf() for _ in range(RING)]
    va = [sb.alloc([8, 65], BF16) for _ in range(RING)]; b_va = [Buf() for _ in range(RING)]
    uT = [sb.alloc([4, 160], F32) for _ in range(URING)]; b_uT = [Buf() for _ in range(URING)]
    qA = [sb.alloc([4, 128], BF16) for _ in range(QRING)]; b_qA = [Buf() for _ in range(QRING)]
    qB = [sb.alloc([4, 128], BF16) for _ in range(QRING)]; b_qB = [Buf() for _ in range(QRING)]
    for r in range(RING):
        S.op("pool", lambda e, r=r: e.memset(va[r], 1.0), writes=[b_va[r]])
    for r in range(QRING):
        S.op("pool", lambda e, r=r: e.memset(qA[r], 0.0), writes=[b_qA[r]])
        S.op("pool", lambda e, r=r: e.memset(qB[r], 0.0), writes=[b_qB[r]])
    for r in range(URING):
        S.op("pool", lambda e, r=r: e.memset(uT[r], 0.0), writes=[b_uT[r]])
    xt = [sb.alloc([1024], F32) for _ in range(2)]; b_xt = [Buf(), Buf()]
    xr = [sb.alloc([1024], F32) for _ in range(2)]; b_xr = [Buf(), Buf()]
    junk = sb.alloc([1024], BF16); b_junk = Buf()
    small = sb.alloc([16], F32); b_small = Buf()
    xn = sb.alloc([1024], BF16); b_xn = Buf()
    hT = sb.alloc([8, 128], BF16); b_hT = Buf()
    qkr = sb.alloc([8, 128], F32); b_qkr = Buf()
    sq = sb.alloc([8, 128], BF16); b_sq = Buf()
    qks = sb.alloc([8, 128], F32); b_qks = Buf()
    sg = sb.alloc([4, 128], F32); b_sg = Buf()
    NTMP = 4
    tmp = [sb.alloc([128], F32) for _ in range(NTMP)]; b_tmp = [Buf() for _ in range(NTMP)]
    PT = [sb.alloc([128], BF16) for _ in range(NTMP)]; b_PT = [Buf() for _ in range(NTMP)]
    ya = sb.alloc([8, 64], F32); b_ya = Buf()
    yan = sb.alloc([512], BF16); b_yan = Buf()
    yaT = sb.alloc([4, 128], BF16); b_yaT = Buf()
    yconv = sb.alloc([4, 128], F32); b_yc = [Buf() for _ in range(4)]
    csq = sb.alloc([4, 128], F32); b_csq = Buf()
    ctmp = sb.alloc([128], F32); b_ctmp = Buf()
    st = sb.alloc([4, 128], F32); b_st = Buf()
    z = sb.alloc([4, 128], F32); b_z = Buf()
    ycn2 = [sb.alloc([4, 128], BF16) for _ in range(2)]; b_ycn2 = [Buf(), Buf()]
    bank0 = ps.alloc([512], F32, align=512); b_b0 = Buf()
    pT = bank0.bitcast(BF16); b_pT = b_b0
    pTv = pT.rearrange("p (c n) -> p c n", c=8)
    pV = bank0; b_pV = b_b0
    pX = ps.alloc([8, 128], F32, align=512); b_pX = Buf()
    pS = ps.alloc([8, 128], F32, align=512); b_pS = Buf()
    pY = pS.rearrange("p a b -> p (a b)")
    pMa = ps.alloc([4, 128], F32, align=512)
    pMb = ps.alloc([4, 128], F32, align=512)
    b_bk5 = Buf(); b_bk6 = Buf()
    pMs = [pMa[:, 0, :], pMb[:, 0, :], pMa[:, 2, :]]; b_pMs = [b_bk5, b_bk6, b_bk5]
    pO = ps.alloc([4, 128], F32, align=512); b_pO = Buf()
    pM = pO; b_pM = [b_pO, b_pO, b_pO, b_pO]

    def stage_p_gen(sl, s):
        x = xt[s % 2]; bx = b_xt[s % 2]
        r = s % RING
        S.dma("sp", lambda e: e.dma_start(out=x, in_=xs_tiles[sl][s]), writes=[bx])
        yield
        S.op("act", lambda e: e.activation(out=junk, in_=x, func=AF.Square, accum_out=small[:, 0:1]),
             reads=[bx], writes=[b_junk, b_small])
        yield
        S.op("act", lambda e: e.activation(out=small[:, 1:2], in_=small[:, 0:1], func=AF.Sqrt, bias=EPS, scale=1.0 / D),
             reads=[b_small], writes=[b_small])
        yield
        S.op("dve", lambda e: e.reciprocal(out=small[:, 2:3], in_=small[:, 1:2]), reads=[b_small], writes=[b_small])
        yield
        S.op("dve", lambda e: e.tensor_scalar(out=xn, in0=x, scalar1=small[:, 2:3], scalar2=None, op0=ALU.mult),
             reads=[bx, b_small], writes=[b_xn])
        yield
        for c in range(8):
            S.op("pe", lambda e, c=c: e.transpose(out=pTv[:, c, :], in_=xn[:, c * 128:(c + 1) * 128], identity=ident),
                 reads=[b_xn, b_ident], writes=[b_pT])
            yield
        S.op("act", lambda e: e.copy(out=hT, in_=pTv), reads=[b_pT], writes=[b_hT])
        yield
        for j in range(8):
            for c in range(8):
                S.op("pe", lambda e, j=j, c=c: e.matmul(pX[:, j, :], lhsT=win[:, c, j * 128:(j + 1) * 128], rhs=hT[:, c, :],
                                                        start=(c == 0), stop=(c == 7)),
                     reads=[b_win, b_hT], writes=[b_pX])
                yield
        ur = s % URING
        S.op("act", lambda e: e.activation(out=sg, in_=pX[:, 4:8, :], func=AF.Sigmoid), reads=[b_pX], writes=[b_sg])
        yield
        S.op("dve", lambda e: e.tensor_tensor(out=uT[ur][:, :, 16:144], in0=pX[:, 0:4, :], in1=sg, op=ALU.mult),
             reads=[b_pX, b_sg], writes=[b_uT[ur]])
        yield
        if s >= 1:
            up = (s - 1) % URING
            S.op("pool", lambda e: e.tensor_copy(out=uT[up][:, :, 144:159], in_=uT[ur][:, :, 16:31]),
                 reads=[b_uT[ur]], writes=[b_uT[up]])
            yield
        if s + 1 < NS:
            un = (s + 1) % URING
            S.op("pool", lambda e: e.tensor_copy(out=uT[un][:, :, 1:16], in_=uT[ur][:, :, 129:144]),
                 reads=[b_uT[ur]], writes=[b_uT[un]])
            yield
        for j in range(8):
            for c in range(8):
                S.op("pe", lambda e, j=j, c=c: e.matmul(pX[:, j, :], lhsT=win[:, c, 1024 + j * 128:1024 + (j + 1) * 128],
                                                        rhs=hT[:, c, :], start=(c == 0), stop=(c == 7)),
                     reads=[b_win, b_hT], writes=[b_pX])
                yield
        S.op("act", lambda e: e.copy(out=qkr, in_=pX), reads=[b_pX], writes=[b_qkr])
        yield
        S.op("act", lambda e: e.activation(out=sq, in_=qkr, func=AF.Square), reads=[b_qkr], writes=[b_sq])
        yield
        for j in range(8):
            S.op("pe", lambda e, j=j: e.matmul(pS[:, j, :], lhsT=blk, rhs=sq[:, j, :], start=True, stop=True),
                 reads=[b_blk, b_sq], writes=[b_pS])
            yield
        S.op("act", lambda e: e.activation(out=qks, in_=pS, func=AF.Sqrt, bias=EPS, scale=1.0 / 64), reads=[b_pS], writes=[b_qks])
        yield
        S.op("dve", lambda e: e.reciprocal(out=qks, in_=qks), reads=[b_qks], writes=[b_qks])
        yield
        S.op("dve", lambda e: e.tensor_tensor(out=qkr, in0=qkr, in1=qks, op=ALU.mult), reads=[b_qkr, b_qks], writes=[b_qkr])
        yield
        S.op("dve", lambda e: e.tensor_scalar(out=kT[r], in0=qkr[:, 4:8, :], scalar1=pp[:, PP_GK:PP_GK + 1], scalar2=None,
                                              op0=ALU.mult), reads=[b_qkr, b_pp], writes=[b_kT[r]])
        yield
        if 2 <= s < NQ + 2:
            qr = s % QRING
            S.op("dve", lambda e: e.tensor_scalar(out=qA[qr][0:64], in0=qkr[0:64, 0:4, :], scalar1=gq8[0:64], scalar2=None,
                                                  op0=ALU.mult), reads=[b_qkr, b_gq8], writes=[b_qA[qr]])
            yield
            S.op("dve", lambda e: e.tensor_scalar(out=qB[qr][64:128], in0=qkr[64:128, 0:4, :], scalar1=gq8[64:128],
                                                  scalar2=None, op0=ALU.mult), reads=[b_qkr, b_gq8], writes=[b_qB[qr]])
            yield
        for c in range(8):
            S.op("pe", lambda e, c=c: e.matmul(pV, lhsT=hT[:, c, :], rhs=win[:, c, 2048:2560], start=(c == 0), stop=(c == 7)),
                 reads=[b_win, b_hT], writes=[b_pV])
            yield
        S.op("act", lambda e: e.copy(out=va[r][:, :, 0:64], in_=pV.rearrange("p (h d) -> p h d", h=8)),
             reads=[b_pV], writes=[b_va[r]])
        yield

    cnt = [0]

    def m_conv_gen(sl, s):
        ur = s % URING
        ycn_s = ycn2[s % 2]; b_ycn_s = b_ycn2[s % 2]
        for c in range(4):
            S.op("dve", lambda e, c=c: e.tensor_scalar(out=yconv[:, c, :], in0=uT[ur][:, c, 1:129], scalar1=cw[:, c, 0:1],
                                                       scalar2=pp[:, PP_CB + c:PP_CB + c + 1], op0=ALU.mult, op1=ALU.add),
                 reads=[b_uT[ur], b_pp], writes=[b_yc[c]])
            yield
        for k in range(1, 31):
            for c in range(4):
                S.op("dve", lambda e, c=c, k=k: e.scalar_tensor_tensor(
                    out=yconv[:, c, :], in0=uT[ur][:, c, k + 1:k + 129], scalar=cw[:, c, k:k + 1], in1=yconv[:, c, :],
                    op0=ALU.mult, op1=ALU.add), reads=[b_uT[ur], b_pp], writes=[b_yc[c]])
                yield
        S.op("act", lambda e: e.activation(out=csq, in_=yconv, func=AF.Square), reads=b_yc, writes=[b_csq])
        yield
        for c in range(4):
            S.op("pe", lambda e, c=c: e.matmul(pS[:, 0, :], lhsT=onesf, rhs=yconv[:, c, :], start=(c == 0), stop=(c == 3)),
                 reads=[b_onesf] + b_yc, writes=[b_pS])
            yield
        for c in range(4):
            S.op("pe", lambda e, c=c: e.matmul(pS[:, 1, :], lhsT=onesf, rhs=csq[:, c, :], start=(c == 0), stop=(c == 3)),
                 reads=[b_onesf, b_csq], writes=[b_pS])
            yield
        S.op("dve", lambda e: e.tensor_scalar(out=st[:, 0, :], in0=pS[:, 0, :], scalar1=1.0 / 512, scalar2=None, op0=ALU.mult),
             reads=[b_pS], writes=[b_st])
        yield
        S.op("dve", lambda e: e.tensor_tensor(out=st[:, 1, :], in0=st[:, 0, :], in1=st[:, 0, :], op=ALU.mult),
             reads=[b_st], writes=[b_st])
        yield
        S.op("dve", lambda e: e.scalar_tensor_tensor(out=st[:, 2, :], in0=pS[:, 1, :], scalar=1.0 / 512, in1=st[:, 1, :],
                                                     op0=ALU.mult, op1=ALU.subtract), reads=[b_pS, b_st], writes=[b_st])
        yield
        S.op("act", lambda e: e.activation(out=st[:, 3, :], in_=st[:, 2, :], func=AF.Sqrt, bias=EPS, scale=1.0),
             reads=[b_st], writes=[b_st])
        yield
        S.op("dve", lambda e: e.reciprocal(out=st[:, 3, :], in_=st[:, 3, :]), reads=[b_st], writes=[b_st])
        yield
        S.op("dve", lambda e: e.tensor_tensor(out=z, in0=yconv, in1=st[:, 0, :].unsqueeze(1).broadcast_to([128, 4, 128]),
                                              op=ALU.subtract), reads=b_yc + [b_st], writes=[b_z])
        yield
        S.op("dve", lambda e: e.tensor_tensor(out=z, in0=z, in1=st[:, 3, :].unsqueeze(1).broadcast_to([128, 4, 128]),
                                              op=ALU.mult), reads=[b_z, b_st], writes=[b_z])
        yield
        for c in range(4):
            S.op("dve", lambda e, c=c: e.tensor_scalar(out=z[:, c, :], in0=z[:, c, :], scalar1=pp[:, PP_LNG + c:PP_LNG + c + 1],
                                                       scalar2=pp[:, PP_LNB + c:PP_LNB + c + 1], op0=ALU.mult, op1=ALU.add),
                 reads=[b_z, b_pp], writes=[b_z])
            yield
        S.op("act", lambda e: e.activation(out=z, in_=z, func=AF.Silu), reads=[b_z], writes=[b_z])
        yield
        S.op("act", lambda e: e.activation(out=csq, in_=z, func=AF.Square), reads=[b_z], writes=[b_csq])
        yield
        for c in range(4):
            S.op("pe", lambda e, c=c: e.matmul(pS[:, 0, :], lhsT=onesf, rhs=csq[:, c, :], start=(c == 0), stop=(c == 3)),
                 reads=[b_onesf, b_csq], writes=[b_pS])
            yield
        S.op("act", lambda e: e.activation(out=st[:, 0, :], in_=pS[:, 0, :], func=AF.Sqrt, bias=EPS, scale=1.0 / 512),
             reads=[b_pS], writes=[b_st])
        yield
        S.op("dve", lambda e: e.reciprocal(out=st[:, 0, :], in_=st[:, 0, :]), reads=[b_st], writes=[b_st])
        yield
        S.op("dve", lambda e: e.tensor_tensor(out=ycn_s, in0=z, in1=st[:, 0, :].unsqueeze(1).broadcast_to([128, 4, 128]),
                                              op=ALU.mult), reads=[b_z, b_st], writes=[b_ycn_s])
        yield

    def m_attn(sl, s, filler, rate):
        i = s - 2
        qr = s % QRING
        x = xr[i % 2]; bx = b_xr[i % 2]
        S.dma("sp", lambda e: e.dma_start(out=x, in_=xs_tiles[sl][s]), writes=[bx])
        kts = key_tiles(s, NQ)
        units = []
        for grp in range(2):
            for hh in range(4):
                for n_, kt in enumerate(kts):
                    units.append((grp, hh, n_, kt))
        LOOK = 2
        NSLOT = 3
        info = {}

        def emit_st(u):
            grp, hh, n_, kt = units[u]
            h = grp * 4 + hh
            c = h // 2
            qm = (qA if h % 2 == 0 else qB)[qr]
            bqm = (b_qA if h % 2 == 0 else b_qB)[qr]
            kr = kt % RING
            m = cnt[0] % NSLOT; cnt[0] += 1
            info[u] = m
            S.op("pe", lambda e: e.matmul(pMs[m], lhsT=kT[kr][:, c, :], rhs=qm[:, c, :], start=True, stop=True),
                 reads=[b_kT[kr], bqm], writes=[b_pMs[m]])

        def emit_rest(u):
            grp, hh, n_, kt = units[u]
            h = grp * 4 + hh
            kr = kt % RING
            m = info[u]
            t_ = u % NTMP
            j0 = 8 - 2 * (kt - s)
            ti = ntile_idx[(s, kt)]
            S.op("dve", lambda e: e.tensor_tensor(
                out=tmp[t_], in0=pMs[m], in1=T[:, h, j0:j0 + 2, :].rearrange("p a b -> p (a b)"), op=ALU.add),
                 reads=[b_pMs[m], b_T], writes=[b_tmp[t_]])
            for qh in range(2):
                S.op("act", lambda e, qh=qh: e.activation(
                    out=PT[t_][:, qh * 64:(qh + 1) * 64], in_=tmp[t_][:, qh * 64:(qh + 1) * 64], func=AF.Exp,
                    bias=rmask[:, sl, ti, qh:qh + 1]), reads=[b_tmp[t_], b_rmask], writes=[b_PT[t_]])
            S.op("pe", lambda e: e.matmul(pO[:, hh, 0:65], lhsT=PT[t_], rhs=va[kr][:, h, :],
                                          start=(n_ == 0), stop=(n_ == len(kts) - 1)),
                 reads=[b_PT[t_], b_va[kr]], writes=[b_pO])
            if hh == 3 and n_ == len(kts) - 1:
                S.op("dve", lambda e: e.reciprocal(out=small[:, 4:8], in_=pO[:, :, 64]), reads=[b_pO], writes=[b_small])
                S.op("dve", lambda e: e.tensor_tensor(
                    out=ya[:, grp * 4:(grp + 1) * 4, :], in0=pO[:, :, 0:64],
                    in1=small[:, 4:8].unsqueeze(2).broadcast_to([128, 4, 64]), op=ALU.mult),
                     reads=[b_pO, b_small], writes=[b_ya])

        for u in range(min(LOOK, len(units))):
            emit_st(u)
        for u in range(len(units)):
            emit_rest(u)
            if u + LOOK < len(units):
                emit_st(u + LOOK)
            if filler is not None:
                for _ in range(rate):
                    next(filler, None)
        if filler is not None:
            for _ in filler:
                pass
        yaf = ya.rearrange("p h d -> p (h d)")
        S.op("act", lambda e: e.activation(out=junk[:, 0:512], in_=yaf, func=AF.Square, accum_out=small[:, 8:9]),
             reads=[b_ya], writes=[b_junk, b_small])
        S.op("act", lambda e: e.activation(out=small[:, 9:10], in_=small[:, 8:9], func=AF.Sqrt, bias=EPS, scale=1.0 / 512),
             reads=[b_small], writes=[b_small])
        S.op("dve", lambda e: e.reciprocal(out=small[:, 10:11], in_=small[:, 9:10]), reads=[b_small], writes=[b_small])
        S.op("dve", lambda e: e.tensor_scalar(out=yan, in0=yaf, scalar1=small[:, 10:11], scalar2=None, op0=ALU.mult),
             reads=[b_ya, b_small], writes=[b_yan])
        for c in range(4):
            S.op("pe", lambda e, c=c: e.transpose(out=pTv[:, c, :], in_=yan[:, c * 128:(c + 1) * 128], identity=ident),
                 reads=[b_yan, b_ident], writes=[b_pT])
        S.op("act", lambda e: e.copy(out=yaT, in_=pTv[:, 0:4, :]), reads=[b_pT], writes=[b_yaT])

    def m_out(sl, s):
        i = s - 2
        x = xr[i % 2]; bx = b_xr[i % 2]
        ycn_s = ycn2[s % 2]; b_ycn_s = b_ycn2[s % 2]
        for hf in range(2):
            for c in range(8):
                lhs = ycn_s[:, c, :] if c < 4 else yaT[:, c - 4, :]
                S.op("pe", lambda e, c=c, hf=hf, lhs=lhs: e.matmul(pY[:, hf * 512:(hf + 1) * 512], lhsT=lhs,
                                                                   rhs=wout[:, c, hf * 512:(hf + 1) * 512],
                                                                   start=(c == 0), stop=(c == 7)),
                     reads=[b_ycn_s, b_yaT, b_wout], writes=[b_pS])
        S.op("dve", lambda e: e.tensor_tensor(out=x, in0=pY, in1=x, op=ALU.add), reads=[b_pS, bx], writes=[bx])
        S.dma("sp", lambda e: e.dma_start(out=x1_tiles[sl][i], in_=x), reads=[bx])


    def chain(*gens):
        for g in gens:
            if g is not None:
                for _ in g:
                    yield

    for sl in range(nslab):
        for _ in stage_p_gen(sl, 0):
            pass
        for step in range(NS + 3):
            gp = stage_p_gen(sl, step + 1) if step + 1 < NS else None
            sc = step - 2
            gc = m_conv_gen(sl, sc) if 2 <= sc < NQ + 2 else None
            g = chain(gc, gp)
            s = step - 3
            if 2 <= s < NQ + 2:
                m_attn(sl, s, g, 8)
            for _ in g:
                pass
            if 2 <= s < NQ + 2:
                m_out(sl, s)


def row_mask_table(NQ, kind, q=0, R=None):
    idx = {}
    for s in range(2, NQ + 2):
        for kt in key_tiles(s, NQ):
            idx[(s, kt)] = len(idx)
    out = np.full((128, len(idx), 2), NEGM, np.float32)
    nrows = 2 * NQ
    if kind == "full":
        R = nrows; base = 0
    else:
        base = q
    for (s, kt), ti in idx.items():
        for qh in range(2):
            r = base + 2 * (s - 2) + qh
            rs = min(max(r - 4, 0), R - 8)
            for kh in range(2):
                rk = base + 2 * kt + kh - 4
                if rs <= rk < rs + 8:
                    out[kh * 64:(kh + 1) * 64, ti, qh] = 0.0
    return out


def bias_table(rpb):
    T = np.full((128, 8, 16, 64), NEGM, np.float32)
    cq = np.arange(64)
    cs = np.clip(cq - 8, 0, 48)
    for kh in range(2):
        for j in range(16):
            dr = kh - j + 8
            if abs(dr) > 7:
                continue
            for cp in range(64):
                ok = (cp >= cs) & (cp < cs + 16)
                off = np.clip(cp - cq + 15, 0, 30)
                vals = rpb[:, dr + 7, :][:, off]
                T[kh * 64 + cp, :, j, :] = np.where(ok[None, :], vals, NEGM)
    return T


def small_params(g_mix, g_out_conv, g_out_attn, conv_w, conv_b, ln_g, ln_b, q_g, k_g):
    pp = np.zeros((128, PP_N), np.float32)
    pp[:, PP_GMIX:PP_GMIX + 8] = g_mix.reshape(8, 128).T
    pp[:, PP_GOUT:PP_GOUT + 4] = g_out_conv.reshape(4, 128).T
    pp[:, PP_GOUT + 4:PP_GOUT + 8] = g_out_attn.reshape(4, 128).T
    pp[:, PP_CB:PP_CB + 4] = conv_b.reshape(4, 128).T
    pp[:, PP_LNG:PP_LNG + 4] = ln_g.reshape(4, 128).T
    pp[:, PP_LNB:PP_LNB + 4] = ln_b.reshape(4, 128).T
    pp[:, PP_GQ] = np.tile(q_g, 2)
    pp[:, PP_GK] = np.tile(k_g, 2)
    pp[:, PP_CW:PP_CW + 124] = conv_w.T.reshape(4, 128, 31).transpose(1, 0, 2).reshape(128, 124)
    return pp


NQ_FULL = 32
N_CORES = 8
ARENA_WORDS = 51200


def build_program(NQ=NQ_FULL):
    NS = NQ + 4
    nc = bass.Bass("TRN2", target_bir_lowering=False)
    xs = nc.dram_tensor("xs", [2, NS * 128, D], F32, kind="ExternalInput").ap()
    y = nc.dram_tensor("y", [2, NQ * 128, D], F32, kind="ExternalOutput").ap()
    nti = sum(len(key_tiles(s, NQ)) for s in range(2, NQ + 2))
    C = {}
    for name, shape in (("consts", [128, 144]), ("pp", [128, PP_N]), ("ttab", [128, 8 * 16 * 64]),
                        ("rmask", [128, 2 * nti * 2]), ("w_in", [D, 2560]), ("w_out", [D, D]),
                        ("g_ffn", [D]), ("w_query", [D, 2048]), ("sub_keys", [16, 128, 128]),
                        ("expert_u", [NEXP, D]), ("expert_v", [NEXP, D])):
        C[name] = nc.dram_tensor(name, shape, F32, kind="ExternalInput").ap()
    S = Sched(nc)
    sbh = nc.alloc_sbuf_tensor("arena", [128, ARENA_WORDS], F32)
    psh = nc.alloc_psum_tensor("parena", [128, 4096], F32)
    uv16 = nc.dram_tensor("uv16", [NEXP, 2048], BF16, kind="Internal").ap()
    b_uv16 = Buf()
    convert_tables(S, C, uv16, b_uv16)
    xst = xs.rearrange("a (n p) d -> a n p d", p=128)
    yt = y.rearrange("a (n p) d -> a n p d", p=128)
    sb = Arena(sbh, ARENA_WORDS); ps = Arena(psh, 4096)
    phase_a(S, nc, sb, ps, C, NQ, [[xst[a, i] for i in range(NS)] for a in range(2)],
            [[yt[a, i] for i in range(NQ)] for a in range(2)])
    S.barrier()
    sb = Arena(sbh, ARENA_WORDS); ps = Arena(psh, 4096)
    ytl = [yt[a, i] for a in range(2) for i in range(NQ)]
    phase_b3(S, nc, sb, ps, C, ytl, ytl, uv16, b_uv16)
    S.emit()
    return nc, S


def kernel(x_prompt, x_sample, g_mix, w_in, conv_w, conv_b, conv_ln_g, conv_ln_b,
           q_norm_g, k_norm_g, rpb, g_out_conv, g_out_attn, w_out, g_ffn,
           w_query, sub_keys, expert_u, expert_v):
    f = lambda a: np.ascontiguousarray(np.asarray(a, dtype=np.float32))
    x_prompt, x_sample = f(x_prompt), f(x_sample)
    NQ = NQ_FULL
    NS = NQ + 4
    T = NQ * 128
    H = 256
    shared = {
        "consts": make_consts(),
        "pp": small_params(f(g_mix)[0], f(g_out_conv)[0], f(g_out_attn)[0], f(conv_w)[0], f(conv_b)[0],
                           f(conv_ln_g)[0], f(conv_ln_b)[0], f(q_norm_g)[0], f(k_norm_g)[0]),
        "ttab": bias_table(f(rpb)[0]).reshape(128, -1),
        "w_in": f(w_in)[0], "w_out": f(w_out)[0], "g_ffn": f(g_ffn)[0], "w_query": f(w_query)[0],
        "sub_keys": f(sub_keys)[0].reshape(16, 128, 128),
        "expert_u": f(expert_u)[0], "expert_v": f(expert_v)[0],
    }
    rm_full = row_mask_table(NQ, "full")
    in_maps = []
    for c in range(N_CORES):
        b, q = c // 4, c % 4
        xs = np.zeros((2, NS * 128, D), np.float32)
        xs[0, H:H + T] = x_sample[c]
        lo, hi = q * T - H, q * T + T + H
        clo, chi = max(lo, 0), min(hi, x_prompt.shape[1])
        xs[1, clo - lo:clo - lo + (chi - clo)] = x_prompt[b, clo:chi]
        rm = np.stack([rm_full, row_mask_table(NQ, "chunk", q=2 * NQ * q, R=x_prompt.shape[1] // 64)], axis=1)
        m = dict(shared)
        m["xs"] = xs
        m["rmask"] = np.ascontiguousarray(rm.reshape(128, -1))
        in_maps.append(m)
    nc, _ = build_program(NQ)
    res = run_bass_kernel_spmd(nc, in_maps, core_ids=list(range(N_CORES)))
    y_prompt = np.zeros_like(x_prompt)
    y_sample = np.zeros_like(x_sample)
    for c in range(N_CORES):
        yc = np.asarray(res.results[c]["y"], dtype=np.float32)
        y_sample[c] = yc[0]
        y_prompt[c // 4, (c % 4) * T:(c % 4 + 1) * T] = yc[1]
    return (y_prompt, y_sample)
```
